# Optimizing a Trainium2 kernel written in Bass

```python
import jax, jax.numpy as jnp
from jax import lax
import numpy as np

D_MODEL = 1024
BATCH = 8
SEQ = 4096
DEPTH = 2

D_MIX = D_MODEL
N_MIXERS = 4
G_WIDTH = D_MIX // N_MIXERS
HEAD_DIM = 64
N_GROUP_HEADS = G_WIDTH // HEAD_DIM
D_FF = 2816
RMS_EPS = 1e-6
POOL_WINDOWS = (2, 4, 8, 16)
POOL_CH = G_WIDTH // len(POOL_WINDOWS)
RWKV_HEADS = N_GROUP_HEADS
RWKV_N = HEAD_DIM
RWKV_W_RANK = 64
RWKV_A_RANK = 32
RWKV_G_RANK = 64
RWKV_SIZES = (G_WIDTH, G_WIDTH, G_WIDTH, RWKV_W_RANK, RWKV_A_RANK, RWKV_G_RANK)
RWKV_COLS = sum(RWKV_SIZES)
RWKV_LN_EPS = 64e-5
NSA_HEADS = N_GROUP_HEADS
NSA_CMP_LEN = 32
NSA_CMP_STRIDE = 16
NSA_CMP_HIDDEN = 128
NSA_SEL_BLOCK = 64
NSA_TOP_N = 16
NSA_WINDOW = 512
NSA_QBLK = 64
NSA_FORCE_BONUS = 1e4
NSA_NEG = -1e9
NSA_SIZES = (G_WIDTH,) + (HEAD_DIM,) * 6 + (3 * NSA_HEADS,)
NSA_COLS = sum(NSA_SIZES)
CONV_WIDTH = 3
CONV_COLS = 3 * G_WIDTH
P_SIZES = (G_WIDTH, RWKV_COLS, NSA_COLS, CONV_COLS)
P_TOTAL = sum(P_SIZES)

kernel_name = "hybrid_parallel_heads_block"


def _splits(sizes):
    return [int(s) for s in np.cumsum(sizes)[:-1]]


def rms_norm(x, g, eps=RMS_EPS):
    xf = x.astype(jnp.float32)
    y = xf * lax.rsqrt(jnp.mean(xf * xf, axis=-1, keepdims=True) + eps)
    return (y * g.astype(jnp.float32)).astype(x.dtype)


def swiglu_ffn(x, g, w_gate, w_up, w_down):
    h = rms_norm(x, g)
    return (jax.nn.silu(h @ w_gate) * (h @ w_up)) @ w_down


def token_shift(u):
    return jnp.pad(u, ((0, 0), (1, 0), (0, 0)))[:, :-1]


def alibi_slopes(h):
    return 2.0 ** (-8.0 * jnp.arange(1, h + 1, dtype=jnp.float32) / h)


def pool_mixer(u, pool_w, pool_scale):
    B, S, _ = u.shape
    c = jnp.pad(jnp.cumsum(u.astype(jnp.float32), axis=1), ((0, 0), (1, 0), (0, 0)))
    t = jnp.arange(S)
    outs = []
    for gi, w in enumerate(POOL_WINDOWS):
        sl = slice(gi * POOL_CH, (gi + 1) * POOL_CH)
        lo = jnp.maximum(t + 1 - w, 0)
        total = c[:, 1:, sl] - c[:, lo, sl]
        cnt = (t + 1 - lo).astype(jnp.float32)[None, :, None]
        outs.append(total / cnt)
    pooled = jnp.stack(outs, axis=2).astype(u.dtype)
    ug = u.reshape(B, S, len(POOL_WINDOWS), POOL_CH)
    y = jnp.einsum('bsgc,gcd->bsgd', pooled - ug, pool_w).reshape(B, S, G_WIDTH)
    return y * pool_scale


def _rwkv_step(state, inp):
    r_t, k_t, v_t, w_t, kk_t, akk_t = inp
    sa = jnp.einsum('bhvk,bhk->bhv', state, kk_t)
    state = (state * w_t[:, :, None, :]
             - sa[..., None] * akk_t[:, :, None, :]
             + v_t[..., None] * k_t[:, :, None, :])
    y = jnp.einsum('bhvk,bhk->bhv', state, r_t)
    return state, y


def rwkv7_mixer(p, mu, w0, w_up, a0, a_up, g_up, k_k, k_a, r_k, ln_w, ln_b):
    B, S, _ = p.shape
    H, N = RWKV_HEADS, RWKV_N
    p = p + mu * (token_shift(p) - p)
    r, k, v, wd, ad, gd = jnp.split(p, _splits(RWKV_SIZES), axis=-1)
    w = (w0 + jnp.tanh(wd) @ w_up).astype(jnp.float32)
    decay = jnp.exp(-jnp.exp(-jax.nn.softplus(-w) - 0.5))
    a = jax.nn.sigmoid(a0 + ad @ a_up)
    g = jax.nn.sigmoid(gd) @ g_up
    kk = (k * k_k).reshape(B, S, H, N).astype(jnp.float32)
    kk = kk * lax.rsqrt(jnp.maximum(jnp.sum(kk * kk, axis=-1, keepdims=True), 1e-24))
    k = k * (1 + (a - 1) * k_a)

    def heads(z):
        return z.reshape(B, S, H, N).astype(jnp.float32)

    r_h, k_h, v_h, a_h = heads(r), heads(k), heads(v), heads(a)
    xs = tuple(jnp.swapaxes(z, 0, 1) for z in (r_h, k_h, v_h, heads(decay), kk, kk * a_h))
    state0 = jnp.zeros((B, H, N, N), jnp.float32)
    _, ys = lax.scan(_rwkv_step, state0, xs)
    y = jnp.swapaxes(ys, 0, 1)
    mean = jnp.mean(y, axis=-1, keepdims=True)
    var = jnp.mean(jnp.square(y - mean), axis=-1, keepdims=True)
    y = ((y - mean) * lax.rsqrt(var + RWKV_LN_EPS)).reshape(B, S, G_WIDTH) * ln_w + ln_b
    bonus = jnp.sum(r_h * k_h * r_k, axis=-1, keepdims=True) * v_h
    out = (y + bonus.reshape(B, S, G_WIDTH)) * g
    return out.astype(p.dtype)


def nsa_mixer(p, q_norm_w, k_norm_w, cmp_pos, cmp_k_w1, cmp_k_w2, cmp_v_w1, cmp_v_w2):
    B, S, _ = p.shape
    H, Dh = NSA_HEADS, HEAD_DIM
    q, kc, vc, ksl, vsl, kwn, vwn, gates = jnp.split(p, _splits(NSA_SIZES), axis=-1)
    q = rms_norm(q.reshape(B, S, H, Dh), q_norm_w)
    scale = Dh ** -0.5
    slopes = alibi_slopes(H)
    t = jnp.arange(S)

    n_cmp = (S - NSA_CMP_LEN) // NSA_CMP_STRIDE + 1
    cmp_start = jnp.arange(n_cmp) * NSA_CMP_STRIDE
    idx = cmp_start[:, None] + jnp.arange(NSA_CMP_LEN)[None]

    def compress(u, w1, w2):
        blocks = u[:, idx] + cmp_pos
        return jax.nn.gelu(blocks.reshape(B, n_cmp, NSA_CMP_LEN * Dh) @ w1) @ w2

    k_cmp = rms_norm(compress(kc, cmp_k_w1, cmp_k_w2), k_norm_w[0])
    v_cmp = compress(vc, cmp_v_w1, cmp_v_w2)
    blk_end = cmp_start + NSA_CMP_LEN - 1
    dist = (t[:, None] - blk_end[None]).astype(jnp.float32)
    valid = dist >= 0
    s = jnp.einsum('bshd,bnd->bhsn', q, k_cmp).astype(jnp.float32) * scale - slopes[:, None, None] * dist
    s = jnp.where(valid, s, NSA_NEG)
    p_cmp = jax.nn.softmax(s, axis=-1) * valid
    o_cmp = jnp.einsum('bhsn,bnd->bshd', p_cmp.astype(v_cmp.dtype), v_cmp)

    n_sel = S // NSA_SEL_BLOCK
    n_top = min(NSA_TOP_N, n_sel)
    sel_start = jnp.arange(n_sel) * NSA_SEL_BLOCK
    overlap = ((cmp_start[:, None] <= sel_start[None] + NSA_SEL_BLOCK - 1)
               & (blk_end[:, None] >= sel_start[None])).astype(jnp.float32)
    imp = jnp.einsum('bhsn,nj->bsj', p_cmp, overlap)
    cur = t // NSA_SEL_BLOCK
    j = jnp.arange(n_sel)
    sel_ok = j[None] <= cur[:, None]
    forced = (j[None] == 0) | (j[None] == cur[:, None]) | (j[None] == cur[:, None] - 1)
    imp = jnp.where(sel_ok, imp + jnp.where(forced, NSA_FORCE_BONUS, 0.0), -1.0)
    _, sel_idx = lax.top_k(imp, n_top)

    k_sel_blocks = rms_norm(ksl, k_norm_w[1]).reshape(B, n_sel, NSA_SEL_BLOCK, Dh)
    v_sel_blocks = vsl.reshape(B, n_sel, NSA_SEL_BLOCK, Dh)
    pad = ((0, 0), (NSA_WINDOW, 0), (0, 0))
    k_win_p = jnp.pad(rms_norm(kwn, k_norm_w[2]), pad)
    v_win_p = jnp.pad(vwn, pad)

    n_qb = S // NSA_QBLK
    q_b = q.reshape(B, n_qb, NSA_QBLK, H, Dh).transpose(1, 0, 2, 3, 4)
    idx_b = sel_idx.reshape(B, n_qb, NSA_QBLK, n_top).transpose(1, 0, 2, 3)
    win_len = NSA_WINDOW + NSA_QBLK

    def block_fn(args):
        i, q_i, idx_i = args
        t_i = i * NSA_QBLK + jnp.arange(NSA_QBLK)
        k_g = jax.vmap(lambda kb, ib: kb[ib])(k_sel_blocks, idx_i)
        v_g = jax.vmap(lambda vb, ib: vb[ib])(v_sel_blocks, idx_i)
        pos = idx_i[..., None] * NSA_SEL_BLOCK + jnp.arange(NSA_SEL_BLOCK)
        d = (t_i[None, :, None, None] - pos).astype(jnp.float32)
        ss = (jnp.einsum('bqhd,bqnkd->bhqnk', q_i, k_g).astype(jnp.float32) * scale
              - slopes[None, :, None, None, None] * d[:, None])
        ss = jnp.where((d >= 0)[:, None], ss, NSA_NEG).reshape(B, H, NSA_QBLK, n_top * NSA_SEL_BLOCK)
        ps = jax.nn.softmax(ss, axis=-1).reshape(B, H, NSA_QBLK, n_top, NSA_SEL_BLOCK)
        o_sel = jnp.einsum('bhqnk,bqnkd->bqhd', ps.astype(v_g.dtype), v_g)
        kw_i = lax.dynamic_slice_in_dim(k_win_p, i * NSA_QBLK, win_len, axis=1)
        vw_i = lax.dynamic_slice_in_dim(v_win_p, i * NSA_QBLK, win_len, axis=1)
        s_pos = i * NSA_QBLK - NSA_WINDOW + jnp.arange(win_len)
        dw = (t_i[:, None] - s_pos[None]).astype(jnp.float32)
        okw = (dw >= 0) & (dw < NSA_WINDOW) & (s_pos[None] >= 0)
        sw = (jnp.einsum('bqhd,bkd->bhqk', q_i, kw_i).astype(jnp.float32) * scale
              - slopes[None, :, None, None] * dw[None, None])
        sw = jnp.where(okw, sw, NSA_NEG)
        pw = jax.nn.softmax(sw, axis=-1)
        o_win = jnp.einsum('bhqk,bkd->bqhd', pw.astype(vw_i.dtype), vw_i)
        return o_sel, o_win

    o_sel, o_win = lax.map(block_fn, (jnp.arange(n_qb), q_b, idx_b))
    o_sel = o_sel.transpose(1, 0, 2, 3, 4).reshape(B, S, H, Dh)
    o_win = o_win.transpose(1, 0, 2, 3, 4).reshape(B, S, H, Dh)
    gt = jax.nn.sigmoid(gates).reshape(B, S, H, 3)
    out = gt[..., 0:1] * o_cmp + gt[..., 1:2] * o_sel + gt[..., 2:3] * o_win
    return out.reshape(B, S, G_WIDTH)


def short_conv_mixer(p, conv_w):
    u, b, c = jnp.split(p, 3, axis=-1)
    z = c * u
    S = z.shape[1]
    zp = jnp.pad(z, ((0, 0), (CONV_WIDTH - 1, 0), (0, 0)))
    y = conv_w[0] * zp[:, 0:S]
    for jw in range(1, CONV_WIDTH):
        y = y + conv_w[jw] * zp[:, jw:jw + S]
    return b * y


def setup_inputs(seed: int = 0) -> dict:
    key = jax.random.key(seed)
    keys = iter(jax.random.split(key, 48))
    f32 = jnp.float32

    def nrm(shape, scale):
        return scale * jax.random.normal(next(keys), shape, f32)

    def gain(shape):
        return 1.0 + nrm(shape, 0.02)

    def unif(shape, lo, hi):
        return jax.random.uniform(next(keys), shape, f32, lo, hi)

    L, D, F, G, Dh = DEPTH, D_MODEL, D_FF, G_WIDTH, HEAD_DIM
    cmp_in = NSA_CMP_LEN * Dh
    return {
        "x": nrm((BATCH, SEQ, D), 1.0),
        "ffn1_norm": gain((L, D)),
        "ffn1_w_gate": nrm((L, D, F), D ** -0.5),
        "ffn1_w_up": nrm((L, D, F), D ** -0.5),
        "ffn1_w_down": nrm((L, F, D), F ** -0.5),
        "mix_norm": gain((L, D)),
        "w_in": nrm((L, D, P_TOTAL), D ** -0.5),
        "pool_w": nrm((L, len(POOL_WINDOWS), POOL_CH, POOL_CH), POOL_CH ** -0.5),
        "pool_scale": 1.0 + nrm((L, G), 0.1),
        "rwkv_mu": unif((L, RWKV_COLS), 0.0, 1.0),
        "rwkv_w0": unif((L, G), -6.0, -1.0),
        "rwkv_w_up": nrm((L, RWKV_W_RANK, G), 0.1 * RWKV_W_RANK ** -0.5),
        "rwkv_a0": nrm((L, G), 0.3),
        "rwkv_a_up": nrm((L, RWKV_A_RANK, G), 0.1 * RWKV_A_RANK ** -0.5),
        "rwkv_g_up": nrm((L, RWKV_G_RANK, G), RWKV_G_RANK ** -0.5),
        "rwkv_k_k": 0.85 + nrm((L, G), 0.1),
        "rwkv_k_a": 1.0 + nrm((L, G), 0.1),
        "rwkv_r_k": nrm((L, RWKV_HEADS, RWKV_N), 0.1),
        "rwkv_ln_w": gain((L, G)),
        "rwkv_ln_b": nrm((L, G), 0.02),
        "nsa_q_norm": gain((L, Dh)),
        "nsa_k_norm": gain((L, 3, Dh)),
        "nsa_cmp_pos": nrm((L, NSA_CMP_LEN, Dh), 0.1),
        "nsa_cmp_k_w1": nrm((L, cmp_in, NSA_CMP_HIDDEN), cmp_in ** -0.5),
        "nsa_cmp_k_w2": nrm((L, NSA_CMP_HIDDEN, Dh), NSA_CMP_HIDDEN ** -0.5),
        "nsa_cmp_v_w1": nrm((L, cmp_in, NSA_CMP_HIDDEN), cmp_in ** -0.5),
        "nsa_cmp_v_w2": nrm((L, NSA_CMP_HIDDEN, Dh), NSA_CMP_HIDDEN ** -0.5),
        "conv_w": nrm((L, CONV_WIDTH, G), CONV_WIDTH ** -0.5),
        "w_out": nrm((L, D_MIX, D), D_MIX ** -0.5),
        "ffn2_norm": gain((L, D)),
        "ffn2_w_gate": nrm((L, D, F), D ** -0.5),
        "ffn2_w_up": nrm((L, D, F), D ** -0.5),
        "ffn2_w_down": nrm((L, F, D), F ** -0.5),
    }


def reference(x, ffn1_norm, ffn1_w_gate, ffn1_w_up, ffn1_w_down, mix_norm, w_in,
              pool_w, pool_scale,
              rwkv_mu, rwkv_w0, rwkv_w_up, rwkv_a0, rwkv_a_up, rwkv_g_up, rwkv_k_k, rwkv_k_a,
              rwkv_r_k, rwkv_ln_w, rwkv_ln_b,
              nsa_q_norm, nsa_k_norm, nsa_cmp_pos, nsa_cmp_k_w1, nsa_cmp_k_w2, nsa_cmp_v_w1,
              nsa_cmp_v_w2,
              conv_w, w_out, ffn2_norm, ffn2_w_gate, ffn2_w_up, ffn2_w_down):
    for l in range(DEPTH):
        x = x + 0.5 * swiglu_ffn(x, ffn1_norm[l], ffn1_w_gate[l], ffn1_w_up[l], ffn1_w_down[l])
        h = rms_norm(x, mix_norm[l])
        p = h @ w_in[l]
        p_a, p_b, p_c, p_d = jnp.split(p, _splits(P_SIZES), axis=-1)
        y_a = pool_mixer(p_a, pool_w[l], pool_scale[l])
        y_b = rwkv7_mixer(p_b, rwkv_mu[l], rwkv_w0[l], rwkv_w_up[l], rwkv_a0[l], rwkv_a_up[l],
                          rwkv_g_up[l], rwkv_k_k[l], rwkv_k_a[l], rwkv_r_k[l], rwkv_ln_w[l],
                          rwkv_ln_b[l])
        y_c = nsa_mixer(p_c, nsa_q_norm[l], nsa_k_norm[l], nsa_cmp_pos[l], nsa_cmp_k_w1[l],
                        nsa_cmp_k_w2[l], nsa_cmp_v_w1[l], nsa_cmp_v_w2[l])
        y_d = short_conv_mixer(p_d, conv_w[l])
        y = jnp.concatenate([y_a, y_b.astype(y_a.dtype), y_c.astype(y_a.dtype), y_d], axis=-1)
        x = x + y @ w_out[l]
        x = x + 0.5 * swiglu_ffn(x, ffn2_norm[l], ffn2_w_gate[l], ffn2_w_up[l], ffn2_w_down[l])
    return x
```

```python
import numpy as np
from contextlib import ExitStack
import concourse.bass as bass
import concourse.mybir as mybir
from concourse.bass_utils import run_bass_kernel_spmd

F32 = mybir.dt.float32
BF16 = mybir.dt.bfloat16
AF = mybir.ActivationFunctionType
ALU = mybir.AluOpType
AX = mybir.AxisListType


class Buf:
    def __init__(self, name, t, space):
        self.name = name
        self.t = t
        self.space = space

    def __getitem__(self, idx):
        return self.t[idx]


def _norm(lst):
    out = []
    for x in lst:
        if isinstance(x, tuple):
            out.append(x)
        else:
            out.append((x, None))
    return out


def _conf(a, b):
    return a is None or b is None or a == b


class Builder:
    ENG = ("pe", "act", "dve", "pool", "sp")
    NDMA = 12

    def __init__(self):
        self.nc = bass.Bass("TRN2", target_bir_lowering=False)
        self.gstack = ExitStack()
        self.sems = []
        self.ekey = {}
        for e in ("pe", "act", "dve", "pool"):
            self.ekey[e] = self._newsem("c_" + e)
        self.dkeys = {q: [self._newsem(f"d_{q}{i}") for i in range(self.NDMA)] for q in ("sp", "pool", "act")}
        self.drr = {"sp": 0, "pool": 0, "act": 0}
        self.semval = [0] * len(self.sems)
        self.seen = {e: {} for e in self.ENG}
        self.recs = {}
        self.ops = {e: [] for e in self.ENG}
        self.pstack = None
        self.nphase = 0
        self.uid = 0

    def _newsem(self, name):
        s = self.gstack.enter_context(self.nc.semaphore(name))
        self.sems.append(s)
        return len(self.sems) - 1

    def dram(self, name, shape, dtype, kind="Internal"):
        t = self.nc.dram_tensor(name, list(shape), dtype, kind=kind)
        return Buf(name, t.ap(), "dram")

    def begin(self):
        assert self.pstack is None
        self.pstack = ExitStack()
        self.nphase += 1

    def sb(self, name, shape, dtype):
        self.uid += 1
        nm = f"{name}_{self.uid}"
        t = self.pstack.enter_context(self.nc.sbuf_tensor(nm, list(shape), dtype))
        return Buf(nm, t, "sbuf")

    def ps(self, name, shape, dtype=F32):
        self.uid += 1
        nm = f"{name}_{self.uid}"
        t = self.pstack.enter_context(self.nc.psum_tensor(nm, list(shape), dtype))
        return Buf(nm, t, "psum")

    def op(self, eng, fn, reads=(), writes=(), dma=False, acc=False):
        reads = _norm(reads)
        writes = _norm(writes)
        if not acc:
            for (bb, tt) in reads:
                if bb.space == "psum" and (bb, tt) not in writes:
                    writes = writes + [(bb, tt)]
        waits = {}

        def need(k, v):
            if v > waits.get(k, 0):
                waits[k] = v

        for (b, t) in _norm(reads):
            for (tag, isw, k), v in self.recs.get(b.name, {}).items():
                if isw and _conf(tag, t):
                    need(k, v)
        for (b, t) in _norm(writes):
            for (tag, isw, k), v in self.recs.get(b.name, {}).items():
                if _conf(tag, t):
                    if acc and isw and k == self.ekey["pe"]:
                        continue
                    need(k, v)
        if dma:
            q = eng
            kd = self.dkeys[q][self.drr[q] % self.NDMA]
            self.drr[q] += 1
            need(kd, self.semval[kd])
            self.semval[kd] += 16
            tok = (kd, self.semval[kd])
            inc = 16
        else:
            kd = self.ekey[eng]
            self.semval[kd] += 1
            tok = (kd, self.semval[kd])
            inc = 1
        wl = []
        for k, v in waits.items():
            if v > 0 and self.seen[eng].get(k, 0) < v:
                self.seen[eng][k] = v
                wl.append((k, v))
        self.ops[eng].append((wl, fn, tok[0], inc))
        for (b, t) in _norm(reads):
            r = self.recs.setdefault(b.name, {})
            key = (t, False, tok[0])
            r[key] = max(r.get(key, 0), tok[1])
        for (b, t) in _norm(writes):
            r = self.recs.setdefault(b.name, {})
            for key in list(r.keys()):
                if t is None or key[0] == t:
                    del r[key]
            r[(t, True, tok[0])] = tok[1]
        return tok

    def end(self):
        wl = []
        for q in self.dkeys:
            for kd in self.dkeys[q]:
                v = self.semval[kd]
                if v > 0 and self.seen["sp"].get(kd, 0) < v:
                    self.seen["sp"][kd] = v
                    wl.append((kd, v))
        self.ops["sp"].append((wl, None, None, 0))
        nc = self.nc
        ops = self.ops
        sems = self.sems

        def replay(lst, e):
            for wl_, fn, k, inc in lst:
                for kk, v in wl_:
                    e.wait_ge(sems[kk], v)
                if fn is not None:
                    ins = fn(e)
                    ins.then_inc(sems[k], inc)

        with nc.Block() as block:
            @block.tensor
            def _(e):
                replay(ops["pe"], e)

            @block.scalar
            def _(e):
                replay(ops["act"], e)

            @block.vector
            def _(e):
                replay(ops["dve"], e)

            @block.gpsimd
            def _(e):
                replay(ops["pool"], e)

            @block.sync
            def _(e):
                replay(ops["sp"], e)
        self.ops = {e: [] for e in self.ENG}
        self.pstack.close()
        self.pstack = None
        for name in list(self.recs.keys()):
            pass

    def dma(self, out, in_, reads, writes, q="sp", **kw):
        return self.op(q, lambda e: e.dma_start(out=out, in_=in_, **kw), reads, writes, dma=True)

    def mm(self, out, lhsT, rhs, start, stop, reads, writes):
        return self.op("pe", lambda e: e.matmul(out, lhsT, rhs, start=start, stop=stop), reads, writes, acc=True)

    def tr(self, out, in_, ident, reads, writes):
        return self.op("pe", lambda e: e.transpose(out, in_, ident), reads, writes, acc=True)

    def actf(self, out, in_, func, reads, writes, bias=None, scale=1.0, eng="act"):
        kw = {}
        if bias is not None:
            kw["bias"] = bias
        return self.op(eng, lambda e: e.activation(out=out, in_=in_, func=func, scale=scale, **kw), reads, writes)


S = 4096
T = 1024
NT = S // T
NH = T // 512
EPS = 1e-6


def load_norm_h(b, xt_view_fn, x, g, ones, hT, sq, rstd, pst, tile):
    for k in range(8):
        s = sq[k % 2]
        b.op("act", lambda e, s=s, k=k: e.activation(out=s[:], in_=x[:, k, :], func=AF.Square), [x], [s])
        for h in range(NH):
            b.mm(pst[h][:], ones[:], s[:, h * 512:(h + 1) * 512], k == 0, k == 7, [ones, s], [pst[h]])
    for h in range(NH):
        sl = slice(h * 512, (h + 1) * 512)
        b.op("act", lambda e, h=h, sl=sl: e.activation(out=rstd[:, sl], in_=pst[h][:], func=AF.Sqrt, scale=1.0 / 1024, bias=EPS),
             [pst[h]], [(rstd, h)])
        b.op("dve", lambda e, sl=sl: e.reciprocal(out=rstd[:, sl], in_=rstd[:, sl]), [(rstd, h)], [(rstd, h)])
    for k in range(8):
        b.op("dve", lambda e, k=k: e.scalar_tensor_tensor(out=hT[:, k, :], in0=x[:, k, :], scalar=g[:, k:k + 1], in1=rstd[:],
                                                         op0=ALU.mult, op1=ALU.mult), [x, g, rstd], [(hT, k)])


def ffn_phase(b, xT, gD, WgD, WuD, WdD):
    b.begin()
    ones = b.sb("ones", [128, 128], BF16)
    b.op("pool", lambda e: e.memset(ones[:], 1.0), [], [ones])
    epsb = None
    g = b.sb("g", [128, 8], F32)
    b.dma(g[:], gD[:], [gD], [g])
    xs = [b.sb("x", [128, 8, T], F32) for _ in range(2)]
    hT = b.sb("hT", [128, 8, T], BF16)
    hid = b.sb("hid", [128, 22, T], BF16)
    sq = [b.sb("sq", [128, T], BF16) for _ in range(2)]
    rstd = b.sb("rstd", [128, T], F32)
    wg = [b.sb("wg", [128, 8, 128], BF16) for _ in range(2)]
    wu = [b.sb("wu", [128, 8, 128], BF16) for _ in range(2)]
    wd = [b.sb("wd", [128, 22, 128], BF16) for _ in range(2)]
    sg = [b.sb("sg", [128, 512], F32) for _ in range(2)]
    pg = [b.ps("pg", [128, 512]) for _ in range(2)]
    pu = [b.ps("pu", [128, 512]) for _ in range(2)]
    po = [b.ps("po", [128, 512]) for _ in range(2)]
    pst = [b.ps("pst", [128, 512]) for _ in range(2)]
    cnt = 0
    for ti in range(NT):
        x = xs[ti % 2]
        tsl = slice(ti * T, (ti + 1) * T)
        b.dma(x[:], xT[:, :, tsl].rearrange("k p t -> p k t"), [(xT, ti)], [x])
        load_norm_h(b, None, x, g, ones, hT, sq, rstd, pst, ti)
        for f in range(22):
            a, u = wg[f % 2], wu[f % 2]
            b.dma(a[:], WgD[f], [WgD], [a], q="pool", max_dma_last_dim=4096)
            b.dma(u[:], WuD[f], [WuD], [u], q="pool", max_dma_last_dim=4096)
            for h in range(NH):
                sl = slice(h * 512, (h + 1) * 512)
                G, U, SG = pg[cnt % 2], pu[cnt % 2], sg[cnt % 2]
                cnt += 1
                for k in range(8):
                    b.mm(G[:], a[:, k, :], hT[:, k, sl], k == 0, k == 7, [a, (hT, k)], [G])
                for k in range(8):
                    b.mm(U[:], u[:, k, :], hT[:, k, sl], k == 0, k == 7, [u, (hT, k)], [U])
                b.op("act", lambda e, G=G, SG=SG: e.activation(out=SG[:], in_=G[:], func=AF.Silu), [G], [SG])
                b.op("dve", lambda e, U=U, SG=SG, f=f, sl=sl: e.tensor_tensor(out=hid[:, f, sl], in0=U[:], in1=SG[:], op=ALU.mult),
                     [U, SG], [(hid, (f, h))])
        for d in range(8):
            w = wd[d % 2]
            b.dma(w[:], WdD[d], [WdD], [w], q="pool", max_dma_last_dim=4096)
            for h in range(NH):
                sl = slice(h * 512, (h + 1) * 512)
                O = po[cnt % 2]
                cnt += 1
                for f in range(22):
                    b.mm(O[:], w[:, f, :], hid[:, f, sl], f == 0, f == 21, [w, (hid, (f, h))], [O])
                b.op("dve", lambda e, O=O, d=d, sl=sl, x=x: e.scalar_tensor_tensor(out=x[:, d, sl], in0=O[:], scalar=0.5, in1=x[:, d, sl],
                                                                               op0=ALU.mult, op1=ALU.add), [O, x], [x])
        b.dma(xT[:, :, tsl].rearrange("k p t -> p k t"), x[:], [x], [(xT, ti)])
    b.end()


BLK_A = (0, 1)
BLK_R, BLK_K, BLK_V, BLK_WG, BLK_AD = (2, 3), (4, 5), (6, 7), 8, 9
BLK_Q, BLK_KVC, BLK_KVS, BLK_KVW, BLK_GT = (10, 11, 12, 13), 14, 15, 16, 17
BLK_U, BLK_B, BLK_C = (18, 19), (20, 21), (22, 23)
NBLK = 24


def inproj_phase(b, xT, gD, WinD, pT):
    b.begin()
    ones = b.sb("ones", [128, 128], BF16)
    b.op("pool", lambda e: e.memset(ones[:], 1.0), [], [ones])
    g = b.sb("g", [128, 8], F32)
    b.dma(g[:], gD[:], [gD], [g])
    xs = [b.sb("x", [128, 8, T], F32) for _ in range(2)]
    hT = b.sb("hT", [128, 8, T], BF16)
    sq = [b.sb("sq", [128, T], BF16) for _ in range(2)]
    rstd = b.sb("rstd", [128, T], F32)
    ws = [b.sb("w", [128, 8, 128], BF16) for _ in range(2)]
    os_ = [b.sb("o", [128, T], F32) for _ in range(2)]
    pp = [b.ps("pp", [128, 512]) for _ in range(4)]
    pst = [b.ps("pst", [128, 512]) for _ in range(2)]
    cnt = 0
    for ti in range(NT):
        x = xs[ti % 2]
        tsl = slice(ti * T, (ti + 1) * T)
        b.dma(x[:], xT[:, :, tsl].rearrange("k p t -> p k t"), [(xT, ti)], [x])
        load_norm_h(b, None, x, g, ones, hT, sq, rstd, pst, ti)
        for blk in range(NBLK):
            w = ws[blk % 2]
            o = os_[blk % 2]
            b.dma(w[:], WinD[blk], [WinD], [w], q="pool", max_dma_last_dim=4096)
            for h in range(NH):
                sl = slice(h * 512, (h + 1) * 512)
                P = pp[cnt % 4]
                cnt += 1
                for k in range(8):
                    b.mm(P[:], w[:, k, :], hT[:, k, sl], k == 0, k == 7, [w, (hT, k)], [P])
                if h % 2 == 0:
                    b.op("act", lambda e, P=P, o=o, sl=sl: e.activation(out=o[:, sl], in_=P[:], func=AF.Copy), [P], [(o, h)])
                else:
                    b.op("dve", lambda e, P=P, o=o, sl=sl: e.tensor_copy(out=o[:, sl], in_=P[:]), [P], [(o, h)])
            b.dma(pT[blk, :, tsl], o[:], [o], [(pT, (blk, ti))])
    b.end()


def outproj_phase(b, xT, yT, WoutD):
    b.begin()
    xs = [b.sb("x", [128, 8, T], F32) for _ in range(2)]
    ys = [b.sb("y", [128, 8, T], BF16) for _ in range(2)]
    ws = [b.sb("w", [128, 8, 128], BF16) for _ in range(2)]
    pp = [b.ps("pp", [128, 512]) for _ in range(4)]
    cnt = 0
    for ti in range(NT):
        x = xs[ti % 2]
        y = ys[ti % 2]
        tsl = slice(ti * T, (ti + 1) * T)
        b.dma(x[:], xT[:, :, tsl].rearrange("k p t -> p k t"), [(xT, ti)], [x])
        b.dma(y[:], yT[:, :, tsl].rearrange("k p t -> p k t"), [yT], [y])
        for d in range(8):
            w = ws[d % 2]
            b.dma(w[:], WoutD[d], [WoutD], [w], q="pool", max_dma_last_dim=4096)
            for h in range(NH):
                sl = slice(h * 512, (h + 1) * 512)
                P = pp[cnt % 4]
                cnt += 1
                for c in range(8):
                    b.mm(P[:], w[:, c, :], y[:, c, sl], c == 0, c == 7, [w, y], [P])
                b.op("dve", lambda e, P=P, x=x, d=d, sl=sl: e.tensor_tensor(out=x[:, d, sl], in0=P[:], in1=x[:, d, sl], op=ALU.add), [P, x], [x])
        b.dma(xT[:, :, tsl].rearrange("k p t -> p k t"), x[:], [x], [(xT, ti)])
    b.end()


def conv_pool_phase(b, pT, yT, cwD, pwD, pscD, pinvD, pcorrD):
    b.begin()
    cw = b.sb("cw", [128, 2, 3], F32)
    b.dma(cw[:], cwD[:], [cwD], [cw])
    psc = b.sb("psc", [128, 2], F32)
    b.dma(psc[:], pscD[:], [pscD], [psc])
    pinv = b.sb("pinv", [128, 2], F32)
    b.dma(pinv[:], pinvD[:], [pinvD], [pinv])
    pcorr = b.sb("pcorr", [128, 2, 16], F32)
    b.dma(pcorr[:], pcorrD[:], [pcorrD], [pcorr])
    pw = b.sb("pw", [128, 2, 128], BF16)
    for j in range(2):
        b.dma(pw[:, j, :], pwD[j], [pwD], [pw], q="pool")
    u = b.sb("u", [128, 16 + S], F32)
    bb = b.sb("bb", [128, S], F32)
    cc = b.sb("cc", [128, S], F32)
    z = b.sb("z", [128, 16 + S], F32)
    acc = b.sb("acc", [128, 16 + S], F32)
    yb = b.sb("yb", [128, S], BF16)
    pp = [b.ps("pp", [128, 512]) for _ in range(2)]
    for j in range(2):
        b.dma(u[:, 16:], pT[BLK_U[j]], [pT], [u])
        b.dma(bb[:], pT[BLK_B[j]], [pT], [bb])
        b.dma(cc[:], pT[BLK_C[j]], [pT], [cc])
        b.op("pool", lambda e: e.memset(z[:, 0:16], 0.0), [], [z])
        b.op("dve", lambda e: e.tensor_tensor(out=z[:, 16:], in0=cc[:], in1=u[:, 16:], op=ALU.mult), [cc, u], [z])
        b.op("dve", lambda e, j=j: e.tensor_scalar(out=acc[:, 16:], in0=z[:, 16:], scalar1=cw[:, j, 2:3], scalar2=None, op0=ALU.mult), [z, cw], [acc])
        b.op("dve", lambda e, j=j: e.scalar_tensor_tensor(out=acc[:, 16:], in0=z[:, 15:15 + S], scalar=cw[:, j, 1:2], in1=acc[:, 16:],
                                                         op0=ALU.mult, op1=ALU.add), [z, cw, acc], [acc])
        b.op("dve", lambda e, j=j: e.scalar_tensor_tensor(out=acc[:, 16:], in0=z[:, 14:14 + S], scalar=cw[:, j, 0:1], in1=acc[:, 16:],
                                                         op0=ALU.mult, op1=ALU.add), [z, cw, acc], [acc])
        b.op("dve", lambda e: e.tensor_tensor(out=yb[:], in0=bb[:], in1=acc[:, 16:], op=ALU.mult), [bb, acc], [yb])
        b.dma(yT[6 + j], yb[:], [yb], [(yT, 6 + j)])
    for j in range(2):
        b.op("pool", lambda e: e.memset(u[:, 0:16], 0.0), [], [u])
        b.op("pool", lambda e: e.memset(z[:, 0:16], 0.0), [], [z])
        b.op("pool", lambda e: e.memset(acc[:, 0:16], 0.0), [], [acc])
        b.dma(u[:, 16:], pT[BLK_A[j]], [pT], [u])
        R = slice(16, 16 + S)

        def sh(k):
            return slice(16 - k, 16 - k + S)
        if j == 0:
            b.op("dve", lambda e: e.tensor_tensor(out=z[:, R], in0=u[:, R], in1=u[:, sh(1)], op=ALU.add), [u], [z])
            b.op("dve", lambda e: e.tensor_copy(out=acc[0:64, R], in_=z[0:64, R]), [z], [acc])
            b.op("dve", lambda e: e.tensor_tensor(out=acc[64:128, R], in0=z[64:128, R], in1=z[64:128, sh(2)], op=ALU.add), [z], [acc])
        else:
            b.op("dve", lambda e: e.tensor_tensor(out=z[:, R], in0=u[:, R], in1=u[:, sh(1)], op=ALU.add), [u], [z])
            b.op("dve", lambda e: e.tensor_tensor(out=acc[:, R], in0=z[:, R], in1=z[:, sh(2)], op=ALU.add), [z], [acc])
            b.op("dve", lambda e: e.tensor_tensor(out=z[:, R], in0=acc[:, R], in1=acc[:, sh(4)], op=ALU.add), [acc], [z])
            b.op("dve", lambda e: e.tensor_copy(out=acc[0:64, R], in_=z[0:64, R]), [z], [acc])
            b.op("dve", lambda e: e.tensor_tensor(out=acc[64:128, R], in0=z[64:128, R], in1=z[64:128, sh(8)], op=ALU.add), [z], [acc])
        b.op("dve", lambda e, j=j: e.tensor_scalar(out=cc[:], in0=acc[:, R], scalar1=pinv[:, j:j + 1], scalar2=None, op0=ALU.mult), [acc, pinv], [cc])
        b.op("dve", lambda e, j=j: e.tensor_tensor(out=cc[:, 0:16], in0=acc[:, 16:32], in1=pcorr[:, j, :], op=ALU.mult), [acc, pcorr, cc], [cc])
        b.op("dve", lambda e: e.tensor_tensor(out=yb[:], in0=cc[:], in1=u[:, R], op=ALU.subtract), [cc, u], [yb])
        for h in range(S // 512):
            sl = slice(h * 512, (h + 1) * 512)
            P = pp[h % 2]
            b.mm(P[:], pw[:, j, :], yb[:, sl], True, True, [pw, yb], [P])
            b.op("act", lambda e, P=P, sl=sl, j=j: e.activation(out=z[:, sl], in_=P[:], func=AF.Copy, scale=psc[:, j:j + 1]), [P, psc], [(z, h)])
        b.op("dve", lambda e: e.tensor_copy(out=yb[:], in_=z[:, 0:S]), [z], [yb])
        b.dma(yT[j], yb[:], [yb], [(yT, j)])
    b.end()


TT_ = 512
NCH = 8


def run_rr(gens):
    gens = list(gens)
    while gens:
        for g in list(gens):
            try:
                next(g)
            except StopIteration:
                gens.remove(g)


DBG = []


def rwkv_phase(b, pT, yT, d, Cn, ntiles=None, debug=False):
    b.begin()

    def dump(name, buf, ap, shape):
        if not debug:
            return
        dd = b.dram("dbg_" + name, list(shape), F32, "ExternalOutput")
        DBG.append("dbg_" + name)
        b.dma(dd[:], ap, [buf], [dd])
    sbF = lambda nm, shp=(128, TT_): b.sb(nm, list(shp), F32)
    def ld(nm, src, shp):
        t = b.sb(nm, shp, F32)
        b.dma(t[:], src[:], [src], [t])
        return t
    bones = ld("bones", Cn["bones"], [128, 128])
    bones64 = ld("bones64", Cn["bones64"], [128, 128])
    ident = ld("ident", Cn["ident"], [128, 128])
    sel2 = ld("sel2", Cn["sel2"], [128, 64])
    rmask = ld("rmask", Cn["rmask"], [128, TT_])
    LPmask = ld("LPmask", Cn["LPmask"], [128, 512])
    Lamask = ld("Lamask", Cn["Lamask"], [128, 128])
    mu8 = ld("mu8", d["rw_mu8"], [128, 8])
    vec = ld("vec", d["rw_vec"], [128, 2, 7])
    WUP = ld("WUP", d["rw_wup"], [128, 2, 128])
    GUP = ld("GUP", d["rw_gup"], [128, 2, 128])
    AUP = ld("AUP", d["rw_aup"], [128, 2, 128])
    V_W0, V_A0, V_KK, V_KA, V_RK, V_LNW, V_LNB = range(7)

    pb2 = [b.sb(f"pb{i}", [128, 1 + TT_], F32) for i in range(2)]
    pb = [pb2[i % 2] for i in range(8)]
    dtmp = sbF("dtmp")
    lp = [sbF(f"lp{i}") for i in range(8)]
    twg = sbF("twg")
    psw = [b.ps("psw", [128, 512]) for _ in range(1)]
    psT2 = [b.ps("psT", [128, 512]) for _ in range(2)]
    psN2 = [b.ps("psN", [128, 512]) for _ in range(2)]
    psS2 = [b.ps("psS", [128, 512]) for _ in range(2)]
    wcnt = [0]

    def wide():
        wcnt[0] += 1
        return psw[0]

    H = {}
    SH = {nm: sbF("sh_" + nm) for nm in ("lw", "cum", "cexc", "dW", "epos", "eneg", "eexc", "eW", "t1", "t2", "kk", "YT", "yc", "sq", "rs")}
    for hp in range(2):
        h = {}
        for nm in ("a", "g", "kkn", "kpp", "akk", "akkh", "bonus"):
            h[nm] = sbF(nm + str(hp))
        for nm in ("lw", "cum", "cexc", "dW", "epos", "eneg", "eexc", "eW", "t1", "t2", "kk", "YT", "yc", "sq", "rs"):
            h[nm] = SH[nm]
        h["wC"] = b.sb(f"wC{hp}", [128, NCH], F32)
        h["QR"] = b.sb(f"QR{hp}", [128, NCH, 2, 64], F32)
        for nm in ("AKKbd", "KHbd", "AKKWbd", "KWbd", "Vbd", "Ybd"):
            h[nm] = b.sb(nm + str(hp), [128, NCH, 128], F32)
            b.op("pool", lambda e, t=h[nm]: e.memset(t[:], 0.0), [], [h[nm]])
        h["QRbd"] = b.sb(f"QRbd{hp}", [128, NCH, 2, 128], F32)
        b.op("pool", lambda e, t=h["QRbd"]: e.memset(t[:], 0.0), [], [h["QRbd"]])
        h["Sst"] = b.sb(f"Sst{hp}", [128, 64], F32)
        b.op("pool", lambda e, t=h["Sst"]: e.memset(t[:], 0.0), [], [h["Sst"]])
        h["LP"] = [b.sb(f"LP{hp}_{i}", [128, 512], F32) for i in range(2)]
        h["AKWT"] = [b.sb(f"AKWT{hp}_{i}", [128, 256], F32) for i in range(2)]
        h["Vst"] = [b.sb(f"Vst{hp}_{i}", [128, 64], F32) for i in range(2)]
        h["TT"] = [b.sb(f"TT{hp}_{i}", [128, 128], F32) for i in range(2)]
        h["M"] = [b.sb(f"M{hp}_{i}", [128, 128], F32) for i in range(2)]
        h["N"] = [b.sb(f"N{hp}_{i}", [128, 128], F32) for i in range(2)]
        h["P"] = [b.sb(f"P{hp}_{i}", [128, 128], F32) for i in range(2)]
        h["Q"] = [b.sb(f"Q{hp}_{i}", [128, 128], F32) for i in range(2)]
        h["Xst"] = b.sb(f"Xst{hp}", [128, 64], F32)
        h["Ust"] = b.sb(f"Ust{hp}", [128, 64], F32)
        h["ob"] = b.sb(f"ob{hp}", [128, TT_], BF16)
        H[hp] = h

    def v3(t, R):
        return t[R, :].rearrange("p (c t) -> p c t", t=64)

    for ti in range(ntiles or (S // TT_)):
        t0 = ti * TT_
        for i in range(8):
            blk = 2 + i
            if ti == 0:
                b.op("pool", lambda e, i=i: e.memset(pb[i][:, 0:1], 0.0), [], [pb[i]])
                b.dma(pb[i][:, 1:], pT[blk, :, 0:TT_], [pT], [pb[i]])
            else:
                b.dma(pb[i][:], pT[blk, :, t0 - 1:t0 + TT_], [pT], [pb[i]])
            eng = "dve" if i % 2 == 0 else "pool"
            b.op("dve", lambda e, i=i: e.tensor_tensor(out=dtmp[:], in0=pb[i][:, 0:TT_], in1=pb[i][:, 1:], op=ALU.subtract), [pb[i]], [dtmp])
            b.op("dve", lambda e, i=i: e.scalar_tensor_tensor(out=lp[i][:], in0=dtmp[:], scalar=mu8[:, i:i + 1], in1=pb[i][:, 1:],
                                                             op0=ALU.mult, op1=ALU.add), [dtmp, mu8, pb[i]], [lp[i]])
        b.op("act", lambda e: e.activation(out=twg[0:64, :], in_=lp[6][0:64, :], func=AF.Tanh), [lp[6]], [(twg, 0)])
        b.op("act", lambda e: e.activation(out=twg[64:128, :], in_=lp[6][64:128, :], func=AF.Sigmoid), [lp[6]], [(twg, 1)])
        for hp in range(2):
            h = H[hp]
            rp, kp, vp = lp[hp], lp[2 + hp], lp[4 + hp]
            vv = lambda i, hp=hp: vec[:, hp, i:i + 1]
            pw_ = wide()
            b.mm(pw_[:], WUP[:, hp, :], twg[:], True, True, [WUP, twg], [pw_])
            b.op("act", lambda e, pw_=pw_, h=h, vv=vv: e.activation(out=h["t1"][:], in_=pw_[:], func=AF.Sigmoid, bias=vv(V_W0)), [pw_, vec], [h["t1"]])
            b.op("dve", lambda e, h=h: e.tensor_scalar(out=h["lw"][:], in0=h["t1"][:], scalar1=-0.6065306597126334, scalar2=None, op0=ALU.mult), [h["t1"]], [h["lw"]])
            pa_ = wide()
            b.mm(pa_[:], AUP[:, hp, :], lp[7][:], True, True, [AUP, lp[7]], [pa_])
            b.op("act", lambda e, pa_=pa_, h=h, vv=vv: e.activation(out=h["a"][:], in_=pa_[:], func=AF.Sigmoid, bias=vv(V_A0)), [pa_, vec], [h["a"]])
            pg_ = wide()
            b.mm(pg_[:], GUP[:, hp, :], twg[:], True, True, [GUP, twg], [pg_])
            b.op("act", lambda e, pg_=pg_, h=h: e.activation(out=h["g"][:], in_=pg_[:], func=AF.Copy), [pg_], [h["g"]])
            b.op("dve", lambda e, h=h, kp=kp, vv=vv: e.tensor_scalar(out=h["kk"][:], in0=kp[:], scalar1=vv(V_KK), scalar2=None, op0=ALU.mult), [kp, vec], [h["kk"]])
            b.op("act", lambda e, h=h: e.activation(out=h["t1"][:], in_=h["kk"][:], func=AF.Square), [h["kk"]], [h["t1"]])
            pn_ = wide()
            b.mm(pn_[:], bones[:], h["t1"][:], True, True, [bones, h["t1"]], [pn_])
            b.op("dve", lambda e, pn_=pn_, h=h: e.tensor_scalar(out=h["t2"][:], in0=pn_[:], scalar1=1e-24, scalar2=None, op0=ALU.max), [pn_], [h["t2"]])
            b.op("act", lambda e, h=h: e.activation(out=h["t2"][:], in_=h["t2"][:], func=AF.Sqrt), [h["t2"]], [h["t2"]])
            b.op("dve", lambda e, h=h: e.reciprocal(out=h["t2"][:], in_=h["t2"][:]), [h["t2"]], [h["t2"]])
            b.op("dve", lambda e, h=h: e.tensor_tensor(out=h["kkn"][:], in0=h["kk"][:], in1=h["t2"][:], op=ALU.mult), [h["kk"], h["t2"]], [h["kkn"]])
            b.op("dve", lambda e, h=h, vv=vv: e.tensor_scalar(out=h["t1"][:], in0=h["a"][:], scalar1=-1.0, scalar2=vv(V_KA), op0=ALU.add, op1=ALU.mult), [h["a"], vec], [h["t1"]])
            b.op("dve", lambda e, h=h, kp=kp: e.scalar_tensor_tensor(out=h["kpp"][:], in0=h["t1"][:], scalar=1.0, in1=kp[:], op0=ALU.add, op1=ALU.mult), [h["t1"], kp], [h["kpp"]])
            b.op("dve", lambda e, h=h: e.tensor_tensor(out=h["akk"][:], in0=h["kkn"][:], in1=h["a"][:], op=ALU.mult), [h["kkn"], h["a"]], [h["akk"]])
            b.op("dve", lambda e, h=h, rp=rp, vv=vv: e.scalar_tensor_tensor(out=h["t1"][:], in0=rp[:], scalar=vv(V_RK), in1=h["kpp"][:], op0=ALU.mult, op1=ALU.mult), [rp, vec, h["kpp"]], [h["t1"]])
            pb_ = wide()
            b.mm(pb_[:], bones[:], h["t1"][:], True, True, [bones, h["t1"]], [pb_])
            b.op("dve", lambda e, pb_=pb_, h=h, vp=vp: e.tensor_tensor(out=h["bonus"][:], in0=pb_[:], in1=vp[:], op=ALU.mult), [pb_, vp], [h["bonus"]])
            b.op("dve", lambda e, h=h: e.tensor_tensor_scan(out=h["cum"][:], data0=rmask[:], data1=h["lw"][:], initial=0.0, op0=ALU.mult, op1=ALU.add), [rmask, h["lw"]], [h["cum"]])
            b.op("dve", lambda e, h=h: e.tensor_tensor(out=h["cexc"][:], in0=h["cum"][:], in1=h["lw"][:], op=ALU.subtract), [h["cum"], h["lw"]], [h["cexc"]])
            for c in range(NCH):
                cs = slice(c * 64, (c + 1) * 64)
                b.op("pool" if c % 2 else "dve", lambda e, h=h, cs=cs, c=c: e.tensor_scalar(out=h["dW"][:, cs], in0=h["cum"][:, cs], scalar1=-1.0,
                                                                 scalar2=h["cum"][:, c * 64 + 63:c * 64 + 64], op0=ALU.mult, op1=ALU.add), [h["cum"]], [(h["dW"], c)])
            b.op("act", lambda e, h=h: e.activation(out=h["epos"][:], in_=h["cum"][:], func=AF.Exp), [h["cum"]], [h["epos"]])
            b.op("act", lambda e, h=h: e.activation(out=h["eneg"][:], in_=h["cum"][:], func=AF.Exp, scale=-1.0), [h["cum"]], [h["eneg"]])
            b.op("act", lambda e, h=h: e.activation(out=h["eexc"][:], in_=h["cexc"][:], func=AF.Exp), [h["cexc"]], [h["eexc"]])
            b.op("act", lambda e, h=h: e.activation(out=h["eW"][:], in_=h["dW"][:], func=AF.Exp), [h["dW"]], [h["eW"]])
            b.op("pool", lambda e, h=h: e.tensor_copy(out=h["wC"][:], in_=h["epos"][:, 63::64]), [h["epos"]], [h["wC"]])
            b.op("dve", lambda e, h=h: e.tensor_tensor(out=h["QR"][:, :, 0, :], in0=v3(h["kkn"], slice(0, 128)), in1=v3(h["eexc"], slice(0, 128)), op=ALU.mult), [h["kkn"], h["eexc"]], [(h["QR"], 0)])
            b.op("pool", lambda e, h=h, rp=rp: e.tensor_tensor(out=h["QR"][:, :, 1, :], in0=v3(rp, slice(0, 128)), in1=v3(h["epos"], slice(0, 128)), op=ALU.mult), [rp, h["epos"]], [(h["QR"], 1)])
            b.op("dve", lambda e, h=h: e.tensor_tensor(out=h["akkh"][:], in0=h["akk"][:], in1=h["eneg"][:], op=ALU.mult), [h["akk"], h["eneg"]], [h["akkh"]])
            for hh in range(2):
                R = slice(64 * hh, 64 * hh + 64)
                eng = "pool" if hh else "dve"
                b.op(eng, lambda e, h=h, R=R: e.tensor_copy(out=h["QRbd"][R, :, 0, R], in_=h["QR"][R, :, 0, :]), [(h["QR"], 0)], [(h["QRbd"], hh)])
                b.op(eng, lambda e, h=h, R=R: e.tensor_copy(out=h["QRbd"][R, :, 1, R], in_=h["QR"][R, :, 1, :]), [(h["QR"], 1)], [(h["QRbd"], hh)])
                b.op(eng, lambda e, h=h, R=R: e.tensor_copy(out=h["AKKbd"][R, :, R], in_=v3(h["akkh"], R)), [h["akkh"]], [(h["AKKbd"], hh)])
                b.op(eng, lambda e, h=h, R=R: e.tensor_tensor(out=h["KHbd"][R, :, R], in0=v3(h["kpp"], R), in1=v3(h["eneg"], R), op=ALU.mult), [h["kpp"], h["eneg"]], [(h["KHbd"], hh)])
                b.op(eng, lambda e, h=h, R=R: e.tensor_tensor(out=h["AKKWbd"][R, :, R], in0=v3(h["akk"], R), in1=v3(h["eW"], R), op=ALU.mult), [h["akk"], h["eW"]], [(h["AKKWbd"], hh)])
                b.op(eng, lambda e, h=h, R=R: e.tensor_tensor(out=h["KWbd"][R, :, R], in0=v3(h["kpp"], R), in1=v3(h["eW"], R), op=ALU.mult), [h["kpp"], h["eW"]], [(h["KWbd"], hh)])
                b.op(eng, lambda e, h=h, R=R, vp=vp: e.tensor_copy(out=h["Vbd"][R, :, R], in_=v3(vp, R)), [vp], [(h["Vbd"], hh)])

        if ti == 0:
            for hp in range(2):
                h = H[hp]
                for nm in ("a", "g", "kkn", "kpp", "akk", "akkh", "bonus"):
                    dump(f"{nm}{hp}", h[nm], h[nm][:], [128, TT_])
                dump(f"wC{hp}", h["wC"], h["wC"][:], [128, NCH])
                dump(f"QRbd{hp}", h["QRbd"], h["QRbd"][:], [128, NCH, 2, 128])
                dump(f"KWbd{hp}", h["KWbd"], h["KWbd"][:], [128, NCH, 128])
            for i in range(8):
                dump(f"lp{i}", lp[i], lp[i][:], [128, TT_])

        def prep_gen(hp, c):
            h = H[hp]
            par = c % 2
            LP, AKWT, Vst, TT = h["LP"][par], h["AKWT"][par], h["Vst"][par], h["TT"][par]
            psT = psT2[hp]
            psL = psT2[hp]
            psN = psN2[hp]
            tb = psT
            o = 0
            b.tr(psT[:, o:o + 128], h["AKKWbd"][:, c, :], ident[:], [h["AKKWbd"], ident], [tb])
            b.tr(psT[:, o + 128:o + 256], h["KWbd"][:, c, :], ident[:], [h["KWbd"], ident], [tb])
            yield
            b.op("act", lambda e: e.activation(out=AKWT[:], in_=psT[:, o:o + 256], func=AF.Copy), [tb], [AKWT])
            yield
            b.tr(psT[:, o:o + 128], h["Vbd"][:, c, :], ident[:], [h["Vbd"], ident], [tb])
            yield
            b.op("act", lambda e: e.activation(out=Vst[0:64, :], in_=psT[0:64, o:o + 64], func=AF.Copy), [tb], [(Vst, 0)])
            b.op("act", lambda e: e.activation(out=Vst[64:128, :], in_=psT[64:128, o + 64:o + 128], func=AF.Copy), [tb], [(Vst, 1)])
            yield
            lb = (psL, hp)
            b.mm(psL[:, 0:256], h["AKKbd"][:, c, :], h["QRbd"][:, c, :, :].rearrange("p a t -> p (a t)"), True, True, [h["AKKbd"], h["QRbd"]], [psL])
            b.mm(psL[:, 256:512], h["KHbd"][:, c, :], h["QRbd"][:, c, :, :].rearrange("p a t -> p (a t)"), True, True, [h["KHbd"], h["QRbd"]], [psL])
            yield
            b.op("dve", lambda e: e.tensor_tensor(out=LP[:], in0=psL[:], in1=LPmask[:], op=ALU.mult), [psL, LPmask], [LP])
            yield
            nb = lambda k: psN
            no = 0
            b.mm(psN[:, no:no + 128], h["QRbd"][:, c, 0, :], h["AKKbd"][:, c, :], True, True, [h["QRbd"], h["AKKbd"]], [nb(0)])
            yield
            M, N, P, Q = h["M"], h["N"], h["P"], h["Q"]
            b.op("dve", lambda e: e.tensor_tensor(out=N[0][:], in0=psN[:, no:no + 128], in1=Lamask[:], op=ALU.mult), [nb(0), Lamask], [N[0]])
            b.op("pool", lambda e: e.tensor_tensor(out=Q[0][:], in0=N[0][:], in1=ident[:], op=ALU.add), [N[0], ident], [Q[0]])
            b.op("pool", lambda e: e.tensor_tensor(out=P[0][:], in0=LP[:, 0:128], in1=ident[:], op=ALU.add), [LP, ident], [P[0]])
            yield
            Mc, Nc = LP[:, 0:128], N[0][:]
            Mb, Nb = LP, N[0]
            for i in range(1, 6):
                last = (i == 5)
                pi, po = (i - 1) % 2, i % 2
                Mn, Nn = M[po], N[po]
                b.mm(psN[:, no:no + 128], Nc, Mc, True, True, [Mb, Nb], [nb(0)])
                if not last:
                    b.mm(psN[:, no + 128:no + 256], Mc, Nc, True, True, [Mb, Nb], [nb(1)])
                yield
                VAR = _os.environ.get("VAR", "AB")
                if "A" in VAR:
                    b.op("act", lambda e, Mn=Mn: e.activation(out=Mn[:], in_=psN[:, no:no + 128], func=AF.Copy), [nb(0)], [Mn])
                if not last and "B" in VAR:
                    b.op("dve", lambda e, Nn=Nn: e.tensor_copy(out=Nn[:], in_=psN[:, no + 128:no + 256]), [nb(1)], [Nn])
                yield
                Pout = TT if last else P[po]
                b.mm(psN[:, no:no + 128], Q[pi][:], Mn[:], True, True, [Q[pi], Mn], [nb(0)])
                if not last:
                    b.mm(psN[:, no + 128:no + 256], P[pi][:], Nn[:], True, True, [P[pi], Nn], [nb(1)])
                yield
                b.op("dve", lambda e, Pout=Pout, pi=pi: e.tensor_tensor(out=Pout[:], in0=psN[:, no:no + 128], in1=P[pi][:], op=ALU.add), [nb(0), P[pi]], [Pout])
                if not last:
                    b.op("dve", lambda e, po=po, pi=pi: e.tensor_tensor(out=Q[po][:], in0=psN[:, no + 128:no + 256], in1=Q[pi][:], op=ALU.add), [nb(1), Q[pi]], [Q[po]])
                yield
                Mc, Nc, Mb, Nb = Mn[:], Nn[:], Mn, Nn

        def seq_gen(hp, c):
            h = H[hp]
            par = c % 2
            LP, AKWT, Vst, TT = h["LP"][par], h["AKWT"][par], h["Vst"][par], h["TT"][par]
            Sst, Xst, Ust = h["Sst"], h["Xst"], h["Ust"]
            so = 0
            psS = psS2[hp]
            sb_ = lambda k: psS
            X_, U_, Y_, S_ = (psS[:, so + 64 * k:so + 64 * k + 64] for k in range(4))
            b.mm(X_, LP[:, 256:384], Vst[:], True, False, [LP, Vst], [sb_(0)])
            b.mm(X_, h["QRbd"][:, c, 0, :], Sst[:], False, True, [h["QRbd"], Sst], [sb_(0)])
            yield
            b.op("dve", lambda e: e.tensor_copy(out=Xst[:], in_=X_), [sb_(0)], [Xst])
            yield
            b.mm(U_, TT[:], Xst[:], True, True, [TT, Xst], [sb_(1)])
            yield
            b.op("dve", lambda e: e.tensor_scalar(out=Ust[:], in0=U_, scalar1=-1.0, scalar2=None, op0=ALU.mult), [sb_(1)], [Ust])
            yield
            b.mm(Y_, h["QRbd"][:, c, 1, :], Sst[:], True, False, [h["QRbd"], Sst], [sb_(2)])
            b.mm(Y_, LP[:, 128:256], Ust[:], False, False, [LP, Ust], [sb_(2)])
            b.mm(Y_, LP[:, 384:512], Vst[:], False, True, [LP, Vst], [sb_(2)])
            b.mm(S_, AKWT[:, 128:256], Vst[:], True, False, [AKWT, Vst], [sb_(3)])
            b.mm(S_, AKWT[:, 0:128], Ust[:], False, True, [AKWT, Ust], [sb_(3)])
            yield
            b.op("dve", lambda e: e.scalar_tensor_tensor(out=Sst[:], in0=Sst[:], scalar=h["wC"][:, c:c + 1], in1=S_, op0=ALU.mult, op1=ALU.add),
                 [Sst, h["wC"], sb_(3)], [Sst])
            b.op("act", lambda e: e.activation(out=h["Ybd"][0:64, c, 0:64], in_=psS[0:64, so + 128:so + 192], func=AF.Copy), [sb_(2)], [(h["Ybd"], c)])
            b.op("act", lambda e: e.activation(out=h["Ybd"][64:128, c, 64:128], in_=psS[64:128, so + 128:so + 192], func=AF.Copy), [sb_(2)], [(h["Ybd"], c)])
            yield

        import os as _os
        STOP = int(_os.environ.get("RW_STOP", "9"))
        if STOP <= 1:
            continue
        for c in range(NCH + 1):
            gens = []
            if c < NCH:
                import itertools as _it
                PST = int(_os.environ.get("P_STOP", "999"))
                gens += [_it.islice(prep_gen(0, c), PST), _it.islice(prep_gen(1, c), PST)]
            if c >= 1 and STOP >= 3:
                gens += [seq_gen(0, c - 1), seq_gen(1, c - 1)]
            run_rr(gens)
        if STOP <= 3:
            continue

        for hp in range(2):
            h = H[hp]
            vv = lambda i, hp=hp: vec[:, hp, i:i + 1]
            py = wide()
            for c in range(NCH):
                b.mm(py[:, c * 64:(c + 1) * 64], h["Ybd"][:, c, :], sel2[:], True, True, [(h["Ybd"], c), sel2], [py])
            b.op("act", lambda e, py=py, h=h: e.activation(out=h["YT"][:], in_=py[:], func=AF.Copy), [py], [h["YT"]])
            if ti == 0:
                dump(f"YT{hp}", h["YT"], h["YT"][:], [128, TT_])
                dump(f"Sst{hp}", h["Sst"], h["Sst"][:], [128, 64])
                dump(f"TT{hp}", h["TT"][1], h["TT"][1][:], [128, 128])
                dump(f"LP{hp}", h["LP"][1], h["LP"][1][:], [128, 512])
            pm = wide()
            b.mm(pm[:], bones64[:], h["YT"][:], True, True, [bones64, h["YT"]], [pm])
            b.op("dve", lambda e, pm=pm, h=h: e.tensor_tensor(out=h["yc"][:], in0=h["YT"][:], in1=pm[:], op=ALU.subtract), [h["YT"], pm], [h["yc"]])
            b.op("act", lambda e, h=h: e.activation(out=h["sq"][:], in_=h["yc"][:], func=AF.Square), [h["yc"]], [h["sq"]])
            pv = wide()
            b.mm(pv[:], bones64[:], h["sq"][:], True, True, [bones64, h["sq"]], [pv])
            b.op("act", lambda e, pv=pv, h=h: e.activation(out=h["rs"][:], in_=pv[:], func=AF.Sqrt, bias=64e-5), [pv], [h["rs"]])
            b.op("dve", lambda e, h=h: e.reciprocal(out=h["rs"][:], in_=h["rs"][:]), [h["rs"]], [h["rs"]])
            b.op("dve", lambda e, h=h: e.tensor_tensor(out=h["yc"][:], in0=h["yc"][:], in1=h["rs"][:], op=ALU.mult), [h["yc"], h["rs"]], [h["yc"]])
            b.op("dve", lambda e, h=h, vv=vv: e.tensor_scalar(out=h["yc"][:], in0=h["yc"][:], scalar1=vv(V_LNW), scalar2=vv(V_LNB), op0=ALU.mult, op1=ALU.add), [h["yc"], vec], [h["yc"]])
            b.op("dve", lambda e, h=h: e.tensor_tensor(out=h["yc"][:], in0=h["yc"][:], in1=h["bonus"][:], op=ALU.add), [h["yc"], h["bonus"]], [h["yc"]])
            b.op("dve", lambda e, h=h: e.tensor_tensor(out=h["ob"][:], in0=h["yc"][:], in1=h["g"][:], op=ALU.mult), [h["yc"], h["g"]], [h["ob"]])
            b.dma(yT[2 + hp, :, t0:t0 + TT_], h["ob"][:], [h["ob"]], [(yT, (2 + hp, ti))])
    b.end()


NBIG = 30000.0


def nsa_phase(b, pT, yT, d, Cn, ngroups=None, debug=False):
    b.begin()
    DBGN = []

    def dump(name, buf, ap, shape, dt=F32):
        if not debug:
            return
        dd = b.dram("dbg_" + name, list(shape), dt, "ExternalOutput")
        b.dma(dd[:], ap, [buf], [dd])

    def ld(nm, src, shp):
        t = b.sb(nm, shp, F32)
        b.dma(t[:], src[:], [src], [t])
        return t

    def ldc(nm, src_ap, srcbuf, shp):
        t = b.sb(nm, shp, BF16)
        b.dma(t[:], src_ap, [srcbuf], [t], q="pool", max_dma_last_dim=4096)
        return t
    ident = ld("ident", Cn["ident"], [128, 128])
    identb = ldc("identb", Cn["ident"][:], Cn["ident"], [128, 128])
    bones = ld("bones", Cn["bones"], [128, 128])
    eall = ldc("eall", Cn["n_eall"][:], Cn["n_eall"], [64, S])
    cmsel = b.sb("cmsel", [128, 4, 512], BF16)
    for r in range(4):
        b.dma(cmsel[:, r, :], Cn["n_cmsel"][r], [Cn["n_cmsel"]], [cmsel], q="pool", max_dma_last_dim=4096)
    cmwin = b.sb("cmwin", [128, 8, 512], BF16)
    for r in range(8):
        b.dma(cmwin[:, r, :], Cn["n_cmwin"][r], [Cn["n_cmwin"]], [cmwin], q="pool", max_dma_last_dim=4096)
    negc = b.sb("negc", [128, 5, 512], BF16)
    for r in range(5):
        b.dma(negc[:, r, :], Cn["n_negc"][r], [Cn["n_negc"]], [negc], q="pool", max_dma_last_dim=4096)
    qnw = ld("qnw", d["ns_qnw"], [128, 1])
    knw = ld("knw", d["ns_knw"], [128, 3])
    w2 = ld("w2", d["ns_w2"], [128, 2, 64])
    pos2 = ld("pos2", d["ns_pos2"], [128, 32, 2])
    QA = [b.sb(f"QA{h}", [128, S], BF16) for h in range(4)]
    KS = b.sb("KS", [128, S], BF16)
    KW = b.sb("KW", [128, S], BF16)
    KC = b.sb("KC", [128, 256], BF16)
    NEGM = b.sb("NEGM", [64, S], BF16)
    Vs = b.sb("Vs", [128, 32, 65], BF16)
    Vw = b.sb("Vw", [128, 32, 65], BF16)
    VC = b.sb("VC", [128, 2, 129], BF16)
    Gtm = b.sb("Gtm", [128, 32, 32], F32)
    rawA = b.sb("rawA", [128, S], F32)
    rawB = b.sb("rawB", [128, S], F32)
    tF = [b.sb(f"tF{i}", [128, 512], F32) for i in range(3)]
    psc = [b.ps("psc", [128, 512]) for _ in range(2)]
    pacc = [b.ps("pacc", [128, 512]) for _ in range(2)]
    pcA = b.ps("pcA", [128, 512])
    pcB = b.ps("pcB", [128, 512])
    pm = [b.ps("pm", [128, 512]) for _ in range(2)]
    mcnt = [0]

    def misc():
        mcnt[0] += 1
        return pm[mcnt[0] % 2]

    def rms_rows(raw, out, wcol, NC=S, width=512):
        for c0 in range(0, NC, width):
            wd = min(width, NC - c0)
            cs = slice(c0, c0 + wd)
            t0, t1 = tF[0], tF[1]
            b.op("act", lambda e, cs=cs, wd=wd: e.activation(out=t0[0:64, 0:wd], in_=raw[0:64, cs], func=AF.Square), [raw], [t0])
            P = misc()
            b.mm(P[0:64, 0:wd], bones[0:64, 0:64], t0[0:64, 0:wd], True, True, [bones, t0], [P])
            b.op("act", lambda e, P=P, wd=wd: e.activation(out=t1[0:64, 0:wd], in_=P[0:64, 0:wd], func=AF.Sqrt, scale=1.0 / 64, bias=EPS), [P], [t1])
            b.op("dve", lambda e, wd=wd: e.reciprocal(out=t1[0:64, 0:wd], in_=t1[0:64, 0:wd]), [t1], [t1])
            b.op("dve", lambda e, cs=cs, wd=wd: e.scalar_tensor_tensor(out=out[0:64, cs], in0=raw[0:64, cs], scalar=wcol, in1=t1[0:64, 0:wd],
                                                                    op0=ALU.mult, op1=ALU.mult), [raw, t1, qnw, knw], [out])

    for h in range(4):
        b.dma(rawA[:], pT[BLK_Q[h]], [pT], [rawA])
        rms_rows(rawA, QA[h], qnw[0:64, 0:1])
        b.dma(QA[h][64:68, :], Cn["n_qaug"][h], [Cn["n_qaug"]], [QA[h]], q="pool", max_dma_last_dim=4096)
    for (blk, Kt, Vt, wi) in ((BLK_KVS, KS, Vs, 1), (BLK_KVW, KW, Vw, 2)):
        b.dma(rawA[:], pT[blk], [pT], [rawA])
        rms_rows(rawA, Kt, knw[0:64, wi:wi + 1])
        b.dma(Kt[64:68, :], Cn["n_kaug"][:], [Cn["n_kaug"]], [Kt], q="pool", max_dma_last_dim=4096)
        b.op("pool", lambda e, Vt=Vt: e.memset(Vt[:, :, 64:65], 1.0), [], [Vt])
        for g4 in range(8):
            P = misc()
            for i in range(4):
                kt = g4 * 4 + i
                b.tr(P[:, i * 128:(i + 1) * 128], rawA[:, kt * 128:(kt + 1) * 128], ident[:], [rawA, ident], [P])
            b.op("dve", lambda e, P=P, Vt=Vt, g4=g4: e.tensor_copy(out=Vt[:, g4 * 4:g4 * 4 + 4, 0:64],
                                                                  in_=P[:].rearrange("p (i c) -> p i c", c=128)[:, :, 64:128]), [P], [Vt])
    b.dma(rawA[:], pT[BLK_GT], [pT], [rawA])
    b.op("act", lambda e: e.activation(out=rawA[0:32, :], in_=rawA[0:32, :], func=AF.Sigmoid), [rawA], [rawA])
    for g16 in range(2):
        P = misc()
        for i in range(16):
            kt = g16 * 16 + i
            b.tr(P[:, i * 32:(i + 1) * 32], rawA[0:32, kt * 128:(kt + 1) * 128], ident[0:32, 0:32], [rawA, ident], [P])
        b.op("dve", lambda e, P=P, g16=g16: e.tensor_copy(out=Gtm[:, g16 * 16:(g16 + 1) * 16, :], in_=P[:].rearrange("p (i c) -> p i c", c=32)), [P], [Gtm])
    b.dma(rawA[:], pT[BLK_KVC], [pT], [rawA])
    b.dma(rawB[:], d["ns_w1"][:].rearrange("p l m -> p (l m)"), [d["ns_w1"]], [rawB])
    b.op("pool", lambda e: e.memset(KC[:], 0.0), [], [KC])
    b.op("pool", lambda e: e.memset(VC[:], 0.0), [], [VC])
    bias_sb = b.sb("bias_sb", [128, 4], F32)
    for half in range(2):
        R = slice(64 * half, 64 * half + 64)
        Pb = misc()
        for l in range(32):
            b.mm(Pb[:, 0:2], rawB[R, l * 128:(l + 1) * 128], pos2[R, l, :], l == 0, l == 31, [rawB, pos2], [Pb])
        b.op("dve", lambda e, Pb=Pb, half=half: e.tensor_copy(out=bias_sb[:, 2 * half:2 * half + 2], in_=Pb[:, 0:2]), [Pb], [bias_sb])
    gl = [b.sb(f"gl{i}", [128, 256], F32) for i in range(2)]
    for half in range(2):
        R = slice(64 * half, 64 * half + 64)
        Ph = misc()
        for l in range(32):
            b.mm(Ph[:, 0:255], rawB[R, l * 128:(l + 1) * 128], rawA[R, l:l + 4065:16], l == 0, l == 31, [rawB, rawA], [Ph])
        hb, h2 = tF[0], tF[1]
        b.op("act", lambda e, Ph=Ph, half=half: e.activation(out=hb[:, 0:255], in_=Ph[:, 0:255], func=AF.Identity, bias=bias_sb[:, 2 * half:2 * half + 1]), [Ph, bias_sb], [hb])
        b.op("act", lambda e: e.activation(out=h2[:, 0:255], in_=hb[:, 0:255], func=AF.Square), [hb], [h2])
        b.op("dve", lambda e: e.tensor_scalar(out=h2[:, 0:255], in0=h2[:, 0:255], scalar1=0.044715, scalar2=1.0, op0=ALU.mult, op1=ALU.add), [h2], [h2])
        b.op("dve", lambda e: e.tensor_tensor(out=h2[:, 0:255], in0=h2[:, 0:255], in1=hb[:, 0:255], op=ALU.mult), [h2, hb], [h2])
        b.op("act", lambda e: e.activation(out=h2[:, 0:255], in_=h2[:, 0:255], func=AF.Sigmoid, scale=1.5957691216057308), [h2], [h2])
        b.op("dve", lambda e, half=half: e.tensor_tensor(out=gl[half][:, 0:255], in0=h2[:, 0:255], in1=hb[:, 0:255], op=ALU.mult), [h2, hb], [gl[half]])
    Pk = misc()
    b.mm(Pk[0:64, 0:255], w2[:, 0, :], gl[0][:, 0:255], True, True, [w2, gl[0]], [Pk])
    kc_sb = b.sb("kc_sb", [128, 256], F32)
    b.op("dve", lambda e: e.tensor_copy(out=kc_sb[0:64, 0:255], in_=Pk[0:64, 0:255]), [Pk], [kc_sb])
    rms_rows(kc_sb, KC, knw[0:64, 0:1], NC=255, width=255)
    b.dma(KC[64:68, :], Cn["n_kcaug"][:], [Cn["n_kcaug"]], [KC], q="pool")
    for nt in range(2):
        ncols = 128 if nt == 0 else 127
        Pv = misc()
        b.mm(Pv[0:ncols, 0:64], gl[1][:, nt * 128:nt * 128 + ncols], w2[:, 1, :], True, True, [gl[1], w2], [Pv])
        b.op("dve", lambda e, Pv=Pv, nt=nt, ncols=ncols: e.tensor_copy(out=VC[0:ncols, nt, 0:64], in_=Pv[0:ncols, 0:64]), [Pv], [VC])
        b.dma(VC[:, nt, 64:129], Cn["n_ovl1"][:, nt, :], [Cn["n_ovl1"]], [VC], q="pool")
    if debug:
        dump("gl0", gl[0], gl[0][:, 0:255], [128, 255])
        dump("gl1", gl[1], gl[1][:, 0:255], [128, 255])
        dump("kc_sb", kc_sb, kc_sb[0:64, 0:255], [64, 255])
        dump("bias_sb", bias_sb, bias_sb[:], [128, 4])
        for h in range(4):
            dump(f"QA{h}", QA[h], QA[h][0:68, :], [68, S], BF16)
        dump("KS", KS, KS[0:68, :], [68, S], BF16)
        dump("KC", KC, KC[0:68, :], [68, 256], BF16)
        dump("VC", VC, VC[:], [128, 2, 129], BF16)
        dump("Vs", Vs, Vs[:], [128, 32, 65], BF16)
        dump("Gtm", Gtm, Gtm[:], [128, 32, 32])

    PC = [[b.sb(f"PC{h}_{nt}", [128, 512], BF16) for nt in range(2)] for h in range(4)]
    PT = [b.sb(f"PT{i}", [128, 512], BF16) for i in range(3)]
    OUT = b.sb("OUT", [128, 4, 256], F32)
    okt = b.sb("okt", [128, 4, 64], F32)
    addt = b.sb("addt", [128, 4, 64], F32)
    imp = b.sb("imp", [128, 64], F32)
    imp2 = b.sb("imp2", [128, 64], F32)
    imp3 = b.sb("imp3", [128, 64], F32)
    m8a = b.sb("m8a", [128, 8], F32)
    m8b = b.sb("m8b", [128, 8], F32)
    msk = b.sb("msk", [128, 4, 64], F32)
    l4 = b.sb("l4", [128, 4], F32)
    rg4 = b.sb("rg4", [128, 4], F32)
    rl4 = b.sb("rl4", [128, 4], F32)
    YC = [b.sb(f"YC{i}", [128, 512], BF16) for i in range(2)]
    scnt = [0]
    pcnt = [0]
    acnt = [0]

    for qg in range(ngroups or (S // 512)):
        qs = slice(qg * 512, (qg + 1) * 512)
        b.dma(okt[:], Cn["n_ok"][:, 4 * qg:4 * qg + 4, :], [Cn["n_ok"]], [okt])
        b.dma(addt[:], Cn["n_addc"][:, 4 * qg:4 * qg + 4, :], [Cn["n_addc"]], [addt])
        nts = [0] if qg < 4 else [0, 1]
        for h in range(4):
            for nt in nts:
                g = qg if nt == 0 else qg - 4
                masked = (g <= 4)
                sc = psc[scnt[0] % 2]
                scnt[0] += 1
                b.mm(sc[:], KC[0:68, nt * 128:(nt + 1) * 128], QA[h][0:68, qs], True, not masked, [KC, QA[h]], [sc])
                if masked:
                    b.mm(sc[:], identb[:], negc[:, g, :], False, True, [identb, negc], [sc])
                b.op("act", lambda e, sc=sc, h=h, nt=nt: e.activation(out=PC[h][nt][:], in_=sc[:], func=AF.Exp, scale=0.125), [sc], [PC[h][nt]])
        for s in range(4):
            tile = 4 * qg + s
            ss = slice(s * 128, (s + 1) * 128)
            for h in range(4):
                bank = pcA if h < 2 else pcB
                reg = slice((h % 2) * 129, (h % 2) * 129 + 129)
                for i, nt in enumerate(nts):
                    b.mm(bank[:, reg], PC[h][nt][:, ss], VC[:, nt, :], i == 0, i == len(nts) - 1, [PC[h][nt], VC], [bank])
            for bi, bank in enumerate((pcA, pcB)):
                b.op("dve", lambda e, bank=bank, bi=bi: e.tensor_scalar(out=l4[:, 2 * bi:2 * bi + 2],
                                                                       in0=bank[:, 0:258].rearrange("p (h c) -> p h c", c=129)[:, :, 64],
                                                                       scalar1=1e-30, scalar2=None, op0=ALU.max), [bank], [(l4, bi)])
            b.op("dve", lambda e: e.reciprocal(out=rl4[:], in_=l4[:]), [l4], [rl4])
            b.op("dve", lambda e, tile=tile: e.tensor_tensor(out=rg4[:], in0=rl4[:], in1=Gtm[:, tile, 0:12:3], op=ALU.mult), [rl4, Gtm], [rg4])
            for h in range(4):
                bank = pcA if h < 2 else pcB
                o0 = (h % 2) * 129
                b.op("dve", lambda e, bank=bank, o0=o0, h=h, s=s: e.tensor_scalar(out=OUT[:, s, h * 64:(h + 1) * 64], in0=bank[:, o0:o0 + 64],
                                                                             scalar1=rg4[:, h:h + 1], scalar2=None, op0=ALU.mult), [bank, rg4], [(OUT, s)])
                if h == 0:
                    b.op("dve", lambda e, bank=bank, o0=o0: e.tensor_scalar(out=imp[:], in0=bank[:, o0 + 65:o0 + 129], scalar1=rl4[:, 0:1], scalar2=None, op0=ALU.mult),
                         [bank, rl4], [imp])
                else:
                    b.op("dve", lambda e, bank=bank, o0=o0, h=h: e.scalar_tensor_tensor(out=imp[:], in0=bank[:, o0 + 65:o0 + 129], scalar=rl4[:, h:h + 1], in1=imp[:],
                                                                                   op0=ALU.mult, op1=ALU.add), [bank, rl4, imp], [imp])
            b.op("dve", lambda e, s=s: e.tensor_tensor(out=imp2[:], in0=imp[:], in1=okt[:, s, :], op=ALU.mult), [imp, okt], [imp2])
            b.op("dve", lambda e, s=s: e.tensor_tensor(out=imp2[:], in0=imp2[:], in1=addt[:, s, :], op=ALU.add), [imp2, addt], [imp2])
            b.op("dve", lambda e: e.max(out=m8a[:], in_=imp2[:]), [imp2], [m8a])
            b.op("dve", lambda e: e.match_replace(out=imp3[:], in_to_replace=m8a[:], in_values=imp2[:], imm_value=-1e30), [m8a, imp2], [imp3])
            b.op("dve", lambda e: e.max(out=m8b[:], in_=imp3[:]), [imp3], [m8b])
            b.op("dve", lambda e, s=s: e.tensor_scalar(out=msk[:, s, :], in0=imp2[:], scalar1=m8b[:, 7:8], scalar2=None, op0=ALU.is_ge), [imp2, m8b], [(msk, s)])
        Pm = misc()
        for s in range(4):
            b.tr(Pm[0:64, s * 128:(s + 1) * 128], msk[:, s, :], ident[:], [(msk, s), ident], [Pm])
        b.op("dve", lambda e, Pm=Pm, qs=qs: e.tensor_scalar(out=NEGM[:, qs], in0=Pm[0:64, :], scalar1=-1.0, scalar2=NBIG, op0=ALU.add, op1=ALU.mult), [Pm], [(NEGM, qg)])
        if debug and qg == 0:
            dump("NEGM0", NEGM, NEGM[:, 0:512], [64, 512], BF16)
            dump("msk", msk, msk[:], [128, 4, 64])
            dump("OUTcmp", OUT, OUT[:], [128, 4, 256])
        for br, (Kt, Vt) in ((1, (KS, Vs)), (2, (KW, Vw))):
            for h in range(4):
                acc = pacc[acnt[0] % 2]
                acnt[0] += 1
                if br == 1:
                    kts = list(range(0, 4 * qg + 4))
                else:
                    kts = list(range(max(0, 4 * qg - 4), 4 * qg + 4))
                pv = []
                for kt in kts:
                    for s in range(4):
                        u = 4 * qg + s - kt
                        if u < 0 or (br == 2 and u > 4):
                            continue
                        pv.append((kt, s))
                npv = len(pv)
                pvset = set(pv)
                ipv = [0]

                def emit_sc(kt):
                    sc = psc[scnt[0] % 2]
                    scnt[0] += 1
                    ks = slice(kt * 128, (kt + 1) * 128)
                    b.mm(sc[:], Kt[0:68, ks], QA[h][0:68, qs], True, False, [Kt, QA[h]], [sc])
                    if br == 1:
                        diag = kt >= 4 * qg
                        b.mm(sc[:], eall[0:64, ks], NEGM[0:64, qs], False, not diag, [eall, (NEGM, qg)], [sc])
                        if diag:
                            b.mm(sc[:], identb[:], cmsel[:, kt - 4 * qg, :], False, True, [identb, cmsel], [sc])
                    else:
                        b.mm(sc[:], identb[:], cmwin[:, kt - 4 * qg + 4, :], False, True, [identb, cmwin], [sc])
                    P_ = PT[pcnt[0] % 3]
                    pcnt[0] += 1
                    b.op("act", lambda e, sc=sc, P_=P_: e.activation(out=P_[:], in_=sc[:], func=AF.Exp, scale=0.125), [sc], [P_])
                    return P_

                def emit_pv(kt, P_):
                    for s in range(4):
                        if (kt, s) not in pvset:
                            continue
                        b.mm(acc[:, s * 65:(s + 1) * 65], P_[:, s * 128:(s + 1) * 128], Vt[:, kt, :], ipv[0] == 0, ipv[0] == npv - 1, [P_, Vt], [acc])
                        ipv[0] += 1
                prevP = emit_sc(kts[0])
                for i in range(len(kts)):
                    nxt = emit_sc(kts[i + 1]) if i + 1 < len(kts) else None
                    emit_pv(kts[i], prevP)
                    prevP = nxt
                if debug and qg == (ngroups or 8) - 1:
                    b.op("dve", lambda e, acc=acc: e.tensor_copy(out=tF[2][:, 0:260], in_=acc[:, 0:260]), [acc], [tF[2]])
                    dump(f"acc_br{br}_h{h}", tF[2], tF[2][:, 0:260], [128, 260])
                accv = acc[:, 0:260].rearrange("p (s c) -> p s c", c=65)
                b.op("dve", lambda e, accv=accv: e.reciprocal(out=rl4[:], in_=accv[:, :, 64]), [acc], [rl4])
                b.op("dve", lambda e, h=h, br=br, qg=qg: e.tensor_tensor(out=rg4[:], in0=rl4[:], in1=Gtm[:, 4 * qg:4 * qg + 4, h * 3 + br], op=ALU.mult), [rl4, Gtm], [rg4])
                for s in range(4):
                    b.op("dve", lambda e, acc=acc, s=s, h=h: e.scalar_tensor_tensor(out=OUT[:, s, h * 64:(h + 1) * 64], in0=acc[:, s * 65:s * 65 + 64], scalar=rg4[:, s:s + 1],
                                                                              in1=OUT[:, s, h * 64:(h + 1) * 64], op0=ALU.mult, op1=ALU.add), [acc, rg4, (OUT, s)], [(OUT, s)])
        if debug and qg == 0:
            dump("OUTfin", OUT, OUT[:], [128, 4, 256])
        for hp in range(2):
            Py = misc()
            for s in range(4):
                b.tr(Py[:, s * 128:(s + 1) * 128], OUT[:, s, hp * 128:(hp + 1) * 128], ident[:], [(OUT, s), ident], [Py])
            b.op("dve" if hp else "act", (lambda e, Py=Py, hp=hp: e.tensor_copy(out=YC[hp][:], in_=Py[:])) if hp else
                 (lambda e, Py=Py, hp=hp: e.activation(out=YC[hp][:], in_=Py[:], func=AF.Copy)), [Py], [YC[hp]])
            b.dma(yT[4 + hp, :, qs], YC[hp][:], [YC[hp]], [(yT, (4 + hp, qg))])
            if debug and qg == 0:
                dump(f"YC{hp}", YC[hp], YC[hp][:], [128, 512], BF16)
    b.end()


import numpy as np

S = 4096


def lay_in(w):
    F = w.shape[1]
    return np.ascontiguousarray(w.reshape(8, 128, F // 128, 128).transpose(2, 1, 0, 3))


def lay_dn(w):
    return np.ascontiguousarray(w.reshape(22, 128, 8, 128).transpose(2, 1, 0, 3))


def lay_vec(v):
    return np.ascontiguousarray(v.reshape(-1, 128).T)


def win_padded(w_in):
    out = np.zeros((1024, 24 * 128), np.float32)
    def put(blk, off, c0, n):
        out[:, blk * 128 + off: blk * 128 + off + n] = w_in[:, c0:c0 + n]
    put(0, 0, 0, 128); put(1, 0, 128, 128)
    B = 256
    put(2, 0, B, 128); put(3, 0, B + 128, 128)
    put(4, 0, B + 256, 128); put(5, 0, B + 384, 128)
    put(6, 0, B + 512, 128); put(7, 0, B + 640, 128)
    put(8, 0, B + 768, 64)
    put(8, 64, B + 768 + 64 + 32, 64)
    put(9, 0, B + 768 + 64, 32)
    C = 256 + 928
    for hh in range(4):
        put(10 + hh, 0, C + 64 * hh, 64)
    put(14, 0, C + 256, 128)
    put(15, 0, C + 384, 128)
    put(16, 0, C + 512, 128)
    put(17, 0, C + 640, 12)
    D = 256 + 928 + 652
    for i in range(6):
        put(18 + i, 0, D + 128 * i, 128)
    return out


def prep_layer(inp, l):
    d = {}
    for nm, key in (("f1", "ffn1"), ("f2", "ffn2")):
        d[f"{nm}_g"] = lay_vec(inp[f"{key}_norm"][l])
        d[f"{nm}_wg"] = lay_in(inp[f"{key}_w_gate"][l])
        d[f"{nm}_wu"] = lay_in(inp[f"{key}_w_up"][l])
        d[f"{nm}_wd"] = lay_dn(inp[f"{key}_w_down"][l])
    d["mix_g"] = lay_vec(inp["mix_norm"][l])
    d["win"] = lay_in(win_padded(inp["w_in"][l]))
    d["wout"] = np.ascontiguousarray(inp["w_out"][l].reshape(8, 128, 8, 128).transpose(2, 1, 0, 3))
    d["cw"] = np.ascontiguousarray(inp["conv_w"][l].reshape(3, 2, 128).transpose(2, 1, 0))
    pw = inp["pool_w"][l]
    pwbd = np.zeros((2, 128, 128), np.float32)
    for j in range(2):
        for gg in range(2):
            pwbd[j, gg * 64:(gg + 1) * 64, gg * 64:(gg + 1) * 64] = pw[2 * j + gg]
    d["pw"] = pwbd
    d["psc"] = lay_vec(inp["pool_scale"][l])
    prep_rwkv(inp, l, d)
    prep_nsa(inp, l, d)
    return d


def consts():
    c = {}
    wins = np.array([2, 4, 8, 16], np.float32)
    w_p = np.zeros((128, 2), np.float32)
    for j in range(2):
        w_p[0:64, j] = wins[2 * j]
        w_p[64:128, j] = wins[2 * j + 1]
    c["pinv"] = (1.0 / w_p).astype(np.float32)
    t = np.arange(16, dtype=np.float32)
    c["pcorr"] = (1.0 / np.minimum(t[None, None, :] + 1, w_p[:, :, None])).astype(np.float32)
    consts_rwkv(c)
    consts_nsa(c)
    return c


def prep_rwkv(inp, l, d):
    mu = inp["rwkv_mu"][l]
    mu8 = np.zeros((128, 8), np.float32)
    for i in range(6):
        mu8[:, i] = mu[i * 128:(i + 1) * 128]
    mu8[0:64, 6] = mu[768:832]
    mu8[64:128, 6] = mu[864:928]
    mu8[0:32, 7] = mu[832:864]
    d["rw_mu8"] = mu8
    vec = np.zeros((128, 2, 7), np.float32)
    names = ["rwkv_w0", "rwkv_a0", "rwkv_k_k", "rwkv_k_a", "rwkv_r_k", "rwkv_ln_w", "rwkv_ln_b"]
    for i, nm in enumerate(names):
        v = inp[nm][l].reshape(256)
        vec[:, 0, i] = v[0:128]
        vec[:, 1, i] = v[128:256]
    d["rw_vec"] = vec
    wup = np.zeros((128, 2, 128), np.float32)
    gup = np.zeros((128, 2, 128), np.float32)
    aup = np.zeros((128, 2, 128), np.float32)
    for hp in range(2):
        wup[0:64, hp, :] = inp["rwkv_w_up"][l][:, hp * 128:(hp + 1) * 128]
        gup[64:128, hp, :] = inp["rwkv_g_up"][l][:, hp * 128:(hp + 1) * 128]
        aup[0:32, hp, :] = inp["rwkv_a_up"][l][:, hp * 128:(hp + 1) * 128]
    d["rw_wup"], d["rw_gup"], d["rw_aup"] = wup, gup, aup


def consts_rwkv(c):
    bones = np.zeros((128, 128), np.float32)
    bones[0:64, 0:64] = 1
    bones[64:128, 64:128] = 1
    c["bones"] = bones
    c["bones64"] = bones / 64.0
    c["ident"] = np.eye(128, dtype=np.float32)
    c["sel2"] = np.concatenate([np.eye(64, dtype=np.float32)] * 2, axis=0)
    rm = np.ones((128, 512), np.float32)
    rm[:, 0::64] = 0
    c["rmask"] = rm
    j = np.arange(64)[:, None]
    t = np.arange(64)[None, :]
    strict = (j < t).astype(np.float32)
    incl = (j <= t).astype(np.float32)
    def bd(m):
        o = np.zeros((128, 128), np.float32)
        o[0:64, 0:64] = m
        o[64:128, 64:128] = m
        return o
    c["LPmask"] = np.concatenate([-bd(strict), bd(incl), bd(strict), bd(incl)], axis=1)
    c["Lamask"] = -bd((j > t).astype(np.float32))


def prep_nsa(inp, l, d):
    qn = np.zeros((128, 1), np.float32)
    qn[0:64, 0] = inp["nsa_q_norm"][l]
    d["ns_qnw"] = qn
    kn = np.zeros((128, 3), np.float32)
    kn[0:64, :] = inp["nsa_k_norm"][l].T
    d["ns_knw"] = kn
    w1 = np.zeros((128, 32, 128), np.float32)
    w1[0:64] = inp["nsa_cmp_k_w1"][l].reshape(32, 64, 128).transpose(1, 0, 2)
    w1[64:128] = inp["nsa_cmp_v_w1"][l].reshape(32, 64, 128).transpose(1, 0, 2)
    d["ns_w1"] = w1
    pos = inp["nsa_cmp_pos"][l]
    p2 = np.zeros((128, 32, 2), np.float32)
    p2[0:64, :, 0] = pos.T
    p2[0:64, :, 1] = pos.T
    p2[64:128] = p2[0:64]
    d["ns_pos2"] = p2
    w2 = np.zeros((128, 2, 64), np.float32)
    w2[:, 0, :] = inp["nsa_cmp_k_w2"][l]
    w2[:, 1, :] = inp["nsa_cmp_v_w2"][l]
    d["ns_w2"] = w2


def consts_nsa(c):
    BIG = 30000.0
    key = np.arange(S)
    c["n_eall"] = (key[None, :] // 64 == np.arange(64)[:, None]).astype(np.float32)
    pk = np.arange(128)[:, None]
    tq = np.arange(512)[None, :]
    c["n_cmsel"] = np.stack([np.where(tq - pk - 128 * r >= 0, 0.0, -BIG) for r in range(4)]).astype(np.float32)
    cw = []
    for r in range(8):
        dd = tq - pk - (r - 4) * 128
        cw.append(np.where((dd >= 0) & (dd < 512), 0.0, -BIG))
    c["n_cmwin"] = np.stack(cw).astype(np.float32)
    c["n_negc"] = np.stack([np.where(tq >= 16 * pk + 31 - 512 * g, 0.0, -BIG) for g in range(5)]).astype(np.float32)
    t = np.arange(S)
    cur = t // 64
    j = np.arange(64)
    ok = (j[None, :] <= cur[:, None]).astype(np.float32)
    forced = ((j[None, :] == 0) | (j[None, :] == cur[:, None]) | (j[None, :] == cur[:, None] - 1)).astype(np.float32)
    addc = forced * 1e4 * ok + (ok - 1.0)
    c["n_ok"] = np.ascontiguousarray(ok.reshape(32, 128, 64).transpose(1, 0, 2))
    c["n_addc"] = np.ascontiguousarray(addc.reshape(32, 128, 64).transpose(1, 0, 2)).astype(np.float32)
    slopes = 2.0 ** (-8.0 * np.arange(1, 5) / 4)
    a_t, b_t = t // 64, t % 64
    qaug = np.zeros((4, 4, S), np.float32)
    for h in range(4):
        s8 = 8.0 * slopes[h]
        qaug[h, 0] = -s8 * 64 * a_t
        qaug[h, 1] = -s8 * b_t
        qaug[h, 2] = s8
        qaug[h, 3] = s8
    c["n_qaug"] = qaug
    kaug = np.zeros((4, S), np.float32)
    kaug[0] = 1
    kaug[1] = 1
    kaug[2] = 64 * a_t
    kaug[3] = b_t
    c["n_kaug"] = kaug
    n = np.arange(256)
    be = 16 * n + 31
    kc = np.zeros((4, 256), np.float32)
    kc[0] = 1
    kc[1] = 1
    kc[2] = 64 * (be // 64)
    kc[3] = be % 64
    c["n_kcaug"] = kc
    ovl = ((16 * n[:, None] <= 64 * j[None, :] + 63) & (16 * n[:, None] + 31 >= 64 * j[None, :])).astype(np.float32)
    o1 = np.concatenate([np.ones((256, 1), np.float32), ovl], axis=1)
    o1[255] = 0
    c["n_ovl1"] = np.ascontiguousarray(o1.reshape(2, 128, 65).transpose(1, 0, 2))


HAVE_RWKV = True
HAVE_NSA = True
_CACHE = {}


def build_program(layer_shapes, const_shapes):
    b = Builder()
    xin = b.dram("xin", [8, 128, S], F32, "ExternalInput")
    xout = b.dram("xout", [8, 128, S], F32, "ExternalOutput")
    xT = b.dram("xT", [8, 128, S], F32)
    pT = b.dram("pT", [NBLK, 128, S], F32)
    yT = b.dram("yT", [8, 128, S], BF16)
    D = []
    for l in range(2):
        D.append({k: b.dram(f"L{l}_{k}", list(shp), F32, "ExternalInput") for k, shp in layer_shapes.items()})
    Cn = {k: b.dram(f"C_{k}", list(shp), F32, "ExternalInput") for k, shp in const_shapes.items()}
    b.begin()
    for k in range(8):
        t = b.sb("t", [128, S], F32)
        b.dma(t[:], xin[k], [xin], [t])
        b.dma(xT[k], t[:], [t], [xT])
    b.end()
    for l in range(2):
        d = D[l]
        ffn_phase(b, xT, d["f1_g"], d["f1_wg"], d["f1_wu"], d["f1_wd"])
        inproj_phase(b, xT, d["mix_g"], d["win"], pT)
        conv_pool_phase(b, pT, yT, d["cw"], d["pw"], d["psc"], Cn["pinv"], Cn["pcorr"])
        if HAVE_RWKV:
            rwkv_phase(b, pT, yT, d, Cn)
        if HAVE_NSA:
            nsa_phase(b, pT, yT, d, Cn)
        outproj_phase(b, xT, yT, d["wout"])
        ffn_phase(b, xT, d["f2_g"], d["f2_wg"], d["f2_wu"], d["f2_wd"])
    b.begin()
    for k in range(8):
        t = b.sb("t", [128, S], F32)
        b.dma(t[:], xT[k], [xT], [t])
        b.dma(xout[k], t[:], [t], [xout])
    b.end()
    return b


def kernel(**inputs):
    inp = {k: np.asarray(v) for k, v in inputs.items()}
    layers = [prep_layer(inp, l) for l in range(2)]
    cn = consts()
    b = build_program({k: v.shape for k, v in layers[0].items()}, {k: v.shape for k, v in cn.items()})
    shared = {}
    for l in range(2):
        for k, v in layers[l].items():
            shared[f"L{l}_{k}"] = v
    for k, v in cn.items():
        shared[f"C_{k}"] = v
    x = inp["x"]
    in_maps = []
    for c in range(8):
        m = dict(shared)
        m["xin"] = np.ascontiguousarray(x[c].T.reshape(8, 128, S))
        in_maps.append(m)
    res = run_bass_kernel_spmd(b.nc, in_maps, core_ids=list(range(8)))
    out = np.stack([res.results[c]["xout"].reshape(1024, S).T for c in range(8)], axis=0)
    return np.ascontiguousarray(out.astype(np.float32))
```

```python
import numpy as np
from contextlib import ExitStack
import concourse.bass as bass
import concourse.mybir as mybir
from concourse.bass_utils import run_bass_kernel_spmd

F32 = mybir.dt.float32
BF16 = mybir.dt.bfloat16
AF = mybir.ActivationFunctionType
ALU = mybir.AluOpType
AX = mybir.AxisListType


class Buf:
    def __init__(self, name, t, space):
        self.name = name
        self.t = t
        self.space = space

    def __getitem__(self, idx):
        return self.t[idx]


def _norm(lst):
    out = []
    for x in lst:
        if isinstance(x, tuple):
            out.append(x)
        else:
            out.append((x, None))
    return out


def _conf(a, b):
    return a is None or b is None or a == b


class Builder:
    ENG = ("pe", "act", "dve", "pool", "sp")
    NDMA = 12

    def __init__(self):
        self.nc = bass.Bass("TRN2", target_bir_lowering=False)
        self.gstack = ExitStack()
        self.sems = []
        self.ekey = {}
        for e in ("pe", "act", "dve", "pool"):
            self.ekey[e] = self._newsem("c_" + e)
        self.dkeys = {q: [self._newsem(f"d_{q}{i}") for i in range(self.NDMA)] for q in ("sp", "pool", "act")}
        self.drr = {"sp": 0, "pool": 0, "act": 0}
        self.semval = [0] * len(self.sems)
        self.seen = {e: {} for e in self.ENG}
        self.recs = {}
        self.ops = {e: [] for e in self.ENG}
        self.pstack = None
        self.nphase = 0
        self.uid = 0

    def _newsem(self, name):
        s = self.gstack.enter_context(self.nc.semaphore(name))
        self.sems.append(s)
        return len(self.sems) - 1

    def dram(self, name, shape, dtype, kind="Internal"):
        t = self.nc.dram_tensor(name, list(shape), dtype, kind=kind)
        return Buf(name, t.ap(), "dram")

    def begin(self):
        assert self.pstack is None
        self.pstack = ExitStack()
        self.nphase += 1

    def sb(self, name, shape, dtype):
        self.uid += 1
        nm = f"{name}_{self.uid}"
        t = self.pstack.enter_context(self.nc.sbuf_tensor(nm, list(shape), dtype))
        return Buf(nm, t, "sbuf")

    def ps(self, name, shape, dtype=F32):
        self.uid += 1
        nm = f"{name}_{self.uid}"
        t = self.pstack.enter_context(self.nc.psum_tensor(nm, list(shape), dtype))
        return Buf(nm, t, "psum")

    def op(self, eng, fn, reads=(), writes=(), dma=False, acc=False):
        reads = _norm(reads)
        writes = _norm(writes)
        if not acc:
            for (bb, tt) in reads:
                if bb.space == "psum" and (bb, tt) not in writes:
                    writes = writes + [(bb, tt)]
        waits = {}

        def need(k, v):
            if v > waits.get(k, 0):
                waits[k] = v

        for (b, t) in _norm(reads):
            for (tag, isw, k), v in self.recs.get(b.name, {}).items():
                if isw and _conf(tag, t):
                    need(k, v)
        for (b, t) in _norm(writes):
            for (tag, isw, k), v in self.recs.get(b.name, {}).items():
                if _conf(tag, t):
                    if acc and isw and k == self.ekey["pe"]:
                        continue
                    need(k, v)
        if dma:
            q = eng
            kd = self.dkeys[q][self.drr[q] % self.NDMA]
            self.drr[q] += 1
            need(kd, self.semval[kd])
            self.semval[kd] += 16
            tok = (kd, self.semval[kd])
            inc = 16
        else:
            kd = self.ekey[eng]
            self.semval[kd] += 1
            tok = (kd, self.semval[kd])
            inc = 1
        wl = []
        for k, v in waits.items():
            if v > 0 and self.seen[eng].get(k, 0) < v:
                self.seen[eng][k] = v
                wl.append((k, v))
        self.ops[eng].append((wl, fn, tok[0], inc))
        for (b, t) in _norm(reads):
            r = self.recs.setdefault(b.name, {})
            key = (t, False, tok[0])
            r[key] = max(r.get(key, 0), tok[1])
        for (b, t) in _norm(writes):
            r = self.recs.setdefault(b.name, {})
            for key in list(r.keys()):
                if t is None or key[0] == t:
                    del r[key]
            r[(t, True, tok[0])] = tok[1]
        return tok

    def end(self):
        wl = []
        for q in self.dkeys:
            for kd in self.dkeys[q]:
                v = self.semval[kd]
                if v > 0 and self.seen["sp"].get(kd, 0) < v:
                    self.seen["sp"][kd] = v
                    wl.append((kd, v))
        self.ops["sp"].append((wl, None, None, 0))
        nc = self.nc
        ops = self.ops
        sems = self.sems

        def replay(lst, e):
            for wl_, fn, k, inc in lst:
                for kk, v in wl_:
                    e.wait_ge(sems[kk], v)
                if fn is not None:
                    ins = fn(e)
                    ins.then_inc(sems[k], inc)

        with nc.Block() as block:
            @block.tensor
            def _(e):
                replay(ops["pe"], e)

            @block.scalar
            def _(e):
                replay(ops["act"], e)

            @block.vector
            def _(e):
                replay(ops["dve"], e)

            @block.gpsimd
            def _(e):
                replay(ops["pool"], e)

            @block.sync
            def _(e):
                replay(ops["sp"], e)
        self.ops = {e: [] for e in self.ENG}
        self.pstack.close()
        self.pstack = None
        for name in list(self.recs.keys()):
            pass

    def dma(self, out, in_, reads, writes, q="sp", **kw):
        return self.op(q, lambda e: e.dma_start(out=out, in_=in_, **kw), reads, writes, dma=True)

    def mm(self, out, lhsT, rhs, start, stop, reads, writes):
        return self.op("pe", lambda e: e.matmul(out, lhsT, rhs, start=start, stop=stop), reads, writes, acc=True)

    def tr(self, out, in_, ident, reads, writes):
        return self.op("pe", lambda e: e.transpose(out, in_, ident), reads, writes, acc=True)

    def actf(self, out, in_, func, reads, writes, bias=None, scale=1.0, eng="act"):
        kw = {}
        if bias is not None:
            kw["bias"] = bias
        return self.op(eng, lambda e: e.activation(out=out, in_=in_, func=func, scale=scale, **kw), reads, writes)


S = 4096
T = 1024
NT = S // T
NH = T // 512
EPS = 1e-6
import os as _os1
_NOW = bool(_os1.environ.get('NO_WDMA'))


def load_norm_h(b, xt_view_fn, x, g, ones, hT, sq, rstd, pst, tile):
    for k in range(8):
        s = sq[k % 2]
        b.op("act", lambda e, s=s, k=k: e.activation(out=s[:], in_=x[:, k, :], func=AF.Square), [x], [s])
        for h in range(NH):
            b.mm(pst[h][:], ones[:], s[:, h * 512:(h + 1) * 512], k == 0, k == 7, [ones, s], [pst[h]])
    for h in range(NH):
        sl = slice(h * 512, (h + 1) * 512)
        b.op("act", lambda e, h=h, sl=sl: e.activation(out=rstd[:, sl], in_=pst[h][:], func=AF.Sqrt, scale=1.0 / 1024, bias=EPS),
             [pst[h]], [(rstd, h)])
        b.op("dve", lambda e, sl=sl: e.reciprocal(out=rstd[:, sl], in_=rstd[:, sl]), [(rstd, h)], [(rstd, h)])
    for k in range(8):
        b.op("dve", lambda e, k=k: e.scalar_tensor_tensor(out=hT[:, k, :], in0=x[:, k, :], scalar=g[:, k:k + 1], in1=rstd[:],
                                                         op0=ALU.mult, op1=ALU.mult), [x, g, rstd], [(hT, k)])


def ffn_phase(b, xT, gD, WgD, WuD, WdD):
    b.begin()
    ones = b.sb("ones", [128, 128], BF16)
    b.op("pool", lambda e: e.memset(ones[:], 1.0), [], [ones])
    epsb = None
    g = b.sb("g", [128, 8], F32)
    b.dma(g[:], gD[:], [gD], [g])
    xs = [b.sb("x", [128, 8, T], F32) for _ in range(2)]
    hT = b.sb("hT", [128, 8, T], BF16)
    hid = b.sb("hid", [128, 22, T], BF16)
    sq = [b.sb("sq", [128, T], BF16) for _ in range(2)]
    rstd = b.sb("rstd", [128, T], F32)
    wg = [b.sb("wg", [128, 8, 128], BF16) for _ in range(2)]
    wu = [b.sb("wu", [128, 8, 128], BF16) for _ in range(2)]
    wd = [b.sb("wd", [128, 22, 128], BF16) for _ in range(2)]
    sg = [b.sb("sg", [128, 512], F32) for _ in range(2)]
    pg = [b.ps("pg", [128, 512]) for _ in range(2)]
    pu = [b.ps("pu", [128, 512]) for _ in range(2)]
    po = [b.ps("po", [128, 512]) for _ in range(2)]
    pst = [b.ps("pst", [128, 512]) for _ in range(2)]
    cnt = 0
    for ti in range(NT):
        x = xs[ti % 2]
        tsl = slice(ti * T, (ti + 1) * T)
        b.dma(x[:], xT[:, :, tsl].rearrange("k p t -> p k t"), [(xT, ti)], [x])
        load_norm_h(b, None, x, g, ones, hT, sq, rstd, pst, ti)
        for f in range(22):
            a, u = wg[f % 2], wu[f % 2]
            if not (_NOW and (ti > 0 or f > 1)):
                b.dma(a[:], WgD[f], [WgD], [a], q="pool", max_dma_last_dim=4096)
                b.dma(u[:], WuD[f], [WuD], [u], q="pool", max_dma_last_dim=4096)
            for h in range(NH):
                sl = slice(h * 512, (h + 1) * 512)
                G, U, SG = pg[cnt % 2], pu[cnt % 2], sg[cnt % 2]
                cnt += 1
                for k in range(8):
                    b.mm(G[:], a[:, k, :], hT[:, k, sl], k == 0, k == 7, [a, (hT, k)], [G])
                for k in range(8):
                    b.mm(U[:], u[:, k, :], hT[:, k, sl], k == 0, k == 7, [u, (hT, k)], [U])
                b.op("act", lambda e, G=G, SG=SG: e.activation(out=SG[:], in_=G[:], func=AF.Silu), [G], [SG])
                b.op("dve", lambda e, U=U, SG=SG, f=f, sl=sl: e.tensor_tensor(out=hid[:, f, sl], in0=U[:], in1=SG[:], op=ALU.mult),
                     [U, SG], [(hid, (f, h))])
        for d in range(8):
            w = wd[d % 2]
            if not (_NOW and (ti > 0 or d > 1)):
                b.dma(w[:], WdD[d], [WdD], [w], q="pool", max_dma_last_dim=4096)
            for h in range(NH):
                sl = slice(h * 512, (h + 1) * 512)
                O = po[cnt % 2]
                cnt += 1
                for f in range(22):
                    b.mm(O[:], w[:, f, :], hid[:, f, sl], f == 0, f == 21, [w, (hid, (f, h))], [O])
                b.op("dve", lambda e, O=O, d=d, sl=sl, x=x: e.scalar_tensor_tensor(out=x[:, d, sl], in0=O[:], scalar=0.5, in1=x[:, d, sl],
                                                                               op0=ALU.mult, op1=ALU.add), [O, x], [x])
        b.dma(xT[:, :, tsl].rearrange("k p t -> p k t"), x[:], [x], [(xT, ti)])
    b.end()


BLK_A = (0, 1)
BLK_R, BLK_K, BLK_V, BLK_WG, BLK_AD = (2, 3), (4, 5), (6, 7), 8, 9
BLK_Q, BLK_KVC, BLK_KVS, BLK_KVW, BLK_GT = (10, 11, 12, 13), 14, 15, 16, 17
BLK_U, BLK_B, BLK_C = (18, 19), (20, 21), (22, 23)
NBLK = 24


def inproj_phase(b, xT, gD, WinD, pT):
    b.begin()
    ones = b.sb("ones", [128, 128], BF16)
    b.op("pool", lambda e: e.memset(ones[:], 1.0), [], [ones])
    g = b.sb("g", [128, 8], F32)
    b.dma(g[:], gD[:], [gD], [g])
    xs = [b.sb("x", [128, 8, T], F32) for _ in range(2)]
    hT = b.sb("hT", [128, 8, T], BF16)
    sq = [b.sb("sq", [128, T], BF16) for _ in range(2)]
    rstd = b.sb("rstd", [128, T], F32)
    ws = [b.sb("w", [128, 8, 128], BF16) for _ in range(2)]
    os_ = [b.sb("o", [128, T], F32) for _ in range(2)]
    pp = [b.ps("pp", [128, 512]) for _ in range(4)]
    pst = [b.ps("pst", [128, 512]) for _ in range(2)]
    cnt = 0
    for ti in range(NT):
        x = xs[ti % 2]
        tsl = slice(ti * T, (ti + 1) * T)
        b.dma(x[:], xT[:, :, tsl].rearrange("k p t -> p k t"), [(xT, ti)], [x])
        load_norm_h(b, None, x, g, ones, hT, sq, rstd, pst, ti)
        for blk in range(NBLK):
            w = ws[blk % 2]
            o = os_[blk % 2]
            b.dma(w[:], WinD[blk], [WinD], [w], q="pool", max_dma_last_dim=4096)
            for h in range(NH):
                sl = slice(h * 512, (h + 1) * 512)
                P = pp[cnt % 4]
                cnt += 1
                for k in range(8):
                    b.mm(P[:], w[:, k, :], hT[:, k, sl], k == 0, k == 7, [w, (hT, k)], [P])
                if h % 2 == 0:
                    b.op("act", lambda e, P=P, o=o, sl=sl: e.activation(out=o[:, sl], in_=P[:], func=AF.Copy), [P], [(o, h)])
                else:
                    b.op("dve", lambda e, P=P, o=o, sl=sl: e.tensor_copy(out=o[:, sl], in_=P[:]), [P], [(o, h)])
            b.dma(pT[blk, :, tsl], o[:], [o], [(pT, (blk, ti))])
    b.end()


def outproj_phase(b, xT, yT, WoutD):
    b.begin()
    xs = [b.sb("x", [128, 8, T], F32) for _ in range(2)]
    ys = [b.sb("y", [128, 8, T], BF16) for _ in range(2)]
    ws = [b.sb("w", [128, 8, 128], BF16) for _ in range(2)]
    pp = [b.ps("pp", [128, 512]) for _ in range(4)]
    cnt = 0
    for ti in range(NT):
        x = xs[ti % 2]
        y = ys[ti % 2]
        tsl = slice(ti * T, (ti + 1) * T)
        b.dma(x[:], xT[:, :, tsl].rearrange("k p t -> p k t"), [(xT, ti)], [x])
        b.dma(y[:], yT[:, :, tsl].rearrange("k p t -> p k t"), [yT], [y])
        for d in range(8):
            w = ws[d % 2]
            b.dma(w[:], WoutD[d], [WoutD], [w], q="pool", max_dma_last_dim=4096)
            for h in range(NH):
                sl = slice(h * 512, (h + 1) * 512)
                P = pp[cnt % 4]
                cnt += 1
                for c in range(8):
                    b.mm(P[:], w[:, c, :], y[:, c, sl], c == 0, c == 7, [w, y], [P])
                b.op("dve", lambda e, P=P, x=x, d=d, sl=sl: e.tensor_tensor(out=x[:, d, sl], in0=P[:], in1=x[:, d, sl], op=ALU.add), [P, x], [x])
        b.dma(xT[:, :, tsl].rearrange("k p t -> p k t"), x[:], [x], [(xT, ti)])
    b.end()


def conv_pool_phase(b, pT, yT, cwD, pwD, pscD, pinvD, pcorrD):
    b.begin()
    cw = b.sb("cw", [128, 2, 3], F32)
    b.dma(cw[:], cwD[:], [cwD], [cw])
    psc = b.sb("psc", [128, 2], F32)
    b.dma(psc[:], pscD[:], [pscD], [psc])
    pinv = b.sb("pinv", [128, 2], F32)
    b.dma(pinv[:], pinvD[:], [pinvD], [pinv])
    pcorr = b.sb("pcorr", [128, 2, 16], F32)
    b.dma(pcorr[:], pcorrD[:], [pcorrD], [pcorr])
    pw = b.sb("pw", [128, 2, 128], BF16)
    for j in range(2):
        b.dma(pw[:, j, :], pwD[j], [pwD], [pw], q="pool")
    u = b.sb("u", [128, 16 + S], F32)
    bb = b.sb("bb", [128, S], F32)
    cc = b.sb("cc", [128, S], F32)
    z = b.sb("z", [128, 16 + S], F32)
    acc = b.sb("acc", [128, 16 + S], F32)
    yb = b.sb("yb", [128, S], BF16)
    pp = [b.ps("pp", [128, 512]) for _ in range(2)]
    for j in range(2):
        b.dma(u[:, 16:], pT[BLK_U[j]], [pT], [u])
        b.dma(bb[:], pT[BLK_B[j]], [pT], [bb])
        b.dma(cc[:], pT[BLK_C[j]], [pT], [cc])
        b.op("pool", lambda e: e.memset(z[:, 0:16], 0.0), [], [z])
        b.op("dve", lambda e: e.tensor_tensor(out=z[:, 16:], in0=cc[:], in1=u[:, 16:], op=ALU.mult), [cc, u], [z])
        b.op("dve", lambda e, j=j: e.tensor_scalar(out=acc[:, 16:], in0=z[:, 16:], scalar1=cw[:, j, 2:3], scalar2=None, op0=ALU.mult), [z, cw], [acc])
        b.op("dve", lambda e, j=j: e.scalar_tensor_tensor(out=acc[:, 16:], in0=z[:, 15:15 + S], scalar=cw[:, j, 1:2], in1=acc[:, 16:],
                                                         op0=ALU.mult, op1=ALU.add), [z, cw, acc], [acc])
        b.op("dve", lambda e, j=j: e.scalar_tensor_tensor(out=acc[:, 16:], in0=z[:, 14:14 + S], scalar=cw[:, j, 0:1], in1=acc[:, 16:],
                                                         op0=ALU.mult, op1=ALU.add), [z, cw, acc], [acc])
        b.op("dve", lambda e: e.tensor_tensor(out=yb[:], in0=bb[:], in1=acc[:, 16:], op=ALU.mult), [bb, acc], [yb])
        b.dma(yT[6 + j], yb[:], [yb], [(yT, 6 + j)])
    for j in range(2):
        b.op("pool", lambda e: e.memset(u[:, 0:16], 0.0), [], [u])
        b.op("pool", lambda e: e.memset(z[:, 0:16], 0.0), [], [z])
        b.op("pool", lambda e: e.memset(acc[:, 0:16], 0.0), [], [acc])
        b.dma(u[:, 16:], pT[BLK_A[j]], [pT], [u])
        R = slice(16, 16 + S)

        def sh(k):
            return slice(16 - k, 16 - k + S)
        if j == 0:
            b.op("dve", lambda e: e.tensor_tensor(out=z[:, R], in0=u[:, R], in1=u[:, sh(1)], op=ALU.add), [u], [z])
            b.op("dve", lambda e: e.tensor_copy(out=acc[0:64, R], in_=z[0:64, R]), [z], [acc])
            b.op("dve", lambda e: e.tensor_tensor(out=acc[64:128, R], in0=z[64:128, R], in1=z[64:128, sh(2)], op=ALU.add), [z], [acc])
        else:
            b.op("dve", lambda e: e.tensor_tensor(out=z[:, R], in0=u[:, R], in1=u[:, sh(1)], op=ALU.add), [u], [z])
            b.op("dve", lambda e: e.tensor_tensor(out=acc[:, R], in0=z[:, R], in1=z[:, sh(2)], op=ALU.add), [z], [acc])
            b.op("dve", lambda e: e.tensor_tensor(out=z[:, R], in0=acc[:, R], in1=acc[:, sh(4)], op=ALU.add), [acc], [z])
            b.op("dve", lambda e: e.tensor_copy(out=acc[0:64, R], in_=z[0:64, R]), [z], [acc])
            b.op("dve", lambda e: e.tensor_tensor(out=acc[64:128, R], in0=z[64:128, R], in1=z[64:128, sh(8)], op=ALU.add), [z], [acc])
        b.op("dve", lambda e, j=j: e.tensor_scalar(out=cc[:], in0=acc[:, R], scalar1=pinv[:, j:j + 1], scalar2=None, op0=ALU.mult), [acc, pinv], [cc])
        b.op("dve", lambda e, j=j: e.tensor_tensor(out=cc[:, 0:16], in0=acc[:, 16:32], in1=pcorr[:, j, :], op=ALU.mult), [acc, pcorr, cc], [cc])
        b.op("dve", lambda e: e.tensor_tensor(out=yb[:], in0=cc[:], in1=u[:, R], op=ALU.subtract), [cc, u], [yb])
        for h in range(S // 512):
            sl = slice(h * 512, (h + 1) * 512)
            P = pp[h % 2]
            b.mm(P[:], pw[:, j, :], yb[:, sl], True, True, [pw, yb], [P])
            b.op("act", lambda e, P=P, sl=sl, j=j: e.activation(out=z[:, sl], in_=P[:], func=AF.Copy, scale=psc[:, j:j + 1]), [P, psc], [(z, h)])
        b.op("dve", lambda e: e.tensor_copy(out=yb[:], in_=z[:, 0:S]), [z], [yb])
        b.dma(yT[j], yb[:], [yb], [(yT, j)])
    b.end()


TT_ = 512
NCH = 8


def run_rr(gens, bg=None):
    gens = list(gens)
    while gens:
        for g in list(gens):
            try:
                next(g)
            except StopIteration:
                gens.remove(g)
        if bg is not None:
            try:
                next(bg)
            except StopIteration:
                pass


DBG = []


def rwkv_phase(b, pT, yT, d, Cn, ntiles=None, debug=False):
    b.begin()

    def dump(name, buf, ap, shape):
        if not debug:
            return
        dd = b.dram("dbg_" + name, list(shape), F32, "ExternalOutput")
        DBG.append("dbg_" + name)
        b.dma(dd[:], ap, [buf], [dd])
    sbF = lambda nm, shp=(128, TT_): b.sb(nm, list(shp), F32)
    def ld(nm, src, shp):
        t = b.sb(nm, shp, F32)
        b.dma(t[:], src[:], [src], [t])
        return t
    bones = ld("bones", Cn["bones"], [128, 128])
    bones64 = ld("bones64", Cn["bones64"], [128, 128])
    ident = ld("ident", Cn["ident"], [128, 128])
    sel2 = ld("sel2", Cn["sel2"], [128, 64])
    rmask = ld("rmask", Cn["rmask"], [128, TT_])
    LPmask = ld("LPmask", Cn["LPmask"], [128, 512])
    Lamask = ld("Lamask", Cn["Lamask"], [128, 128])
    mu8 = ld("mu8", d["rw_mu8"], [128, 8])
    vec = ld("vec", d["rw_vec"], [128, 2, 7])
    WUP = ld("WUP", d["rw_wup"], [128, 2, 128])
    GUP = ld("GUP", d["rw_gup"], [128, 2, 128])
    AUP = ld("AUP", d["rw_aup"], [128, 2, 128])
    V_W0, V_A0, V_KK, V_KA, V_RK, V_LNW, V_LNB = range(7)

    pb2 = [b.sb(f"pb{i}", [128, 1 + TT_], F32) for i in range(2)]
    pb = [pb2[i % 2] for i in range(8)]
    dtmp = sbF("dtmp")
    lp = [sbF(f"lp{i}") for i in range(8)]
    twg = sbF("twg")
    psw = [b.ps("psw", [128, 512]) for _ in range(1)]
    psT2 = [b.ps("psT", [128, 512]) for _ in range(2)]
    psN2 = [b.ps("psN", [128, 512]) for _ in range(2)]
    psS2 = [b.ps("psS", [128, 512]) for _ in range(2)]
    wcnt = [0]

    def wide():
        wcnt[0] += 1
        return psw[0]

    H = {}
    SH = {nm: sbF("sh_" + nm) for nm in ("lw", "cum", "cexc", "dW", "epos", "eexc", "t1", "t2", "kk", "YT", "yc", "sq", "rs")}
    for hp in range(2):
        h = {}
        for nm in ("a", "kkn", "kpp", "akk", "akkh", "eneg", "eW"):
            h[nm] = sbF(nm + str(hp))
        for nm in ("g", "bonus"):
            h[nm] = [sbF(nm + str(hp) + "_" + str(i)) for i in range(2)]
        for nm in ("lw", "cum", "cexc", "dW", "epos", "eexc", "t1", "t2", "kk", "YT", "yc", "sq", "rs"):
            h[nm] = SH[nm]
        h["wC"] = [b.sb(f"wC{hp}_{i}", [128, NCH], F32) for i in range(2)]
        h["QR"] = b.sb(f"QR{hp}", [128, NCH, 2, 64], F32)
        for nm in ("AKKbd", "KHbd", "AKKWbd", "KWbd", "Vbd", "Ybd"):
            h[nm] = b.sb(nm + str(hp), [128, NCH, 128], F32)
            b.op("pool", lambda e, t=h[nm]: e.memset(t[:], 0.0), [], [h[nm]])
        h["QRbd"] = b.sb(f"QRbd{hp}", [128, NCH, 2, 128], F32)
        b.op("pool", lambda e, t=h["QRbd"]: e.memset(t[:], 0.0), [], [h["QRbd"]])
        h["Sst"] = b.sb(f"Sst{hp}", [128, 64], F32)
        b.op("pool", lambda e, t=h["Sst"]: e.memset(t[:], 0.0), [], [h["Sst"]])
        h["LP"] = [b.sb(f"LP{hp}_{i}", [128, 512], F32) for i in range(2)]
        h["AKWT"] = [b.sb(f"AKWT{hp}_{i}", [128, 256], F32) for i in range(2)]
        h["Vst"] = [b.sb(f"Vst{hp}_{i}", [128, 64], F32) for i in range(2)]
        h["TT"] = [b.sb(f"TT{hp}_{i}", [128, 128], F32) for i in range(2)]
        h["MY"] = [b.sb(f"MY{hp}_{i}", [128, 256], F32) for i in range(2)]
        h["N"] = [b.sb(f"N{hp}_{i}", [128, 128], F32) for i in range(2)]
        h["Xst"] = b.sb(f"Xst{hp}", [128, 64], F32)
        h["Ust"] = b.sb(f"Ust{hp}", [128, 64], F32)
        h["ob"] = b.sb(f"ob{hp}", [128, TT_], BF16)
        H[hp] = h

    def v3(t, R):
        return t[R, :].rearrange("p (c t) -> p c t", t=64)

    def stage1_gen(ti):
        t0 = ti * TT_
        for i in range(8):
            blk = 2 + i
            if ti == 0:
                b.op("pool", lambda e, i=i: e.memset(pb[i][:, 0:1], 0.0), [], [pb[i]])
                b.dma(pb[i][:, 1:], pT[blk, :, 0:TT_], [pT], [pb[i]])
            else:
                b.dma(pb[i][:], pT[blk, :, t0 - 1:t0 + TT_], [pT], [pb[i]])
            eng = "dve" if i % 2 == 0 else "pool"
            b.op("dve", lambda e, i=i: e.tensor_tensor(out=dtmp[:], in0=pb[i][:, 0:TT_], in1=pb[i][:, 1:], op=ALU.subtract), [pb[i]], [dtmp])
            b.op("dve", lambda e, i=i: e.scalar_tensor_tensor(out=lp[i][:], in0=dtmp[:], scalar=mu8[:, i:i + 1], in1=pb[i][:, 1:],
                                                             op0=ALU.mult, op1=ALU.add), [dtmp, mu8, pb[i]], [lp[i]])
            yield
        b.op("act", lambda e: e.activation(out=twg[0:64, :], in_=lp[6][0:64, :], func=AF.Tanh), [lp[6]], [(twg, 0)])
        b.op("act", lambda e: e.activation(out=twg[64:128, :], in_=lp[6][64:128, :], func=AF.Sigmoid), [lp[6]], [(twg, 1)])
        for hp in range(2):
            h = H[hp]
            rp, kp, vp = lp[hp], lp[2 + hp], lp[4 + hp]
            vv = lambda i, hp=hp: vec[:, hp, i:i + 1]
            pw_ = wide()
            b.mm(pw_[:], WUP[:, hp, :], twg[:], True, True, [WUP, twg], [pw_])
            b.op("act", lambda e, pw_=pw_, h=h, vv=vv: e.activation(out=h["t1"][:], in_=pw_[:], func=AF.Sigmoid, bias=vv(V_W0)), [pw_, vec], [h["t1"]])
            b.op("dve", lambda e, h=h: e.tensor_scalar(out=h["lw"][:], in0=h["t1"][:], scalar1=-0.6065306597126334, scalar2=None, op0=ALU.mult), [h["t1"]], [h["lw"]])
            pa_ = wide()
            b.mm(pa_[:], AUP[:, hp, :], lp[7][:], True, True, [AUP, lp[7]], [pa_])
            b.op("act", lambda e, pa_=pa_, h=h, vv=vv: e.activation(out=h["a"][:], in_=pa_[:], func=AF.Sigmoid, bias=vv(V_A0)), [pa_, vec], [h["a"]])
            pg_ = wide()
            b.mm(pg_[:], GUP[:, hp, :], twg[:], True, True, [GUP, twg], [pg_])
            b.op("act", lambda e, pg_=pg_, h=h: e.activation(out=h["g"][ti % 2][:], in_=pg_[:], func=AF.Copy), [pg_], [h["g"][ti % 2]])
            yield
            b.op("dve", lambda e, h=h, kp=kp, vv=vv: e.tensor_scalar(out=h["kk"][:], in0=kp[:], scalar1=vv(V_KK), scalar2=None, op0=ALU.mult), [kp, vec], [h["kk"]])
            b.op("act", lambda e, h=h: e.activation(out=h["t1"][:], in_=h["kk"][:], func=AF.Square), [h["kk"]], [h["t1"]])
            pn_ = wide()
            b.mm(pn_[:], bones[:], h["t1"][:], True, True, [bones, h["t1"]], [pn_])
            b.op("dve", lambda e, pn_=pn_, h=h: e.tensor_scalar(out=h["t2"][:], in0=pn_[:], scalar1=1e-24, scalar2=None, op0=ALU.max), [pn_], [h["t2"]])
            b.op("act", lambda e, h=h: e.activation(out=h["t2"][:], in_=h["t2"][:], func=AF.Sqrt), [h["t2"]], [h["t2"]])
            b.op("dve", lambda e, h=h: e.reciprocal(out=h["t2"][:], in_=h["t2"][:]), [h["t2"]], [h["t2"]])
            b.op("dve", lambda e, h=h: e.tensor_tensor(out=h["kkn"][:], in0=h["kk"][:], in1=h["t2"][:], op=ALU.mult), [h["kk"], h["t2"]], [h["kkn"]])
            yield
            b.op("dve", lambda e, h=h, vv=vv: e.tensor_scalar(out=h["t1"][:], in0=h["a"][:], scalar1=-1.0, scalar2=vv(V_KA), op0=ALU.add, op1=ALU.mult), [h["a"], vec], [h["t1"]])
            b.op("dve", lambda e, h=h, kp=kp: e.scalar_tensor_tensor(out=h["kpp"][:], in0=h["t1"][:], scalar=1.0, in1=kp[:], op0=ALU.add, op1=ALU.mult), [h["t1"], kp], [h["kpp"]])
            b.op("dve", lambda e, h=h: e.tensor_tensor(out=h["akk"][:], in0=h["kkn"][:], in1=h["a"][:], op=ALU.mult), [h["kkn"], h["a"]], [h["akk"]])
            yield
            b.op("dve", lambda e, h=h, rp=rp, vv=vv: e.scalar_tensor_tensor(out=h["t1"][:], in0=rp[:], scalar=vv(V_RK), in1=h["kpp"][:], op0=ALU.mult, op1=ALU.mult), [rp, vec, h["kpp"]], [h["t1"]])
            pb_ = wide()
            b.mm(pb_[:], bones[:], h["t1"][:], True, True, [bones, h["t1"]], [pb_])
            b.op("dve", lambda e, pb_=pb_, h=h, vp=vp: e.tensor_tensor(out=h["bonus"][ti % 2][:], in0=pb_[:], in1=vp[:], op=ALU.mult), [pb_, vp], [h["bonus"][ti % 2]])
            yield
            b.op("dve", lambda e, h=h: e.tensor_tensor_scan(out=h["cum"][:], data0=rmask[:], data1=h["lw"][:], initial=0.0, op0=ALU.mult, op1=ALU.add), [rmask, h["lw"]], [h["cum"]])
            b.op("dve", lambda e, h=h: e.tensor_tensor(out=h["cexc"][:], in0=h["cum"][:], in1=h["lw"][:], op=ALU.subtract), [h["cum"], h["lw"]], [h["cexc"]])
            for c in range(NCH):
                cs = slice(c * 64, (c + 1) * 64)
                b.op("pool" if c % 2 else "dve", lambda e, h=h, cs=cs, c=c: e.tensor_scalar(out=h["dW"][:, cs], in0=h["cum"][:, cs], scalar1=-1.0,
                                                                 scalar2=h["cum"][:, c * 64 + 63:c * 64 + 64], op0=ALU.mult, op1=ALU.add), [h["cum"]], [(h["dW"], c)])
            yield
            b.op("act", lambda e, h=h: e.activation(out=h["epos"][:], in_=h["cum"][:], func=AF.Exp), [h["cum"]], [h["epos"]])
            b.op("act", lambda e, h=h: e.activation(out=h["eneg"][:], in_=h["cum"][:], func=AF.Exp, scale=-1.0), [h["cum"]], [h["eneg"]])
            b.op("act", lambda e, h=h: e.activation(out=h["eexc"][:], in_=h["cexc"][:], func=AF.Exp), [h["cexc"]], [h["eexc"]])
            b.op("act", lambda e, h=h: e.activation(out=h["eW"][:], in_=h["dW"][:], func=AF.Exp), [h["dW"]], [h["eW"]])
            b.op("pool", lambda e, h=h: e.tensor_copy(out=h["wC"][ti % 2][:], in_=h["epos"][:, 63::64]), [h["epos"]], [h["wC"][ti % 2]])
            yield
            b.op("dve", lambda e, h=h: e.tensor_tensor(out=h["QR"][:, :, 0, :], in0=v3(h["kkn"], slice(0, 128)), in1=v3(h["eexc"], slice(0, 128)), op=ALU.mult), [h["kkn"], h["eexc"]], [(h["QR"], 0)])
            b.op("pool", lambda e, h=h, rp=rp: e.tensor_tensor(out=h["QR"][:, :, 1, :], in0=v3(rp, slice(0, 128)), in1=v3(h["epos"], slice(0, 128)), op=ALU.mult), [rp, h["epos"]], [(h["QR"], 1)])
            b.op("dve", lambda e, h=h: e.tensor_tensor(out=h["akkh"][:], in0=h["akk"][:], in1=h["eneg"][:], op=ALU.mult), [h["akk"], h["eneg"]], [h["akkh"]])
            yield

    def stage2(ti):
        for hp in range(2):
            h = H[hp]
            vp = lp[4 + hp]
            for hh in range(2):
                R = slice(64 * hh, 64 * hh + 64)
                eng = "pool" if hh else "dve"
                b.op(eng, lambda e, h=h, R=R: e.tensor_copy(out=h["QRbd"][R, :, 0, R], in_=h["QR"][R, :, 0, :]), [(h["QR"], 0)], [(h["QRbd"], hh)])
                b.op(eng, lambda e, h=h, R=R: e.tensor_copy(out=h["QRbd"][R, :, 1, R], in_=h["QR"][R, :, 1, :]), [(h["QR"], 1)], [(h["QRbd"], hh)])
                b.op(eng, lambda e, h=h, R=R: e.tensor_copy(out=h["AKKbd"][R, :, R], in_=v3(h["akkh"], R)), [h["akkh"]], [(h["AKKbd"], hh)])
                b.op(eng, lambda e, h=h, R=R: e.tensor_tensor(out=h["KHbd"][R, :, R], in0=v3(h["kpp"], R), in1=v3(h["eneg"], R), op=ALU.mult), [h["kpp"], h["eneg"]], [(h["KHbd"], hh)])
                b.op(eng, lambda e, h=h, R=R: e.tensor_tensor(out=h["AKKWbd"][R, :, R], in0=v3(h["akk"], R), in1=v3(h["eW"], R), op=ALU.mult), [h["akk"], h["eW"]], [(h["AKKWbd"], hh)])
                b.op(eng, lambda e, h=h, R=R: e.tensor_tensor(out=h["KWbd"][R, :, R], in0=v3(h["kpp"], R), in1=v3(h["eW"], R), op=ALU.mult), [h["kpp"], h["eW"]], [(h["KWbd"], hh)])
                b.op(eng, lambda e, h=h, R=R, vp=vp: e.tensor_copy(out=h["Vbd"][R, :, R], in_=v3(vp, R)), [vp], [(h["Vbd"], hh)])


    NTL = ntiles or (S // TT_)
    run_rr([stage1_gen(0)])
    for ti in range(NTL):
        t0 = ti * TT_
        stage2(ti)
        bg = stage1_gen(ti + 1) if ti + 1 < NTL else None

        def prep_gen(hp, c):
            h = H[hp]
            par = c % 2
            LP, AKWT, Vst, TT = h["LP"][par], h["AKWT"][par], h["Vst"][par], h["TT"][par]
            bX, bY = psT2[hp], psN2[hp]
            MY, NN = h["MY"], h["N"]
            qr = h["QRbd"][:, c, :, :].rearrange("p a t -> p (a t)")
            b.tr(bX[:, 0:128], h["AKKWbd"][:, c, :], ident[:], [h["AKKWbd"], ident], [bX])
            b.tr(bX[:, 128:256], h["KWbd"][:, c, :], ident[:], [h["KWbd"], ident], [bX])
            b.tr(bX[:, 256:384], h["Vbd"][:, c, :], ident[:], [h["Vbd"], ident], [bX])
            b.mm(bX[:, 384:512], h["QRbd"][:, c, 0, :], h["AKKbd"][:, c, :], True, True, [h["QRbd"], h["AKKbd"]], [bX])
            b.mm(bY[:, 0:256], h["AKKbd"][:, c, :], qr, True, True, [h["AKKbd"], h["QRbd"]], [bY])
            b.mm(bY[:, 256:512], h["KHbd"][:, c, :], qr, True, True, [h["KHbd"], h["QRbd"]], [bY])
            yield
            b.op("dve", lambda e: e.tensor_tensor(out=NN[0][:], in0=bX[:, 384:512], in1=Lamask[:], op=ALU.mult), [bX, Lamask], [NN[0]])
            b.op("act", lambda e: e.activation(out=AKWT[:], in_=bX[:, 0:256], func=AF.Copy), [bX], [AKWT])
            b.op("act", lambda e: e.activation(out=Vst[0:64, :], in_=bX[0:64, 256:320], func=AF.Copy), [bX], [(Vst, 0)])
            b.op("act", lambda e: e.activation(out=Vst[64:128, :], in_=bX[64:128, 320:384], func=AF.Copy), [bX], [(Vst, 1)])
            b.op("dve", lambda e: e.tensor_tensor(out=LP[:], in0=bY[:], in1=LPmask[:], op=ALU.mult), [bY, LPmask], [LP])
            b.op("pool", lambda e: e.tensor_tensor(out=MY[1][:, 128:256], in0=LP[:, 0:128], in1=ident[:], op=ALU.add), [LP, ident], [(MY[1], 1)])
            yield
            b.mm(bX[:, 0:128], NN[0][:], LP[:, 0:128], True, True, [NN[0], LP], [bX])
            b.mm(bX[:, 256:384], LP[:, 0:128], NN[0][:], True, True, [LP, NN[0]], [bX])
            yield
            b.op("act", lambda e: e.activation(out=MY[1][:, 0:128], in_=bX[:, 0:128], func=AF.Copy), [bX], [(MY[1], 0)])
            b.op("dve", lambda e: e.tensor_copy(out=NN[1][:], in_=bX[:, 256:384]), [bX], [NN[1]])
            yield
            for k in range(1, 6):
                bank = bY if k % 2 else bX
                i, o = k % 2, (k + 1) % 2
                if k < 5:
                    b.mm(bank[:, 0:256], NN[i][:], MY[i][:, 0:256], True, True, [NN[i], MY[i]], [bank])
                    b.mm(bank[:, 256:384], MY[i][:, 0:128], NN[i][:], True, True, [MY[i], NN[i]], [bank])
                    yield
                    b.op("dve", lambda e, bank=bank, i=i, o=o: e.tensor_tensor(out=MY[o][:, 128:256], in0=bank[:, 128:256], in1=MY[i][:, 128:256], op=ALU.add),
                         [bank, (MY[i], 1)], [(MY[o], 1)])
                    b.op("act", lambda e, bank=bank, o=o: e.activation(out=MY[o][:, 0:128], in_=bank[:, 0:128], func=AF.Copy), [bank], [(MY[o], 0)])
                    b.op("act", lambda e, bank=bank, o=o: e.activation(out=NN[o][:], in_=bank[:, 256:384], func=AF.Copy), [bank], [NN[o]])
                    yield
                else:
                    b.mm(bank[:, 0:128], NN[i][:], MY[i][:, 128:256], True, True, [NN[i], MY[i]], [bank])
                    yield
                    b.op("dve", lambda e, bank=bank, i=i: e.tensor_tensor(out=TT[:], in0=bank[:, 0:128], in1=MY[i][:, 128:256], op=ALU.add),
                         [bank, (MY[i], 1)], [TT])
                    yield

        def seq_gen(hp, c):
            h = H[hp]
            par = c % 2
            LP, AKWT, Vst, TT = h["LP"][par], h["AKWT"][par], h["Vst"][par], h["TT"][par]
            Sst, Xst, Ust = h["Sst"], h["Xst"], h["Ust"]
            so = 0
            psS = psS2[hp]
            sb_ = lambda k: psS
            X_, U_, Y_, S_ = (psS[:, so + 64 * k:so + 64 * k + 64] for k in range(4))
            b.mm(X_, LP[:, 256:384], Vst[:], True, False, [LP, Vst], [sb_(0)])
            b.mm(X_, h["QRbd"][:, c, 0, :], Sst[:], False, True, [h["QRbd"], Sst], [sb_(0)])
            yield
            b.op("dve", lambda e: e.tensor_copy(out=Xst[:], in_=X_), [sb_(0)], [Xst])
            yield
            b.mm(U_, TT[:], Xst[:], True, True, [TT, Xst], [sb_(1)])
            yield
            b.op("dve", lambda e: e.tensor_scalar(out=Ust[:], in0=U_, scalar1=-1.0, scalar2=None, op0=ALU.mult), [sb_(1)], [Ust])
            yield
            b.mm(Y_, h["QRbd"][:, c, 1, :], Sst[:], True, False, [h["QRbd"], Sst], [sb_(2)])
            b.mm(Y_, LP[:, 128:256], Ust[:], False, False, [LP, Ust], [sb_(2)])
            b.mm(Y_, LP[:, 384:512], Vst[:], False, True, [LP, Vst], [sb_(2)])
            b.mm(S_, AKWT[:, 128:256], Vst[:], True, False, [AKWT, Vst], [sb_(3)])
            b.mm(S_, AKWT[:, 0:128], Ust[:], False, True, [AKWT, Ust], [sb_(3)])
            yield
            wCt = h["wC"][ti % 2]
            b.op("dve", lambda e: e.scalar_tensor_tensor(out=Sst[:], in0=Sst[:], scalar=wCt[:, c:c + 1], in1=S_, op0=ALU.mult, op1=ALU.add),
                 [Sst, wCt, sb_(3)], [Sst])
            b.op("act", lambda e: e.activation(out=h["Ybd"][0:64, c, 0:64], in_=psS[0:64, so + 128:so + 192], func=AF.Copy), [sb_(2)], [(h["Ybd"], c)])
            b.op("act", lambda e: e.activation(out=h["Ybd"][64:128, c, 64:128], in_=psS[64:128, so + 128:so + 192], func=AF.Copy), [sb_(2)], [(h["Ybd"], c)])
            yield

        import os as _os
        STOP = int(_os.environ.get("RW_STOP", "9"))
        if STOP <= 1:
            continue
        for c in range(NCH + 1):
            gens = []
            if c < NCH:
                import itertools as _it
                PST = int(_os.environ.get("P_STOP", "999"))
                gens += [_it.islice(prep_gen(0, c), PST), _it.islice(prep_gen(1, c), PST)]
            if c >= 1 and STOP >= 3:
                gens += [seq_gen(0, c - 1), seq_gen(1, c - 1)]
            run_rr(gens, bg)
        if bg is not None:
            for _ in bg:
                pass
        if STOP <= 3:
            continue

        for hp in range(2):
            h = H[hp]
            vv = lambda i, hp=hp: vec[:, hp, i:i + 1]
            py = wide()
            for c in range(NCH):
                b.mm(py[:, c * 64:(c + 1) * 64], h["Ybd"][:, c, :], sel2[:], True, True, [(h["Ybd"], c), sel2], [py])
            b.op("act", lambda e, py=py, h=h: e.activation(out=h["YT"][:], in_=py[:], func=AF.Copy), [py], [h["YT"]])
            if ti == 0:
                dump(f"YT{hp}", h["YT"], h["YT"][:], [128, TT_])
                dump(f"Sst{hp}", h["Sst"], h["Sst"][:], [128, 64])
                dump(f"TT{hp}", h["TT"][1], h["TT"][1][:], [128, 128])
                dump(f"LP{hp}", h["LP"][1], h["LP"][1][:], [128, 512])
            pm = wide()
            b.mm(pm[:], bones64[:], h["YT"][:], True, True, [bones64, h["YT"]], [pm])
            b.op("dve", lambda e, pm=pm, h=h: e.tensor_tensor(out=h["yc"][:], in0=h["YT"][:], in1=pm[:], op=ALU.subtract), [h["YT"], pm], [h["yc"]])
            b.op("act", lambda e, h=h: e.activation(out=h["sq"][:], in_=h["yc"][:], func=AF.Square), [h["yc"]], [h["sq"]])
            pv = wide()
            b.mm(pv[:], bones64[:], h["sq"][:], True, True, [bones64, h["sq"]], [pv])
            b.op("act", lambda e, pv=pv, h=h: e.activation(out=h["rs"][:], in_=pv[:], func=AF.Sqrt, bias=64e-5), [pv], [h["rs"]])
            b.op("dve", lambda e, h=h: e.reciprocal(out=h["rs"][:], in_=h["rs"][:]), [h["rs"]], [h["rs"]])
            b.op("dve", lambda e, h=h: e.tensor_tensor(out=h["yc"][:], in0=h["yc"][:], in1=h["rs"][:], op=ALU.mult), [h["yc"], h["rs"]], [h["yc"]])
            b.op("dve", lambda e, h=h, vv=vv: e.tensor_scalar(out=h["yc"][:], in0=h["yc"][:], scalar1=vv(V_LNW), scalar2=vv(V_LNB), op0=ALU.mult, op1=ALU.add), [h["yc"], vec], [h["yc"]])
            bon, gg = h["bonus"][ti % 2], h["g"][ti % 2]
            b.op("dve", lambda e, h=h, bon=bon: e.tensor_tensor(out=h["yc"][:], in0=h["yc"][:], in1=bon[:], op=ALU.add), [h["yc"], bon], [h["yc"]])
            b.op("dve", lambda e, h=h, gg=gg: e.tensor_tensor(out=h["ob"][:], in0=h["yc"][:], in1=gg[:], op=ALU.mult), [h["yc"], gg], [h["ob"]])
            b.dma(yT[2 + hp, :, t0:t0 + TT_], h["ob"][:], [h["ob"]], [(yT, (2 + hp, ti))])
    b.end()


NBIG = 30000.0


def nsa_phase(b, pT, yT, d, Cn, ngroups=None, debug=False):
    b.begin()
    DBGN = []

    def dump(name, buf, ap, shape, dt=F32):
        if not debug:
            return
        dd = b.dram("dbg_" + name, list(shape), dt, "ExternalOutput")
        b.dma(dd[:], ap, [buf], [dd])

    def ld(nm, src, shp):
        t = b.sb(nm, shp, F32)
        b.dma(t[:], src[:], [src], [t])
        return t

    def ldc(nm, src_ap, srcbuf, shp):
        t = b.sb(nm, shp, BF16)
        b.dma(t[:], src_ap, [srcbuf], [t], q="pool", max_dma_last_dim=4096)
        return t
    ident = ld("ident", Cn["ident"], [128, 128])
    identb = ldc("identb", Cn["ident"][:], Cn["ident"], [128, 128])
    bones = ld("bones", Cn["bones"], [128, 128])
    eall = ldc("eall", Cn["n_eall"][:], Cn["n_eall"], [64, S])
    cmsel = b.sb("cmsel", [128, 4, 512], BF16)
    for r in range(4):
        b.dma(cmsel[:, r, :], Cn["n_cmsel"][r], [Cn["n_cmsel"]], [cmsel], q="pool", max_dma_last_dim=4096)
    cmwin = b.sb("cmwin", [128, 8, 512], BF16)
    for r in range(8):
        b.dma(cmwin[:, r, :], Cn["n_cmwin"][r], [Cn["n_cmwin"]], [cmwin], q="pool", max_dma_last_dim=4096)
    negc = b.sb("negc", [128, 5, 512], BF16)
    for r in range(5):
        b.dma(negc[:, r, :], Cn["n_negc"][r], [Cn["n_negc"]], [negc], q="pool", max_dma_last_dim=4096)
    qnw = ld("qnw", d["ns_qnw"], [128, 1])
    knw = ld("knw", d["ns_knw"], [128, 3])
    w2 = ld("w2", d["ns_w2"], [128, 2, 64])
    pos2 = ld("pos2", d["ns_pos2"], [128, 32, 2])
    QA = [b.sb(f"QA{h}", [128, S], BF16) for h in range(4)]
    KS = b.sb("KS", [128, S], BF16)
    KW = b.sb("KW", [128, S], BF16)
    KC = b.sb("KC", [128, 256], BF16)
    NEGM = b.sb("NEGM", [64, S], BF16)
    Vs = b.sb("Vs", [128, 32, 65], BF16)
    Vw = b.sb("Vw", [128, 32, 65], BF16)
    VC = b.sb("VC", [128, 2, 129], BF16)
    Gtm = b.sb("Gtm", [128, 32, 32], F32)
    rawA = b.sb("rawA", [128, S], F32)
    rawB = b.sb("rawB", [128, S], F32)
    tF = [b.sb(f"tF{i}", [128, 512], F32) for i in range(3)]
    psc = [b.ps("psc", [128, 512]) for _ in range(4)]
    pacc = [b.ps("pacc", [128, 512]) for _ in range(2)]
    pcA = pacc[0]
    pcB = pacc[1]
    pm = [b.ps("pm", [128, 512]) for _ in range(2)]
    mcnt = [0]

    def misc():
        mcnt[0] += 1
        return pm[mcnt[0] % 2]

    rT = [b.sb(f"rT{i}", [64, 512], F32) for i in range(6)]
    rcnt = [0]

    def rms_rows(raw, out, wcol, NC=S, width=512):
        for c0 in range(0, NC, width):
            wd = min(width, NC - c0)
            cs = slice(c0, c0 + wd)
            rcnt[0] += 1
            t0, t1 = rT[(rcnt[0] % 3) * 2], rT[(rcnt[0] % 3) * 2 + 1]
            b.op("act", lambda e, cs=cs, wd=wd, t0=t0: e.activation(out=t0[0:64, 0:wd], in_=raw[0:64, cs], func=AF.Square), [raw], [t0])
            P = misc()
            b.mm(P[0:64, 0:wd], bones[0:64, 0:64], t0[0:64, 0:wd], True, True, [bones, t0], [P])
            b.op("act", lambda e, P=P, wd=wd, t1=t1: e.activation(out=t1[0:64, 0:wd], in_=P[0:64, 0:wd], func=AF.Sqrt, scale=1.0 / 64, bias=EPS), [P], [t1])
            b.op("dve", lambda e, wd=wd, t1=t1: e.reciprocal(out=t1[0:64, 0:wd], in_=t1[0:64, 0:wd]), [t1], [t1])
            b.op("dve", lambda e, cs=cs, wd=wd, t1=t1: e.scalar_tensor_tensor(out=out[0:64, cs], in0=raw[0:64, cs], scalar=wcol, in1=t1[0:64, 0:wd],
                                                                    op0=ALU.mult, op1=ALU.mult), [raw, t1, qnw, knw], [(out, c0)])

    for h in range(4):
        b.dma(rawA[:], pT[BLK_Q[h]], [pT], [rawA])
        rms_rows(rawA, QA[h], qnw[0:64, 0:1])
        b.dma(QA[h][64:68, :], Cn["n_qaug"][h], [Cn["n_qaug"]], [QA[h]], q="pool", max_dma_last_dim=4096)
    for (blk, Kt, Vt, wi) in ((BLK_KVS, KS, Vs, 1), (BLK_KVW, KW, Vw, 2)):
        b.dma(rawA[:], pT[blk], [pT], [rawA])
        rms_rows(rawA, Kt, knw[0:64, wi:wi + 1])
        b.dma(Kt[64:68, :], Cn["n_kaug"][:], [Cn["n_kaug"]], [Kt], q="pool", max_dma_last_dim=4096)
        b.op("pool", lambda e, Vt=Vt: e.memset(Vt[:, :, 64:65], 1.0), [], [Vt])
        for g4 in range(8):
            P = misc()
            for i in range(4):
                kt = g4 * 4 + i
                b.tr(P[:, i * 128:(i + 1) * 128], rawA[:, kt * 128:(kt + 1) * 128], ident[:], [rawA, ident], [P])
            b.op("dve", lambda e, P=P, Vt=Vt, g4=g4: e.tensor_copy(out=Vt[:, g4 * 4:g4 * 4 + 4, 0:64],
                                                                  in_=P[:].rearrange("p (i c) -> p i c", c=128)[:, :, 64:128]), [P], [Vt])
    b.dma(rawA[:], pT[BLK_GT], [pT], [rawA])
    b.op("act", lambda e: e.activation(out=rawA[0:32, :], in_=rawA[0:32, :], func=AF.Sigmoid), [rawA], [rawA])
    for g16 in range(2):
        P = misc()
        for i in range(16):
            kt = g16 * 16 + i
            b.tr(P[:, i * 32:(i + 1) * 32], rawA[0:32, kt * 128:(kt + 1) * 128], ident[0:32, 0:32], [rawA, ident], [P])
        b.op("dve", lambda e, P=P, g16=g16: e.tensor_copy(out=Gtm[:, g16 * 16:(g16 + 1) * 16, :], in_=P[:].rearrange("p (i c) -> p i c", c=32)), [P], [Gtm])
    b.dma(rawA[:], pT[BLK_KVC], [pT], [rawA])
    b.dma(rawB[:], d["ns_w1"][:].rearrange("p l m -> p (l m)"), [d["ns_w1"]], [rawB])
    b.op("pool", lambda e: e.memset(KC[:], 0.0), [], [KC])
    b.op("pool", lambda e: e.memset(VC[:], 0.0), [], [VC])
    bias_sb = b.sb("bias_sb", [128, 4], F32)
    for half in range(2):
        R = slice(64 * half, 64 * half + 64)
        Pb = misc()
        for l in range(32):
            b.mm(Pb[:, 0:2], rawB[R, l * 128:(l + 1) * 128], pos2[R, l, :], l == 0, l == 31, [rawB, pos2], [Pb])
        b.op("dve", lambda e, Pb=Pb, half=half: e.tensor_copy(out=bias_sb[:, 2 * half:2 * half + 2], in_=Pb[:, 0:2]), [Pb], [bias_sb])
    gl = [b.sb(f"gl{i}", [128, 256], F32) for i in range(2)]
    for half in range(2):
        R = slice(64 * half, 64 * half + 64)
        Ph = misc()
        for l in range(32):
            b.mm(Ph[:, 0:255], rawB[R, l * 128:(l + 1) * 128], rawA[R, l:l + 4065:16], l == 0, l == 31, [rawB, rawA], [Ph])
        hb, h2 = tF[0], tF[1]
        b.op("act", lambda e, Ph=Ph, half=half: e.activation(out=hb[:, 0:255], in_=Ph[:, 0:255], func=AF.Identity, bias=bias_sb[:, 2 * half:2 * half + 1]), [Ph, bias_sb], [hb])
        b.op("act", lambda e: e.activation(out=h2[:, 0:255], in_=hb[:, 0:255], func=AF.Square), [hb], [h2])
        b.op("dve", lambda e: e.tensor_scalar(out=h2[:, 0:255], in0=h2[:, 0:255], scalar1=0.044715, scalar2=1.0, op0=ALU.mult, op1=ALU.add), [h2], [h2])
        b.op("dve", lambda e: e.tensor_tensor(out=h2[:, 0:255], in0=h2[:, 0:255], in1=hb[:, 0:255], op=ALU.mult), [h2, hb], [h2])
        b.op("act", lambda e: e.activation(out=h2[:, 0:255], in_=h2[:, 0:255], func=AF.Sigmoid, scale=1.5957691216057308), [h2], [h2])
        b.op("dve", lambda e, half=half: e.tensor_tensor(out=gl[half][:, 0:255], in0=h2[:, 0:255], in1=hb[:, 0:255], op=ALU.mult), [h2, hb], [gl[half]])
    Pk = misc()
    b.mm(Pk[0:64, 0:255], w2[:, 0, :], gl[0][:, 0:255], True, True, [w2, gl[0]], [Pk])
    kc_sb = b.sb("kc_sb", [128, 256], F32)
    b.op("dve", lambda e: e.tensor_copy(out=kc_sb[0:64, 0:255], in_=Pk[0:64, 0:255]), [Pk], [kc_sb])
    rms_rows(kc_sb, KC, knw[0:64, 0:1], NC=255, width=255)
    b.dma(KC[64:68, :], Cn["n_kcaug"][:], [Cn["n_kcaug"]], [KC], q="pool")
    for nt in range(2):
        ncols = 128 if nt == 0 else 127
        Pv = misc()
        b.mm(Pv[0:ncols, 0:64], gl[1][:, nt * 128:nt * 128 + ncols], w2[:, 1, :], True, True, [gl[1], w2], [Pv])
        b.op("dve", lambda e, Pv=Pv, nt=nt, ncols=ncols: e.tensor_copy(out=VC[0:ncols, nt, 0:64], in_=Pv[0:ncols, 0:64]), [Pv], [VC])
        b.dma(VC[:, nt, 64:129], Cn["n_ovl1"][:, nt, :], [Cn["n_ovl1"]], [VC], q="pool")
    if debug:
        dump("gl0", gl[0], gl[0][:, 0:255], [128, 255])
        dump("gl1", gl[1], gl[1][:, 0:255], [128, 255])
        dump("kc_sb", kc_sb, kc_sb[0:64, 0:255], [64, 255])
        dump("bias_sb", bias_sb, bias_sb[:], [128, 4])
        for h in range(4):
            dump(f"QA{h}", QA[h], QA[h][0:68, :], [68, S], BF16)
        dump("KS", KS, KS[0:68, :], [68, S], BF16)
        dump("KC", KC, KC[0:68, :], [68, 256], BF16)
        dump("VC", VC, VC[:], [128, 2, 129], BF16)
        dump("Vs", Vs, Vs[:], [128, 32, 65], BF16)
        dump("Gtm", Gtm, Gtm[:], [128, 32, 32])

    PC = [[b.sb(f"PC{h}_{nt}", [128, 512], BF16) for nt in range(2)] for h in range(4)]
    PT = [b.sb(f"PT{i}", [128, 512], BF16) for i in range(5)]
    OUT = b.sb("OUT", [128, 4, 256], F32)
    okt = b.sb("okt", [128, 4, 64], F32)
    addt = b.sb("addt", [128, 4, 64], F32)
    imp = b.sb("imp", [128, 64], F32)
    imp2 = b.sb("imp2", [128, 64], F32)
    imp3 = b.sb("imp3", [128, 64], F32)
    m8a = b.sb("m8a", [128, 8], F32)
    m8b = b.sb("m8b", [128, 8], F32)
    msk = b.sb("msk", [128, 4, 64], F32)
    l4 = b.sb("l4", [128, 4], F32)
    rg4 = b.sb("rg4", [128, 4], F32)
    rl4 = b.sb("rl4", [128, 4], F32)
    YC = [b.sb(f"YC{i}", [128, 512], BF16) for i in range(2)]
    scnt = [0]
    pcnt = [0]
    acnt = [0]

    for qg in range(ngroups or (S // 512)):
        qs = slice(qg * 512, (qg + 1) * 512)
        b.dma(okt[:], Cn["n_ok"][:, 4 * qg:4 * qg + 4, :], [Cn["n_ok"]], [okt])
        b.dma(addt[:], Cn["n_addc"][:, 4 * qg:4 * qg + 4, :], [Cn["n_addc"]], [addt])
        nts = [0] if qg < 4 else [0, 1]
        for h in range(4):
            for nt in nts:
                g = qg if nt == 0 else qg - 4
                masked = (g <= 4)
                sc = psc[scnt[0] % 4]
                scnt[0] += 1
                b.mm(sc[:], KC[0:68, nt * 128:(nt + 1) * 128], QA[h][0:68, qs], True, not masked, [KC, QA[h]], [sc])
                if masked:
                    b.mm(sc[:], identb[:], negc[:, g, :], False, True, [identb, negc], [sc])
                b.op("act", lambda e, sc=sc, h=h, nt=nt: e.activation(out=PC[h][nt][:], in_=sc[:], func=AF.Exp, scale=0.125), [sc], [PC[h][nt]])
        for s in range(4):
            tile = 4 * qg + s
            ss = slice(s * 128, (s + 1) * 128)
            for h in range(4):
                bank = pcA if h < 2 else pcB
                reg = slice((h % 2) * 129, (h % 2) * 129 + 129)
                for i, nt in enumerate(nts):
                    b.mm(bank[:, reg], PC[h][nt][:, ss], VC[:, nt, :], i == 0, i == len(nts) - 1, [PC[h][nt], VC], [bank])
            for bi, bank in enumerate((pcA, pcB)):
                b.op("dve", lambda e, bank=bank, bi=bi: e.tensor_scalar(out=l4[:, 2 * bi:2 * bi + 2],
                                                                       in0=bank[:, 0:258].rearrange("p (h c) -> p h c", c=129)[:, :, 64],
                                                                       scalar1=1e-30, scalar2=None, op0=ALU.max), [bank], [(l4, bi)])
            b.op("dve", lambda e: e.reciprocal(out=rl4[:], in_=l4[:]), [l4], [rl4])
            b.op("dve", lambda e, tile=tile: e.tensor_tensor(out=rg4[:], in0=rl4[:], in1=Gtm[:, tile, 0:12:3], op=ALU.mult), [rl4, Gtm], [rg4])
            for h in range(4):
                bank = pcA if h < 2 else pcB
                o0 = (h % 2) * 129
                b.op("dve", lambda e, bank=bank, o0=o0, h=h, s=s: e.tensor_scalar(out=OUT[:, s, h * 64:(h + 1) * 64], in0=bank[:, o0:o0 + 64],
                                                                             scalar1=rg4[:, h:h + 1], scalar2=None, op0=ALU.mult), [bank, rg4], [(OUT, s)])
                if h == 0:
                    b.op("dve", lambda e, bank=bank, o0=o0: e.tensor_scalar(out=imp[:], in0=bank[:, o0 + 65:o0 + 129], scalar1=rl4[:, 0:1], scalar2=None, op0=ALU.mult),
                         [bank, rl4], [imp])
                else:
                    b.op("dve", lambda e, bank=bank, o0=o0, h=h: e.scalar_tensor_tensor(out=imp[:], in0=bank[:, o0 + 65:o0 + 129], scalar=rl4[:, h:h + 1], in1=imp[:],
                                                                                   op0=ALU.mult, op1=ALU.add), [bank, rl4, imp], [imp])
            b.op("dve", lambda e, s=s: e.tensor_tensor(out=imp2[:], in0=imp[:], in1=okt[:, s, :], op=ALU.mult), [imp, okt], [imp2])
            b.op("dve", lambda e, s=s: e.tensor_tensor(out=imp2[:], in0=imp2[:], in1=addt[:, s, :], op=ALU.add), [imp2, addt], [imp2])
            b.op("dve", lambda e: e.max(out=m8a[:], in_=imp2[:]), [imp2], [m8a])
            b.op("dve", lambda e: e.match_replace(out=imp3[:], in_to_replace=m8a[:], in_values=imp2[:], imm_value=-1e30), [m8a, imp2], [imp3])
            b.op("dve", lambda e: e.max(out=m8b[:], in_=imp3[:]), [imp3], [m8b])
            b.op("dve", lambda e, s=s: e.tensor_scalar(out=msk[:, s, :], in0=imp2[:], scalar1=m8b[:, 7:8], scalar2=None, op0=ALU.is_ge), [imp2, m8b], [(msk, s)])
        if debug and qg == 0:
            dump("NEGM0", NEGM, NEGM[:, 0:512], [64, 512], BF16)
            dump("msk", msk, msk[:], [128, 4, 64])
            dump("OUTcmp", OUT, OUT[:], [128, 4, 256])
        for br, (Kt, Vt) in ((2, (KW, Vw)), (1, (KS, Vs))):
            if br == 1:
                Pm = misc()
                for s in range(4):
                    b.tr(Pm[0:64, s * 128:(s + 1) * 128], msk[:, s, :], ident[:], [(msk, s), ident], [Pm])
                b.op("dve", lambda e, Pm=Pm, qs=qs: e.tensor_scalar(out=NEGM[:, qs], in0=Pm[0:64, :], scalar1=-1.0, scalar2=NBIG, op0=ALU.add, op1=ALU.mult), [Pm], [(NEGM, qg)])
            for h in range(4):
                acc = pacc[acnt[0] % 2]
                acnt[0] += 1
                if br == 1:
                    kts = list(range(0, 4 * qg + 4))
                else:
                    kts = list(range(max(0, 4 * qg - 4), 4 * qg + 4))
                pv = []
                for kt in kts:
                    for s in range(4):
                        u = 4 * qg + s - kt
                        if u < 0 or (br == 2 and u > 4):
                            continue
                        pv.append((kt, s))
                npv = len(pv)
                pvset = set(pv)
                ipv = [0]

                def emit_sc(kt):
                    sc = psc[scnt[0] % 4]
                    scnt[0] += 1
                    ks = slice(kt * 128, (kt + 1) * 128)
                    b.mm(sc[:], Kt[0:68, ks], QA[h][0:68, qs], True, False, [Kt, QA[h]], [sc])
                    if br == 1:
                        diag = kt >= 4 * qg
                        b.mm(sc[:], eall[0:64, ks], NEGM[0:64, qs], False, not diag, [eall, (NEGM, qg)], [sc])
                        if diag:
                            b.mm(sc[:], identb[:], cmsel[:, kt - 4 * qg, :], False, True, [identb, cmsel], [sc])
                    else:
                        b.mm(sc[:], identb[:], cmwin[:, kt - 4 * qg + 4, :], False, True, [identb, cmwin], [sc])
                    P_ = PT[pcnt[0] % 5]
                    pcnt[0] += 1
                    b.op("act", lambda e, sc=sc, P_=P_: e.activation(out=P_[:], in_=sc[:], func=AF.Exp, scale=0.125), [sc], [P_])
                    return P_

                def emit_pv(kt, P_):
                    for s in range(4):
                        if (kt, s) not in pvset:
                            continue
                        b.mm(acc[:, s * 65:(s + 1) * 65], P_[:, s * 128:(s + 1) * 128], Vt[:, kt, :], ipv[0] == 0, ipv[0] == npv - 1, [P_, Vt], [acc])
                        ipv[0] += 1
                LA = 3
                pend = []
                nk = len(kts)
                for i in range(min(LA, nk)):
                    pend.append(emit_sc(kts[i]))
                for i in range(nk):
                    if i + LA < nk:
                        pend.append(emit_sc(kts[i + LA]))
                    emit_pv(kts[i], pend.pop(0))
                if debug and qg == (ngroups or 8) - 1:
                    b.op("dve", lambda e, acc=acc: e.tensor_copy(out=tF[2][:, 0:260], in_=acc[:, 0:260]), [acc], [tF[2]])
                    dump(f"acc_br{br}_h{h}", tF[2], tF[2][:, 0:260], [128, 260])
                accv = acc[:, 0:260].rearrange("p (s c) -> p s c", c=65)
                b.op("dve", lambda e, accv=accv: e.reciprocal(out=rl4[:], in_=accv[:, :, 64]), [acc], [rl4])
                b.op("dve", lambda e, h=h, br=br, qg=qg: e.tensor_tensor(out=rg4[:], in0=rl4[:], in1=Gtm[:, 4 * qg:4 * qg + 4, h * 3 + br], op=ALU.mult), [rl4, Gtm], [rg4])
                for s in range(4):
                    b.op("dve", lambda e, acc=acc, s=s, h=h: e.scalar_tensor_tensor(out=OUT[:, s, h * 64:(h + 1) * 64], in0=acc[:, s * 65:s * 65 + 64], scalar=rg4[:, s:s + 1],
                                                                              in1=OUT[:, s, h * 64:(h + 1) * 64], op0=ALU.mult, op1=ALU.add), [acc, rg4, (OUT, s)], [(OUT, s)])
        if debug and qg == 0:
            dump("OUTfin", OUT, OUT[:], [128, 4, 256])
        for hp in range(2):
            Py = misc()
            for s in range(4):
                b.tr(Py[:, s * 128:(s + 1) * 128], OUT[:, s, hp * 128:(hp + 1) * 128], ident[:], [(OUT, s), ident], [Py])
            b.op("dve" if hp else "act", (lambda e, Py=Py, hp=hp: e.tensor_copy(out=YC[hp][:], in_=Py[:])) if hp else
                 (lambda e, Py=Py, hp=hp: e.activation(out=YC[hp][:], in_=Py[:], func=AF.Copy)), [Py], [YC[hp]])
            b.dma(yT[4 + hp, :, qs], YC[hp][:], [YC[hp]], [(yT, (4 + hp, qg))])
            if debug and qg == 0:
                dump(f"YC{hp}", YC[hp], YC[hp][:], [128, 512], BF16)
    b.end()


import numpy as np

S = 4096


def lay_in(w):
    F = w.shape[1]
    return np.ascontiguousarray(w.reshape(8, 128, F // 128, 128).transpose(2, 1, 0, 3))


def lay_dn(w):
    return np.ascontiguousarray(w.reshape(22, 128, 8, 128).transpose(2, 1, 0, 3))


def lay_vec(v):
    return np.ascontiguousarray(v.reshape(-1, 128).T)


def win_padded(w_in):
    out = np.zeros((1024, 24 * 128), np.float32)
    def put(blk, off, c0, n):
        out[:, blk * 128 + off: blk * 128 + off + n] = w_in[:, c0:c0 + n]
    put(0, 0, 0, 128); put(1, 0, 128, 128)
    B = 256
    put(2, 0, B, 128); put(3, 0, B + 128, 128)
    put(4, 0, B + 256, 128); put(5, 0, B + 384, 128)
    put(6, 0, B + 512, 128); put(7, 0, B + 640, 128)
    put(8, 0, B + 768, 64)
    put(8, 64, B + 768 + 64 + 32, 64)
    put(9, 0, B + 768 + 64, 32)
    C = 256 + 928
    for hh in range(4):
        put(10 + hh, 0, C + 64 * hh, 64)
    put(14, 0, C + 256, 128)
    put(15, 0, C + 384, 128)
    put(16, 0, C + 512, 128)
    put(17, 0, C + 640, 12)
    D = 256 + 928 + 652
    for i in range(6):
        put(18 + i, 0, D + 128 * i, 128)
    return out


def prep_layer(inp, l):
    d = {}
    for nm, key in (("f1", "ffn1"), ("f2", "ffn2")):
        d[f"{nm}_g"] = lay_vec(inp[f"{key}_norm"][l])
        d[f"{nm}_wg"] = lay_in(inp[f"{key}_w_gate"][l])
        d[f"{nm}_wu"] = lay_in(inp[f"{key}_w_up"][l])
        d[f"{nm}_wd"] = lay_dn(inp[f"{key}_w_down"][l])
    d["mix_g"] = lay_vec(inp["mix_norm"][l])
    d["win"] = lay_in(win_padded(inp["w_in"][l]))
    d["wout"] = np.ascontiguousarray(inp["w_out"][l].reshape(8, 128, 8, 128).transpose(2, 1, 0, 3))
    d["cw"] = np.ascontiguousarray(inp["conv_w"][l].reshape(3, 2, 128).transpose(2, 1, 0))
    pw = inp["pool_w"][l]
    pwbd = np.zeros((2, 128, 128), np.float32)
    for j in range(2):
        for gg in range(2):
            pwbd[j, gg * 64:(gg + 1) * 64, gg * 64:(gg + 1) * 64] = pw[2 * j + gg]
    d["pw"] = pwbd
    d["psc"] = lay_vec(inp["pool_scale"][l])
    prep_rwkv(inp, l, d)
    prep_nsa(inp, l, d)
    return d


def consts():
    c = {}
    wins = np.array([2, 4, 8, 16], np.float32)
    w_p = np.zeros((128, 2), np.float32)
    for j in range(2):
        w_p[0:64, j] = wins[2 * j]
        w_p[64:128, j] = wins[2 * j + 1]
    c["pinv"] = (1.0 / w_p).astype(np.float32)
    t = np.arange(16, dtype=np.float32)
    c["pcorr"] = (1.0 / np.minimum(t[None, None, :] + 1, w_p[:, :, None])).astype(np.float32)
    consts_rwkv(c)
    consts_nsa(c)
    return c


def prep_rwkv(inp, l, d):
    mu = inp["rwkv_mu"][l]
    mu8 = np.zeros((128, 8), np.float32)
    for i in range(6):
        mu8[:, i] = mu[i * 128:(i + 1) * 128]
    mu8[0:64, 6] = mu[768:832]
    mu8[64:128, 6] = mu[864:928]
    mu8[0:32, 7] = mu[832:864]
    d["rw_mu8"] = mu8
    vec = np.zeros((128, 2, 7), np.float32)
    names = ["rwkv_w0", "rwkv_a0", "rwkv_k_k", "rwkv_k_a", "rwkv_r_k", "rwkv_ln_w", "rwkv_ln_b"]
    for i, nm in enumerate(names):
        v = inp[nm][l].reshape(256)
        vec[:, 0, i] = v[0:128]
        vec[:, 1, i] = v[128:256]
    d["rw_vec"] = vec
    wup = np.zeros((128, 2, 128), np.float32)
    gup = np.zeros((128, 2, 128), np.float32)
    aup = np.zeros((128, 2, 128), np.float32)
    for hp in range(2):
        wup[0:64, hp, :] = inp["rwkv_w_up"][l][:, hp * 128:(hp + 1) * 128]
        gup[64:128, hp, :] = inp["rwkv_g_up"][l][:, hp * 128:(hp + 1) * 128]
        aup[0:32, hp, :] = inp["rwkv_a_up"][l][:, hp * 128:(hp + 1) * 128]
    d["rw_wup"], d["rw_gup"], d["rw_aup"] = wup, gup, aup


def consts_rwkv(c):
    bones = np.zeros((128, 128), np.float32)
    bones[0:64, 0:64] = 1
    bones[64:128, 64:128] = 1
    c["bones"] = bones
    c["bones64"] = bones / 64.0
    c["ident"] = np.eye(128, dtype=np.float32)
    c["sel2"] = np.concatenate([np.eye(64, dtype=np.float32)] * 2, axis=0)
    rm = np.ones((128, 512), np.float32)
    rm[:, 0::64] = 0
    c["rmask"] = rm
    j = np.arange(64)[:, None]
    t = np.arange(64)[None, :]
    strict = (j < t).astype(np.float32)
    incl = (j <= t).astype(np.float32)
    def bd(m):
        o = np.zeros((128, 128), np.float32)
        o[0:64, 0:64] = m
        o[64:128, 64:128] = m
        return o
    c["LPmask"] = np.concatenate([-bd(strict), bd(incl), bd(strict), bd(incl)], axis=1)
    c["Lamask"] = -bd((j > t).astype(np.float32))


def prep_nsa(inp, l, d):
    qn = np.zeros((128, 1), np.float32)
    qn[0:64, 0] = inp["nsa_q_norm"][l]
    d["ns_qnw"] = qn
    kn = np.zeros((128, 3), np.float32)
    kn[0:64, :] = inp["nsa_k_norm"][l].T
    d["ns_knw"] = kn
    w1 = np.zeros((128, 32, 128), np.float32)
    w1[0:64] = inp["nsa_cmp_k_w1"][l].reshape(32, 64, 128).transpose(1, 0, 2)
    w1[64:128] = inp["nsa_cmp_v_w1"][l].reshape(32, 64, 128).transpose(1, 0, 2)
    d["ns_w1"] = w1
    pos = inp["nsa_cmp_pos"][l]
    p2 = np.zeros((128, 32, 2), np.float32)
    p2[0:64, :, 0] = pos.T
    p2[0:64, :, 1] = pos.T
    p2[64:128] = p2[0:64]
    d["ns_pos2"] = p2
    w2 = np.zeros((128, 2, 64), np.float32)
    w2[:, 0, :] = inp["nsa_cmp_k_w2"][l]
    w2[:, 1, :] = inp["nsa_cmp_v_w2"][l]
    d["ns_w2"] = w2


def consts_nsa(c):
    BIG = 30000.0
    key = np.arange(S)
    c["n_eall"] = (key[None, :] // 64 == np.arange(64)[:, None]).astype(np.float32)
    pk = np.arange(128)[:, None]
    tq = np.arange(512)[None, :]
    c["n_cmsel"] = np.stack([np.where(tq - pk - 128 * r >= 0, 0.0, -BIG) for r in range(4)]).astype(np.float32)
    cw = []
    for r in range(8):
        dd = tq - pk - (r - 4) * 128
        cw.append(np.where((dd >= 0) & (dd < 512), 0.0, -BIG))
    c["n_cmwin"] = np.stack(cw).astype(np.float32)
    c["n_negc"] = np.stack([np.where(tq >= 16 * pk + 31 - 512 * g, 0.0, -BIG) for g in range(5)]).astype(np.float32)
    t = np.arange(S)
    cur = t // 64
    j = np.arange(64)
    ok = (j[None, :] <= cur[:, None]).astype(np.float32)
    forced = ((j[None, :] == 0) | (j[None, :] == cur[:, None]) | (j[None, :] == cur[:, None] - 1)).astype(np.float32)
    addc = forced * 1e4 * ok + (ok - 1.0)
    c["n_ok"] = np.ascontiguousarray(ok.reshape(32, 128, 64).transpose(1, 0, 2))
    c["n_addc"] = np.ascontiguousarray(addc.reshape(32, 128, 64).transpose(1, 0, 2)).astype(np.float32)
    slopes = 2.0 ** (-8.0 * np.arange(1, 5) / 4)
    a_t, b_t = t // 64, t % 64
    qaug = np.zeros((4, 4, S), np.float32)
    for h in range(4):
        s8 = 8.0 * slopes[h]
        qaug[h, 0] = -s8 * 64 * a_t
        qaug[h, 1] = -s8 * b_t
        qaug[h, 2] = s8
        qaug[h, 3] = s8
    c["n_qaug"] = qaug
    kaug = np.zeros((4, S), np.float32)
    kaug[0] = 1
    kaug[1] = 1
    kaug[2] = 64 * a_t
    kaug[3] = b_t
    c["n_kaug"] = kaug
    n = np.arange(256)
    be = 16 * n + 31
    kc = np.zeros((4, 256), np.float32)
    kc[0] = 1
    kc[1] = 1
    kc[2] = 64 * (be // 64)
    kc[3] = be % 64
    c["n_kcaug"] = kc
    ovl = ((16 * n[:, None] <= 64 * j[None, :] + 63) & (16 * n[:, None] + 31 >= 64 * j[None, :])).astype(np.float32)
    o1 = np.concatenate([np.ones((256, 1), np.float32), ovl], axis=1)
    o1[255] = 0
    c["n_ovl1"] = np.ascontiguousarray(o1.reshape(2, 128, 65).transpose(1, 0, 2))


HAVE_RWKV = True
HAVE_NSA = True
_CACHE = {}


def build_program(layer_shapes, const_shapes):
    b = Builder()
    xin = b.dram("xin", [8, 128, S], F32, "ExternalInput")
    xout = b.dram("xout", [8, 128, S], F32, "ExternalOutput")
    xT = b.dram("xT", [8, 128, S], F32)
    pT = b.dram("pT", [NBLK, 128, S], F32)
    yT = b.dram("yT", [8, 128, S], BF16)
    D = []
    for l in range(2):
        D.append({k: b.dram(f"L{l}_{k}", list(shp), F32, "ExternalInput") for k, shp in layer_shapes.items()})
    Cn = {k: b.dram(f"C_{k}", list(shp), F32, "ExternalInput") for k, shp in const_shapes.items()}
    b.begin()
    for k in range(8):
        t = b.sb("t", [128, S], F32)
        b.dma(t[:], xin[k], [xin], [t])
        b.dma(xT[k], t[:], [t], [xT])
    b.end()
    for l in range(2):
        d = D[l]
        ffn_phase(b, xT, d["f1_g"], d["f1_wg"], d["f1_wu"], d["f1_wd"])
        inproj_phase(b, xT, d["mix_g"], d["win"], pT)
        conv_pool_phase(b, pT, yT, d["cw"], d["pw"], d["psc"], Cn["pinv"], Cn["pcorr"])
        if HAVE_RWKV:
            rwkv_phase(b, pT, yT, d, Cn)
        if HAVE_NSA:
            nsa_phase(b, pT, yT, d, Cn)
        outproj_phase(b, xT, yT, d["wout"])
        ffn_phase(b, xT, d["f2_g"], d["f2_wg"], d["f2_wu"], d["f2_wd"])
    b.begin()
    for k in range(8):
        t = b.sb("t", [128, S], F32)
        b.dma(t[:], xT[k], [xT], [t])
        b.dma(xout[k], t[:], [t], [xout])
    b.end()
    return b


def kernel(**inputs):
    inp = {k: np.asarray(v) for k, v in inputs.items()}
    layers = [prep_layer(inp, l) for l in range(2)]
    cn = consts()
    b = build_program({k: v.shape for k, v in layers[0].items()}, {k: v.shape for k, v in cn.items()})
    shared = {}
    for l in range(2):
        for k, v in layers[l].items():
            shared[f"L{l}_{k}"] = v
    for k, v in cn.items():
        shared[f"C_{k}"] = v
    x = inp["x"]
    in_maps = []
    for c in range(8):
        m = dict(shared)
        m["xin"] = np.ascontiguousarray(x[c].T.reshape(8, 128, S))
        in_maps.append(m)
    res = run_bass_kernel_spmd(b.nc, in_maps, core_ids=list(range(8)))
    out = np.stack([res.results[c]["xout"].reshape(1024, S).T for c in range(8)], axis=0)
    return np.ascontiguousarray(out.astype(np.float32))
```

```python
import numpy as np
from contextlib import ExitStack
import concourse.bass as bass
import concourse.mybir as mybir
from concourse.bass_utils import run_bass_kernel_spmd

F32 = mybir.dt.float32
BF16 = mybir.dt.bfloat16
AF = mybir.ActivationFunctionType
ALU = mybir.AluOpType
AX = mybir.AxisListType


class Buf:
    def __init__(self, name, t, space):
        self.name = name
        self.t = t
        self.space = space

    def __getitem__(self, idx):
        return self.t[idx]


def _norm(lst):
    out = []
    for x in lst:
        if isinstance(x, tuple):
            out.append(x)
        else:
            out.append((x, None))
    return out


def _conf(a, b):
    return a is None or b is None or a == b


class Builder:
    ENG = ("pe", "act", "dve", "pool", "sp")
    NDMA = 12

    def __init__(self):
        self.nc = bass.Bass("TRN2", target_bir_lowering=False)
        self.gstack = ExitStack()
        self.sems = []
        self.ekey = {}
        for e in ("pe", "act", "dve", "pool"):
            self.ekey[e] = self._newsem("c_" + e)
        self.dkeys = {q: [self._newsem(f"d_{q}{i}") for i in range(self.NDMA)] for q in ("sp", "pool", "act")}
        self.drr = {"sp": 0, "pool": 0, "act": 0}
        self.semval = [0] * len(self.sems)
        self.seen = {e: {} for e in self.ENG}
        self.recs = {}
        self.ops = {e: [] for e in self.ENG}
        self.pstack = None
        self.nphase = 0
        self.uid = 0

    def _newsem(self, name):
        s = self.gstack.enter_context(self.nc.semaphore(name))
        self.sems.append(s)
        return len(self.sems) - 1

    def dram(self, name, shape, dtype, kind="Internal"):
        t = self.nc.dram_tensor(name, list(shape), dtype, kind=kind)
        return Buf(name, t.ap(), "dram")

    def begin(self):
        assert self.pstack is None
        self.pstack = ExitStack()
        self.nphase += 1

    def sb(self, name, shape, dtype):
        self.uid += 1
        nm = f"{name}_{self.uid}"
        t = self.pstack.enter_context(self.nc.sbuf_tensor(nm, list(shape), dtype))
        return Buf(nm, t, "sbuf")

    def ps(self, name, shape, dtype=F32):
        self.uid += 1
        nm = f"{name}_{self.uid}"
        t = self.pstack.enter_context(self.nc.psum_tensor(nm, list(shape), dtype))
        return Buf(nm, t, "psum")

    def op(self, eng, fn, reads=(), writes=(), dma=False, acc=False):
        reads = _norm(reads)
        writes = _norm(writes)
        if not acc:
            for (bb, tt) in reads:
                if bb.space == "psum" and (bb, tt) not in writes:
                    writes = writes + [(bb, tt)]
        waits = {}

        def need(k, v):
            if v > waits.get(k, 0):
                waits[k] = v

        for (b, t) in _norm(reads):
            for (tag, isw, k), v in self.recs.get(b.name, {}).items():
                if isw and _conf(tag, t):
                    need(k, v)
        for (b, t) in _norm(writes):
            for (tag, isw, k), v in self.recs.get(b.name, {}).items():
                if _conf(tag, t):
                    if acc and isw and k == self.ekey["pe"]:
                        continue
                    need(k, v)
        if dma:
            q = eng
            kd = self.dkeys[q][self.drr[q] % self.NDMA]
            self.drr[q] += 1
            need(kd, self.semval[kd])
            self.semval[kd] += 16
            tok = (kd, self.semval[kd])
            inc = 16
        else:
            kd = self.ekey[eng]
            self.semval[kd] += 1
            tok = (kd, self.semval[kd])
            inc = 1
        wl = []
        for k, v in waits.items():
            if v > 0 and self.seen[eng].get(k, 0) < v:
                self.seen[eng][k] = v
                wl.append((k, v))
        self.ops[eng].append((wl, fn, tok[0], inc))
        for (b, t) in _norm(reads):
            r = self.recs.setdefault(b.name, {})
            key = (t, False, tok[0])
            r[key] = max(r.get(key, 0), tok[1])
        for (b, t) in _norm(writes):
            r = self.recs.setdefault(b.name, {})
            for key in list(r.keys()):
                if t is None or key[0] == t:
                    del r[key]
            r[(t, True, tok[0])] = tok[1]
        return tok

    def end(self):
        wl = []
        for q in self.dkeys:
            for kd in self.dkeys[q]:
                v = self.semval[kd]
                if v > 0 and self.seen["sp"].get(kd, 0) < v:
                    self.seen["sp"][kd] = v
                    wl.append((kd, v))
        self.ops["sp"].append((wl, None, None, 0))
        nc = self.nc
        ops = self.ops
        sems = self.sems

        def replay(lst, e):
            for wl_, fn, k, inc in lst:
                for kk, v in wl_:
                    e.wait_ge(sems[kk], v)
                if fn is not None:
                    ins = fn(e)
                    ins.then_inc(sems[k], inc)

        with nc.Block() as block:
            @block.tensor
            def _(e):
                replay(ops["pe"], e)

            @block.scalar
            def _(e):
                replay(ops["act"], e)

            @block.vector
            def _(e):
                replay(ops["dve"], e)

            @block.gpsimd
            def _(e):
                replay(ops["pool"], e)

            @block.sync
            def _(e):
                replay(ops["sp"], e)
        self.ops = {e: [] for e in self.ENG}
        self.pstack.close()
        self.pstack = None
        for name in list(self.recs.keys()):
            pass

    def dma(self, out, in_, reads, writes, q="sp", **kw):
        return self.op(q, lambda e: e.dma_start(out=out, in_=in_, **kw), reads, writes, dma=True)

    def mm(self, out, lhsT, rhs, start, stop, reads, writes):
        return self.op("pe", lambda e: e.matmul(out, lhsT, rhs, start=start, stop=stop), reads, writes, acc=True)

    def tr(self, out, in_, ident, reads, writes):
        return self.op("pe", lambda e: e.transpose(out, in_, ident), reads, writes, acc=True)

    def actf(self, out, in_, func, reads, writes, bias=None, scale=1.0, eng="act"):
        kw = {}
        if bias is not None:
            kw["bias"] = bias
        return self.op(eng, lambda e: e.activation(out=out, in_=in_, func=func, scale=scale, **kw), reads, writes)


S = 4096
T = 1024
NT = S // T
NH = T // 512
EPS = 1e-6
import os as _os1
_NOW = bool(_os1.environ.get('NO_WDMA'))


def load_norm_gen(b, xT, ti, x, g, ones, hT, sq, rstd, pst):
    tsl = slice(ti * T, (ti + 1) * T)
    b.dma(x[:], xT[:, :, tsl].rearrange("k p t -> p k t"), [(xT, ti)], [x])
    yield
    for k in range(8):
        s = sq[k % 2]
        b.op("act", lambda e, s=s, k=k: e.activation(out=s[:], in_=x[:, k, :], func=AF.Square), [x], [s])
        for h in range(NH):
            b.mm(pst[h][:], ones[:], s[:, h * 512:(h + 1) * 512], k == 0, k == 7, [ones, s], [pst[h]])
        yield
    for h in range(NH):
        sl = slice(h * 512, (h + 1) * 512)
        b.op("act", lambda e, h=h, sl=sl: e.activation(out=rstd[:, sl], in_=pst[h][:], func=AF.Sqrt, scale=1.0 / 1024, bias=EPS),
             [pst[h]], [(rstd, h)])
        b.op("dve", lambda e, sl=sl: e.reciprocal(out=rstd[:, sl], in_=rstd[:, sl]), [(rstd, h)], [(rstd, h)])
    yield
    for k in range(8):
        b.op("dve", lambda e, k=k: e.scalar_tensor_tensor(out=hT[:, k, :], in0=x[:, k, :], scalar=g[:, k:k + 1], in1=rstd[:],
                                                         op0=ALU.mult, op1=ALU.mult), [x, g, rstd], [(hT, k)])
        yield


def gstep(gen):
    if gen is None:
        return None
    try:
        next(gen)
        return gen
    except StopIteration:
        return None


def gdrain(gen):
    while gen is not None:
        gen = gstep(gen)


def ffn_phase(b, xT, gD, WgD, WuD, WdD):
    b.begin()
    ones = b.sb("ones", [128, 128], BF16)
    b.op("pool", lambda e: e.memset(ones[:], 1.0), [], [ones])
    epsb = None
    g = b.sb("g", [128, 8], F32)
    b.dma(g[:], gD[:], [gD], [g])
    xs = [b.sb("x", [128, 8, T], F32) for _ in range(2)]
    hTs = [b.sb("hT", [128, 8, T], BF16) for _ in range(2)]
    hid = b.sb("hid", [128, 22, T], BF16)
    sq = [b.sb("sq", [128, T], BF16) for _ in range(2)]
    rstds = [b.sb("rstd", [128, T], F32) for _ in range(2)]
    wg = [b.sb("wg", [128, 8, 128], BF16) for _ in range(2)]
    wu = [b.sb("wu", [128, 8, 128], BF16) for _ in range(2)]
    wd = [b.sb("wd", [128, 22, 128], BF16) for _ in range(2)]
    sg = [b.sb("sg", [128, 512], F32) for _ in range(2)]
    pg = [b.ps("pg", [128, 512]) for _ in range(2)]
    pu = [b.ps("pu", [128, 512]) for _ in range(2)]
    po = [b.ps("po", [128, 512]) for _ in range(2)]
    pst = [b.ps("pst", [128, 512]) for _ in range(2)]
    cnt = 0
    gdrain(load_norm_gen(b, xT, 0, xs[0], g, ones, hTs[0], sq, rstds[0], pst))
    for ti in range(NT):
        x = xs[ti % 2]
        hT = hTs[ti % 2]
        tsl = slice(ti * T, (ti + 1) * T)
        nxt = None
        for f in range(22):
            if f == 2 and ti + 1 < NT:
                nxt = load_norm_gen(b, xT, ti + 1, xs[(ti + 1) % 2], g, ones, hTs[(ti + 1) % 2], sq, rstds[(ti + 1) % 2], pst)
            nxt = gstep(nxt)
            a, u = wg[f % 2], wu[f % 2]
            if not (_NOW and (ti > 0 or f > 1)):
                b.dma(a[:], WgD[f], [WgD], [a], q="pool", max_dma_last_dim=4096)
                b.dma(u[:], WuD[f], [WuD], [u], q="pool", max_dma_last_dim=4096)
            for h in range(NH):
                sl = slice(h * 512, (h + 1) * 512)
                G, U, SG = pg[cnt % 2], pu[cnt % 2], sg[cnt % 2]
                cnt += 1
                for k in range(8):
                    b.mm(G[:], a[:, k, :], hT[:, k, sl], k == 0, k == 7, [a, (hT, k)], [G])
                for k in range(8):
                    b.mm(U[:], u[:, k, :], hT[:, k, sl], k == 0, k == 7, [u, (hT, k)], [U])
                b.op("act", lambda e, G=G, SG=SG: e.activation(out=SG[:], in_=G[:], func=AF.Silu), [G], [SG])
                b.op("dve", lambda e, U=U, SG=SG, f=f, sl=sl: e.tensor_tensor(out=hid[:, f, sl], in0=U[:], in1=SG[:], op=ALU.mult),
                     [U, SG], [(hid, (f, h))])
        for d in range(8):
            w = wd[d % 2]
            if not (_NOW and (ti > 0 or d > 1)):
                b.dma(w[:], WdD[d], [WdD], [w], q="pool", max_dma_last_dim=4096)
            for h in range(NH):
                sl = slice(h * 512, (h + 1) * 512)
                O = po[cnt % 2]
                cnt += 1
                for f in range(22):
                    b.mm(O[:], w[:, f, :], hid[:, f, sl], f == 0, f == 21, [w, (hid, (f, h))], [O])
                b.op("dve", lambda e, O=O, d=d, sl=sl, x=x: e.scalar_tensor_tensor(out=x[:, d, sl], in0=O[:], scalar=0.5, in1=x[:, d, sl],
                                                                               op0=ALU.mult, op1=ALU.add), [O, x], [x])
        gdrain(nxt)
        b.dma(xT[:, :, tsl].rearrange("k p t -> p k t"), x[:], [x], [(xT, ti)])
    b.end()


BLK_A = (0, 1)
BLK_R, BLK_K, BLK_V, BLK_WG, BLK_AD = (2, 3), (4, 5), (6, 7), 8, 9
BLK_Q, BLK_KVC, BLK_KVS, BLK_KVW, BLK_GT = (10, 11, 12, 13), 14, 15, 16, 17
BLK_U, BLK_B, BLK_C = (18, 19), (20, 21), (22, 23)
NBLK = 24


def inproj_phase(b, xT, gD, WinD, pT):
    b.begin()
    ones = b.sb("ones", [128, 128], BF16)
    b.op("pool", lambda e: e.memset(ones[:], 1.0), [], [ones])
    g = b.sb("g", [128, 8], F32)
    b.dma(g[:], gD[:], [gD], [g])
    xs = [b.sb("x", [128, 8, T], F32) for _ in range(2)]
    hTs = [b.sb("hT", [128, 8, T], BF16) for _ in range(2)]
    sq = [b.sb("sq", [128, T], BF16) for _ in range(2)]
    rstds = [b.sb("rstd", [128, T], F32) for _ in range(2)]
    ws = [b.sb("w", [128, 8, 128], BF16) for _ in range(2)]
    os_ = [b.sb("o", [128, T], F32) for _ in range(2)]
    pp = [b.ps("pp", [128, 512]) for _ in range(4)]
    pst = [b.ps("pst", [128, 512]) for _ in range(2)]
    cnt = 0
    gdrain(load_norm_gen(b, xT, 0, xs[0], g, ones, hTs[0], sq, rstds[0], pst))
    for ti in range(NT):
        x = xs[ti % 2]
        hT = hTs[ti % 2]
        tsl = slice(ti * T, (ti + 1) * T)
        nxt = None
        for blk in range(NBLK):
            if blk == 2 and ti + 1 < NT:
                nxt = load_norm_gen(b, xT, ti + 1, xs[(ti + 1) % 2], g, ones, hTs[(ti + 1) % 2], sq, rstds[(ti + 1) % 2], pst)
            nxt = gstep(nxt)
            w = ws[blk % 2]
            o = os_[blk % 2]
            b.dma(w[:], WinD[blk], [WinD], [w], q="pool", max_dma_last_dim=4096)
            for h in range(NH):
                sl = slice(h * 512, (h + 1) * 512)
                P = pp[cnt % 4]
                cnt += 1
                for k in range(8):
                    b.mm(P[:], w[:, k, :], hT[:, k, sl], k == 0, k == 7, [w, (hT, k)], [P])
                if h % 2 == 0:
                    b.op("act", lambda e, P=P, o=o, sl=sl: e.activation(out=o[:, sl], in_=P[:], func=AF.Copy), [P], [(o, h)])
                else:
                    b.op("dve", lambda e, P=P, o=o, sl=sl: e.tensor_copy(out=o[:, sl], in_=P[:]), [P], [(o, h)])
            b.dma(pT[blk, :, tsl], o[:], [o], [(pT, (blk, ti))])
        gdrain(nxt)
    b.end()


def outproj_phase(b, xT, yT, WoutD):
    b.begin()
    xs = [b.sb("x", [128, 8, T], F32) for _ in range(2)]
    ys = [b.sb("y", [128, 8, T], BF16) for _ in range(2)]
    ws = [b.sb("w", [128, 8, 128], BF16) for _ in range(2)]
    pp = [b.ps("pp", [128, 512]) for _ in range(4)]
    cnt = 0
    for ti in range(NT):
        x = xs[ti % 2]
        y = ys[ti % 2]
        tsl = slice(ti * T, (ti + 1) * T)
        b.dma(x[:], xT[:, :, tsl].rearrange("k p t -> p k t"), [(xT, ti)], [x])
        b.dma(y[:], yT[:, :, tsl].rearrange("k p t -> p k t"), [yT], [y])
        for d in range(8):
            w = ws[d % 2]
            b.dma(w[:], WoutD[d], [WoutD], [w], q="pool", max_dma_last_dim=4096)
            for h in range(NH):
                sl = slice(h * 512, (h + 1) * 512)
                P = pp[cnt % 4]
                cnt += 1
                for c in range(8):
                    b.mm(P[:], w[:, c, :], y[:, c, sl], c == 0, c == 7, [w, y], [P])
                b.op("dve", lambda e, P=P, x=x, d=d, sl=sl: e.tensor_tensor(out=x[:, d, sl], in0=P[:], in1=x[:, d, sl], op=ALU.add), [P, x], [x])
        b.dma(xT[:, :, tsl].rearrange("k p t -> p k t"), x[:], [x], [(xT, ti)])
    b.end()


def conv_pool_phase(b, pT, yT, cwD, pwD, pscD, pinvD, pcorrD):
    b.begin()
    cw = b.sb("cw", [128, 2, 3], F32)
    b.dma(cw[:], cwD[:], [cwD], [cw])
    psc = b.sb("psc", [128, 2], F32)
    b.dma(psc[:], pscD[:], [pscD], [psc])
    pinv = b.sb("pinv", [128, 2], F32)
    b.dma(pinv[:], pinvD[:], [pinvD], [pinv])
    pcorr = b.sb("pcorr", [128, 2, 16], F32)
    b.dma(pcorr[:], pcorrD[:], [pcorrD], [pcorr])
    pw = b.sb("pw", [128, 2, 128], BF16)
    for j in range(2):
        b.dma(pw[:, j, :], pwD[j], [pwD], [pw], q="pool")
    u = b.sb("u", [128, 16 + S], F32)
    bb = b.sb("bb", [128, S], F32)
    cc = b.sb("cc", [128, S], F32)
    z = b.sb("z", [128, 16 + S], F32)
    acc = b.sb("acc", [128, 16 + S], F32)
    yb = b.sb("yb", [128, S], BF16)
    pp = [b.ps("pp", [128, 512]) for _ in range(2)]
    for j in range(2):
        b.dma(u[:, 16:], pT[BLK_U[j]], [pT], [u])
        b.dma(bb[:], pT[BLK_B[j]], [pT], [bb])
        b.dma(cc[:], pT[BLK_C[j]], [pT], [cc])
        b.op("pool", lambda e: e.memset(z[:, 0:16], 0.0), [], [z])
        b.op("dve", lambda e: e.tensor_tensor(out=z[:, 16:], in0=cc[:], in1=u[:, 16:], op=ALU.mult), [cc, u], [z])
        b.op("dve", lambda e, j=j: e.tensor_scalar(out=acc[:, 16:], in0=z[:, 16:], scalar1=cw[:, j, 2:3], scalar2=None, op0=ALU.mult), [z, cw], [acc])
        b.op("dve", lambda e, j=j: e.scalar_tensor_tensor(out=acc[:, 16:], in0=z[:, 15:15 + S], scalar=cw[:, j, 1:2], in1=acc[:, 16:],
                                                         op0=ALU.mult, op1=ALU.add), [z, cw, acc], [acc])
        b.op("dve", lambda e, j=j: e.scalar_tensor_tensor(out=acc[:, 16:], in0=z[:, 14:14 + S], scalar=cw[:, j, 0:1], in1=acc[:, 16:],
                                                         op0=ALU.mult, op1=ALU.add), [z, cw, acc], [acc])
        b.op("dve", lambda e: e.tensor_tensor(out=yb[:], in0=bb[:], in1=acc[:, 16:], op=ALU.mult), [bb, acc], [yb])
        b.dma(yT[6 + j], yb[:], [yb], [(yT, 6 + j)])
    for j in range(2):
        b.op("pool", lambda e: e.memset(u[:, 0:16], 0.0), [], [u])
        b.op("pool", lambda e: e.memset(z[:, 0:16], 0.0), [], [z])
        b.op("pool", lambda e: e.memset(acc[:, 0:16], 0.0), [], [acc])
        b.dma(u[:, 16:], pT[BLK_A[j]], [pT], [u])
        R = slice(16, 16 + S)

        def sh(k):
            return slice(16 - k, 16 - k + S)
        if j == 0:
            b.op("dve", lambda e: e.tensor_tensor(out=z[:, R], in0=u[:, R], in1=u[:, sh(1)], op=ALU.add), [u], [z])
            b.op("dve", lambda e: e.tensor_copy(out=acc[0:64, R], in_=z[0:64, R]), [z], [acc])
            b.op("dve", lambda e: e.tensor_tensor(out=acc[64:128, R], in0=z[64:128, R], in1=z[64:128, sh(2)], op=ALU.add), [z], [acc])
        else:
            b.op("dve", lambda e: e.tensor_tensor(out=z[:, R], in0=u[:, R], in1=u[:, sh(1)], op=ALU.add), [u], [z])
            b.op("dve", lambda e: e.tensor_tensor(out=acc[:, R], in0=z[:, R], in1=z[:, sh(2)], op=ALU.add), [z], [acc])
            b.op("dve", lambda e: e.tensor_tensor(out=z[:, R], in0=acc[:, R], in1=acc[:, sh(4)], op=ALU.add), [acc], [z])
            b.op("dve", lambda e: e.tensor_copy(out=acc[0:64, R], in_=z[0:64, R]), [z], [acc])
            b.op("dve", lambda e: e.tensor_tensor(out=acc[64:128, R], in0=z[64:128, R], in1=z[64:128, sh(8)], op=ALU.add), [z], [acc])
        b.op("dve", lambda e, j=j: e.tensor_scalar(out=cc[:], in0=acc[:, R], scalar1=pinv[:, j:j + 1], scalar2=None, op0=ALU.mult), [acc, pinv], [cc])
        b.op("dve", lambda e, j=j: e.tensor_tensor(out=cc[:, 0:16], in0=acc[:, 16:32], in1=pcorr[:, j, :], op=ALU.mult), [acc, pcorr, cc], [cc])
        b.op("dve", lambda e: e.tensor_tensor(out=yb[:], in0=cc[:], in1=u[:, R], op=ALU.subtract), [cc, u], [yb])
        for h in range(S // 512):
            sl = slice(h * 512, (h + 1) * 512)
            P = pp[h % 2]
            b.mm(P[:], pw[:, j, :], yb[:, sl], True, True, [pw, yb], [P])
            b.op("act", lambda e, P=P, sl=sl, j=j: e.activation(out=z[:, sl], in_=P[:], func=AF.Copy, scale=psc[:, j:j + 1]), [P, psc], [(z, h)])
        b.op("dve", lambda e: e.tensor_copy(out=yb[:], in_=z[:, 0:S]), [z], [yb])
        b.dma(yT[j], yb[:], [yb], [(yT, j)])
    b.end()


TT_ = 512
NCH = 8


def run_rr(gens, bg=None):
    gens = list(gens)
    while gens:
        for g in list(gens):
            try:
                next(g)
            except StopIteration:
                gens.remove(g)
        if bg is not None:
            try:
                next(bg)
            except StopIteration:
                pass


DBG = []


def rwkv_phase(b, pT, yT, d, Cn, ntiles=None, debug=False):
    b.begin()

    def dump(name, buf, ap, shape):
        if not debug:
            return
        dd = b.dram("dbg_" + name, list(shape), F32, "ExternalOutput")
        DBG.append("dbg_" + name)
        b.dma(dd[:], ap, [buf], [dd])
    sbF = lambda nm, shp=(128, TT_): b.sb(nm, list(shp), F32)
    def ld(nm, src, shp):
        t = b.sb(nm, shp, F32)
        b.dma(t[:], src[:], [src], [t])
        return t
    bones = ld("bones", Cn["bones"], [128, 128])
    bones64 = ld("bones64", Cn["bones64"], [128, 128])
    ident = ld("ident", Cn["ident"], [128, 128])
    sel2 = ld("sel2", Cn["sel2"], [128, 64])
    rmask = ld("rmask", Cn["rmask"], [128, TT_])
    LPmask = ld("LPmask", Cn["LPmask"], [128, 512])
    Lamask = ld("Lamask", Cn["Lamask"], [128, 128])
    mu8 = ld("mu8", d["rw_mu8"], [128, 8])
    vec = ld("vec", d["rw_vec"], [128, 2, 7])
    WUP = ld("WUP", d["rw_wup"], [128, 2, 128])
    GUP = ld("GUP", d["rw_gup"], [128, 2, 128])
    AUP = ld("AUP", d["rw_aup"], [128, 2, 128])
    V_W0, V_A0, V_KK, V_KA, V_RK, V_LNW, V_LNB = range(7)

    pb2 = [b.sb(f"pb{i}", [128, 1 + TT_], F32) for i in range(2)]
    pb = [pb2[i % 2] for i in range(8)]
    dtmp = sbF("dtmp")
    lp = [sbF(f"lp{i}") for i in range(8)]
    twg = sbF("twg")
    psw = [b.ps("psw", [128, 512]) for _ in range(1)]
    psT2 = [b.ps("psT", [128, 512]) for _ in range(2)]
    psN2 = [b.ps("psN", [128, 512]) for _ in range(2)]
    psS2 = [b.ps("psS", [128, 512]) for _ in range(2)]
    wcnt = [0]

    def wide():
        wcnt[0] += 1
        return psw[0]

    H = {}
    SH = {nm: sbF("sh_" + nm) for nm in ("lw", "cum", "cexc", "dW", "epos", "eexc", "t1", "t2", "kk", "YT", "yc", "sq", "rs")}
    for hp in range(2):
        h = {}
        for nm in ("a", "kkn", "kpp", "akk", "akkh", "eneg", "eW"):
            h[nm] = sbF(nm + str(hp))
        for nm in ("g", "bonus"):
            h[nm] = [sbF(nm + str(hp) + "_" + str(i)) for i in range(2)]
        for nm in ("lw", "cum", "cexc", "dW", "epos", "eexc", "t1", "t2", "kk", "YT", "yc", "sq", "rs"):
            h[nm] = SH[nm]
        h["wC"] = [b.sb(f"wC{hp}_{i}", [128, NCH], F32) for i in range(2)]
        h["QR"] = b.sb(f"QR{hp}", [128, NCH, 2, 64], F32)
        for nm in ("AKKbd", "KHbd", "AKKWbd", "KWbd", "Vbd", "Ybd"):
            h[nm] = b.sb(nm + str(hp), [128, NCH, 128], F32)
            b.op("pool", lambda e, t=h[nm]: e.memset(t[:], 0.0), [], [h[nm]])
        h["QRbd"] = b.sb(f"QRbd{hp}", [128, NCH, 2, 128], F32)
        b.op("pool", lambda e, t=h["QRbd"]: e.memset(t[:], 0.0), [], [h["QRbd"]])
        h["Sst"] = b.sb(f"Sst{hp}", [128, 64], F32)
        b.op("pool", lambda e, t=h["Sst"]: e.memset(t[:], 0.0), [], [h["Sst"]])
        h["LP"] = [b.sb(f"LP{hp}_{i}", [128, 512], F32) for i in range(2)]
        h["AKWT"] = [b.sb(f"AKWT{hp}_{i}", [128, 256], F32) for i in range(2)]
        h["Vst"] = [b.sb(f"Vst{hp}_{i}", [128, 64], F32) for i in range(2)]
        h["TT"] = [b.sb(f"TT{hp}_{i}", [128, 128], F32) for i in range(2)]
        h["MY"] = [b.sb(f"MY{hp}_{i}", [128, 256], F32) for i in range(2)]
        h["N"] = [b.sb(f"N{hp}_{i}", [128, 128], F32) for i in range(2)]
        h["Xst"] = b.sb(f"Xst{hp}", [128, 64], F32)
        h["Ust"] = b.sb(f"Ust{hp}", [128, 64], F32)
        h["ob"] = b.sb(f"ob{hp}", [128, TT_], BF16)
        H[hp] = h

    def v3(t, R):
        return t[R, :].rearrange("p (c t) -> p c t", t=64)

    def stage1_gen(ti):
        t0 = ti * TT_
        for i in range(8):
            blk = 2 + i
            if ti == 0:
                b.op("pool", lambda e, i=i: e.memset(pb[i][:, 0:1], 0.0), [], [pb[i]])
                b.dma(pb[i][:, 1:], pT[blk, :, 0:TT_], [pT], [pb[i]])
            else:
                b.dma(pb[i][:], pT[blk, :, t0 - 1:t0 + TT_], [pT], [pb[i]])
            eng = "dve" if i % 2 == 0 else "pool"
            b.op("dve", lambda e, i=i: e.tensor_tensor(out=dtmp[:], in0=pb[i][:, 0:TT_], in1=pb[i][:, 1:], op=ALU.subtract), [pb[i]], [dtmp])
            b.op("dve", lambda e, i=i: e.scalar_tensor_tensor(out=lp[i][:], in0=dtmp[:], scalar=mu8[:, i:i + 1], in1=pb[i][:, 1:],
                                                             op0=ALU.mult, op1=ALU.add), [dtmp, mu8, pb[i]], [lp[i]])
            yield
        b.op("act", lambda e: e.activation(out=twg[0:64, :], in_=lp[6][0:64, :], func=AF.Tanh), [lp[6]], [(twg, 0)])
        b.op("act", lambda e: e.activation(out=twg[64:128, :], in_=lp[6][64:128, :], func=AF.Sigmoid), [lp[6]], [(twg, 1)])
        for hp in range(2):
            h = H[hp]
            rp, kp, vp = lp[hp], lp[2 + hp], lp[4 + hp]
            vv = lambda i, hp=hp: vec[:, hp, i:i + 1]
            pw_ = wide()
            b.mm(pw_[:], WUP[:, hp, :], twg[:], True, True, [WUP, twg], [pw_])
            b.op("act", lambda e, pw_=pw_, h=h, vv=vv: e.activation(out=h["t1"][:], in_=pw_[:], func=AF.Sigmoid, bias=vv(V_W0)), [pw_, vec], [h["t1"]])
            b.op("dve", lambda e, h=h: e.tensor_scalar(out=h["lw"][:], in0=h["t1"][:], scalar1=-0.6065306597126334, scalar2=None, op0=ALU.mult), [h["t1"]], [h["lw"]])
            pa_ = wide()
            b.mm(pa_[:], AUP[:, hp, :], lp[7][:], True, True, [AUP, lp[7]], [pa_])
            b.op("act", lambda e, pa_=pa_, h=h, vv=vv: e.activation(out=h["a"][:], in_=pa_[:], func=AF.Sigmoid, bias=vv(V_A0)), [pa_, vec], [h["a"]])
            pg_ = wide()
            b.mm(pg_[:], GUP[:, hp, :], twg[:], True, True, [GUP, twg], [pg_])
            b.op("act", lambda e, pg_=pg_, h=h: e.activation(out=h["g"][ti % 2][:], in_=pg_[:], func=AF.Copy), [pg_], [h["g"][ti % 2]])
            yield
            b.op("dve", lambda e, h=h, kp=kp, vv=vv: e.tensor_scalar(out=h["kk"][:], in0=kp[:], scalar1=vv(V_KK), scalar2=None, op0=ALU.mult), [kp, vec], [h["kk"]])
            b.op("act", lambda e, h=h: e.activation(out=h["t1"][:], in_=h["kk"][:], func=AF.Square), [h["kk"]], [h["t1"]])
            pn_ = wide()
            b.mm(pn_[:], bones[:], h["t1"][:], True, True, [bones, h["t1"]], [pn_])
            b.op("dve", lambda e, pn_=pn_, h=h: e.tensor_scalar(out=h["t2"][:], in0=pn_[:], scalar1=1e-24, scalar2=None, op0=ALU.max), [pn_], [h["t2"]])
            b.op("act", lambda e, h=h: e.activation(out=h["t2"][:], in_=h["t2"][:], func=AF.Sqrt), [h["t2"]], [h["t2"]])
            b.op("dve", lambda e, h=h: e.reciprocal(out=h["t2"][:], in_=h["t2"][:]), [h["t2"]], [h["t2"]])
            b.op("dve", lambda e, h=h: e.tensor_tensor(out=h["kkn"][:], in0=h["kk"][:], in1=h["t2"][:], op=ALU.mult), [h["kk"], h["t2"]], [h["kkn"]])
            yield
            b.op("dve", lambda e, h=h, vv=vv: e.tensor_scalar(out=h["t1"][:], in0=h["a"][:], scalar1=-1.0, scalar2=vv(V_KA), op0=ALU.add, op1=ALU.mult), [h["a"], vec], [h["t1"]])
            b.op("dve", lambda e, h=h, kp=kp: e.scalar_tensor_tensor(out=h["kpp"][:], in0=h["t1"][:], scalar=1.0, in1=kp[:], op0=ALU.add, op1=ALU.mult), [h["t1"], kp], [h["kpp"]])
            b.op("dve", lambda e, h=h: e.tensor_tensor(out=h["akk"][:], in0=h["kkn"][:], in1=h["a"][:], op=ALU.mult), [h["kkn"], h["a"]], [h["akk"]])
            yield
            b.op("dve", lambda e, h=h, rp=rp, vv=vv: e.scalar_tensor_tensor(out=h["t1"][:], in0=rp[:], scalar=vv(V_RK), in1=h["kpp"][:], op0=ALU.mult, op1=ALU.mult), [rp, vec, h["kpp"]], [h["t1"]])
            pb_ = wide()
            b.mm(pb_[:], bones[:], h["t1"][:], True, True, [bones, h["t1"]], [pb_])
            b.op("dve", lambda e, pb_=pb_, h=h, vp=vp: e.tensor_tensor(out=h["bonus"][ti % 2][:], in0=pb_[:], in1=vp[:], op=ALU.mult), [pb_, vp], [h["bonus"][ti % 2]])
            yield
            b.op("dve", lambda e, h=h: e.tensor_tensor_scan(out=h["cum"][:], data0=rmask[:], data1=h["lw"][:], initial=0.0, op0=ALU.mult, op1=ALU.add), [rmask, h["lw"]], [h["cum"]])
            b.op("dve", lambda e, h=h: e.tensor_tensor(out=h["cexc"][:], in0=h["cum"][:], in1=h["lw"][:], op=ALU.subtract), [h["cum"], h["lw"]], [h["cexc"]])
            for c in range(NCH):
                cs = slice(c * 64, (c + 1) * 64)
                b.op("pool" if c % 2 else "dve", lambda e, h=h, cs=cs, c=c: e.tensor_scalar(out=h["dW"][:, cs], in0=h["cum"][:, cs], scalar1=-1.0,
                                                                 scalar2=h["cum"][:, c * 64 + 63:c * 64 + 64], op0=ALU.mult, op1=ALU.add), [h["cum"]], [(h["dW"], c)])
            yield
            b.op("act", lambda e, h=h: e.activation(out=h["epos"][:], in_=h["cum"][:], func=AF.Exp), [h["cum"]], [h["epos"]])
            b.op("act", lambda e, h=h: e.activation(out=h["eneg"][:], in_=h["cum"][:], func=AF.Exp, scale=-1.0), [h["cum"]], [h["eneg"]])
            b.op("act", lambda e, h=h: e.activation(out=h["eexc"][:], in_=h["cexc"][:], func=AF.Exp), [h["cexc"]], [h["eexc"]])
            b.op("act", lambda e, h=h: e.activation(out=h["eW"][:], in_=h["dW"][:], func=AF.Exp), [h["dW"]], [h["eW"]])
            b.op("pool", lambda e, h=h: e.tensor_copy(out=h["wC"][ti % 2][:], in_=h["epos"][:, 63::64]), [h["epos"]], [h["wC"][ti % 2]])
            yield
            b.op("dve", lambda e, h=h: e.tensor_tensor(out=h["QR"][:, :, 0, :], in0=v3(h["kkn"], slice(0, 128)), in1=v3(h["eexc"], slice(0, 128)), op=ALU.mult), [h["kkn"], h["eexc"]], [(h["QR"], 0)])
            b.op("pool", lambda e, h=h, rp=rp: e.tensor_tensor(out=h["QR"][:, :, 1, :], in0=v3(rp, slice(0, 128)), in1=v3(h["epos"], slice(0, 128)), op=ALU.mult), [rp, h["epos"]], [(h["QR"], 1)])
            b.op("dve", lambda e, h=h: e.tensor_tensor(out=h["akkh"][:], in0=h["akk"][:], in1=h["eneg"][:], op=ALU.mult), [h["akk"], h["eneg"]], [h["akkh"]])
            yield

    def stage2(ti):
        for hp in range(2):
            h = H[hp]
            vp = lp[4 + hp]
            for hh in range(2):
                R = slice(64 * hh, 64 * hh + 64)
                eng = "pool" if hh else "dve"
                b.op(eng, lambda e, h=h, R=R: e.tensor_copy(out=h["QRbd"][R, :, 0, R], in_=h["QR"][R, :, 0, :]), [(h["QR"], 0)], [(h["QRbd"], hh)])
                b.op(eng, lambda e, h=h, R=R: e.tensor_copy(out=h["QRbd"][R, :, 1, R], in_=h["QR"][R, :, 1, :]), [(h["QR"], 1)], [(h["QRbd"], hh)])
                b.op(eng, lambda e, h=h, R=R: e.tensor_copy(out=h["AKKbd"][R, :, R], in_=v3(h["akkh"], R)), [h["akkh"]], [(h["AKKbd"], hh)])
                b.op(eng, lambda e, h=h, R=R: e.tensor_tensor(out=h["KHbd"][R, :, R], in0=v3(h["kpp"], R), in1=v3(h["eneg"], R), op=ALU.mult), [h["kpp"], h["eneg"]], [(h["KHbd"], hh)])
                b.op(eng, lambda e, h=h, R=R: e.tensor_tensor(out=h["AKKWbd"][R, :, R], in0=v3(h["akk"], R), in1=v3(h["eW"], R), op=ALU.mult), [h["akk"], h["eW"]], [(h["AKKWbd"], hh)])
                b.op(eng, lambda e, h=h, R=R: e.tensor_tensor(out=h["KWbd"][R, :, R], in0=v3(h["kpp"], R), in1=v3(h["eW"], R), op=ALU.mult), [h["kpp"], h["eW"]], [(h["KWbd"], hh)])
                b.op(eng, lambda e, h=h, R=R, vp=vp: e.tensor_copy(out=h["Vbd"][R, :, R], in_=v3(vp, R)), [vp], [(h["Vbd"], hh)])


    NTL = ntiles or (S // TT_)
    run_rr([stage1_gen(0)])
    for ti in range(NTL):
        t0 = ti * TT_
        stage2(ti)
        bg = stage1_gen(ti + 1) if ti + 1 < NTL else None

        def prep_gen(hp, c):
            h = H[hp]
            par = c % 2
            LP, AKWT, Vst, TT = h["LP"][par], h["AKWT"][par], h["Vst"][par], h["TT"][par]
            bX, bY = psT2[hp], psN2[hp]
            MY, NN = h["MY"], h["N"]
            qr = h["QRbd"][:, c, :, :].rearrange("p a t -> p (a t)")
            b.tr(bX[:, 0:128], h["AKKWbd"][:, c, :], ident[:], [h["AKKWbd"], ident], [bX])
            b.tr(bX[:, 128:256], h["KWbd"][:, c, :], ident[:], [h["KWbd"], ident], [bX])
            b.tr(bX[:, 256:384], h["Vbd"][:, c, :], ident[:], [h["Vbd"], ident], [bX])
            b.mm(bX[:, 384:512], h["QRbd"][:, c, 0, :], h["AKKbd"][:, c, :], True, True, [h["QRbd"], h["AKKbd"]], [bX])
            b.mm(bY[:, 0:256], h["AKKbd"][:, c, :], qr, True, True, [h["AKKbd"], h["QRbd"]], [bY])
            b.mm(bY[:, 256:512], h["KHbd"][:, c, :], qr, True, True, [h["KHbd"], h["QRbd"]], [bY])
            yield
            b.op("dve", lambda e: e.tensor_tensor(out=NN[0][:], in0=bX[:, 384:512], in1=Lamask[:], op=ALU.mult), [bX, Lamask], [NN[0]])
            b.op("act", lambda e: e.activation(out=AKWT[:], in_=bX[:, 0:256], func=AF.Copy), [bX], [AKWT])
            b.op("act", lambda e: e.activation(out=Vst[0:64, :], in_=bX[0:64, 256:320], func=AF.Copy), [bX], [(Vst, 0)])
            b.op("act", lambda e: e.activation(out=Vst[64:128, :], in_=bX[64:128, 320:384], func=AF.Copy), [bX], [(Vst, 1)])
            b.op("dve", lambda e: e.tensor_tensor(out=LP[:], in0=bY[:], in1=LPmask[:], op=ALU.mult), [bY, LPmask], [LP])
            b.op("pool", lambda e: e.tensor_tensor(out=MY[1][:, 128:256], in0=LP[:, 0:128], in1=ident[:], op=ALU.add), [LP, ident], [(MY[1], 1)])
            yield
            b.mm(bX[:, 0:128], NN[0][:], LP[:, 0:128], True, True, [NN[0], LP], [bX])
            b.mm(bX[:, 256:384], LP[:, 0:128], NN[0][:], True, True, [LP, NN[0]], [bX])
            yield
            b.op("act", lambda e: e.activation(out=MY[1][:, 0:128], in_=bX[:, 0:128], func=AF.Copy), [bX], [(MY[1], 0)])
            b.op("dve", lambda e: e.tensor_copy(out=NN[1][:], in_=bX[:, 256:384]), [bX], [NN[1]])
            yield
            for k in range(1, 6):
                bank = bY if k % 2 else bX
                i, o = k % 2, (k + 1) % 2
                if k < 5:
                    b.mm(bank[:, 0:256], NN[i][:], MY[i][:, 0:256], True, True, [NN[i], MY[i]], [bank])
                    b.mm(bank[:, 256:384], MY[i][:, 0:128], NN[i][:], True, True, [MY[i], NN[i]], [bank])
                    yield
                    b.op("dve", lambda e, bank=bank, i=i, o=o: e.tensor_tensor(out=MY[o][:, 128:256], in0=bank[:, 128:256], in1=MY[i][:, 128:256], op=ALU.add),
                         [bank, (MY[i], 1)], [(MY[o], 1)])
                    b.op("act", lambda e, bank=bank, o=o: e.activation(out=MY[o][:, 0:128], in_=bank[:, 0:128], func=AF.Copy), [bank], [(MY[o], 0)])
                    b.op("act", lambda e, bank=bank, o=o: e.activation(out=NN[o][:], in_=bank[:, 256:384], func=AF.Copy), [bank], [NN[o]])
                    yield
                else:
                    b.mm(bank[:, 0:128], NN[i][:], MY[i][:, 128:256], True, True, [NN[i], MY[i]], [bank])
                    yield
                    b.op("dve", lambda e, bank=bank, i=i: e.tensor_tensor(out=TT[:], in0=bank[:, 0:128], in1=MY[i][:, 128:256], op=ALU.add),
                         [bank, (MY[i], 1)], [TT])
                    yield

        def seq_gen(hp, c):
            h = H[hp]
            par = c % 2
            LP, AKWT, Vst, TT = h["LP"][par], h["AKWT"][par], h["Vst"][par], h["TT"][par]
            Sst, Xst, Ust = h["Sst"], h["Xst"], h["Ust"]
            so = 0
            psS = psS2[hp]
            sb_ = lambda k: psS
            X_, U_, Y_, S_ = (psS[:, so + 64 * k:so + 64 * k + 64] for k in range(4))
            b.mm(X_, LP[:, 256:384], Vst[:], True, False, [LP, Vst], [sb_(0)])
            b.mm(X_, h["QRbd"][:, c, 0, :], Sst[:], False, True, [h["QRbd"], Sst], [sb_(0)])
            yield
            b.op("dve", lambda e: e.tensor_copy(out=Xst[:], in_=X_), [sb_(0)], [Xst])
            yield
            b.mm(U_, TT[:], Xst[:], True, True, [TT, Xst], [sb_(1)])
            yield
            b.op("dve", lambda e: e.tensor_scalar(out=Ust[:], in0=U_, scalar1=-1.0, scalar2=None, op0=ALU.mult), [sb_(1)], [Ust])
            yield
            b.mm(Y_, h["QRbd"][:, c, 1, :], Sst[:], True, False, [h["QRbd"], Sst], [sb_(2)])
            b.mm(Y_, LP[:, 128:256], Ust[:], False, False, [LP, Ust], [sb_(2)])
            b.mm(Y_, LP[:, 384:512], Vst[:], False, True, [LP, Vst], [sb_(2)])
            b.mm(S_, AKWT[:, 128:256], Vst[:], True, False, [AKWT, Vst], [sb_(3)])
            b.mm(S_, AKWT[:, 0:128], Ust[:], False, True, [AKWT, Ust], [sb_(3)])
            yield
            wCt = h["wC"][ti % 2]
            b.op("dve", lambda e: e.scalar_tensor_tensor(out=Sst[:], in0=Sst[:], scalar=wCt[:, c:c + 1], in1=S_, op0=ALU.mult, op1=ALU.add),
                 [Sst, wCt, sb_(3)], [Sst])
            b.op("act", lambda e: e.activation(out=h["Ybd"][0:64, c, 0:64], in_=psS[0:64, so + 128:so + 192], func=AF.Copy), [sb_(2)], [(h["Ybd"], c)])
            b.op("act", lambda e: e.activation(out=h["Ybd"][64:128, c, 64:128], in_=psS[64:128, so + 128:so + 192], func=AF.Copy), [sb_(2)], [(h["Ybd"], c)])
            yield

        import os as _os
        STOP = int(_os.environ.get("RW_STOP", "9"))
        if STOP <= 1:
            continue
        for c in range(NCH + 1):
            gens = []
            if c < NCH:
                import itertools as _it
                PST = int(_os.environ.get("P_STOP", "999"))
                gens += [_it.islice(prep_gen(0, c), PST), _it.islice(prep_gen(1, c), PST)]
            if c >= 1 and STOP >= 3:
                gens += [seq_gen(0, c - 1), seq_gen(1, c - 1)]
            run_rr(gens, bg)
        if bg is not None:
            for _ in bg:
                pass
        if STOP <= 3:
            continue

        for hp in range(2):
            h = H[hp]
            vv = lambda i, hp=hp: vec[:, hp, i:i + 1]
            py = wide()
            for c in range(NCH):
                b.mm(py[:, c * 64:(c + 1) * 64], h["Ybd"][:, c, :], sel2[:], True, True, [(h["Ybd"], c), sel2], [py])
            b.op("act", lambda e, py=py, h=h: e.activation(out=h["YT"][:], in_=py[:], func=AF.Copy), [py], [h["YT"]])
            if ti == 0:
                dump(f"YT{hp}", h["YT"], h["YT"][:], [128, TT_])
                dump(f"Sst{hp}", h["Sst"], h["Sst"][:], [128, 64])
                dump(f"TT{hp}", h["TT"][1], h["TT"][1][:], [128, 128])
                dump(f"LP{hp}", h["LP"][1], h["LP"][1][:], [128, 512])
            pm = wide()
            b.mm(pm[:], bones64[:], h["YT"][:], True, True, [bones64, h["YT"]], [pm])
            b.op("dve", lambda e, pm=pm, h=h: e.tensor_tensor(out=h["yc"][:], in0=h["YT"][:], in1=pm[:], op=ALU.subtract), [h["YT"], pm], [h["yc"]])
            b.op("act", lambda e, h=h: e.activation(out=h["sq"][:], in_=h["yc"][:], func=AF.Square), [h["yc"]], [h["sq"]])
            pv = wide()
            b.mm(pv[:], bones64[:], h["sq"][:], True, True, [bones64, h["sq"]], [pv])
            b.op("act", lambda e, pv=pv, h=h: e.activation(out=h["rs"][:], in_=pv[:], func=AF.Sqrt, bias=64e-5), [pv], [h["rs"]])
            b.op("dve", lambda e, h=h: e.reciprocal(out=h["rs"][:], in_=h["rs"][:]), [h["rs"]], [h["rs"]])
            b.op("dve", lambda e, h=h: e.tensor_tensor(out=h["yc"][:], in0=h["yc"][:], in1=h["rs"][:], op=ALU.mult), [h["yc"], h["rs"]], [h["yc"]])
            b.op("dve", lambda e, h=h, vv=vv: e.tensor_scalar(out=h["yc"][:], in0=h["yc"][:], scalar1=vv(V_LNW), scalar2=vv(V_LNB), op0=ALU.mult, op1=ALU.add), [h["yc"], vec], [h["yc"]])
            bon, gg = h["bonus"][ti % 2], h["g"][ti % 2]
            b.op("dve", lambda e, h=h, bon=bon: e.tensor_tensor(out=h["yc"][:], in0=h["yc"][:], in1=bon[:], op=ALU.add), [h["yc"], bon], [h["yc"]])
            b.op("dve", lambda e, h=h, gg=gg: e.tensor_tensor(out=h["ob"][:], in0=h["yc"][:], in1=gg[:], op=ALU.mult), [h["yc"], gg], [h["ob"]])
            b.dma(yT[2 + hp, :, t0:t0 + TT_], h["ob"][:], [h["ob"]], [(yT, (2 + hp, ti))])
    b.end()


NBIG = 30000.0


def nsa_phase(b, pT, yT, d, Cn, ngroups=None, debug=False):
    b.begin()
    DBGN = []

    def dump(name, buf, ap, shape, dt=F32):
        if not debug:
            return
        dd = b.dram("dbg_" + name, list(shape), dt, "ExternalOutput")
        b.dma(dd[:], ap, [buf], [dd])

    def ld(nm, src, shp):
        t = b.sb(nm, shp, F32)
        b.dma(t[:], src[:], [src], [t])
        return t

    def ldc(nm, src_ap, srcbuf, shp):
        t = b.sb(nm, shp, BF16)
        b.dma(t[:], src_ap, [srcbuf], [t], q="pool", max_dma_last_dim=4096)
        return t
    ident = ld("ident", Cn["ident"], [128, 128])
    identb = ldc("identb", Cn["ident"][:], Cn["ident"], [128, 128])
    bones = ld("bones", Cn["bones"], [128, 128])
    eall = ldc("eall", Cn["n_eall"][:], Cn["n_eall"], [64, S])
    cmsel = b.sb("cmsel", [128, 4, 512], BF16)
    for r in range(4):
        b.dma(cmsel[:, r, :], Cn["n_cmsel"][r], [Cn["n_cmsel"]], [cmsel], q="pool", max_dma_last_dim=4096)
    cmwin = b.sb("cmwin", [128, 8, 512], BF16)
    for r in range(8):
        b.dma(cmwin[:, r, :], Cn["n_cmwin"][r], [Cn["n_cmwin"]], [cmwin], q="pool", max_dma_last_dim=4096)
    negc = b.sb("negc", [128, 5, 512], BF16)
    for r in range(5):
        b.dma(negc[:, r, :], Cn["n_negc"][r], [Cn["n_negc"]], [negc], q="pool", max_dma_last_dim=4096)
    qnw = ld("qnw", d["ns_qnw"], [128, 1])
    knw = ld("knw", d["ns_knw"], [128, 3])
    w2 = ld("w2", d["ns_w2"], [128, 2, 64])
    pos2 = ld("pos2", d["ns_pos2"], [128, 32, 2])
    QA = [b.sb(f"QA{h}", [128, S], BF16) for h in range(4)]
    KS = b.sb("KS", [128, S], BF16)
    KW = b.sb("KW", [128, S], BF16)
    KC = b.sb("KC", [128, 256], BF16)
    NEGM = b.sb("NEGM", [64, S], BF16)
    Vs = b.sb("Vs", [128, 32, 65], BF16)
    Vw = b.sb("Vw", [128, 32, 65], BF16)
    VC = b.sb("VC", [128, 2, 129], BF16)
    Gtm = b.sb("Gtm", [128, 32, 32], F32)
    rawA = b.sb("rawA", [128, S], F32)
    rawB = b.sb("rawB", [128, S], F32)
    tF = [b.sb(f"tF{i}", [128, 512], F32) for i in range(3)]
    psc = [b.ps("psc", [128, 512]) for _ in range(4)]
    pacc = [b.ps("pacc", [128, 512]) for _ in range(2)]
    pcA = pacc[0]
    pcB = pacc[1]
    pm = [b.ps("pm", [128, 512]) for _ in range(2)]
    mcnt = [0]

    def misc():
        mcnt[0] += 1
        return pm[mcnt[0] % 2]

    rT = [b.sb(f"rT{i}", [64, 512], F32) for i in range(6)]
    rcnt = [0]

    def rms_rows(raw, out, wcol, NC=S, width=512):
        for c0 in range(0, NC, width):
            wd = min(width, NC - c0)
            cs = slice(c0, c0 + wd)
            rcnt[0] += 1
            t0, t1 = rT[(rcnt[0] % 3) * 2], rT[(rcnt[0] % 3) * 2 + 1]
            b.op("act", lambda e, cs=cs, wd=wd, t0=t0: e.activation(out=t0[0:64, 0:wd], in_=raw[0:64, cs], func=AF.Square), [raw], [t0])
            P = misc()
            b.mm(P[0:64, 0:wd], bones[0:64, 0:64], t0[0:64, 0:wd], True, True, [bones, t0], [P])
            b.op("act", lambda e, P=P, wd=wd, t1=t1: e.activation(out=t1[0:64, 0:wd], in_=P[0:64, 0:wd], func=AF.Sqrt, scale=1.0 / 64, bias=EPS), [P], [t1])
            b.op("dve", lambda e, wd=wd, t1=t1: e.reciprocal(out=t1[0:64, 0:wd], in_=t1[0:64, 0:wd]), [t1], [t1])
            b.op("dve", lambda e, cs=cs, wd=wd, t1=t1: e.scalar_tensor_tensor(out=out[0:64, cs], in0=raw[0:64, cs], scalar=wcol, in1=t1[0:64, 0:wd],
                                                                    op0=ALU.mult, op1=ALU.mult), [raw, t1, qnw, knw], [(out, c0)])

    for h in range(4):
        b.dma(rawA[:], pT[BLK_Q[h]], [pT], [rawA])
        rms_rows(rawA, QA[h], qnw[0:64, 0:1])
        b.dma(QA[h][64:68, :], Cn["n_qaug"][h], [Cn["n_qaug"]], [QA[h]], q="pool", max_dma_last_dim=4096)
    for (blk, Kt, Vt, wi) in ((BLK_KVS, KS, Vs, 1), (BLK_KVW, KW, Vw, 2)):
        b.dma(rawA[:], pT[blk], [pT], [rawA])
        rms_rows(rawA, Kt, knw[0:64, wi:wi + 1])
        b.dma(Kt[64:68, :], Cn["n_kaug"][:], [Cn["n_kaug"]], [Kt], q="pool", max_dma_last_dim=4096)
        b.op("pool", lambda e, Vt=Vt: e.memset(Vt[:, :, 64:65], 1.0), [], [Vt])
        for g4 in range(8):
            P = misc()
            for i in range(4):
                kt = g4 * 4 + i
                b.tr(P[:, i * 128:(i + 1) * 128], rawA[:, kt * 128:(kt + 1) * 128], ident[:], [rawA, ident], [P])
            b.op("dve", lambda e, P=P, Vt=Vt, g4=g4: e.tensor_copy(out=Vt[:, g4 * 4:g4 * 4 + 4, 0:64],
                                                                  in_=P[:].rearrange("p (i c) -> p i c", c=128)[:, :, 64:128]), [P], [Vt])
    b.dma(rawA[:], pT[BLK_GT], [pT], [rawA])
    b.op("act", lambda e: e.activation(out=rawA[0:32, :], in_=rawA[0:32, :], func=AF.Sigmoid), [rawA], [rawA])
    for g16 in range(2):
        P = misc()
        for i in range(16):
            kt = g16 * 16 + i
            b.tr(P[:, i * 32:(i + 1) * 32], rawA[0:32, kt * 128:(kt + 1) * 128], ident[0:32, 0:32], [rawA, ident], [P])
        b.op("dve", lambda e, P=P, g16=g16: e.tensor_copy(out=Gtm[:, g16 * 16:(g16 + 1) * 16, :], in_=P[:].rearrange("p (i c) -> p i c", c=32)), [P], [Gtm])
    b.dma(rawA[:], pT[BLK_KVC], [pT], [rawA])
    b.dma(rawB[:], d["ns_w1"][:].rearrange("p l m -> p (l m)"), [d["ns_w1"]], [rawB])
    b.op("pool", lambda e: e.memset(KC[:], 0.0), [], [KC])
    b.op("pool", lambda e: e.memset(VC[:], 0.0), [], [VC])
    bias_sb = b.sb("bias_sb", [128, 4], F32)
    for half in range(2):
        R = slice(64 * half, 64 * half + 64)
        Pb = misc()
        for l in range(32):
            b.mm(Pb[:, 0:2], rawB[R, l * 128:(l + 1) * 128], pos2[R, l, :], l == 0, l == 31, [rawB, pos2], [Pb])
        b.op("dve", lambda e, Pb=Pb, half=half: e.tensor_copy(out=bias_sb[:, 2 * half:2 * half + 2], in_=Pb[:, 0:2]), [Pb], [bias_sb])
    gl = [b.sb(f"gl{i}", [128, 256], F32) for i in range(2)]
    for half in range(2):
        R = slice(64 * half, 64 * half + 64)
        Ph = misc()
        for l in range(32):
            b.mm(Ph[:, 0:255], rawB[R, l * 128:(l + 1) * 128], rawA[R, l:l + 4065:16], l == 0, l == 31, [rawB, rawA], [Ph])
        hb, h2 = tF[0], tF[1]
        b.op("act", lambda e, Ph=Ph, half=half: e.activation(out=hb[:, 0:255], in_=Ph[:, 0:255], func=AF.Identity, bias=bias_sb[:, 2 * half:2 * half + 1]), [Ph, bias_sb], [hb])
        b.op("act", lambda e: e.activation(out=h2[:, 0:255], in_=hb[:, 0:255], func=AF.Square), [hb], [h2])
        b.op("dve", lambda e: e.tensor_scalar(out=h2[:, 0:255], in0=h2[:, 0:255], scalar1=0.044715, scalar2=1.0, op0=ALU.mult, op1=ALU.add), [h2], [h2])
        b.op("dve", lambda e: e.tensor_tensor(out=h2[:, 0:255], in0=h2[:, 0:255], in1=hb[:, 0:255], op=ALU.mult), [h2, hb], [h2])
        b.op("act", lambda e: e.activation(out=h2[:, 0:255], in_=h2[:, 0:255], func=AF.Sigmoid, scale=1.5957691216057308), [h2], [h2])
        b.op("dve", lambda e, half=half: e.tensor_tensor(out=gl[half][:, 0:255], in0=h2[:, 0:255], in1=hb[:, 0:255], op=ALU.mult), [h2, hb], [gl[half]])
    Pk = misc()
    b.mm(Pk[0:64, 0:255], w2[:, 0, :], gl[0][:, 0:255], True, True, [w2, gl[0]], [Pk])
    kc_sb = b.sb("kc_sb", [128, 256], F32)
    b.op("dve", lambda e: e.tensor_copy(out=kc_sb[0:64, 0:255], in_=Pk[0:64, 0:255]), [Pk], [kc_sb])
    rms_rows(kc_sb, KC, knw[0:64, 0:1], NC=255, width=255)
    b.dma(KC[64:68, :], Cn["n_kcaug"][:], [Cn["n_kcaug"]], [KC], q="pool")
    for nt in range(2):
        ncols = 128 if nt == 0 else 127
        Pv = misc()
        b.mm(Pv[0:ncols, 0:64], gl[1][:, nt * 128:nt * 128 + ncols], w2[:, 1, :], True, True, [gl[1], w2], [Pv])
        b.op("dve", lambda e, Pv=Pv, nt=nt, ncols=ncols: e.tensor_copy(out=VC[0:ncols, nt, 0:64], in_=Pv[0:ncols, 0:64]), [Pv], [VC])
        b.dma(VC[:, nt, 64:129], Cn["n_ovl1"][:, nt, :], [Cn["n_ovl1"]], [VC], q="pool")
    if debug:
        dump("gl0", gl[0], gl[0][:, 0:255], [128, 255])
        dump("gl1", gl[1], gl[1][:, 0:255], [128, 255])
        dump("kc_sb", kc_sb, kc_sb[0:64, 0:255], [64, 255])
        dump("bias_sb", bias_sb, bias_sb[:], [128, 4])
        for h in range(4):
            dump(f"QA{h}", QA[h], QA[h][0:68, :], [68, S], BF16)
        dump("KS", KS, KS[0:68, :], [68, S], BF16)
        dump("KC", KC, KC[0:68, :], [68, 256], BF16)
        dump("VC", VC, VC[:], [128, 2, 129], BF16)
        dump("Vs", Vs, Vs[:], [128, 32, 65], BF16)
        dump("Gtm", Gtm, Gtm[:], [128, 32, 32])

    PC = [[b.sb(f"PC{h}_{nt}", [128, 512], BF16) for nt in range(2)] for h in range(4)]
    PT = [b.sb(f"PT{i}", [128, 512], BF16) for i in range(5)]
    OUT = b.sb("OUT", [128, 4, 256], F32)
    okt = b.sb("okt", [128, 4, 64], F32)
    addt = b.sb("addt", [128, 4, 64], F32)
    imp = b.sb("imp", [128, 64], F32)
    imp2 = b.sb("imp2", [128, 64], F32)
    imp3 = b.sb("imp3", [128, 64], F32)
    m8a = b.sb("m8a", [128, 8], F32)
    m8b = b.sb("m8b", [128, 8], F32)
    msk = b.sb("msk", [128, 4, 64], F32)
    l4 = b.sb("l4", [128, 4], F32)
    rg4 = b.sb("rg4", [128, 4], F32)
    rl4 = b.sb("rl4", [128, 4], F32)
    YC = [b.sb(f"YC{i}", [128, 512], BF16) for i in range(2)]
    scnt = [0]
    pcnt = [0]
    acnt = [0]

    for qg in range(ngroups or (S // 512)):
        qs = slice(qg * 512, (qg + 1) * 512)
        b.dma(okt[:], Cn["n_ok"][:, 4 * qg:4 * qg + 4, :], [Cn["n_ok"]], [okt])
        b.dma(addt[:], Cn["n_addc"][:, 4 * qg:4 * qg + 4, :], [Cn["n_addc"]], [addt])
        nts = [0] if qg < 4 else [0, 1]
        for h in range(4):
            for nt in nts:
                g = qg if nt == 0 else qg - 4
                masked = (g <= 4)
                sc = psc[scnt[0] % 4]
                scnt[0] += 1
                b.mm(sc[:], KC[0:68, nt * 128:(nt + 1) * 128], QA[h][0:68, qs], True, not masked, [KC, QA[h]], [sc])
                if masked:
                    b.mm(sc[:], identb[:], negc[:, g, :], False, True, [identb, negc], [sc])
                b.op("act", lambda e, sc=sc, h=h, nt=nt: e.activation(out=PC[h][nt][:], in_=sc[:], func=AF.Exp, scale=0.125), [sc], [PC[h][nt]])
        for s in range(4):
            tile = 4 * qg + s
            ss = slice(s * 128, (s + 1) * 128)
            for h in range(4):
                bank = pcA if h < 2 else pcB
                reg = slice((h % 2) * 129, (h % 2) * 129 + 129)
                for i, nt in enumerate(nts):
                    b.mm(bank[:, reg], PC[h][nt][:, ss], VC[:, nt, :], i == 0, i == len(nts) - 1, [PC[h][nt], VC], [bank])
            for bi, bank in enumerate((pcA, pcB)):
                b.op("dve", lambda e, bank=bank, bi=bi: e.tensor_scalar(out=l4[:, 2 * bi:2 * bi + 2],
                                                                       in0=bank[:, 0:258].rearrange("p (h c) -> p h c", c=129)[:, :, 64],
                                                                       scalar1=1e-30, scalar2=None, op0=ALU.max), [bank], [(l4, bi)])
            b.op("dve", lambda e: e.reciprocal(out=rl4[:], in_=l4[:]), [l4], [rl4])
            b.op("dve", lambda e, tile=tile: e.tensor_tensor(out=rg4[:], in0=rl4[:], in1=Gtm[:, tile, 0:12:3], op=ALU.mult), [rl4, Gtm], [rg4])
            for h in range(4):
                bank = pcA if h < 2 else pcB
                o0 = (h % 2) * 129
                b.op("dve", lambda e, bank=bank, o0=o0, h=h, s=s: e.tensor_scalar(out=OUT[:, s, h * 64:(h + 1) * 64], in0=bank[:, o0:o0 + 64],
                                                                             scalar1=rg4[:, h:h + 1], scalar2=None, op0=ALU.mult), [bank, rg4], [(OUT, s)])
                if h == 0:
                    b.op("dve", lambda e, bank=bank, o0=o0: e.tensor_scalar(out=imp[:], in0=bank[:, o0 + 65:o0 + 129], scalar1=rl4[:, 0:1], scalar2=None, op0=ALU.mult),
                         [bank, rl4], [imp])
                else:
                    b.op("dve", lambda e, bank=bank, o0=o0, h=h: e.scalar_tensor_tensor(out=imp[:], in0=bank[:, o0 + 65:o0 + 129], scalar=rl4[:, h:h + 1], in1=imp[:],
                                                                                   op0=ALU.mult, op1=ALU.add), [bank, rl4, imp], [imp])
            b.op("dve", lambda e, s=s: e.tensor_tensor(out=imp2[:], in0=imp[:], in1=okt[:, s, :], op=ALU.mult), [imp, okt], [imp2])
            b.op("dve", lambda e, s=s: e.tensor_tensor(out=imp2[:], in0=imp2[:], in1=addt[:, s, :], op=ALU.add), [imp2, addt], [imp2])
            b.op("dve", lambda e: e.max(out=m8a[:], in_=imp2[:]), [imp2], [m8a])
            b.op("dve", lambda e: e.match_replace(out=imp3[:], in_to_replace=m8a[:], in_values=imp2[:], imm_value=-1e30), [m8a, imp2], [imp3])
            b.op("dve", lambda e: e.max(out=m8b[:], in_=imp3[:]), [imp3], [m8b])
            b.op("dve", lambda e, s=s: e.tensor_scalar(out=msk[:, s, :], in0=imp2[:], scalar1=m8b[:, 7:8], scalar2=None, op0=ALU.is_ge), [imp2, m8b], [(msk, s)])
        if debug and qg == 0:
            dump("NEGM0", NEGM, NEGM[:, 0:512], [64, 512], BF16)
            dump("msk", msk, msk[:], [128, 4, 64])
            dump("OUTcmp", OUT, OUT[:], [128, 4, 256])
        for br, (Kt, Vt) in ((2, (KW, Vw)), (1, (KS, Vs))):
            if br == 1:
                Pm = misc()
                for s in range(4):
                    b.tr(Pm[0:64, s * 128:(s + 1) * 128], msk[:, s, :], ident[:], [(msk, s), ident], [Pm])
                b.op("dve", lambda e, Pm=Pm, qs=qs: e.tensor_scalar(out=NEGM[:, qs], in0=Pm[0:64, :], scalar1=-1.0, scalar2=NBIG, op0=ALU.add, op1=ALU.mult), [Pm], [(NEGM, qg)])
            for h in range(4):
                acc = pacc[acnt[0] % 2]
                acnt[0] += 1
                if br == 1:
                    kts = list(range(0, 4 * qg + 4))
                else:
                    kts = list(range(max(0, 4 * qg - 4), 4 * qg + 4))
                pv = []
                for kt in kts:
                    for s in range(4):
                        u = 4 * qg + s - kt
                        if u < 0 or (br == 2 and u > 4):
                            continue
                        pv.append((kt, s))
                npv = len(pv)
                pvset = set(pv)
                ipv = [0]

                def emit_sc(kt):
                    sc = psc[scnt[0] % 4]
                    scnt[0] += 1
                    ks = slice(kt * 128, (kt + 1) * 128)
                    b.mm(sc[:], Kt[0:68, ks], QA[h][0:68, qs], True, False, [Kt, QA[h]], [sc])
                    if br == 1:
                        diag = kt >= 4 * qg
                        b.mm(sc[:], eall[0:64, ks], NEGM[0:64, qs], False, not diag, [eall, (NEGM, qg)], [sc])
                        if diag:
                            b.mm(sc[:], identb[:], cmsel[:, kt - 4 * qg, :], False, True, [identb, cmsel], [sc])
                    else:
                        b.mm(sc[:], identb[:], cmwin[:, kt - 4 * qg + 4, :], False, True, [identb, cmwin], [sc])
                    P_ = PT[pcnt[0] % 5]
                    pcnt[0] += 1
                    b.op("act", lambda e, sc=sc, P_=P_: e.activation(out=P_[:], in_=sc[:], func=AF.Exp, scale=0.125), [sc], [P_])
                    return P_

                def emit_pv(kt, P_):
                    for s in range(4):
                        if (kt, s) not in pvset:
                            continue
                        b.mm(acc[:, s * 65:(s + 1) * 65], P_[:, s * 128:(s + 1) * 128], Vt[:, kt, :], ipv[0] == 0, ipv[0] == npv - 1, [P_, Vt], [acc])
                        ipv[0] += 1
                LA = 3
                pend = []
                nk = len(kts)
                for i in range(min(LA, nk)):
                    pend.append(emit_sc(kts[i]))
                for i in range(nk):
                    if i + LA < nk:
                        pend.append(emit_sc(kts[i + LA]))
                    emit_pv(kts[i], pend.pop(0))
                if debug and qg == (ngroups or 8) - 1:
                    b.op("dve", lambda e, acc=acc: e.tensor_copy(out=tF[2][:, 0:260], in_=acc[:, 0:260]), [acc], [tF[2]])
                    dump(f"acc_br{br}_h{h}", tF[2], tF[2][:, 0:260], [128, 260])
                accv = acc[:, 0:260].rearrange("p (s c) -> p s c", c=65)
                b.op("dve", lambda e, accv=accv: e.reciprocal(out=rl4[:], in_=accv[:, :, 64]), [acc], [rl4])
                b.op("dve", lambda e, h=h, br=br, qg=qg: e.tensor_tensor(out=rg4[:], in0=rl4[:], in1=Gtm[:, 4 * qg:4 * qg + 4, h * 3 + br], op=ALU.mult), [rl4, Gtm], [rg4])
                for s in range(4):
                    b.op("dve", lambda e, acc=acc, s=s, h=h: e.scalar_tensor_tensor(out=OUT[:, s, h * 64:(h + 1) * 64], in0=acc[:, s * 65:s * 65 + 64], scalar=rg4[:, s:s + 1],
                                                                              in1=OUT[:, s, h * 64:(h + 1) * 64], op0=ALU.mult, op1=ALU.add), [acc, rg4, (OUT, s)], [(OUT, s)])
        if debug and qg == 0:
            dump("OUTfin", OUT, OUT[:], [128, 4, 256])
        for hp in range(2):
            Py = misc()
            for s in range(4):
                b.tr(Py[:, s * 128:(s + 1) * 128], OUT[:, s, hp * 128:(hp + 1) * 128], ident[:], [(OUT, s), ident], [Py])
            b.op("dve" if hp else "act", (lambda e, Py=Py, hp=hp: e.tensor_copy(out=YC[hp][:], in_=Py[:])) if hp else
                 (lambda e, Py=Py, hp=hp: e.activation(out=YC[hp][:], in_=Py[:], func=AF.Copy)), [Py], [YC[hp]])
            b.dma(yT[4 + hp, :, qs], YC[hp][:], [YC[hp]], [(yT, (4 + hp, qg))])
            if debug and qg == 0:
                dump(f"YC{hp}", YC[hp], YC[hp][:], [128, 512], BF16)
    b.end()


import numpy as np

S = 4096


def lay_in(w):
    F = w.shape[1]
    return np.ascontiguousarray(w.reshape(8, 128, F // 128, 128).transpose(2, 1, 0, 3))


def lay_dn(w):
    return np.ascontiguousarray(w.reshape(22, 128, 8, 128).transpose(2, 1, 0, 3))


def lay_vec(v):
    return np.ascontiguousarray(v.reshape(-1, 128).T)


def win_padded(w_in):
    out = np.zeros((1024, 24 * 128), np.float32)
    def put(blk, off, c0, n):
        out[:, blk * 128 + off: blk * 128 + off + n] = w_in[:, c0:c0 + n]
    put(0, 0, 0, 128); put(1, 0, 128, 128)
    B = 256
    put(2, 0, B, 128); put(3, 0, B + 128, 128)
    put(4, 0, B + 256, 128); put(5, 0, B + 384, 128)
    put(6, 0, B + 512, 128); put(7, 0, B + 640, 128)
    put(8, 0, B + 768, 64)
    put(8, 64, B + 768 + 64 + 32, 64)
    put(9, 0, B + 768 + 64, 32)
    C = 256 + 928
    for hh in range(4):
        put(10 + hh, 0, C + 64 * hh, 64)
    put(14, 0, C + 256, 128)
    put(15, 0, C + 384, 128)
    put(16, 0, C + 512, 128)
    put(17, 0, C + 640, 12)
    D = 256 + 928 + 652
    for i in range(6):
        put(18 + i, 0, D + 128 * i, 128)
    return out


def prep_layer(inp, l):
    d = {}
    for nm, key in (("f1", "ffn1"), ("f2", "ffn2")):
        d[f"{nm}_g"] = lay_vec(inp[f"{key}_norm"][l])
        d[f"{nm}_wg"] = lay_in(inp[f"{key}_w_gate"][l])
        d[f"{nm}_wu"] = lay_in(inp[f"{key}_w_up"][l])
        d[f"{nm}_wd"] = lay_dn(inp[f"{key}_w_down"][l])
    d["mix_g"] = lay_vec(inp["mix_norm"][l])
    d["win"] = lay_in(win_padded(inp["w_in"][l]))
    d["wout"] = np.ascontiguousarray(inp["w_out"][l].reshape(8, 128, 8, 128).transpose(2, 1, 0, 3))
    d["cw"] = np.ascontiguousarray(inp["conv_w"][l].reshape(3, 2, 128).transpose(2, 1, 0))
    pw = inp["pool_w"][l]
    pwbd = np.zeros((2, 128, 128), np.float32)
    for j in range(2):
        for gg in range(2):
            pwbd[j, gg * 64:(gg + 1) * 64, gg * 64:(gg + 1) * 64] = pw[2 * j + gg]
    d["pw"] = pwbd
    d["psc"] = lay_vec(inp["pool_scale"][l])
    prep_rwkv(inp, l, d)
    prep_nsa(inp, l, d)
    return d


def consts():
    c = {}
    wins = np.array([2, 4, 8, 16], np.float32)
    w_p = np.zeros((128, 2), np.float32)
    for j in range(2):
        w_p[0:64, j] = wins[2 * j]
        w_p[64:128, j] = wins[2 * j + 1]
    c["pinv"] = (1.0 / w_p).astype(np.float32)
    t = np.arange(16, dtype=np.float32)
    c["pcorr"] = (1.0 / np.minimum(t[None, None, :] + 1, w_p[:, :, None])).astype(np.float32)
    consts_rwkv(c)
    consts_nsa(c)
    return c


def prep_rwkv(inp, l, d):
    mu = inp["rwkv_mu"][l]
    mu8 = np.zeros((128, 8), np.float32)
    for i in range(6):
        mu8[:, i] = mu[i * 128:(i + 1) * 128]
    mu8[0:64, 6] = mu[768:832]
    mu8[64:128, 6] = mu[864:928]
    mu8[0:32, 7] = mu[832:864]
    d["rw_mu8"] = mu8
    vec = np.zeros((128, 2, 7), np.float32)
    names = ["rwkv_w0", "rwkv_a0", "rwkv_k_k", "rwkv_k_a", "rwkv_r_k", "rwkv_ln_w", "rwkv_ln_b"]
    for i, nm in enumerate(names):
        v = inp[nm][l].reshape(256)
        vec[:, 0, i] = v[0:128]
        vec[:, 1, i] = v[128:256]
    d["rw_vec"] = vec
    wup = np.zeros((128, 2, 128), np.float32)
    gup = np.zeros((128, 2, 128), np.float32)
    aup = np.zeros((128, 2, 128), np.float32)
    for hp in range(2):
        wup[0:64, hp, :] = inp["rwkv_w_up"][l][:, hp * 128:(hp + 1) * 128]
        gup[64:128, hp, :] = inp["rwkv_g_up"][l][:, hp * 128:(hp + 1) * 128]
        aup[0:32, hp, :] = inp["rwkv_a_up"][l][:, hp * 128:(hp + 1) * 128]
    d["rw_wup"], d["rw_gup"], d["rw_aup"] = wup, gup, aup


def consts_rwkv(c):
    bones = np.zeros((128, 128), np.float32)
    bones[0:64, 0:64] = 1
    bones[64:128, 64:128] = 1
    c["bones"] = bones
    c["bones64"] = bones / 64.0
    c["ident"] = np.eye(128, dtype=np.float32)
    c["sel2"] = np.concatenate([np.eye(64, dtype=np.float32)] * 2, axis=0)
    rm = np.ones((128, 512), np.float32)
    rm[:, 0::64] = 0
    c["rmask"] = rm
    j = np.arange(64)[:, None]
    t = np.arange(64)[None, :]
    strict = (j < t).astype(np.float32)
    incl = (j <= t).astype(np.float32)
    def bd(m):
        o = np.zeros((128, 128), np.float32)
        o[0:64, 0:64] = m
        o[64:128, 64:128] = m
        return o
    c["LPmask"] = np.concatenate([-bd(strict), bd(incl), bd(strict), bd(incl)], axis=1)
    c["Lamask"] = -bd((j > t).astype(np.float32))


def prep_nsa(inp, l, d):
    qn = np.zeros((128, 1), np.float32)
    qn[0:64, 0] = inp["nsa_q_norm"][l]
    d["ns_qnw"] = qn
    kn = np.zeros((128, 3), np.float32)
    kn[0:64, :] = inp["nsa_k_norm"][l].T
    d["ns_knw"] = kn
    w1 = np.zeros((128, 32, 128), np.float32)
    w1[0:64] = inp["nsa_cmp_k_w1"][l].reshape(32, 64, 128).transpose(1, 0, 2)
    w1[64:128] = inp["nsa_cmp_v_w1"][l].reshape(32, 64, 128).transpose(1, 0, 2)
    d["ns_w1"] = w1
    pos = inp["nsa_cmp_pos"][l]
    p2 = np.zeros((128, 32, 2), np.float32)
    p2[0:64, :, 0] = pos.T
    p2[0:64, :, 1] = pos.T
    p2[64:128] = p2[0:64]
    d["ns_pos2"] = p2
    w2 = np.zeros((128, 2, 64), np.float32)
    w2[:, 0, :] = inp["nsa_cmp_k_w2"][l]
    w2[:, 1, :] = inp["nsa_cmp_v_w2"][l]
    d["ns_w2"] = w2


def consts_nsa(c):
    BIG = 30000.0
    key = np.arange(S)
    c["n_eall"] = (key[None, :] // 64 == np.arange(64)[:, None]).astype(np.float32)
    pk = np.arange(128)[:, None]
    tq = np.arange(512)[None, :]
    c["n_cmsel"] = np.stack([np.where(tq - pk - 128 * r >= 0, 0.0, -BIG) for r in range(4)]).astype(np.float32)
    cw = []
    for r in range(8):
        dd = tq - pk - (r - 4) * 128
        cw.append(np.where((dd >= 0) & (dd < 512), 0.0, -BIG))
    c["n_cmwin"] = np.stack(cw).astype(np.float32)
    c["n_negc"] = np.stack([np.where(tq >= 16 * pk + 31 - 512 * g, 0.0, -BIG) for g in range(5)]).astype(np.float32)
    t = np.arange(S)
    cur = t // 64
    j = np.arange(64)
    ok = (j[None, :] <= cur[:, None]).astype(np.float32)
    forced = ((j[None, :] == 0) | (j[None, :] == cur[:, None]) | (j[None, :] == cur[:, None] - 1)).astype(np.float32)
    addc = forced * 1e4 * ok + (ok - 1.0)
    c["n_ok"] = np.ascontiguousarray(ok.reshape(32, 128, 64).transpose(1, 0, 2))
    c["n_addc"] = np.ascontiguousarray(addc.reshape(32, 128, 64).transpose(1, 0, 2)).astype(np.float32)
    slopes = 2.0 ** (-8.0 * np.arange(1, 5) / 4)
    a_t, b_t = t // 64, t % 64
    qaug = np.zeros((4, 4, S), np.float32)
    for h in range(4):
        s8 = 8.0 * slopes[h]
        qaug[h, 0] = -s8 * 64 * a_t
        qaug[h, 1] = -s8 * b_t
        qaug[h, 2] = s8
        qaug[h, 3] = s8
    c["n_qaug"] = qaug
    kaug = np.zeros((4, S), np.float32)
    kaug[0] = 1
    kaug[1] = 1
    kaug[2] = 64 * a_t
    kaug[3] = b_t
    c["n_kaug"] = kaug
    n = np.arange(256)
    be = 16 * n + 31
    kc = np.zeros((4, 256), np.float32)
    kc[0] = 1
    kc[1] = 1
    kc[2] = 64 * (be // 64)
    kc[3] = be % 64
    c["n_kcaug"] = kc
    ovl = ((16 * n[:, None] <= 64 * j[None, :] + 63) & (16 * n[:, None] + 31 >= 64 * j[None, :])).astype(np.float32)
    o1 = np.concatenate([np.ones((256, 1), np.float32), ovl], axis=1)
    o1[255] = 0
    c["n_ovl1"] = np.ascontiguousarray(o1.reshape(2, 128, 65).transpose(1, 0, 2))


HAVE_RWKV = True
HAVE_NSA = True
_CACHE = {}


def build_program(layer_shapes, const_shapes):
    b = Builder()
    xin = b.dram("xin", [8, 128, S], F32, "ExternalInput")
    xout = b.dram("xout", [8, 128, S], F32, "ExternalOutput")
    xT = b.dram("xT", [8, 128, S], F32)
    pT = b.dram("pT", [NBLK, 128, S], F32)
    yT = b.dram("yT", [8, 128, S], BF16)
    D = []
    for l in range(2):
        D.append({k: b.dram(f"L{l}_{k}", list(shp), F32, "ExternalInput") for k, shp in layer_shapes.items()})
    Cn = {k: b.dram(f"C_{k}", list(shp), F32, "ExternalInput") for k, shp in const_shapes.items()}
    b.begin()
    for k in range(8):
        t = b.sb("t", [128, S], F32)
        b.dma(t[:], xin[k], [xin], [t])
        b.dma(xT[k], t[:], [t], [xT])
    b.end()
    for l in range(2):
        d = D[l]
        ffn_phase(b, xT, d["f1_g"], d["f1_wg"], d["f1_wu"], d["f1_wd"])
        inproj_phase(b, xT, d["mix_g"], d["win"], pT)
        conv_pool_phase(b, pT, yT, d["cw"], d["pw"], d["psc"], Cn["pinv"], Cn["pcorr"])
        if HAVE_RWKV:
            rwkv_phase(b, pT, yT, d, Cn)
        if HAVE_NSA:
            nsa_phase(b, pT, yT, d, Cn)
        outproj_phase(b, xT, yT, d["wout"])
        ffn_phase(b, xT, d["f2_g"], d["f2_wg"], d["f2_wu"], d["f2_wd"])
    b.begin()
    for k in range(8):
        t = b.sb("t", [128, S], F32)
        b.dma(t[:], xT[k], [xT], [t])
        b.dma(xout[k], t[:], [t], [xout])
    b.end()
    return b


def kernel(**inputs):
    inp = {k: np.asarray(v) for k, v in inputs.items()}
    layers = [prep_layer(inp, l) for l in range(2)]
    cn = consts()
    b = build_program({k: v.shape for k, v in layers[0].items()}, {k: v.shape for k, v in cn.items()})
    shared = {}
    for l in range(2):
        for k, v in layers[l].items():
            shared[f"L{l}_{k}"] = v
    for k, v in cn.items():
        shared[f"C_{k}"] = v
    x = inp["x"]
    in_maps = []
    for c in range(8):
        m = dict(shared)
        m["xin"] = np.ascontiguousarray(x[c].T.reshape(8, 128, S))
        in_maps.append(m)
    res = run_bass_kernel_spmd(b.nc, in_maps, core_ids=list(range(8)))
    out = np.stack([res.results[c]["xout"].reshape(1024, S).T for c in range(8)], axis=0)
    return np.ascontiguousarray(out.astype(np.float32))
```

```python
import numpy as np
from contextlib import ExitStack
import concourse.bass as bass
import concourse.mybir as mybir
from concourse.bass_utils import run_bass_kernel_spmd

F32 = mybir.dt.float32
BF16 = mybir.dt.bfloat16
AF = mybir.ActivationFunctionType
ALU = mybir.AluOpType
AX = mybir.AxisListType


class Buf:
    def __init__(self, name, t, space):
        self.name = name
        self.t = t
        self.space = space

    def __getitem__(self, idx):
        return self.t[idx]


def _norm(lst):
    out = []
    for x in lst:
        if isinstance(x, tuple):
            out.append(x)
        else:
            out.append((x, None))
    return out


def _conf(a, b):
    return a is None or b is None or a == b


class Builder:
    ENG = ("pe", "act", "dve", "pool", "sp")
    NDMA = 12

    def __init__(self):
        self.nc = bass.Bass("TRN2", target_bir_lowering=False)
        self.gstack = ExitStack()
        self.sems = []
        self.ekey = {}
        for e in ("pe", "act", "dve", "pool"):
            self.ekey[e] = self._newsem("c_" + e)
        self.dkeys = {q: [self._newsem(f"d_{q}{i}") for i in range(self.NDMA)] for q in ("sp", "pool", "act")}
        self.drr = {"sp": 0, "pool": 0, "act": 0}
        self.semval = [0] * len(self.sems)
        self.seen = {e: {} for e in self.ENG}
        self.recs = {}
        self.ops = {e: [] for e in self.ENG}
        self.pstack = None
        self.nphase = 0
        self.uid = 0

    def _newsem(self, name):
        s = self.gstack.enter_context(self.nc.semaphore(name))
        self.sems.append(s)
        return len(self.sems) - 1

    def dram(self, name, shape, dtype, kind="Internal"):
        t = self.nc.dram_tensor(name, list(shape), dtype, kind=kind)
        return Buf(name, t.ap(), "dram")

    def begin(self):
        assert self.pstack is None
        self.pstack = ExitStack()
        self.nphase += 1

    def sb(self, name, shape, dtype):
        self.uid += 1
        nm = f"{name}_{self.uid}"
        t = self.pstack.enter_context(self.nc.sbuf_tensor(nm, list(shape), dtype))
        return Buf(nm, t, "sbuf")

    def ps(self, name, shape, dtype=F32):
        self.uid += 1
        nm = f"{name}_{self.uid}"
        t = self.pstack.enter_context(self.nc.psum_tensor(nm, list(shape), dtype))
        return Buf(nm, t, "psum")

    def op(self, eng, fn, reads=(), writes=(), dma=False, acc=False):
        reads = _norm(reads)
        writes = _norm(writes)
        if not acc:
            for (bb, tt) in reads:
                if bb.space == "psum" and (bb, tt) not in writes:
                    writes = writes + [(bb, tt)]
        waits = {}

        def need(k, v):
            if v > waits.get(k, 0):
                waits[k] = v

        for (b, t) in _norm(reads):
            for (tag, isw, k), v in self.recs.get(b.name, {}).items():
                if isw and _conf(tag, t):
                    need(k, v)
        for (b, t) in _norm(writes):
            for (tag, isw, k), v in self.recs.get(b.name, {}).items():
                if _conf(tag, t):
                    if acc and isw and k == self.ekey["pe"]:
                        continue
                    need(k, v)
        if dma:
            q = eng
            kd = self.dkeys[q][self.drr[q] % self.NDMA]
            self.drr[q] += 1
            need(kd, self.semval[kd])
            self.semval[kd] += 16
            tok = (kd, self.semval[kd])
            inc = 16
        else:
            kd = self.ekey[eng]
            self.semval[kd] += 1
            tok = (kd, self.semval[kd])
            inc = 1
        wl = []
        for k, v in waits.items():
            if v > 0 and self.seen[eng].get(k, 0) < v:
                self.seen[eng][k] = v
                wl.append((k, v))
        self.ops[eng].append((wl, fn, tok[0], inc))
        for (b, t) in _norm(reads):
            r = self.recs.setdefault(b.name, {})
            key = (t, False, tok[0])
            r[key] = max(r.get(key, 0), tok[1])
        for (b, t) in _norm(writes):
            r = self.recs.setdefault(b.name, {})
            for key in list(r.keys()):
                if t is None or key[0] == t:
                    del r[key]
            r[(t, True, tok[0])] = tok[1]
        return tok

    def end(self):
        wl = []
        for q in self.dkeys:
            for kd in self.dkeys[q]:
                v = self.semval[kd]
                if v > 0 and self.seen["sp"].get(kd, 0) < v:
                    self.seen["sp"][kd] = v
                    wl.append((kd, v))
        self.ops["sp"].append((wl, None, None, 0))
        nc = self.nc
        ops = self.ops
        sems = self.sems

        def replay(lst, e):
            for wl_, fn, k, inc in lst:
                for kk, v in wl_:
                    e.wait_ge(sems[kk], v)
                if fn is not None:
                    ins = fn(e)
                    ins.then_inc(sems[k], inc)

        with nc.Block() as block:
            @block.tensor
            def _(e):
                replay(ops["pe"], e)

            @block.scalar
            def _(e):
                replay(ops["act"], e)

            @block.vector
            def _(e):
                replay(ops["dve"], e)

            @block.gpsimd
            def _(e):
                replay(ops["pool"], e)

            @block.sync
            def _(e):
                replay(ops["sp"], e)
        self.ops = {e: [] for e in self.ENG}
        self.pstack.close()
        self.pstack = None
        for name in list(self.recs.keys()):
            pass

    def dma(self, out, in_, reads, writes, q="sp", **kw):
        return self.op(q, lambda e: e.dma_start(out=out, in_=in_, **kw), reads, writes, dma=True)

    def mm(self, out, lhsT, rhs, start, stop, reads, writes):
        return self.op("pe", lambda e: e.matmul(out, lhsT, rhs, start=start, stop=stop), reads, writes, acc=True)

    def tr(self, out, in_, ident, reads, writes):
        return self.op("pe", lambda e: e.transpose(out, in_, ident), reads, writes, acc=True)

    def actf(self, out, in_, func, reads, writes, bias=None, scale=1.0, eng="act"):
        kw = {}
        if bias is not None:
            kw["bias"] = bias
        return self.op(eng, lambda e: e.activation(out=out, in_=in_, func=func, scale=scale, **kw), reads, writes)


S = 4096
T = 1024
NT = S // T
NH = T // 512
EPS = 1e-6
import os as _os1
_NOW = bool(_os1.environ.get('NO_WDMA'))


def load_norm_gen(b, xT, ti, x, g, ones, hT, sq, rstd, pst):
    tsl = slice(ti * T, (ti + 1) * T)
    b.dma(x[:], xT[:, :, tsl].rearrange("k p t -> p k t"), [(xT, ti)], [x])
    yield
    for k in range(8):
        s = sq[k % 2]
        b.op("act", lambda e, s=s, k=k: e.activation(out=s[:], in_=x[:, k, :], func=AF.Square), [x], [s])
        for h in range(NH):
            b.mm(pst[h][:], ones[:], s[:, h * 512:(h + 1) * 512], k == 0, k == 7, [ones, s], [pst[h]])
        yield
    for h in range(NH):
        sl = slice(h * 512, (h + 1) * 512)
        b.op("act", lambda e, h=h, sl=sl: e.activation(out=rstd[:, sl], in_=pst[h][:], func=AF.Sqrt, scale=1.0 / 1024, bias=EPS),
             [pst[h]], [(rstd, h)])
        b.op("dve", lambda e, sl=sl: e.reciprocal(out=rstd[:, sl], in_=rstd[:, sl]), [(rstd, h)], [(rstd, h)])
    yield
    for k in range(8):
        b.op("dve", lambda e, k=k: e.scalar_tensor_tensor(out=hT[:, k, :], in0=x[:, k, :], scalar=g[:, k:k + 1], in1=rstd[:],
                                                         op0=ALU.mult, op1=ALU.mult), [x, g, rstd], [(hT, k)])
        yield


def gstep(gen):
    if gen is None:
        return None
    try:
        next(gen)
        return gen
    except StopIteration:
        return None


def gdrain(gen):
    while gen is not None:
        gen = gstep(gen)


def ffn_phase(b, xT, gD, WgD, WuD, WdD):
    b.begin()
    ones = b.sb("ones", [128, 128], BF16)
    b.op("pool", lambda e: e.memset(ones[:], 1.0), [], [ones])
    epsb = None
    g = b.sb("g", [128, 8], F32)
    b.dma(g[:], gD[:], [gD], [g])
    xs = [b.sb("x", [128, 8, T], F32) for _ in range(2)]
    hTs = [b.sb("hT", [128, 8, T], BF16) for _ in range(2)]
    hid = b.sb("hid", [128, 22, T], BF16)
    sq = [b.sb("sq", [128, T], BF16) for _ in range(2)]
    rstds = [b.sb("rstd", [128, T], F32) for _ in range(2)]
    wg = [b.sb("wg", [128, 8, 128], BF16) for _ in range(2)]
    wu = [b.sb("wu", [128, 8, 128], BF16) for _ in range(2)]
    wd = [b.sb("wd", [128, 22, 128], BF16) for _ in range(2)]
    sg = [b.sb("sg", [128, 512], F32) for _ in range(2)]
    pg = [b.ps("pg", [128, 512]) for _ in range(2)]
    pu = [b.ps("pu", [128, 512]) for _ in range(2)]
    po = [b.ps("po", [128, 512]) for _ in range(2)]
    pst = [b.ps("pst", [128, 512]) for _ in range(2)]
    cnt = 0
    gdrain(load_norm_gen(b, xT, 0, xs[0], g, ones, hTs[0], sq, rstds[0], pst))
    for ti in range(NT):
        x = xs[ti % 2]
        hT = hTs[ti % 2]
        tsl = slice(ti * T, (ti + 1) * T)
        nxt = None
        for f in range(22):
            if f == 2 and ti + 1 < NT:
                nxt = load_norm_gen(b, xT, ti + 1, xs[(ti + 1) % 2], g, ones, hTs[(ti + 1) % 2], sq, rstds[(ti + 1) % 2], pst)
            nxt = gstep(nxt)
            a, u = wg[f % 2], wu[f % 2]
            if not (_NOW and (ti > 0 or f > 1)):
                b.dma(a[:], WgD[f], [WgD], [a], q="pool", max_dma_last_dim=4096)
                b.dma(u[:], WuD[f], [WuD], [u], q="pool", max_dma_last_dim=4096)
            for h in range(NH):
                sl = slice(h * 512, (h + 1) * 512)
                G, U, SG = pg[cnt % 2], pu[cnt % 2], sg[cnt % 2]
                cnt += 1
                for k in range(8):
                    b.mm(G[:], a[:, k, :], hT[:, k, sl], k == 0, k == 7, [a, (hT, k)], [G])
                for k in range(8):
                    b.mm(U[:], u[:, k, :], hT[:, k, sl], k == 0, k == 7, [u, (hT, k)], [U])
                b.op("act", lambda e, G=G, SG=SG: e.activation(out=SG[:], in_=G[:], func=AF.Silu), [G], [SG])
                b.op("dve", lambda e, U=U, SG=SG, f=f, sl=sl: e.tensor_tensor(out=hid[:, f, sl], in0=U[:], in1=SG[:], op=ALU.mult),
                     [U, SG], [(hid, (f, h))])
        for d in range(8):
            w = wd[d % 2]
            if not (_NOW and (ti > 0 or d > 1)):
                b.dma(w[:], WdD[d], [WdD], [w], q="pool", max_dma_last_dim=4096)
            for h in range(NH):
                sl = slice(h * 512, (h + 1) * 512)
                O = po[cnt % 2]
                cnt += 1
                for f in range(22):
                    b.mm(O[:], w[:, f, :], hid[:, f, sl], f == 0, f == 21, [w, (hid, (f, h))], [O])
                b.op("dve", lambda e, O=O, d=d, sl=sl, x=x: e.scalar_tensor_tensor(out=x[:, d, sl], in0=O[:], scalar=0.5, in1=x[:, d, sl],
                                                                               op0=ALU.mult, op1=ALU.add), [O, x], [x])
        gdrain(nxt)
        b.dma(xT[:, :, tsl].rearrange("k p t -> p k t"), x[:], [x], [(xT, ti)])
    b.end()


BLK_A = (0, 1)
BLK_R, BLK_K, BLK_V, BLK_WG, BLK_AD = (2, 3), (4, 5), (6, 7), 8, 9
BLK_Q, BLK_KVC, BLK_KVS, BLK_KVW, BLK_GT = (10, 11, 12, 13), 14, 15, 16, 17
BLK_U, BLK_B, BLK_C = (18, 19), (20, 21), (22, 23)
NBLK = 24


def inproj_phase(b, xT, gD, WinD, pT):
    b.begin()
    ones = b.sb("ones", [128, 128], BF16)
    b.op("pool", lambda e: e.memset(ones[:], 1.0), [], [ones])
    g = b.sb("g", [128, 8], F32)
    b.dma(g[:], gD[:], [gD], [g])
    xs = [b.sb("x", [128, 8, T], F32) for _ in range(2)]
    hTs = [b.sb("hT", [128, 8, T], BF16) for _ in range(2)]
    sq = [b.sb("sq", [128, T], BF16) for _ in range(2)]
    rstds = [b.sb("rstd", [128, T], F32) for _ in range(2)]
    ws = [b.sb("w", [128, 8, 128], BF16) for _ in range(2)]
    os_ = [b.sb("o", [128, T], F32) for _ in range(2)]
    pp = [b.ps("pp", [128, 512]) for _ in range(4)]
    pst = [b.ps("pst", [128, 512]) for _ in range(2)]
    cnt = 0
    gdrain(load_norm_gen(b, xT, 0, xs[0], g, ones, hTs[0], sq, rstds[0], pst))
    for ti in range(NT):
        x = xs[ti % 2]
        hT = hTs[ti % 2]
        tsl = slice(ti * T, (ti + 1) * T)
        nxt = None
        for blk in range(NBLK):
            if blk == 2 and ti + 1 < NT:
                nxt = load_norm_gen(b, xT, ti + 1, xs[(ti + 1) % 2], g, ones, hTs[(ti + 1) % 2], sq, rstds[(ti + 1) % 2], pst)
            nxt = gstep(nxt)
            w = ws[blk % 2]
            o = os_[blk % 2]
            b.dma(w[:], WinD[blk], [WinD], [w], q="pool", max_dma_last_dim=4096)
            for h in range(NH):
                sl = slice(h * 512, (h + 1) * 512)
                P = pp[cnt % 4]
                cnt += 1
                for k in range(8):
                    b.mm(P[:], w[:, k, :], hT[:, k, sl], k == 0, k == 7, [w, (hT, k)], [P])
                if h % 2 == 0:
                    b.op("act", lambda e, P=P, o=o, sl=sl: e.activation(out=o[:, sl], in_=P[:], func=AF.Copy), [P], [(o, h)])
                else:
                    b.op("dve", lambda e, P=P, o=o, sl=sl: e.tensor_copy(out=o[:, sl], in_=P[:]), [P], [(o, h)])
            b.dma(pT[blk, :, tsl], o[:], [o], [(pT, (blk, ti))])
        gdrain(nxt)
    b.end()


def outproj_phase(b, xT, yT, WoutD):
    b.begin()
    xs = [b.sb("x", [128, 8, T], F32) for _ in range(2)]
    ys = [b.sb("y", [128, 8, T], BF16) for _ in range(2)]
    ws = [b.sb("w", [128, 8, 128], BF16) for _ in range(2)]
    pp = [b.ps("pp", [128, 512]) for _ in range(4)]
    cnt = 0
    for ti in range(NT):
        x = xs[ti % 2]
        y = ys[ti % 2]
        tsl = slice(ti * T, (ti + 1) * T)
        b.dma(x[:], xT[:, :, tsl].rearrange("k p t -> p k t"), [(xT, ti)], [x])
        b.dma(y[:], yT[:, :, tsl].rearrange("k p t -> p k t"), [yT], [y])
        for d in range(8):
            w = ws[d % 2]
            b.dma(w[:], WoutD[d], [WoutD], [w], q="pool", max_dma_last_dim=4096)
            for h in range(NH):
                sl = slice(h * 512, (h + 1) * 512)
                P = pp[cnt % 4]
                cnt += 1
                for c in range(8):
                    b.mm(P[:], w[:, c, :], y[:, c, sl], c == 0, c == 7, [w, y], [P])
                b.op("dve", lambda e, P=P, x=x, d=d, sl=sl: e.tensor_tensor(out=x[:, d, sl], in0=P[:], in1=x[:, d, sl], op=ALU.add), [P, x], [x])
        b.dma(xT[:, :, tsl].rearrange("k p t -> p k t"), x[:], [x], [(xT, ti)])
    b.end()


def conv_pool_phase(b, pT, yT, cwD, pwD, pscD, pinvD, pcorrD):
    b.begin()
    cw = b.sb("cw", [128, 2, 3], F32)
    b.dma(cw[:], cwD[:], [cwD], [cw])
    psc = b.sb("psc", [128, 2], F32)
    b.dma(psc[:], pscD[:], [pscD], [psc])
    pinv = b.sb("pinv", [128, 2], F32)
    b.dma(pinv[:], pinvD[:], [pinvD], [pinv])
    pcorr = b.sb("pcorr", [128, 2, 16], F32)
    b.dma(pcorr[:], pcorrD[:], [pcorrD], [pcorr])
    pw = b.sb("pw", [128, 2, 128], BF16)
    for j in range(2):
        b.dma(pw[:, j, :], pwD[j], [pwD], [pw], q="pool")
    u = b.sb("u", [128, 16 + S], F32)
    bb = b.sb("bb", [128, S], F32)
    cc = b.sb("cc", [128, S], F32)
    z = b.sb("z", [128, 16 + S], F32)
    acc = b.sb("acc", [128, 16 + S], F32)
    yb = b.sb("yb", [128, S], BF16)
    pp = [b.ps("pp", [128, 512]) for _ in range(2)]
    for j in range(2):
        b.dma(u[:, 16:], pT[BLK_U[j]], [pT], [u])
        b.dma(bb[:], pT[BLK_B[j]], [pT], [bb])
        b.dma(cc[:], pT[BLK_C[j]], [pT], [cc])
        b.op("pool", lambda e: e.memset(z[:, 0:16], 0.0), [], [z])
        b.op("dve", lambda e: e.tensor_tensor(out=z[:, 16:], in0=cc[:], in1=u[:, 16:], op=ALU.mult), [cc, u], [z])
        b.op("dve", lambda e, j=j: e.tensor_scalar(out=acc[:, 16:], in0=z[:, 16:], scalar1=cw[:, j, 2:3], scalar2=None, op0=ALU.mult), [z, cw], [acc])
        b.op("dve", lambda e, j=j: e.scalar_tensor_tensor(out=acc[:, 16:], in0=z[:, 15:15 + S], scalar=cw[:, j, 1:2], in1=acc[:, 16:],
                                                         op0=ALU.mult, op1=ALU.add), [z, cw, acc], [acc])
        b.op("dve", lambda e, j=j: e.scalar_tensor_tensor(out=acc[:, 16:], in0=z[:, 14:14 + S], scalar=cw[:, j, 0:1], in1=acc[:, 16:],
                                                         op0=ALU.mult, op1=ALU.add), [z, cw, acc], [acc])
        b.op("dve", lambda e: e.tensor_tensor(out=yb[:], in0=bb[:], in1=acc[:, 16:], op=ALU.mult), [bb, acc], [yb])
        b.dma(yT[6 + j], yb[:], [yb], [(yT, 6 + j)])
    for j in range(2):
        b.op("pool", lambda e: e.memset(u[:, 0:16], 0.0), [], [u])
        b.op("pool", lambda e: e.memset(z[:, 0:16], 0.0), [], [z])
        b.op("pool", lambda e: e.memset(acc[:, 0:16], 0.0), [], [acc])
        b.dma(u[:, 16:], pT[BLK_A[j]], [pT], [u])
        R = slice(16, 16 + S)

        def sh(k):
            return slice(16 - k, 16 - k + S)
        if j == 0:
            b.op("dve", lambda e: e.tensor_tensor(out=z[:, R], in0=u[:, R], in1=u[:, sh(1)], op=ALU.add), [u], [z])
            b.op("dve", lambda e: e.tensor_copy(out=acc[0:64, R], in_=z[0:64, R]), [z], [acc])
            b.op("dve", lambda e: e.tensor_tensor(out=acc[64:128, R], in0=z[64:128, R], in1=z[64:128, sh(2)], op=ALU.add), [z], [acc])
        else:
            b.op("dve", lambda e: e.tensor_tensor(out=z[:, R], in0=u[:, R], in1=u[:, sh(1)], op=ALU.add), [u], [z])
            b.op("dve", lambda e: e.tensor_tensor(out=acc[:, R], in0=z[:, R], in1=z[:, sh(2)], op=ALU.add), [z], [acc])
            b.op("dve", lambda e: e.tensor_tensor(out=z[:, R], in0=acc[:, R], in1=acc[:, sh(4)], op=ALU.add), [acc], [z])
            b.op("dve", lambda e: e.tensor_copy(out=acc[0:64, R], in_=z[0:64, R]), [z], [acc])
            b.op("dve", lambda e: e.tensor_tensor(out=acc[64:128, R], in0=z[64:128, R], in1=z[64:128, sh(8)], op=ALU.add), [z], [acc])
        b.op("dve", lambda e, j=j: e.tensor_scalar(out=cc[:], in0=acc[:, R], scalar1=pinv[:, j:j + 1], scalar2=None, op0=ALU.mult), [acc, pinv], [cc])
        b.op("dve", lambda e, j=j: e.tensor_tensor(out=cc[:, 0:16], in0=acc[:, 16:32], in1=pcorr[:, j, :], op=ALU.mult), [acc, pcorr, cc], [cc])
        b.op("dve", lambda e: e.tensor_tensor(out=yb[:], in0=cc[:], in1=u[:, R], op=ALU.subtract), [cc, u], [yb])
        for h in range(S // 512):
            sl = slice(h * 512, (h + 1) * 512)
            P = pp[h % 2]
            b.mm(P[:], pw[:, j, :], yb[:, sl], True, True, [pw, yb], [P])
            b.op("act", lambda e, P=P, sl=sl, j=j: e.activation(out=z[:, sl], in_=P[:], func=AF.Copy, scale=psc[:, j:j + 1]), [P, psc], [(z, h)])
        b.op("dve", lambda e: e.tensor_copy(out=yb[:], in_=z[:, 0:S]), [z], [yb])
        b.dma(yT[j], yb[:], [yb], [(yT, j)])
    b.end()


TT_ = 512
NCH = 8


F32R = mybir.dt.float32r


def mmr(b, out, lhsT, rhs, start, stop, reads, writes):
    return b.op("pe", lambda e: e.matmul(out, lhsT.bitcast(F32R), rhs.bitcast(F32R), start=start, stop=stop), reads, writes, acc=True)


def run_rr(gens, bg=None):
    gens = list(gens)
    while gens:
        for g in list(gens):
            try:
                next(g)
            except StopIteration:
                gens.remove(g)
        if bg is not None:
            try:
                next(bg)
            except StopIteration:
                pass


DBG = []


def rwkv_phase(b, pT, yT, d, Cn, ntiles=None, debug=False):
    b.begin()

    def dump(name, buf, ap, shape):
        if not debug:
            return
        dd = b.dram("dbg_" + name, list(shape), F32, "ExternalOutput")
        DBG.append("dbg_" + name)
        b.dma(dd[:], ap, [buf], [dd])
    sbF = lambda nm, shp=(128, TT_): b.sb(nm, list(shp), F32)
    def ld(nm, src, shp):
        t = b.sb(nm, shp, F32)
        b.dma(t[:], src[:], [src], [t])
        return t
    bones = ld("bones", Cn["bones"], [128, 128])
    bones64 = ld("bones64", Cn["bones64"], [128, 128])
    ident = ld("ident", Cn["ident"], [128, 128])
    sel2 = ld("sel2", Cn["sel2"], [128, 64])
    rmask = ld("rmask", Cn["rmask"], [128, TT_])
    LPmask = ld("LPmask", Cn["LPmask"], [128, 512])
    Lamask = ld("Lamask", Cn["Lamask"], [128, 128])
    mu8 = ld("mu8", d["rw_mu8"], [128, 8])
    vec = ld("vec", d["rw_vec"], [128, 2, 7])
    WUP = ld("WUP", d["rw_wup"], [128, 2, 128])
    GUP = ld("GUP", d["rw_gup"], [128, 2, 128])
    AUP = ld("AUP", d["rw_aup"], [128, 2, 128])
    V_W0, V_A0, V_KK, V_KA, V_RK, V_LNW, V_LNB = range(7)

    pb2 = [b.sb(f"pb{i}", [128, 1 + TT_], F32) for i in range(2)]
    pb = [pb2[i % 2] for i in range(8)]
    dtmp = sbF("dtmp")
    lp = [sbF(f"lp{i}") for i in range(8)]
    twg = sbF("twg")
    psw = [b.ps("psw", [128, 512]) for _ in range(1)]
    psT2 = [b.ps("psT", [128, 512]) for _ in range(2)]
    psN2 = [b.ps("psN", [128, 512]) for _ in range(2)]
    psS2 = [b.ps("psS", [128, 512]) for _ in range(2)]
    wcnt = [0]

    def wide():
        wcnt[0] += 1
        return psw[0]

    H = {}
    SH = {nm: sbF("sh_" + nm) for nm in ("lw", "cum", "cexc", "dW", "epos", "eexc", "t1", "t2", "kk", "YT", "yc", "sq", "rs")}
    for hp in range(2):
        h = {}
        for nm in ("a", "kkn", "kpp", "akk", "akkh", "eneg", "eW"):
            h[nm] = sbF(nm + str(hp))
        for nm in ("g", "bonus"):
            h[nm] = [sbF(nm + str(hp) + "_" + str(i)) for i in range(2)]
        for nm in ("lw", "cum", "cexc", "dW", "epos", "eexc", "t1", "t2", "kk", "YT", "yc", "sq", "rs"):
            h[nm] = SH[nm]
        h["wC"] = [b.sb(f"wC{hp}_{i}", [128, NCH], F32) for i in range(2)]
        h["QR"] = b.sb(f"QR{hp}", [128, NCH, 2, 64], F32)
        for nm in ("AKKbd", "KHbd", "AKKWbd", "KWbd", "Vbd", "Ybd"):
            h[nm] = b.sb(nm + str(hp), [128, NCH, 128], F32)
            b.op("pool", lambda e, t=h[nm]: e.memset(t[:], 0.0), [], [h[nm]])
        h["QRbd"] = b.sb(f"QRbd{hp}", [128, NCH, 2, 128], F32)
        b.op("pool", lambda e, t=h["QRbd"]: e.memset(t[:], 0.0), [], [h["QRbd"]])
        h["Sst"] = b.sb(f"Sst{hp}", [128, 64], F32)
        b.op("pool", lambda e, t=h["Sst"]: e.memset(t[:], 0.0), [], [h["Sst"]])
        h["LP"] = [b.sb(f"LP{hp}_{i}", [128, 512], F32) for i in range(2)]
        h["AKWT"] = [b.sb(f"AKWT{hp}_{i}", [128, 256], F32) for i in range(2)]
        h["Vst"] = [b.sb(f"Vst{hp}_{i}", [128, 64], F32) for i in range(2)]
        h["TT"] = [b.sb(f"TT{hp}_{i}", [128, 128], F32) for i in range(2)]
        h["MY"] = [b.sb(f"MY{hp}_{i}", [128, 256], F32) for i in range(2)]
        h["N"] = [b.sb(f"N{hp}_{i}", [128, 128], F32) for i in range(2)]
        h["Xst"] = b.sb(f"Xst{hp}", [128, 64], F32)
        h["Ust"] = b.sb(f"Ust{hp}", [128, 64], F32)
        h["ob"] = b.sb(f"ob{hp}", [128, TT_], BF16)
        H[hp] = h

    def v3(t, R):
        return t[R, :].rearrange("p (c t) -> p c t", t=64)

    def stage1_gen(ti):
        t0 = ti * TT_
        for i in range(8):
            blk = 2 + i
            if ti == 0:
                b.op("pool", lambda e, i=i: e.memset(pb[i][:, 0:1], 0.0), [], [pb[i]])
                b.dma(pb[i][:, 1:], pT[blk, :, 0:TT_], [pT], [pb[i]])
            else:
                b.dma(pb[i][:], pT[blk, :, t0 - 1:t0 + TT_], [pT], [pb[i]])
            eng = "dve" if i % 2 == 0 else "pool"
            b.op("dve", lambda e, i=i: e.tensor_tensor(out=dtmp[:], in0=pb[i][:, 0:TT_], in1=pb[i][:, 1:], op=ALU.subtract), [pb[i]], [dtmp])
            b.op("dve", lambda e, i=i: e.scalar_tensor_tensor(out=lp[i][:], in0=dtmp[:], scalar=mu8[:, i:i + 1], in1=pb[i][:, 1:],
                                                             op0=ALU.mult, op1=ALU.add), [dtmp, mu8, pb[i]], [lp[i]])
            yield
        b.op("act", lambda e: e.activation(out=twg[0:64, :], in_=lp[6][0:64, :], func=AF.Tanh), [lp[6]], [(twg, 0)])
        b.op("act", lambda e: e.activation(out=twg[64:128, :], in_=lp[6][64:128, :], func=AF.Sigmoid), [lp[6]], [(twg, 1)])
        for hp in range(2):
            h = H[hp]
            rp, kp, vp = lp[hp], lp[2 + hp], lp[4 + hp]
            vv = lambda i, hp=hp: vec[:, hp, i:i + 1]
            pw_ = wide()
            b.mm(pw_[:], WUP[:, hp, :], twg[:], True, True, [WUP, twg], [pw_])
            b.op("act", lambda e, pw_=pw_, h=h, vv=vv: e.activation(out=h["t1"][:], in_=pw_[:], func=AF.Sigmoid, bias=vv(V_W0)), [pw_, vec], [h["t1"]])
            b.op("dve", lambda e, h=h: e.tensor_scalar(out=h["lw"][:], in0=h["t1"][:], scalar1=-0.6065306597126334, scalar2=None, op0=ALU.mult), [h["t1"]], [h["lw"]])
            pa_ = wide()
            b.mm(pa_[:], AUP[:, hp, :], lp[7][:], True, True, [AUP, lp[7]], [pa_])
            b.op("act", lambda e, pa_=pa_, h=h, vv=vv: e.activation(out=h["a"][:], in_=pa_[:], func=AF.Sigmoid, bias=vv(V_A0)), [pa_, vec], [h["a"]])
            pg_ = wide()
            b.mm(pg_[:], GUP[:, hp, :], twg[:], True, True, [GUP, twg], [pg_])
            b.op("act", lambda e, pg_=pg_, h=h: e.activation(out=h["g"][ti % 2][:], in_=pg_[:], func=AF.Copy), [pg_], [h["g"][ti % 2]])
            yield
            b.op("dve", lambda e, h=h, kp=kp, vv=vv: e.tensor_scalar(out=h["kk"][:], in0=kp[:], scalar1=vv(V_KK), scalar2=None, op0=ALU.mult), [kp, vec], [h["kk"]])
            b.op("act", lambda e, h=h: e.activation(out=h["t1"][:], in_=h["kk"][:], func=AF.Square), [h["kk"]], [h["t1"]])
            pn_ = wide()
            b.mm(pn_[:], bones[:], h["t1"][:], True, True, [bones, h["t1"]], [pn_])
            b.op("dve", lambda e, pn_=pn_, h=h: e.tensor_scalar(out=h["t2"][:], in0=pn_[:], scalar1=1e-24, scalar2=None, op0=ALU.max), [pn_], [h["t2"]])
            b.op("act", lambda e, h=h: e.activation(out=h["t2"][:], in_=h["t2"][:], func=AF.Sqrt), [h["t2"]], [h["t2"]])
            b.op("dve", lambda e, h=h: e.reciprocal(out=h["t2"][:], in_=h["t2"][:]), [h["t2"]], [h["t2"]])
            b.op("dve", lambda e, h=h: e.tensor_tensor(out=h["kkn"][:], in0=h["kk"][:], in1=h["t2"][:], op=ALU.mult), [h["kk"], h["t2"]], [h["kkn"]])
            yield
            b.op("dve", lambda e, h=h, vv=vv: e.tensor_scalar(out=h["t1"][:], in0=h["a"][:], scalar1=-1.0, scalar2=vv(V_KA), op0=ALU.add, op1=ALU.mult), [h["a"], vec], [h["t1"]])
            b.op("dve", lambda e, h=h, kp=kp: e.scalar_tensor_tensor(out=h["kpp"][:], in0=h["t1"][:], scalar=1.0, in1=kp[:], op0=ALU.add, op1=ALU.mult), [h["t1"], kp], [h["kpp"]])
            b.op("dve", lambda e, h=h: e.tensor_tensor(out=h["akk"][:], in0=h["kkn"][:], in1=h["a"][:], op=ALU.mult), [h["kkn"], h["a"]], [h["akk"]])
            yield
            b.op("dve", lambda e, h=h, rp=rp, vv=vv: e.scalar_tensor_tensor(out=h["t1"][:], in0=rp[:], scalar=vv(V_RK), in1=h["kpp"][:], op0=ALU.mult, op1=ALU.mult), [rp, vec, h["kpp"]], [h["t1"]])
            pb_ = wide()
            b.mm(pb_[:], bones[:], h["t1"][:], True, True, [bones, h["t1"]], [pb_])
            b.op("dve", lambda e, pb_=pb_, h=h, vp=vp: e.tensor_tensor(out=h["bonus"][ti % 2][:], in0=pb_[:], in1=vp[:], op=ALU.mult), [pb_, vp], [h["bonus"][ti % 2]])
            yield
            b.op("dve", lambda e, h=h: e.tensor_tensor_scan(out=h["cum"][:], data0=rmask[:], data1=h["lw"][:], initial=0.0, op0=ALU.mult, op1=ALU.add), [rmask, h["lw"]], [h["cum"]])
            b.op("dve", lambda e, h=h: e.tensor_tensor(out=h["cexc"][:], in0=h["cum"][:], in1=h["lw"][:], op=ALU.subtract), [h["cum"], h["lw"]], [h["cexc"]])
            for c in range(NCH):
                cs = slice(c * 64, (c + 1) * 64)
                b.op("pool" if c % 2 else "dve", lambda e, h=h, cs=cs, c=c: e.tensor_scalar(out=h["dW"][:, cs], in0=h["cum"][:, cs], scalar1=-1.0,
                                                                 scalar2=h["cum"][:, c * 64 + 63:c * 64 + 64], op0=ALU.mult, op1=ALU.add), [h["cum"]], [(h["dW"], c)])
            yield
            b.op("act", lambda e, h=h: e.activation(out=h["epos"][:], in_=h["cum"][:], func=AF.Exp), [h["cum"]], [h["epos"]])
            b.op("act", lambda e, h=h: e.activation(out=h["eneg"][:], in_=h["cum"][:], func=AF.Exp, scale=-1.0), [h["cum"]], [h["eneg"]])
            b.op("act", lambda e, h=h: e.activation(out=h["eexc"][:], in_=h["cexc"][:], func=AF.Exp), [h["cexc"]], [h["eexc"]])
            b.op("act", lambda e, h=h: e.activation(out=h["eW"][:], in_=h["dW"][:], func=AF.Exp), [h["dW"]], [h["eW"]])
            b.op("pool", lambda e, h=h: e.tensor_copy(out=h["wC"][ti % 2][:], in_=h["epos"][:, 63::64]), [h["epos"]], [h["wC"][ti % 2]])
            yield
            b.op("dve", lambda e, h=h: e.tensor_tensor(out=h["QR"][:, :, 0, :], in0=v3(h["kkn"], slice(0, 128)), in1=v3(h["eexc"], slice(0, 128)), op=ALU.mult), [h["kkn"], h["eexc"]], [(h["QR"], 0)])
            b.op("pool", lambda e, h=h, rp=rp: e.tensor_tensor(out=h["QR"][:, :, 1, :], in0=v3(rp, slice(0, 128)), in1=v3(h["epos"], slice(0, 128)), op=ALU.mult), [rp, h["epos"]], [(h["QR"], 1)])
            b.op("dve", lambda e, h=h: e.tensor_tensor(out=h["akkh"][:], in0=h["akk"][:], in1=h["eneg"][:], op=ALU.mult), [h["akk"], h["eneg"]], [h["akkh"]])
            yield

    def stage2(ti):
        for hp in range(2):
            h = H[hp]
            vp = lp[4 + hp]
            for hh in range(2):
                R = slice(64 * hh, 64 * hh + 64)
                eng = "pool" if hh else "dve"
                b.op(eng, lambda e, h=h, R=R: e.tensor_copy(out=h["QRbd"][R, :, 0, R].bitcast(F32R), in_=h["QR"][R, :, 0, :]), [(h["QR"], 0)], [(h["QRbd"], hh)])
                b.op(eng, lambda e, h=h, R=R: e.tensor_copy(out=h["QRbd"][R, :, 1, R].bitcast(F32R), in_=h["QR"][R, :, 1, :]), [(h["QR"], 1)], [(h["QRbd"], hh)])
                b.op(eng, lambda e, h=h, R=R: e.tensor_copy(out=h["AKKbd"][R, :, R].bitcast(F32R), in_=v3(h["akkh"], R)), [h["akkh"]], [(h["AKKbd"], hh)])
                b.op(eng, lambda e, h=h, R=R: e.tensor_tensor(out=h["KHbd"][R, :, R].bitcast(F32R), in0=v3(h["kpp"], R), in1=v3(h["eneg"], R), op=ALU.mult), [h["kpp"], h["eneg"]], [(h["KHbd"], hh)])
                b.op(eng, lambda e, h=h, R=R: e.tensor_tensor(out=h["AKKWbd"][R, :, R], in0=v3(h["akk"], R), in1=v3(h["eW"], R), op=ALU.mult), [h["akk"], h["eW"]], [(h["AKKWbd"], hh)])
                b.op(eng, lambda e, h=h, R=R: e.tensor_tensor(out=h["KWbd"][R, :, R], in0=v3(h["kpp"], R), in1=v3(h["eW"], R), op=ALU.mult), [h["kpp"], h["eW"]], [(h["KWbd"], hh)])
                b.op(eng, lambda e, h=h, R=R, vp=vp: e.tensor_copy(out=h["Vbd"][R, :, R], in_=v3(vp, R)), [vp], [(h["Vbd"], hh)])


    NTL = ntiles or (S // TT_)
    run_rr([stage1_gen(0)])
    for ti in range(NTL):
        t0 = ti * TT_
        stage2(ti)
        bg = stage1_gen(ti + 1) if ti + 1 < NTL else None

        def prep_gen(hp, c):
            h = H[hp]
            par = c % 2
            LP, AKWT, Vst, TT = h["LP"][par], h["AKWT"][par], h["Vst"][par], h["TT"][par]
            bX, bY = psT2[hp], psN2[hp]
            MY, NN = h["MY"], h["N"]
            qr = h["QRbd"][:, c, :, :].rearrange("p a t -> p (a t)")
            b.tr(bX[:, 0:128], h["AKKWbd"][:, c, :], ident[:], [h["AKKWbd"], ident], [bX])
            b.tr(bX[:, 128:256], h["KWbd"][:, c, :], ident[:], [h["KWbd"], ident], [bX])
            b.tr(bX[:, 256:384], h["Vbd"][:, c, :], ident[:], [h["Vbd"], ident], [bX])
            mmr(b, bX[:, 384:512], h["QRbd"][:, c, 0, :], h["AKKbd"][:, c, :], True, True, [h["QRbd"], h["AKKbd"]], [bX])
            mmr(b, bY[:, 0:256], h["AKKbd"][:, c, :], qr, True, True, [h["AKKbd"], h["QRbd"]], [bY])
            mmr(b, bY[:, 256:512], h["KHbd"][:, c, :], qr, True, True, [h["KHbd"], h["QRbd"]], [bY])
            yield
            b.op("dve", lambda e: e.tensor_tensor(out=NN[0][:].bitcast(F32R), in0=bX[:, 384:512], in1=Lamask[:], op=ALU.mult), [bX, Lamask], [NN[0]])
            b.op("act", lambda e: e.activation(out=AKWT[:].bitcast(F32R), in_=bX[:, 0:256], func=AF.Copy), [bX], [AKWT])
            b.op("act", lambda e: e.activation(out=Vst[0:64, :].bitcast(F32R), in_=bX[0:64, 256:320], func=AF.Copy), [bX], [(Vst, 0)])
            b.op("act", lambda e: e.activation(out=Vst[64:128, :].bitcast(F32R), in_=bX[64:128, 320:384], func=AF.Copy), [bX], [(Vst, 1)])
            b.op("dve", lambda e: e.tensor_tensor(out=LP[:].bitcast(F32R), in0=bY[:], in1=LPmask[:], op=ALU.mult), [bY, LPmask], [LP])
            b.op("pool", lambda e: e.tensor_tensor(out=MY[1][:, 128:256].bitcast(F32R), in0=LP[:, 0:128], in1=ident[:], op=ALU.add), [LP, ident], [(MY[1], 1)])
            yield
            mmr(b, bX[:, 0:128], NN[0][:], LP[:, 0:128], True, True, [NN[0], LP], [bX])
            mmr(b, bX[:, 256:384], LP[:, 0:128], NN[0][:], True, True, [LP, NN[0]], [bX])
            yield
            b.op("act", lambda e: e.activation(out=MY[1][:, 0:128].bitcast(F32R), in_=bX[:, 0:128], func=AF.Copy), [bX], [(MY[1], 0)])
            b.op("dve", lambda e: e.tensor_copy(out=NN[1][:].bitcast(F32R), in_=bX[:, 256:384]), [bX], [NN[1]])
            yield
            for k in range(1, 6):
                bank = bY if k % 2 else bX
                i, o = k % 2, (k + 1) % 2
                if k < 5:
                    mmr(b, bank[:, 0:256], NN[i][:], MY[i][:, 0:256], True, True, [NN[i], MY[i]], [bank])
                    mmr(b, bank[:, 256:384], MY[i][:, 0:128], NN[i][:], True, True, [MY[i], NN[i]], [bank])
                    yield
                    b.op("dve", lambda e, bank=bank, i=i, o=o: e.tensor_tensor(out=MY[o][:, 128:256].bitcast(F32R), in0=bank[:, 128:256], in1=MY[i][:, 128:256], op=ALU.add),
                         [bank, (MY[i], 1)], [(MY[o], 1)])
                    b.op("act", lambda e, bank=bank, o=o: e.activation(out=MY[o][:, 0:128].bitcast(F32R), in_=bank[:, 0:128], func=AF.Copy), [bank], [(MY[o], 0)])
                    b.op("act", lambda e, bank=bank, o=o: e.activation(out=NN[o][:].bitcast(F32R), in_=bank[:, 256:384], func=AF.Copy), [bank], [NN[o]])
                    yield
                else:
                    mmr(b, bank[:, 0:128], NN[i][:], MY[i][:, 128:256], True, True, [NN[i], MY[i]], [bank])
                    yield
                    b.op("dve", lambda e, bank=bank, i=i: e.tensor_tensor(out=TT[:].bitcast(F32R), in0=bank[:, 0:128], in1=MY[i][:, 128:256], op=ALU.add),
                         [bank, (MY[i], 1)], [TT])
                    yield

        def seq_gen(hp, c):
            h = H[hp]
            par = c % 2
            LP, AKWT, Vst, TT = h["LP"][par], h["AKWT"][par], h["Vst"][par], h["TT"][par]
            Sst, Xst, Ust = h["Sst"], h["Xst"], h["Ust"]
            so = 0
            psS = psS2[hp]
            sb_ = lambda k: psS
            X_, U_, Y_, S_ = (psS[:, so + 64 * k:so + 64 * k + 64] for k in range(4))
            mmr(b, X_, LP[:, 256:384], Vst[:], True, False, [LP, Vst], [sb_(0)])
            mmr(b, X_, h["QRbd"][:, c, 0, :], Sst[:], False, True, [h["QRbd"], Sst], [sb_(0)])
            yield
            b.op("dve", lambda e: e.tensor_copy(out=Xst[:].bitcast(F32R), in_=X_), [sb_(0)], [Xst])
            yield
            mmr(b, U_, TT[:], Xst[:], True, True, [TT, Xst], [sb_(1)])
            yield
            b.op("dve", lambda e: e.tensor_scalar(out=Ust[:].bitcast(F32R), in0=U_, scalar1=-1.0, scalar2=None, op0=ALU.mult), [sb_(1)], [Ust])
            yield
            mmr(b, Y_, h["QRbd"][:, c, 1, :], Sst[:], True, False, [h["QRbd"], Sst], [sb_(2)])
            mmr(b, Y_, LP[:, 128:256], Ust[:], False, False, [LP, Ust], [sb_(2)])
            mmr(b, Y_, LP[:, 384:512], Vst[:], False, True, [LP, Vst], [sb_(2)])
            mmr(b, S_, AKWT[:, 128:256], Vst[:], True, False, [AKWT, Vst], [sb_(3)])
            mmr(b, S_, AKWT[:, 0:128], Ust[:], False, True, [AKWT, Ust], [sb_(3)])
            yield
            wCt = h["wC"][ti % 2]
            b.op("dve", lambda e: e.scalar_tensor_tensor(out=Sst[:].bitcast(F32R), in0=Sst[:], scalar=wCt[:, c:c + 1], in1=S_, op0=ALU.mult, op1=ALU.add),
                 [Sst, wCt, sb_(3)], [Sst])
            b.op("act", lambda e: e.activation(out=h["Ybd"][0:64, c, 0:64], in_=psS[0:64, so + 128:so + 192], func=AF.Copy), [sb_(2)], [(h["Ybd"], c)])
            b.op("act", lambda e: e.activation(out=h["Ybd"][64:128, c, 64:128], in_=psS[64:128, so + 128:so + 192], func=AF.Copy), [sb_(2)], [(h["Ybd"], c)])
            yield

        import os as _os
        STOP = int(_os.environ.get("RW_STOP", "9"))
        if STOP <= 1:
            continue
        for c in range(NCH + 1):
            gens = []
            if c < NCH:
                import itertools as _it
                PST = int(_os.environ.get("P_STOP", "999"))
                g0 = _it.islice(prep_gen(0, c), PST)
                g1 = _it.islice(prep_gen(1, c), PST)
                next(g0, None)
                gens += [g0, g1]
            if c >= 1 and STOP >= 3:
                s0 = seq_gen(0, c - 1)
                s1 = seq_gen(1, c - 1)
                next(s0, None)
                gens += [s0, s1]
            run_rr(gens, bg)
        if bg is not None:
            for _ in bg:
                pass
        if STOP <= 3:
            continue

        for hp in range(2):
            h = H[hp]
            vv = lambda i, hp=hp: vec[:, hp, i:i + 1]
            py = wide()
            for c in range(NCH):
                b.mm(py[:, c * 64:(c + 1) * 64], h["Ybd"][:, c, :], sel2[:], True, True, [(h["Ybd"], c), sel2], [py])
            b.op("act", lambda e, py=py, h=h: e.activation(out=h["YT"][:], in_=py[:], func=AF.Copy), [py], [h["YT"]])
            if ti == 0:
                dump(f"YT{hp}", h["YT"], h["YT"][:], [128, TT_])
                dump(f"Sst{hp}", h["Sst"], h["Sst"][:], [128, 64])
                dump(f"TT{hp}", h["TT"][1], h["TT"][1][:], [128, 128])
                dump(f"LP{hp}", h["LP"][1], h["LP"][1][:], [128, 512])
            pm = wide()
            b.mm(pm[:], bones64[:], h["YT"][:], True, True, [bones64, h["YT"]], [pm])
            b.op("dve", lambda e, pm=pm, h=h: e.tensor_tensor(out=h["yc"][:], in0=h["YT"][:], in1=pm[:], op=ALU.subtract), [h["YT"], pm], [h["yc"]])
            b.op("act", lambda e, h=h: e.activation(out=h["sq"][:], in_=h["yc"][:], func=AF.Square), [h["yc"]], [h["sq"]])
            pv = wide()
            b.mm(pv[:], bones64[:], h["sq"][:], True, True, [bones64, h["sq"]], [pv])
            b.op("act", lambda e, pv=pv, h=h: e.activation(out=h["rs"][:], in_=pv[:], func=AF.Sqrt, bias=64e-5), [pv], [h["rs"]])
            b.op("dve", lambda e, h=h: e.reciprocal(out=h["rs"][:], in_=h["rs"][:]), [h["rs"]], [h["rs"]])
            b.op("dve", lambda e, h=h: e.tensor_tensor(out=h["yc"][:], in0=h["yc"][:], in1=h["rs"][:], op=ALU.mult), [h["yc"], h["rs"]], [h["yc"]])
            b.op("dve", lambda e, h=h, vv=vv: e.tensor_scalar(out=h["yc"][:], in0=h["yc"][:], scalar1=vv(V_LNW), scalar2=vv(V_LNB), op0=ALU.mult, op1=ALU.add), [h["yc"], vec], [h["yc"]])
            bon, gg = h["bonus"][ti % 2], h["g"][ti % 2]
            b.op("dve", lambda e, h=h, bon=bon: e.tensor_tensor(out=h["yc"][:], in0=h["yc"][:], in1=bon[:], op=ALU.add), [h["yc"], bon], [h["yc"]])
            b.op("dve", lambda e, h=h, gg=gg: e.tensor_tensor(out=h["ob"][:], in0=h["yc"][:], in1=gg[:], op=ALU.mult), [h["yc"], gg], [h["ob"]])
            b.dma(yT[2 + hp, :, t0:t0 + TT_], h["ob"][:], [h["ob"]], [(yT, (2 + hp, ti))])
    b.end()


NBIG = 30000.0


def nsa_phase(b, pT, yT, d, Cn, ngroups=None, debug=False):
    b.begin()
    DBGN = []

    def dump(name, buf, ap, shape, dt=F32):
        if not debug:
            return
        dd = b.dram("dbg_" + name, list(shape), dt, "ExternalOutput")
        b.dma(dd[:], ap, [buf], [dd])

    def ld(nm, src, shp):
        t = b.sb(nm, shp, F32)
        b.dma(t[:], src[:], [src], [t])
        return t

    def ldc(nm, src_ap, srcbuf, shp):
        t = b.sb(nm, shp, BF16)
        b.dma(t[:], src_ap, [srcbuf], [t], q="pool", max_dma_last_dim=4096)
        return t
    ident = ld("ident", Cn["ident"], [128, 128])
    identb = ldc("identb", Cn["ident"][:], Cn["ident"], [128, 128])
    bones = ld("bones", Cn["bones"], [128, 128])
    eall = b.sb("eall", [128, S], BF16)
    b.op("pool", lambda e: e.memset(eall[64:128, :], 0.0), [], [eall])
    b.dma(eall[0:64, :], Cn["n_eall"][:], [Cn["n_eall"]], [eall], q="pool", max_dma_last_dim=4096)
    cmsel = b.sb("cmsel", [128, 4, 512], BF16)
    for r in range(4):
        b.dma(cmsel[:, r, :], Cn["n_cmsel"][r], [Cn["n_cmsel"]], [cmsel], q="pool", max_dma_last_dim=4096)
    cmwin = b.sb("cmwin", [128, 8, 512], BF16)
    for r in range(8):
        b.dma(cmwin[:, r, :], Cn["n_cmwin"][r], [Cn["n_cmwin"]], [cmwin], q="pool", max_dma_last_dim=4096)
    negc = b.sb("negc", [128, 5, 512], BF16)
    for r in range(5):
        b.dma(negc[:, r, :], Cn["n_negc"][r], [Cn["n_negc"]], [negc], q="pool", max_dma_last_dim=4096)
    qnw = ld("qnw", d["ns_qnw"], [128, 1])
    knw = ld("knw", d["ns_knw"], [128, 3])
    w2 = ld("w2", d["ns_w2"], [128, 2, 64])
    pos2 = ld("pos2", d["ns_pos2"], [128, 32, 2])
    QA = [b.sb(f"QA{h}", [128, S], BF16) for h in range(4)]
    KS = b.sb("KS", [128, S], BF16)
    KW = b.sb("KW", [128, S], BF16)
    KC = b.sb("KC", [128, 256], BF16)
    NEGM = b.sb("NEGM", [128, S], BF16)
    b.op("pool", lambda e: e.memset(NEGM[64:128, :], 0.0), [], [NEGM])
    Vs = b.sb("Vs", [128, 32, 65], BF16)
    Vw = b.sb("Vw", [128, 32, 65], BF16)
    VC = b.sb("VC", [128, 2, 129], BF16)
    Gtm = b.sb("Gtm", [128, 32, 32], F32)
    rawA = b.sb("rawA", [128, S], F32)
    rawB = b.sb("rawB", [128, S], F32)
    tF = [b.sb(f"tF{i}", [128, 512], F32) for i in range(3)]
    psc = [b.ps("psc", [128, 512]) for _ in range(4)]
    pacc = [b.ps("pacc", [128, 512]) for _ in range(2)]
    pcA = pacc[0]
    pcB = pacc[1]
    pm = [b.ps("pm", [128, 512]) for _ in range(2)]
    mcnt = [0]

    def misc():
        mcnt[0] += 1
        return pm[mcnt[0] % 2]

    rT = [b.sb(f"rT{i}", [64, 512], F32) for i in range(6)]
    rcnt = [0]

    def rms_rows(raw, out, wcol, NC=S, width=512):
        for c0 in range(0, NC, width):
            wd = min(width, NC - c0)
            cs = slice(c0, c0 + wd)
            rcnt[0] += 1
            t0, t1 = rT[(rcnt[0] % 3) * 2], rT[(rcnt[0] % 3) * 2 + 1]
            b.op("act", lambda e, cs=cs, wd=wd, t0=t0: e.activation(out=t0[0:64, 0:wd], in_=raw[0:64, cs], func=AF.Square), [raw], [t0])
            P = misc()
            b.mm(P[0:64, 0:wd], bones[0:64, 0:64], t0[0:64, 0:wd], True, True, [bones, t0], [P])
            b.op("act", lambda e, P=P, wd=wd, t1=t1: e.activation(out=t1[0:64, 0:wd], in_=P[0:64, 0:wd], func=AF.Sqrt, scale=1.0 / 64, bias=EPS), [P], [t1])
            b.op("dve", lambda e, wd=wd, t1=t1: e.reciprocal(out=t1[0:64, 0:wd], in_=t1[0:64, 0:wd]), [t1], [t1])
            b.op("dve", lambda e, cs=cs, wd=wd, t1=t1: e.scalar_tensor_tensor(out=out[0:64, cs], in0=raw[0:64, cs], scalar=wcol, in1=t1[0:64, 0:wd],
                                                                    op0=ALU.mult, op1=ALU.mult), [raw, t1, qnw, knw], [(out, c0)])

    for h in range(4):
        b.dma(rawA[:], pT[BLK_Q[h]], [pT], [rawA])
        rms_rows(rawA, QA[h], qnw[0:64, 0:1])
        b.dma(QA[h][64:68, :], Cn["n_qaug"][h], [Cn["n_qaug"]], [QA[h]], q="pool", max_dma_last_dim=4096)
    for (blk, Kt, Vt, wi) in ((BLK_KVS, KS, Vs, 1), (BLK_KVW, KW, Vw, 2)):
        b.dma(rawA[:], pT[blk], [pT], [rawA])
        rms_rows(rawA, Kt, knw[0:64, wi:wi + 1])
        b.dma(Kt[64:68, :], Cn["n_kaug"][:], [Cn["n_kaug"]], [Kt], q="pool", max_dma_last_dim=4096)
        b.op("pool", lambda e, Vt=Vt: e.memset(Vt[:, :, 64:65], 1.0), [], [Vt])
        for g4 in range(8):
            P = misc()
            for i in range(4):
                kt = g4 * 4 + i
                b.tr(P[:, i * 128:(i + 1) * 128], rawA[:, kt * 128:(kt + 1) * 128], ident[:], [rawA, ident], [P])
            b.op("dve", lambda e, P=P, Vt=Vt, g4=g4: e.tensor_copy(out=Vt[:, g4 * 4:g4 * 4 + 4, 0:64],
                                                                  in_=P[:].rearrange("p (i c) -> p i c", c=128)[:, :, 64:128]), [P], [Vt])
    b.dma(rawA[:], pT[BLK_GT], [pT], [rawA])
    b.op("act", lambda e: e.activation(out=rawA[0:32, :], in_=rawA[0:32, :], func=AF.Sigmoid), [rawA], [rawA])
    for g16 in range(2):
        P = misc()
        for i in range(16):
            kt = g16 * 16 + i
            b.tr(P[:, i * 32:(i + 1) * 32], rawA[0:32, kt * 128:(kt + 1) * 128], ident[0:32, 0:32], [rawA, ident], [P])
        b.op("dve", lambda e, P=P, g16=g16: e.tensor_copy(out=Gtm[:, g16 * 16:(g16 + 1) * 16, :], in_=P[:].rearrange("p (i c) -> p i c", c=32)), [P], [Gtm])
    b.dma(rawA[:], pT[BLK_KVC], [pT], [rawA])
    b.dma(rawB[:], d["ns_w1"][:].rearrange("p l m -> p (l m)"), [d["ns_w1"]], [rawB])
    b.op("pool", lambda e: e.memset(KC[:], 0.0), [], [KC])
    b.op("pool", lambda e: e.memset(VC[:], 0.0), [], [VC])
    bias_sb = b.sb("bias_sb", [128, 4], F32)
    for half in range(2):
        R = slice(64 * half, 64 * half + 64)
        Pb = misc()
        for l in range(32):
            b.mm(Pb[:, 0:2], rawB[R, l * 128:(l + 1) * 128], pos2[R, l, :], l == 0, l == 31, [rawB, pos2], [Pb])
        b.op("dve", lambda e, Pb=Pb, half=half: e.tensor_copy(out=bias_sb[:, 2 * half:2 * half + 2], in_=Pb[:, 0:2]), [Pb], [bias_sb])
    gl = [b.sb(f"gl{i}", [128, 256], F32) for i in range(2)]
    for half in range(2):
        R = slice(64 * half, 64 * half + 64)
        Ph = misc()
        for l in range(32):
            b.mm(Ph[:, 0:255], rawB[R, l * 128:(l + 1) * 128], rawA[R, l:l + 4065:16], l == 0, l == 31, [rawB, rawA], [Ph])
        hb, h2 = tF[0], tF[1]
        b.op("act", lambda e, Ph=Ph, half=half: e.activation(out=hb[:, 0:255], in_=Ph[:, 0:255], func=AF.Identity, bias=bias_sb[:, 2 * half:2 * half + 1]), [Ph, bias_sb], [hb])
        b.op("act", lambda e: e.activation(out=h2[:, 0:255], in_=hb[:, 0:255], func=AF.Square), [hb], [h2])
        b.op("dve", lambda e: e.tensor_scalar(out=h2[:, 0:255], in0=h2[:, 0:255], scalar1=0.044715, scalar2=1.0, op0=ALU.mult, op1=ALU.add), [h2], [h2])
        b.op("dve", lambda e: e.tensor_tensor(out=h2[:, 0:255], in0=h2[:, 0:255], in1=hb[:, 0:255], op=ALU.mult), [h2, hb], [h2])
        b.op("act", lambda e: e.activation(out=h2[:, 0:255], in_=h2[:, 0:255], func=AF.Sigmoid, scale=1.5957691216057308), [h2], [h2])
        b.op("dve", lambda e, half=half: e.tensor_tensor(out=gl[half][:, 0:255], in0=h2[:, 0:255], in1=hb[:, 0:255], op=ALU.mult), [h2, hb], [gl[half]])
    Pk = misc()
    b.mm(Pk[0:64, 0:255], w2[:, 0, :], gl[0][:, 0:255], True, True, [w2, gl[0]], [Pk])
    kc_sb = b.sb("kc_sb", [128, 256], F32)
    b.op("dve", lambda e: e.tensor_copy(out=kc_sb[0:64, 0:255], in_=Pk[0:64, 0:255]), [Pk], [kc_sb])
    rms_rows(kc_sb, KC, knw[0:64, 0:1], NC=255, width=255)
    b.dma(KC[64:68, :], Cn["n_kcaug"][:], [Cn["n_kcaug"]], [KC], q="pool")
    for nt in range(2):
        ncols = 128 if nt == 0 else 127
        Pv = misc()
        b.mm(Pv[0:ncols, 0:64], gl[1][:, nt * 128:nt * 128 + ncols], w2[:, 1, :], True, True, [gl[1], w2], [Pv])
        b.op("dve", lambda e, Pv=Pv, nt=nt, ncols=ncols: e.tensor_copy(out=VC[0:ncols, nt, 0:64], in_=Pv[0:ncols, 0:64]), [Pv], [VC])
        b.dma(VC[:, nt, 64:129], Cn["n_ovl1"][:, nt, :], [Cn["n_ovl1"]], [VC], q="pool")
    if debug:
        dump("gl0", gl[0], gl[0][:, 0:255], [128, 255])
        dump("gl1", gl[1], gl[1][:, 0:255], [128, 255])
        dump("kc_sb", kc_sb, kc_sb[0:64, 0:255], [64, 255])
        dump("bias_sb", bias_sb, bias_sb[:], [128, 4])
        for h in range(4):
            dump(f"QA{h}", QA[h], QA[h][0:68, :], [68, S], BF16)
        dump("KS", KS, KS[0:68, :], [68, S], BF16)
        dump("KC", KC, KC[0:68, :], [68, 256], BF16)
        dump("VC", VC, VC[:], [128, 2, 129], BF16)
        dump("Vs", Vs, Vs[:], [128, 32, 65], BF16)
        dump("Gtm", Gtm, Gtm[:], [128, 32, 32])

    PC = [[b.sb(f"PC{h}_{nt}", [128, 512], BF16) for nt in range(2)] for h in range(4)]
    PT = [b.sb(f"PT{i}", [128, 512], BF16) for i in range(5)]
    OUT = b.sb("OUT", [128, 4, 256], F32)
    okt = b.sb("okt", [128, 4, 64], F32)
    addt = b.sb("addt", [128, 4, 64], F32)
    imp = b.sb("imp", [128, 64], F32)
    imp2 = b.sb("imp2", [128, 64], F32)
    imp3 = b.sb("imp3", [128, 64], F32)
    m8a = b.sb("m8a", [128, 8], F32)
    m8b = b.sb("m8b", [128, 8], F32)
    msk = b.sb("msk", [128, 4, 64], F32)
    l4 = b.sb("l4", [128, 4], F32)
    rg4 = b.sb("rg4", [128, 4], F32)
    rl4 = b.sb("rl4", [128, 4], F32)
    YC = [b.sb(f"YC{i}", [128, 512], BF16) for i in range(2)]
    scnt = [0]
    pcnt = [0]
    acnt = [0]

    for qg in range(ngroups or (S // 512)):
        qs = slice(qg * 512, (qg + 1) * 512)
        b.dma(okt[:], Cn["n_ok"][:, 4 * qg:4 * qg + 4, :], [Cn["n_ok"]], [okt])
        b.dma(addt[:], Cn["n_addc"][:, 4 * qg:4 * qg + 4, :], [Cn["n_addc"]], [addt])
        nts = [0] if qg < 4 else [0, 1]
        for h in range(4):
            for nt in nts:
                g = qg if nt == 0 else qg - 4
                masked = (g <= 4)
                sc = psc[scnt[0] % 4]
                scnt[0] += 1
                b.mm(sc[:], KC[0:68, nt * 128:(nt + 1) * 128], QA[h][0:68, qs], True, not masked, [KC, QA[h]], [sc])
                if masked:
                    b.mm(sc[:], identb[:], negc[:, g, :], False, True, [identb, negc], [sc])
                b.op("act", lambda e, sc=sc, h=h, nt=nt: e.activation(out=PC[h][nt][:], in_=sc[:], func=AF.Exp, scale=0.125), [sc], [PC[h][nt]])
        for s in range(4):
            tile = 4 * qg + s
            ss = slice(s * 128, (s + 1) * 128)
            for h in range(4):
                bank = pcA if h < 2 else pcB
                reg = slice((h % 2) * 129, (h % 2) * 129 + 129)
                for i, nt in enumerate(nts):
                    b.mm(bank[:, reg], PC[h][nt][:, ss], VC[:, nt, :], i == 0, i == len(nts) - 1, [PC[h][nt], VC], [bank])
            for bi, bank in enumerate((pcA, pcB)):
                b.op("dve", lambda e, bank=bank, bi=bi: e.tensor_scalar(out=l4[:, 2 * bi:2 * bi + 2],
                                                                       in0=bank[:, 0:258].rearrange("p (h c) -> p h c", c=129)[:, :, 64],
                                                                       scalar1=1e-30, scalar2=None, op0=ALU.max), [bank], [(l4, bi)])
            b.op("dve", lambda e: e.reciprocal(out=rl4[:], in_=l4[:]), [l4], [rl4])
            b.op("dve", lambda e, tile=tile: e.tensor_tensor(out=rg4[:], in0=rl4[:], in1=Gtm[:, tile, 0:12:3], op=ALU.mult), [rl4, Gtm], [rg4])
            for h in range(4):
                bank = pcA if h < 2 else pcB
                o0 = (h % 2) * 129
                b.op("dve", lambda e, bank=bank, o0=o0, h=h, s=s: e.tensor_scalar(out=OUT[:, s, h * 64:(h + 1) * 64], in0=bank[:, o0:o0 + 64],
                                                                             scalar1=rg4[:, h:h + 1], scalar2=None, op0=ALU.mult), [bank, rg4], [(OUT, s)])
                if h == 0:
                    b.op("dve", lambda e, bank=bank, o0=o0: e.tensor_scalar(out=imp[:], in0=bank[:, o0 + 65:o0 + 129], scalar1=rl4[:, 0:1], scalar2=None, op0=ALU.mult),
                         [bank, rl4], [imp])
                else:
                    b.op("dve", lambda e, bank=bank, o0=o0, h=h: e.scalar_tensor_tensor(out=imp[:], in0=bank[:, o0 + 65:o0 + 129], scalar=rl4[:, h:h + 1], in1=imp[:],
                                                                                   op0=ALU.mult, op1=ALU.add), [bank, rl4, imp], [imp])
            b.op("dve", lambda e, s=s: e.tensor_tensor(out=imp2[:], in0=imp[:], in1=okt[:, s, :], op=ALU.mult), [imp, okt], [imp2])
            b.op("dve", lambda e, s=s: e.tensor_tensor(out=imp2[:], in0=imp2[:], in1=addt[:, s, :], op=ALU.add), [imp2, addt], [imp2])
            b.op("dve", lambda e: e.max(out=m8a[:], in_=imp2[:]), [imp2], [m8a])
            b.op("dve", lambda e: e.match_replace(out=imp3[:], in_to_replace=m8a[:], in_values=imp2[:], imm_value=-1e30), [m8a, imp2], [imp3])
            b.op("dve", lambda e: e.max(out=m8b[:], in_=imp3[:]), [imp3], [m8b])
            b.op("dve", lambda e, s=s: e.tensor_scalar(out=msk[:, s, :], in0=imp2[:], scalar1=m8b[:, 7:8], scalar2=None, op0=ALU.is_ge), [imp2, m8b], [(msk, s)])
        if debug and qg == 0:
            dump("NEGM0", NEGM, NEGM[0:64, 0:512], [64, 512], BF16)
            dump("msk", msk, msk[:], [128, 4, 64])
            dump("OUTcmp", OUT, OUT[:], [128, 4, 256])
        for br, (Kt, Vt) in ((2, (KW, Vw)), (1, (KS, Vs))):
            if br == 1:
                Pm = misc()
                for s in range(4):
                    b.tr(Pm[0:64, s * 128:(s + 1) * 128], msk[:, s, :], ident[:], [(msk, s), ident], [Pm])
                b.op("dve", lambda e, Pm=Pm, qs=qs: e.tensor_scalar(out=NEGM[0:64, qs], in0=Pm[0:64, :], scalar1=-1.0, scalar2=NBIG, op0=ALU.add, op1=ALU.mult), [Pm], [(NEGM, qg)])
            for h in range(4):
                acc = pacc[acnt[0] % 2]
                acnt[0] += 1
                if br == 1:
                    kts = list(range(0, 4 * qg + 4))
                else:
                    kts = list(range(max(0, 4 * qg - 4), 4 * qg + 4))
                pv = []
                for kt in kts:
                    for s in range(4):
                        u = 4 * qg + s - kt
                        if u < 0 or (br == 2 and u > 4):
                            continue
                        pv.append((kt, s))
                npv = len(pv)
                pvset = set(pv)
                ipv = [0]

                def emit_sc(kt):
                    sc = psc[scnt[0] % 4]
                    scnt[0] += 1
                    ks = slice(kt * 128, (kt + 1) * 128)
                    b.mm(sc[:], Kt[0:68, ks], QA[h][0:68, qs], True, False, [Kt, QA[h]], [sc])
                    if br == 1:
                        diag = kt >= 4 * qg
                        b.mm(sc[:], eall[:, ks], NEGM[:, qs], False, not diag, [eall, (NEGM, qg)], [sc])
                        if diag:
                            b.mm(sc[:], identb[:], cmsel[:, kt - 4 * qg, :], False, True, [identb, cmsel], [sc])
                    else:
                        b.mm(sc[:], identb[:], cmwin[:, kt - 4 * qg + 4, :], False, True, [identb, cmwin], [sc])
                    P_ = PT[pcnt[0] % 5]
                    pcnt[0] += 1
                    b.op("act", lambda e, sc=sc, P_=P_: e.activation(out=P_[:], in_=sc[:], func=AF.Exp, scale=0.125), [sc], [P_])
                    return P_

                def emit_pv(kt, P_):
                    for s in range(4):
                        if (kt, s) not in pvset:
                            continue
                        b.mm(acc[:, s * 65:(s + 1) * 65], P_[:, s * 128:(s + 1) * 128], Vt[:, kt, :], ipv[0] == 0, ipv[0] == npv - 1, [P_, Vt], [acc])
                        ipv[0] += 1
                LA = 3
                pend = []
                nk = len(kts)
                for i in range(min(LA, nk)):
                    pend.append(emit_sc(kts[i]))
                for i in range(nk):
                    if i + LA < nk:
                        pend.append(emit_sc(kts[i + LA]))
                    emit_pv(kts[i], pend.pop(0))
                if debug and qg == (ngroups or 8) - 1:
                    b.op("dve", lambda e, acc=acc: e.tensor_copy(out=tF[2][:, 0:260], in_=acc[:, 0:260]), [acc], [tF[2]])
                    dump(f"acc_br{br}_h{h}", tF[2], tF[2][:, 0:260], [128, 260])
                accv = acc[:, 0:260].rearrange("p (s c) -> p s c", c=65)
                b.op("dve", lambda e, accv=accv: e.reciprocal(out=rl4[:], in_=accv[:, :, 64]), [acc], [rl4])
                b.op("dve", lambda e, h=h, br=br, qg=qg: e.tensor_tensor(out=rg4[:], in0=rl4[:], in1=Gtm[:, 4 * qg:4 * qg + 4, h * 3 + br], op=ALU.mult), [rl4, Gtm], [rg4])
                for s in range(4):
                    b.op("dve", lambda e, acc=acc, s=s, h=h: e.scalar_tensor_tensor(out=OUT[:, s, h * 64:(h + 1) * 64], in0=acc[:, s * 65:s * 65 + 64], scalar=rg4[:, s:s + 1],
                                                                              in1=OUT[:, s, h * 64:(h + 1) * 64], op0=ALU.mult, op1=ALU.add), [acc, rg4, (OUT, s)], [(OUT, s)])
        if debug and qg == 0:
            dump("OUTfin", OUT, OUT[:], [128, 4, 256])
        for hp in range(2):
            Py = misc()
            for s in range(4):
                b.tr(Py[:, s * 128:(s + 1) * 128], OUT[:, s, hp * 128:(hp + 1) * 128], ident[:], [(OUT, s), ident], [Py])
            b.op("dve" if hp else "act", (lambda e, Py=Py, hp=hp: e.tensor_copy(out=YC[hp][:], in_=Py[:])) if hp else
                 (lambda e, Py=Py, hp=hp: e.activation(out=YC[hp][:], in_=Py[:], func=AF.Copy)), [Py], [YC[hp]])
            b.dma(yT[4 + hp, :, qs], YC[hp][:], [YC[hp]], [(yT, (4 + hp, qg))])
            if debug and qg == 0:
                dump(f"YC{hp}", YC[hp], YC[hp][:], [128, 512], BF16)
    b.end()


import numpy as np

S = 4096


def lay_in(w):
    F = w.shape[1]
    return np.ascontiguousarray(w.reshape(8, 128, F // 128, 128).transpose(2, 1, 0, 3))


def lay_dn(w):
    return np.ascontiguousarray(w.reshape(22, 128, 8, 128).transpose(2, 1, 0, 3))


def lay_vec(v):
    return np.ascontiguousarray(v.reshape(-1, 128).T)


def win_padded(w_in):
    out = np.zeros((1024, 24 * 128), np.float32)
    def put(blk, off, c0, n):
        out[:, blk * 128 + off: blk * 128 + off + n] = w_in[:, c0:c0 + n]
    put(0, 0, 0, 128); put(1, 0, 128, 128)
    B = 256
    put(2, 0, B, 128); put(3, 0, B + 128, 128)
    put(4, 0, B + 256, 128); put(5, 0, B + 384, 128)
    put(6, 0, B + 512, 128); put(7, 0, B + 640, 128)
    put(8, 0, B + 768, 64)
    put(8, 64, B + 768 + 64 + 32, 64)
    put(9, 0, B + 768 + 64, 32)
    C = 256 + 928
    for hh in range(4):
        put(10 + hh, 0, C + 64 * hh, 64)
    put(14, 0, C + 256, 128)
    put(15, 0, C + 384, 128)
    put(16, 0, C + 512, 128)
    put(17, 0, C + 640, 12)
    D = 256 + 928 + 652
    for i in range(6):
        put(18 + i, 0, D + 128 * i, 128)
    return out


def prep_layer(inp, l):
    d = {}
    for nm, key in (("f1", "ffn1"), ("f2", "ffn2")):
        d[f"{nm}_g"] = lay_vec(inp[f"{key}_norm"][l])
        d[f"{nm}_wg"] = lay_in(inp[f"{key}_w_gate"][l])
        d[f"{nm}_wu"] = lay_in(inp[f"{key}_w_up"][l])
        d[f"{nm}_wd"] = lay_dn(inp[f"{key}_w_down"][l])
    d["mix_g"] = lay_vec(inp["mix_norm"][l])
    d["win"] = lay_in(win_padded(inp["w_in"][l]))
    d["wout"] = np.ascontiguousarray(inp["w_out"][l].reshape(8, 128, 8, 128).transpose(2, 1, 0, 3))
    d["cw"] = np.ascontiguousarray(inp["conv_w"][l].reshape(3, 2, 128).transpose(2, 1, 0))
    pw = inp["pool_w"][l]
    pwbd = np.zeros((2, 128, 128), np.float32)
    for j in range(2):
        for gg in range(2):
            pwbd[j, gg * 64:(gg + 1) * 64, gg * 64:(gg + 1) * 64] = pw[2 * j + gg]
    d["pw"] = pwbd
    d["psc"] = lay_vec(inp["pool_scale"][l])
    prep_rwkv(inp, l, d)
    prep_nsa(inp, l, d)
    return d


def consts():
    c = {}
    wins = np.array([2, 4, 8, 16], np.float32)
    w_p = np.zeros((128, 2), np.float32)
    for j in range(2):
        w_p[0:64, j] = wins[2 * j]
        w_p[64:128, j] = wins[2 * j + 1]
    c["pinv"] = (1.0 / w_p).astype(np.float32)
    t = np.arange(16, dtype=np.float32)
    c["pcorr"] = (1.0 / np.minimum(t[None, None, :] + 1, w_p[:, :, None])).astype(np.float32)
    consts_rwkv(c)
    consts_nsa(c)
    return c


def prep_rwkv(inp, l, d):
    mu = inp["rwkv_mu"][l]
    mu8 = np.zeros((128, 8), np.float32)
    for i in range(6):
        mu8[:, i] = mu[i * 128:(i + 1) * 128]
    mu8[0:64, 6] = mu[768:832]
    mu8[64:128, 6] = mu[864:928]
    mu8[0:32, 7] = mu[832:864]
    d["rw_mu8"] = mu8
    vec = np.zeros((128, 2, 7), np.float32)
    names = ["rwkv_w0", "rwkv_a0", "rwkv_k_k", "rwkv_k_a", "rwkv_r_k", "rwkv_ln_w", "rwkv_ln_b"]
    for i, nm in enumerate(names):
        v = inp[nm][l].reshape(256)
        vec[:, 0, i] = v[0:128]
        vec[:, 1, i] = v[128:256]
    d["rw_vec"] = vec
    wup = np.zeros((128, 2, 128), np.float32)
    gup = np.zeros((128, 2, 128), np.float32)
    aup = np.zeros((128, 2, 128), np.float32)
    for hp in range(2):
        wup[0:64, hp, :] = inp["rwkv_w_up"][l][:, hp * 128:(hp + 1) * 128]
        gup[64:128, hp, :] = inp["rwkv_g_up"][l][:, hp * 128:(hp + 1) * 128]
        aup[0:32, hp, :] = inp["rwkv_a_up"][l][:, hp * 128:(hp + 1) * 128]
    d["rw_wup"], d["rw_gup"], d["rw_aup"] = wup, gup, aup


def consts_rwkv(c):
    bones = np.zeros((128, 128), np.float32)
    bones[0:64, 0:64] = 1
    bones[64:128, 64:128] = 1
    c["bones"] = bones
    c["bones64"] = bones / 64.0
    c["ident"] = np.eye(128, dtype=np.float32)
    c["sel2"] = np.concatenate([np.eye(64, dtype=np.float32)] * 2, axis=0)
    rm = np.ones((128, 512), np.float32)
    rm[:, 0::64] = 0
    c["rmask"] = rm
    j = np.arange(64)[:, None]
    t = np.arange(64)[None, :]
    strict = (j < t).astype(np.float32)
    incl = (j <= t).astype(np.float32)
    def bd(m):
        o = np.zeros((128, 128), np.float32)
        o[0:64, 0:64] = m
        o[64:128, 64:128] = m
        return o
    c["LPmask"] = np.concatenate([-bd(strict), bd(incl), bd(strict), bd(incl)], axis=1)
    c["Lamask"] = -bd((j > t).astype(np.float32))


def prep_nsa(inp, l, d):
    qn = np.zeros((128, 1), np.float32)
    qn[0:64, 0] = inp["nsa_q_norm"][l]
    d["ns_qnw"] = qn
    kn = np.zeros((128, 3), np.float32)
    kn[0:64, :] = inp["nsa_k_norm"][l].T
    d["ns_knw"] = kn
    w1 = np.zeros((128, 32, 128), np.float32)
    w1[0:64] = inp["nsa_cmp_k_w1"][l].reshape(32, 64, 128).transpose(1, 0, 2)
    w1[64:128] = inp["nsa_cmp_v_w1"][l].reshape(32, 64, 128).transpose(1, 0, 2)
    d["ns_w1"] = w1
    pos = inp["nsa_cmp_pos"][l]
    p2 = np.zeros((128, 32, 2), np.float32)
    p2[0:64, :, 0] = pos.T
    p2[0:64, :, 1] = pos.T
    p2[64:128] = p2[0:64]
    d["ns_pos2"] = p2
    w2 = np.zeros((128, 2, 64), np.float32)
    w2[:, 0, :] = inp["nsa_cmp_k_w2"][l]
    w2[:, 1, :] = inp["nsa_cmp_v_w2"][l]
    d["ns_w2"] = w2


def consts_nsa(c):
    BIG = 30000.0
    key = np.arange(S)
    c["n_eall"] = (key[None, :] // 64 == np.arange(64)[:, None]).astype(np.float32)
    pk = np.arange(128)[:, None]
    tq = np.arange(512)[None, :]
    c["n_cmsel"] = np.stack([np.where(tq - pk - 128 * r >= 0, 0.0, -BIG) for r in range(4)]).astype(np.float32)
    cw = []
    for r in range(8):
        dd = tq - pk - (r - 4) * 128
        cw.append(np.where((dd >= 0) & (dd < 512), 0.0, -BIG))
    c["n_cmwin"] = np.stack(cw).astype(np.float32)
    c["n_negc"] = np.stack([np.where(tq >= 16 * pk + 31 - 512 * g, 0.0, -BIG) for g in range(5)]).astype(np.float32)
    t = np.arange(S)
    cur = t // 64
    j = np.arange(64)
    ok = (j[None, :] <= cur[:, None]).astype(np.float32)
    forced = ((j[None, :] == 0) | (j[None, :] == cur[:, None]) | (j[None, :] == cur[:, None] - 1)).astype(np.float32)
    addc = forced * 1e4 * ok + (ok - 1.0)
    c["n_ok"] = np.ascontiguousarray(ok.reshape(32, 128, 64).transpose(1, 0, 2))
    c["n_addc"] = np.ascontiguousarray(addc.reshape(32, 128, 64).transpose(1, 0, 2)).astype(np.float32)
    slopes = 2.0 ** (-8.0 * np.arange(1, 5) / 4)
    a_t, b_t = t // 64, t % 64
    qaug = np.zeros((4, 4, S), np.float32)
    for h in range(4):
        s8 = 8.0 * slopes[h]
        qaug[h, 0] = -s8 * 64 * a_t
        qaug[h, 1] = -s8 * b_t
        qaug[h, 2] = s8
        qaug[h, 3] = s8
    c["n_qaug"] = qaug
    kaug = np.zeros((4, S), np.float32)
    kaug[0] = 1
    kaug[1] = 1
    kaug[2] = 64 * a_t
    kaug[3] = b_t
    c["n_kaug"] = kaug
    n = np.arange(256)
    be = 16 * n + 31
    kc = np.zeros((4, 256), np.float32)
    kc[0] = 1
    kc[1] = 1
    kc[2] = 64 * (be // 64)
    kc[3] = be % 64
    c["n_kcaug"] = kc
    ovl = ((16 * n[:, None] <= 64 * j[None, :] + 63) & (16 * n[:, None] + 31 >= 64 * j[None, :])).astype(np.float32)
    o1 = np.concatenate([np.ones((256, 1), np.float32), ovl], axis=1)
    o1[255] = 0
    c["n_ovl1"] = np.ascontiguousarray(o1.reshape(2, 128, 65).transpose(1, 0, 2))


HAVE_RWKV = True
HAVE_NSA = True
_CACHE = {}


def build_program(layer_shapes, const_shapes):
    b = Builder()
    xin = b.dram("xin", [8, 128, S], F32, "ExternalInput")
    xout = b.dram("xout", [8, 128, S], F32, "ExternalOutput")
    xT = b.dram("xT", [8, 128, S], F32)
    pT = b.dram("pT", [NBLK, 128, S], F32)
    yT = b.dram("yT", [8, 128, S], BF16)
    D = []
    for l in range(2):
        D.append({k: b.dram(f"L{l}_{k}", list(shp), F32, "ExternalInput") for k, shp in layer_shapes.items()})
    Cn = {k: b.dram(f"C_{k}", list(shp), F32, "ExternalInput") for k, shp in const_shapes.items()}
    b.begin()
    for k in range(8):
        t = b.sb("t", [128, S], F32)
        b.dma(t[:], xin[k], [xin], [t])
        b.dma(xT[k], t[:], [t], [xT])
    b.end()
    for l in range(2):
        d = D[l]
        ffn_phase(b, xT, d["f1_g"], d["f1_wg"], d["f1_wu"], d["f1_wd"])
        inproj_phase(b, xT, d["mix_g"], d["win"], pT)
        conv_pool_phase(b, pT, yT, d["cw"], d["pw"], d["psc"], Cn["pinv"], Cn["pcorr"])
        if HAVE_RWKV:
            rwkv_phase(b, pT, yT, d, Cn)
        if HAVE_NSA:
            nsa_phase(b, pT, yT, d, Cn)
        outproj_phase(b, xT, yT, d["wout"])
        ffn_phase(b, xT, d["f2_g"], d["f2_wg"], d["f2_wu"], d["f2_wd"])
    b.begin()
    for k in range(8):
        t = b.sb("t", [128, S], F32)
        b.dma(t[:], xT[k], [xT], [t])
        b.dma(xout[k], t[:], [t], [xout])
    b.end()
    return b


def kernel(**inputs):
    inp = {k: np.asarray(v) for k, v in inputs.items()}
    layers = [prep_layer(inp, l) for l in range(2)]
    cn = consts()
    b = build_program({k: v.shape for k, v in layers[0].items()}, {k: v.shape for k, v in cn.items()})
    shared = {}
    for l in range(2):
        for k, v in layers[l].items():
            shared[f"L{l}_{k}"] = v
    for k, v in cn.items():
        shared[f"C_{k}"] = v
    x = inp["x"]
    in_maps = []
    for c in range(8):
        m = dict(shared)
        m["xin"] = np.ascontiguousarray(x[c].T.reshape(8, 128, S))
        in_maps.append(m)
    res = run_bass_kernel_spmd(b.nc, in_maps, core_ids=list(range(8)))
    out = np.stack([res.results[c]["xout"].reshape(1024, S).T for c in range(8)], axis=0)
    return np.ascontiguousarray(out.astype(np.float32))
```

```python
import numpy as np
from contextlib import ExitStack
import concourse.bass as bass
import concourse.mybir as mybir
from concourse.bass_utils import run_bass_kernel_spmd

F32 = mybir.dt.float32
BF16 = mybir.dt.bfloat16
AF = mybir.ActivationFunctionType
ALU = mybir.AluOpType
AX = mybir.AxisListType


class Buf:
    def __init__(self, name, t, space):
        self.name = name
        self.t = t
        self.space = space

    def __getitem__(self, idx):
        return self.t[idx]


def _norm(lst):
    out = []
    for x in lst:
        if isinstance(x, tuple):
            out.append(x)
        else:
            out.append((x, None))
    return out


def _conf(a, b):
    return a is None or b is None or a == b


class Builder:
    ENG = ("pe", "act", "dve", "pool", "sp")
    NDMA = 12

    def __init__(self):
        self.nc = bass.Bass("TRN2", target_bir_lowering=False)
        self.gstack = ExitStack()
        self.sems = []
        self.ekey = {}
        for e in ("pe", "act", "dve", "pool"):
            self.ekey[e] = self._newsem("c_" + e)
        self.dkeys = {q: [self._newsem(f"d_{q}{i}") for i in range(self.NDMA)] for q in ("sp", "pool", "act")}
        self.drr = {"sp": 0, "pool": 0, "act": 0}
        self.semval = [0] * len(self.sems)
        self.seen = {e: {} for e in self.ENG}
        self.recs = {}
        self.ops = {e: [] for e in self.ENG}
        self.pstack = None
        self.nphase = 0
        self.uid = 0

    def _newsem(self, name):
        s = self.gstack.enter_context(self.nc.semaphore(name))
        self.sems.append(s)
        return len(self.sems) - 1

    def dram(self, name, shape, dtype, kind="Internal"):
        t = self.nc.dram_tensor(name, list(shape), dtype, kind=kind)
        return Buf(name, t.ap(), "dram")

    def begin(self):
        assert self.pstack is None
        self.pstack = ExitStack()
        self.nphase += 1

    def sb(self, name, shape, dtype):
        self.uid += 1
        nm = f"{name}_{self.uid}"
        t = self.pstack.enter_context(self.nc.sbuf_tensor(nm, list(shape), dtype))
        return Buf(nm, t, "sbuf")

    def ps(self, name, shape, dtype=F32):
        self.uid += 1
        nm = f"{name}_{self.uid}"
        t = self.pstack.enter_context(self.nc.psum_tensor(nm, list(shape), dtype))
        return Buf(nm, t, "psum")

    def op(self, eng, fn, reads=(), writes=(), dma=False, acc=False):
        reads = _norm(reads)
        writes = _norm(writes)
        if not acc:
            for (bb, tt) in reads:
                if bb.space == "psum" and (bb, tt) not in writes:
                    writes = writes + [(bb, tt)]
        waits = {}

        def need(k, v):
            if v > waits.get(k, 0):
                waits[k] = v

        for (b, t) in _norm(reads):
            for (tag, isw, k), v in self.recs.get(b.name, {}).items():
                if isw and _conf(tag, t):
                    need(k, v)
        for (b, t) in _norm(writes):
            for (tag, isw, k), v in self.recs.get(b.name, {}).items():
                if _conf(tag, t):
                    if acc and isw and k == self.ekey["pe"]:
                        continue
                    need(k, v)
        if dma:
            q = eng
            kd = self.dkeys[q][self.drr[q] % self.NDMA]
            self.drr[q] += 1
            need(kd, self.semval[kd])
            self.semval[kd] += 16
            tok = (kd, self.semval[kd])
            inc = 16
        else:
            kd = self.ekey[eng]
            self.semval[kd] += 1
            tok = (kd, self.semval[kd])
            inc = 1
        wl = []
        for k, v in waits.items():
            if v > 0 and self.seen[eng].get(k, 0) < v:
                self.seen[eng][k] = v
                wl.append((k, v))
        self.ops[eng].append((wl, fn, tok[0], inc))
        for (b, t) in _norm(reads):
            r = self.recs.setdefault(b.name, {})
            key = (t, False, tok[0])
            r[key] = max(r.get(key, 0), tok[1])
        for (b, t) in _norm(writes):
            r = self.recs.setdefault(b.name, {})
            for key in list(r.keys()):
                if t is None or key[0] == t:
                    del r[key]
            r[(t, True, tok[0])] = tok[1]
        return tok

    def end(self):
        wl = []
        for q in self.dkeys:
            for kd in self.dkeys[q]:
                v = self.semval[kd]
                if v > 0 and self.seen["sp"].get(kd, 0) < v:
                    self.seen["sp"][kd] = v
                    wl.append((kd, v))
        self.ops["sp"].append((wl, None, None, 0))
        nc = self.nc
        ops = self.ops
        sems = self.sems

        def replay(lst, e):
            for wl_, fn, k, inc in lst:
                for kk, v in wl_:
                    e.wait_ge(sems[kk], v)
                if fn is not None:
                    ins = fn(e)
                    ins.then_inc(sems[k], inc)

        with nc.Block() as block:
            @block.tensor
            def _(e):
                replay(ops["pe"], e)

            @block.scalar
            def _(e):
                replay(ops["act"], e)

            @block.vector
            def _(e):
                replay(ops["dve"], e)

            @block.gpsimd
            def _(e):
                replay(ops["pool"], e)

            @block.sync
            def _(e):
                replay(ops["sp"], e)
        self.ops = {e: [] for e in self.ENG}
        self.pstack.close()
        self.pstack = None
        for name in list(self.recs.keys()):
            pass

    def dma(self, out, in_, reads, writes, q="sp", **kw):
        return self.op(q, lambda e: e.dma_start(out=out, in_=in_, **kw), reads, writes, dma=True)

    def mm(self, out, lhsT, rhs, start, stop, reads, writes):
        return self.op("pe", lambda e: e.matmul(out, lhsT, rhs, start=start, stop=stop), reads, writes, acc=True)

    def tr(self, out, in_, ident, reads, writes):
        return self.op("pe", lambda e: e.transpose(out, in_, ident), reads, writes, acc=True)

    def actf(self, out, in_, func, reads, writes, bias=None, scale=1.0, eng="act"):
        kw = {}
        if bias is not None:
            kw["bias"] = bias
        return self.op(eng, lambda e: e.activation(out=out, in_=in_, func=func, scale=scale, **kw), reads, writes)


S = 4096
T = 1024
NT = S // T
NH = T // 512
EPS = 1e-6
import os as _os1
_NOW = bool(_os1.environ.get('NO_WDMA'))


def load_norm_gen(b, xT, ti, x, g, ones, hT, sq, rstd, pst):
    tsl = slice(ti * T, (ti + 1) * T)
    b.dma(x[:], xT[:, :, tsl].rearrange("k p t -> p k t"), [(xT, ti)], [x])
    yield
    for k in range(8):
        s = sq[k % 2]
        b.op("act", lambda e, s=s, k=k: e.activation(out=s[:], in_=x[:, k, :], func=AF.Square), [x], [s])
        for h in range(NH):
            b.mm(pst[h][:], ones[:], s[:, h * 512:(h + 1) * 512], k == 0, k == 7, [ones, s], [pst[h]])
        yield
    for h in range(NH):
        sl = slice(h * 512, (h + 1) * 512)
        b.op("act", lambda e, h=h, sl=sl: e.activation(out=rstd[:, sl], in_=pst[h][:], func=AF.Sqrt, scale=1.0 / 1024, bias=EPS),
             [pst[h]], [(rstd, h)])
        b.op("dve", lambda e, sl=sl: e.reciprocal(out=rstd[:, sl], in_=rstd[:, sl]), [(rstd, h)], [(rstd, h)])
    yield
    for k in range(8):
        b.op("dve", lambda e, k=k: e.scalar_tensor_tensor(out=hT[:, k, :], in0=x[:, k, :], scalar=g[:, k:k + 1], in1=rstd[:],
                                                         op0=ALU.mult, op1=ALU.mult), [x, g, rstd], [(hT, k)])
        yield


def gstep(gen):
    if gen is None:
        return None
    try:
        next(gen)
        return gen
    except StopIteration:
        return None


def gdrain(gen):
    while gen is not None:
        gen = gstep(gen)


def ffn_phase(b, xT, gD, WgD, WuD, WdD, xsrc=None, xdst=None):
    xsrc = xsrc or xT
    xdst = xdst or xT
    b.begin()
    ones = b.sb("ones", [128, 128], BF16)
    b.op("pool", lambda e: e.memset(ones[:], 1.0), [], [ones])
    epsb = None
    g = b.sb("g", [128, 8], F32)
    b.dma(g[:], gD[:], [gD], [g])
    xs = [b.sb("x", [128, 8, T], F32) for _ in range(2)]
    hTs = [b.sb("hT", [128, 8, T], BF16) for _ in range(2)]
    hid = b.sb("hid", [128, 22, T], BF16)
    sq = [b.sb("sq", [128, T], BF16) for _ in range(2)]
    rstds = [b.sb("rstd", [128, T], F32) for _ in range(2)]
    wg = [b.sb("wg", [128, 8, 128], BF16) for _ in range(2)]
    wu = [b.sb("wu", [128, 8, 128], BF16) for _ in range(2)]
    wd = [b.sb("wd", [128, 22, 128], BF16) for _ in range(2)]
    sg = [b.sb("sg", [128, 512], F32) for _ in range(2)]
    pg = [b.ps("pg", [128, 512]) for _ in range(2)]
    pu = [b.ps("pu", [128, 512]) for _ in range(2)]
    po = [b.ps("po", [128, 512]) for _ in range(2)]
    pst = [b.ps("pst", [128, 512]) for _ in range(2)]
    cnt = 0
    gdrain(load_norm_gen(b, xsrc, 0, xs[0], g, ones, hTs[0], sq, rstds[0], pst))
    for ti in range(NT):
        x = xs[ti % 2]
        hT = hTs[ti % 2]
        tsl = slice(ti * T, (ti + 1) * T)
        nxt = None
        for f in range(22):
            if f == 2 and ti + 1 < NT:
                nxt = load_norm_gen(b, xsrc, ti + 1, xs[(ti + 1) % 2], g, ones, hTs[(ti + 1) % 2], sq, rstds[(ti + 1) % 2], pst)
            nxt = gstep(nxt)
            a, u = wg[f % 2], wu[f % 2]
            if not (_NOW and (ti > 0 or f > 1)):
                b.dma(a[:], WgD[f], [WgD], [a], q="pool", max_dma_last_dim=4096)
                b.dma(u[:], WuD[f], [WuD], [u], q="pool", max_dma_last_dim=4096)
            for h in range(NH):
                sl = slice(h * 512, (h + 1) * 512)
                G, U, SG = pg[cnt % 2], pu[cnt % 2], sg[cnt % 2]
                cnt += 1
                for k in range(8):
                    b.mm(G[:], a[:, k, :], hT[:, k, sl], k == 0, k == 7, [a, (hT, k)], [G])
                for k in range(8):
                    b.mm(U[:], u[:, k, :], hT[:, k, sl], k == 0, k == 7, [u, (hT, k)], [U])
                b.op("act", lambda e, G=G, SG=SG: e.activation(out=SG[:], in_=G[:], func=AF.Silu), [G], [SG])
                b.op("dve", lambda e, U=U, SG=SG, f=f, sl=sl: e.tensor_tensor(out=hid[:, f, sl], in0=U[:], in1=SG[:], op=ALU.mult),
                     [U, SG], [(hid, (f, h))])
        for d in range(8):
            w = wd[d % 2]
            if not (_NOW and (ti > 0 or d > 1)):
                b.dma(w[:], WdD[d], [WdD], [w], q="pool", max_dma_last_dim=4096)
            for h in range(NH):
                sl = slice(h * 512, (h + 1) * 512)
                O = po[cnt % 2]
                cnt += 1
                for f in range(22):
                    b.mm(O[:], w[:, f, :], hid[:, f, sl], f == 0, f == 21, [w, (hid, (f, h))], [O])
                b.op("dve", lambda e, O=O, d=d, sl=sl, x=x: e.scalar_tensor_tensor(out=x[:, d, sl], in0=O[:], scalar=0.5, in1=x[:, d, sl],
                                                                               op0=ALU.mult, op1=ALU.add), [O, x], [x])
        gdrain(nxt)
        b.dma(xdst[:, :, tsl].rearrange("k p t -> p k t"), x[:], [x], [(xdst, ti)])
    b.end()


BLK_A = (0, 1)
BLK_R, BLK_K, BLK_V, BLK_WG, BLK_AD = (2, 3), (4, 5), (6, 7), 8, 9
BLK_Q, BLK_KVC, BLK_KVS, BLK_KVW, BLK_GT = (10, 11, 12, 13), 14, 15, 16, 17
BLK_U, BLK_B, BLK_C = (18, 19), (20, 21), (22, 23)
NBLK = 24


def inproj_phase(b, xT, gD, WinD, pT):
    b.begin()
    ones = b.sb("ones", [128, 128], BF16)
    b.op("pool", lambda e: e.memset(ones[:], 1.0), [], [ones])
    g = b.sb("g", [128, 8], F32)
    b.dma(g[:], gD[:], [gD], [g])
    xs = [b.sb("x", [128, 8, T], F32) for _ in range(2)]
    hTs = [b.sb("hT", [128, 8, T], BF16) for _ in range(2)]
    sq = [b.sb("sq", [128, T], BF16) for _ in range(2)]
    rstds = [b.sb("rstd", [128, T], F32) for _ in range(2)]
    ws = [b.sb("w", [128, 8, 128], BF16) for _ in range(2)]
    os_ = [b.sb("o", [128, T], F32) for _ in range(2)]
    pp = [b.ps("pp", [128, 512]) for _ in range(4)]
    pst = [b.ps("pst", [128, 512]) for _ in range(2)]
    cnt = 0
    gdrain(load_norm_gen(b, xT, 0, xs[0], g, ones, hTs[0], sq, rstds[0], pst))
    for ti in range(NT):
        x = xs[ti % 2]
        hT = hTs[ti % 2]
        tsl = slice(ti * T, (ti + 1) * T)
        nxt = None
        for blk in range(NBLK):
            if blk == 2 and ti + 1 < NT:
                nxt = load_norm_gen(b, xT, ti + 1, xs[(ti + 1) % 2], g, ones, hTs[(ti + 1) % 2], sq, rstds[(ti + 1) % 2], pst)
            nxt = gstep(nxt)
            w = ws[blk % 2]
            o = os_[blk % 2]
            b.dma(w[:], WinD[blk], [WinD], [w], q="pool", max_dma_last_dim=4096)
            for h in range(NH):
                sl = slice(h * 512, (h + 1) * 512)
                P = pp[cnt % 4]
                cnt += 1
                for k in range(8):
                    b.mm(P[:], w[:, k, :], hT[:, k, sl], k == 0, k == 7, [w, (hT, k)], [P])
                if h % 2 == 0:
                    b.op("act", lambda e, P=P, o=o, sl=sl: e.activation(out=o[:, sl], in_=P[:], func=AF.Copy), [P], [(o, h)])
                else:
                    b.op("dve", lambda e, P=P, o=o, sl=sl: e.tensor_copy(out=o[:, sl], in_=P[:]), [P], [(o, h)])
            b.dma(pT[blk, :, tsl], o[:], [o], [(pT, (blk, ti))])
        gdrain(nxt)
    b.end()


def outproj_phase(b, xT, yT, WoutD):
    b.begin()
    xs = [b.sb("x", [128, 8, T], F32) for _ in range(2)]
    ys = [b.sb("y", [128, 8, T], BF16) for _ in range(2)]
    ws = [b.sb("w", [128, 8, 128], BF16) for _ in range(2)]
    pp = [b.ps("pp", [128, 512]) for _ in range(4)]
    cnt = 0
    def loads(ti):
        tsl_ = slice(ti * T, (ti + 1) * T)
        b.dma(xs[ti % 2][:], xT[:, :, tsl_].rearrange("k p t -> p k t"), [(xT, ti)], [xs[ti % 2]])
        b.dma(ys[ti % 2][:], yT[:, :, tsl_].rearrange("k p t -> p k t"), [yT], [ys[ti % 2]])
    loads(0)
    for ti in range(NT):
        x = xs[ti % 2]
        y = ys[ti % 2]
        tsl = slice(ti * T, (ti + 1) * T)
        if ti + 1 < NT:
            loads(ti + 1)
        for d in range(8):
            w = ws[d % 2]
            b.dma(w[:], WoutD[d], [WoutD], [w], q="pool", max_dma_last_dim=4096)
            for h in range(NH):
                sl = slice(h * 512, (h + 1) * 512)
                P = pp[cnt % 4]
                cnt += 1
                for c in range(8):
                    b.mm(P[:], w[:, c, :], y[:, c, sl], c == 0, c == 7, [w, y], [P])
                b.op("dve", lambda e, P=P, x=x, d=d, sl=sl: e.tensor_tensor(out=x[:, d, sl], in0=P[:], in1=x[:, d, sl], op=ALU.add), [P, x], [x])
        b.dma(xT[:, :, tsl].rearrange("k p t -> p k t"), x[:], [x], [(xT, ti)])
    b.end()


def conv_pool_phase(b, pT, yT, cwD, pwD, pscD, pinvD, pcorrD):
    b.begin()
    cw = b.sb("cw", [128, 2, 3], F32)
    b.dma(cw[:], cwD[:], [cwD], [cw])
    psc = b.sb("psc", [128, 2], F32)
    b.dma(psc[:], pscD[:], [pscD], [psc])
    pinv = b.sb("pinv", [128, 2], F32)
    b.dma(pinv[:], pinvD[:], [pinvD], [pinv])
    pcorr = b.sb("pcorr", [128, 2, 16], F32)
    b.dma(pcorr[:], pcorrD[:], [pcorrD], [pcorr])
    pw = b.sb("pw", [128, 2, 128], BF16)
    for j in range(2):
        b.dma(pw[:, j, :], pwD[j], [pwD], [pw], q="pool")
    u = b.sb("u", [128, 16 + S], F32)
    bb = b.sb("bb", [128, S], F32)
    cc = b.sb("cc", [128, S], F32)
    z = b.sb("z", [128, 16 + S], F32)
    acc = b.sb("acc", [128, 16 + S], F32)
    yb = b.sb("yb", [128, S], BF16)
    pp = [b.ps("pp", [128, 512]) for _ in range(2)]
    for j in range(2):
        b.dma(u[:, 16:], pT[BLK_U[j]], [pT], [u])
        b.dma(bb[:], pT[BLK_B[j]], [pT], [bb])
        b.dma(cc[:], pT[BLK_C[j]], [pT], [cc])
        b.op("pool", lambda e: e.memset(z[:, 0:16], 0.0), [], [z])
        b.op("dve", lambda e: e.tensor_tensor(out=z[:, 16:], in0=cc[:], in1=u[:, 16:], op=ALU.mult), [cc, u], [z])
        b.op("dve", lambda e, j=j: e.tensor_scalar(out=acc[:, 16:], in0=z[:, 16:], scalar1=cw[:, j, 2:3], scalar2=None, op0=ALU.mult), [z, cw], [acc])
        b.op("dve", lambda e, j=j: e.scalar_tensor_tensor(out=acc[:, 16:], in0=z[:, 15:15 + S], scalar=cw[:, j, 1:2], in1=acc[:, 16:],
                                                         op0=ALU.mult, op1=ALU.add), [z, cw, acc], [acc])
        b.op("dve", lambda e, j=j: e.scalar_tensor_tensor(out=acc[:, 16:], in0=z[:, 14:14 + S], scalar=cw[:, j, 0:1], in1=acc[:, 16:],
                                                         op0=ALU.mult, op1=ALU.add), [z, cw, acc], [acc])
        b.op("dve", lambda e: e.tensor_tensor(out=yb[:], in0=bb[:], in1=acc[:, 16:], op=ALU.mult), [bb, acc], [yb])
        b.dma(yT[6 + j], yb[:], [yb], [(yT, 6 + j)])
    for j in range(2):
        b.op("pool", lambda e: e.memset(u[:, 0:16], 0.0), [], [u])
        b.op("pool", lambda e: e.memset(z[:, 0:16], 0.0), [], [z])
        b.op("pool", lambda e: e.memset(acc[:, 0:16], 0.0), [], [acc])
        b.dma(u[:, 16:], pT[BLK_A[j]], [pT], [u])
        R = slice(16, 16 + S)

        def sh(k):
            return slice(16 - k, 16 - k + S)
        if j == 0:
            b.op("dve", lambda e: e.tensor_tensor(out=z[:, R], in0=u[:, R], in1=u[:, sh(1)], op=ALU.add), [u], [z])
            b.op("dve", lambda e: e.tensor_copy(out=acc[0:64, R], in_=z[0:64, R]), [z], [acc])
            b.op("dve", lambda e: e.tensor_tensor(out=acc[64:128, R], in0=z[64:128, R], in1=z[64:128, sh(2)], op=ALU.add), [z], [acc])
        else:
            b.op("dve", lambda e: e.tensor_tensor(out=z[:, R], in0=u[:, R], in1=u[:, sh(1)], op=ALU.add), [u], [z])
            b.op("dve", lambda e: e.tensor_tensor(out=acc[:, R], in0=z[:, R], in1=z[:, sh(2)], op=ALU.add), [z], [acc])
            b.op("dve", lambda e: e.tensor_tensor(out=z[:, R], in0=acc[:, R], in1=acc[:, sh(4)], op=ALU.add), [acc], [z])
            b.op("dve", lambda e: e.tensor_copy(out=acc[0:64, R], in_=z[0:64, R]), [z], [acc])
            b.op("dve", lambda e: e.tensor_tensor(out=acc[64:128, R], in0=z[64:128, R], in1=z[64:128, sh(8)], op=ALU.add), [z], [acc])
        b.op("dve", lambda e, j=j: e.tensor_scalar(out=cc[:], in0=acc[:, R], scalar1=pinv[:, j:j + 1], scalar2=None, op0=ALU.mult), [acc, pinv], [cc])
        b.op("dve", lambda e, j=j: e.tensor_tensor(out=cc[:, 0:16], in0=acc[:, 16:32], in1=pcorr[:, j, :], op=ALU.mult), [acc, pcorr, cc], [cc])
        b.op("dve", lambda e: e.tensor_tensor(out=yb[:], in0=cc[:], in1=u[:, R], op=ALU.subtract), [cc, u], [yb])
        for h in range(S // 512):
            sl = slice(h * 512, (h + 1) * 512)
            P = pp[h % 2]
            b.mm(P[:], pw[:, j, :], yb[:, sl], True, True, [pw, yb], [P])
            b.op("act", lambda e, P=P, sl=sl, j=j: e.activation(out=z[:, sl], in_=P[:], func=AF.Copy, scale=psc[:, j:j + 1]), [P, psc], [(z, h)])
        b.op("dve", lambda e: e.tensor_copy(out=yb[:], in_=z[:, 0:S]), [z], [yb])
        b.dma(yT[j], yb[:], [yb], [(yT, j)])
    b.end()


TT_ = 512
NCH = 8


F32R = mybir.dt.float32r


def mmr(b, out, lhsT, rhs, start, stop, reads, writes):
    return b.op("pe", lambda e: e.matmul(out, lhsT.bitcast(F32R), rhs.bitcast(F32R), start=start, stop=stop), reads, writes, acc=True)


def run_rr(gens, bg=None):
    gens = list(gens)
    while gens:
        for g in list(gens):
            try:
                next(g)
            except StopIteration:
                gens.remove(g)
        if bg is not None:
            try:
                next(bg)
            except StopIteration:
                pass


DBG = []


def rwkv_phase(b, pT, yT, d, Cn, ntiles=None, debug=False):
    b.begin()

    def dump(name, buf, ap, shape):
        if not debug:
            return
        dd = b.dram("dbg_" + name, list(shape), F32, "ExternalOutput")
        DBG.append("dbg_" + name)
        b.dma(dd[:], ap, [buf], [dd])
    sbF = lambda nm, shp=(128, TT_): b.sb(nm, list(shp), F32)
    def ld(nm, src, shp):
        t = b.sb(nm, shp, F32)
        b.dma(t[:], src[:], [src], [t])
        return t
    bones = ld("bones", Cn["bones"], [128, 128])
    bones64 = ld("bones64", Cn["bones64"], [128, 128])
    ident = ld("ident", Cn["ident"], [128, 128])
    sel2 = ld("sel2", Cn["sel2"], [128, 64])
    rmask = ld("rmask", Cn["rmask"], [128, TT_])
    LPmask = ld("LPmask", Cn["LPmask"], [128, 512])
    Lamask = ld("Lamask", Cn["Lamask"], [128, 128])
    mu8 = ld("mu8", d["rw_mu8"], [128, 8])
    vec = ld("vec", d["rw_vec"], [128, 2, 7])
    WUP = ld("WUP", d["rw_wup"], [128, 2, 128])
    GUP = ld("GUP", d["rw_gup"], [128, 2, 128])
    AUP = ld("AUP", d["rw_aup"], [128, 2, 128])
    V_W0, V_A0, V_KK, V_KA, V_RK, V_LNW, V_LNB = range(7)

    pb2 = [b.sb(f"pb{i}", [128, 1 + TT_], F32) for i in range(2)]
    pb = [pb2[i % 2] for i in range(8)]
    dtmp = sbF("dtmp")
    lp = [sbF(f"lp{i}") for i in range(8)]
    twg = sbF("twg")
    psw = [b.ps("psw", [128, 512]) for _ in range(1)]
    psT2 = [b.ps("psT", [128, 512]) for _ in range(2)]
    psN2 = [b.ps("psN", [128, 512]) for _ in range(2)]
    psS2 = [b.ps("psS", [128, 512]) for _ in range(2)]
    wcnt = [0]

    def wide():
        wcnt[0] += 1
        return psw[0]

    H = {}
    SH = {nm: sbF("sh_" + nm) for nm in ("lw", "cum", "cexc", "dW", "epos", "eexc", "t1", "t2", "kk", "YT", "yc", "sq", "rs")}
    for hp in range(2):
        h = {}
        for nm in ("a", "kkn", "kpp", "akk", "akkh", "eneg", "eW"):
            h[nm] = sbF(nm + str(hp))
        for nm in ("g", "bonus"):
            h[nm] = [sbF(nm + str(hp) + "_" + str(i)) for i in range(2)]
        for nm in ("lw", "cum", "cexc", "dW", "epos", "eexc", "t1", "t2", "kk", "YT", "yc", "sq", "rs"):
            h[nm] = SH[nm]
        h["wC"] = [b.sb(f"wC{hp}_{i}", [128, NCH], F32) for i in range(2)]
        h["QR"] = b.sb(f"QR{hp}", [128, NCH, 2, 64], F32)
        for nm in ("AKKbd", "KHbd", "AKKWbd", "KWbd", "Vbd", "Ybd"):
            h[nm] = b.sb(nm + str(hp), [128, NCH, 128], F32)
            b.op("pool", lambda e, t=h[nm]: e.memset(t[:], 0.0), [], [h[nm]])
        h["QRbd"] = b.sb(f"QRbd{hp}", [128, NCH, 2, 128], F32)
        b.op("pool", lambda e, t=h["QRbd"]: e.memset(t[:], 0.0), [], [h["QRbd"]])
        h["Sst"] = b.sb(f"Sst{hp}", [128, 64], F32)
        b.op("pool", lambda e, t=h["Sst"]: e.memset(t[:], 0.0), [], [h["Sst"]])
        h["LP"] = [b.sb(f"LP{hp}_{i}", [128, 512], F32) for i in range(2)]
        h["AKWT"] = [b.sb(f"AKWT{hp}_{i}", [128, 256], F32) for i in range(2)]
        h["Vst"] = [b.sb(f"Vst{hp}_{i}", [128, 64], F32) for i in range(2)]
        h["TT"] = [b.sb(f"TT{hp}_{i}", [128, 128], F32) for i in range(2)]
        h["MY"] = [b.sb(f"MY{hp}_{i}", [128, 256], F32) for i in range(2)]
        h["N"] = [b.sb(f"N{hp}_{i}", [128, 128], F32) for i in range(2)]
        h["Xst"] = b.sb(f"Xst{hp}", [128, 64], F32)
        h["Ust"] = b.sb(f"Ust{hp}", [128, 64], F32)
        h["ob"] = b.sb(f"ob{hp}", [128, TT_], BF16)
        H[hp] = h

    def v3(t, R):
        return t[R, :].rearrange("p (c t) -> p c t", t=64)

    def stage1_gen(ti):
        t0 = ti * TT_
        for i in range(8):
            blk = 2 + i
            if ti == 0:
                b.op("pool", lambda e, i=i: e.memset(pb[i][:, 0:1], 0.0), [], [pb[i]])
                b.dma(pb[i][:, 1:], pT[blk, :, 0:TT_], [pT], [pb[i]])
            else:
                b.dma(pb[i][:], pT[blk, :, t0 - 1:t0 + TT_], [pT], [pb[i]])
            eng = "dve" if i % 2 == 0 else "pool"
            b.op("dve", lambda e, i=i: e.tensor_tensor(out=dtmp[:], in0=pb[i][:, 0:TT_], in1=pb[i][:, 1:], op=ALU.subtract), [pb[i]], [dtmp])
            b.op("dve", lambda e, i=i: e.scalar_tensor_tensor(out=lp[i][:], in0=dtmp[:], scalar=mu8[:, i:i + 1], in1=pb[i][:, 1:],
                                                             op0=ALU.mult, op1=ALU.add), [dtmp, mu8, pb[i]], [lp[i]])
            yield
        b.op("act", lambda e: e.activation(out=twg[0:64, :], in_=lp[6][0:64, :], func=AF.Tanh), [lp[6]], [(twg, 0)])
        b.op("act", lambda e: e.activation(out=twg[64:128, :], in_=lp[6][64:128, :], func=AF.Sigmoid), [lp[6]], [(twg, 1)])
        for hp in range(2):
            h = H[hp]
            rp, kp, vp = lp[hp], lp[2 + hp], lp[4 + hp]
            vv = lambda i, hp=hp: vec[:, hp, i:i + 1]
            pw_ = wide()
            b.mm(pw_[:], WUP[:, hp, :], twg[:], True, True, [WUP, twg], [pw_])
            b.op("act", lambda e, pw_=pw_, h=h, vv=vv: e.activation(out=h["t1"][:], in_=pw_[:], func=AF.Sigmoid, bias=vv(V_W0)), [pw_, vec], [h["t1"]])
            b.op("dve", lambda e, h=h: e.tensor_scalar(out=h["lw"][:], in0=h["t1"][:], scalar1=-0.6065306597126334, scalar2=None, op0=ALU.mult), [h["t1"]], [h["lw"]])
            pa_ = wide()
            b.mm(pa_[:], AUP[:, hp, :], lp[7][:], True, True, [AUP, lp[7]], [pa_])
            b.op("act", lambda e, pa_=pa_, h=h, vv=vv: e.activation(out=h["a"][:], in_=pa_[:], func=AF.Sigmoid, bias=vv(V_A0)), [pa_, vec], [h["a"]])
            pg_ = wide()
            b.mm(pg_[:], GUP[:, hp, :], twg[:], True, True, [GUP, twg], [pg_])
            b.op("act", lambda e, pg_=pg_, h=h: e.activation(out=h["g"][ti % 2][:], in_=pg_[:], func=AF.Copy), [pg_], [h["g"][ti % 2]])
            yield
            b.op("dve", lambda e, h=h, kp=kp, vv=vv: e.tensor_scalar(out=h["kk"][:], in0=kp[:], scalar1=vv(V_KK), scalar2=None, op0=ALU.mult), [kp, vec], [h["kk"]])
            b.op("act", lambda e, h=h: e.activation(out=h["t1"][:], in_=h["kk"][:], func=AF.Square), [h["kk"]], [h["t1"]])
            pn_ = wide()
            b.mm(pn_[:], bones[:], h["t1"][:], True, True, [bones, h["t1"]], [pn_])
            b.op("dve", lambda e, pn_=pn_, h=h: e.tensor_scalar(out=h["t2"][:], in0=pn_[:], scalar1=1e-24, scalar2=None, op0=ALU.max), [pn_], [h["t2"]])
            b.op("act", lambda e, h=h: e.activation(out=h["t2"][:], in_=h["t2"][:], func=AF.Sqrt), [h["t2"]], [h["t2"]])
            b.op("dve", lambda e, h=h: e.reciprocal(out=h["t2"][:], in_=h["t2"][:]), [h["t2"]], [h["t2"]])
            b.op("dve", lambda e, h=h: e.tensor_tensor(out=h["kkn"][:], in0=h["kk"][:], in1=h["t2"][:], op=ALU.mult), [h["kk"], h["t2"]], [h["kkn"]])
            yield
            b.op("dve", lambda e, h=h, vv=vv: e.tensor_scalar(out=h["t1"][:], in0=h["a"][:], scalar1=-1.0, scalar2=vv(V_KA), op0=ALU.add, op1=ALU.mult), [h["a"], vec], [h["t1"]])
            b.op("dve", lambda e, h=h, kp=kp: e.scalar_tensor_tensor(out=h["kpp"][:], in0=h["t1"][:], scalar=1.0, in1=kp[:], op0=ALU.add, op1=ALU.mult), [h["t1"], kp], [h["kpp"]])
            b.op("dve", lambda e, h=h: e.tensor_tensor(out=h["akk"][:], in0=h["kkn"][:], in1=h["a"][:], op=ALU.mult), [h["kkn"], h["a"]], [h["akk"]])
            yield
            b.op("dve", lambda e, h=h, rp=rp, vv=vv: e.scalar_tensor_tensor(out=h["t1"][:], in0=rp[:], scalar=vv(V_RK), in1=h["kpp"][:], op0=ALU.mult, op1=ALU.mult), [rp, vec, h["kpp"]], [h["t1"]])
            pb_ = wide()
            b.mm(pb_[:], bones[:], h["t1"][:], True, True, [bones, h["t1"]], [pb_])
            b.op("dve", lambda e, pb_=pb_, h=h, vp=vp: e.tensor_tensor(out=h["bonus"][ti % 2][:], in0=pb_[:], in1=vp[:], op=ALU.mult), [pb_, vp], [h["bonus"][ti % 2]])
            yield
            b.op("dve", lambda e, h=h: e.tensor_tensor_scan(out=h["cum"][:], data0=rmask[:], data1=h["lw"][:], initial=0.0, op0=ALU.mult, op1=ALU.add), [rmask, h["lw"]], [h["cum"]])
            b.op("dve", lambda e, h=h: e.tensor_tensor(out=h["cexc"][:], in0=h["cum"][:], in1=h["lw"][:], op=ALU.subtract), [h["cum"], h["lw"]], [h["cexc"]])
            for c in range(NCH):
                cs = slice(c * 64, (c + 1) * 64)
                b.op("pool" if c % 2 else "dve", lambda e, h=h, cs=cs, c=c: e.tensor_scalar(out=h["dW"][:, cs], in0=h["cum"][:, cs], scalar1=-1.0,
                                                                 scalar2=h["cum"][:, c * 64 + 63:c * 64 + 64], op0=ALU.mult, op1=ALU.add), [h["cum"]], [(h["dW"], c)])
            yield
            b.op("act", lambda e, h=h: e.activation(out=h["epos"][:], in_=h["cum"][:], func=AF.Exp), [h["cum"]], [h["epos"]])
            b.op("act", lambda e, h=h: e.activation(out=h["eneg"][:], in_=h["cum"][:], func=AF.Exp, scale=-1.0), [h["cum"]], [h["eneg"]])
            b.op("act", lambda e, h=h: e.activation(out=h["eexc"][:], in_=h["cexc"][:], func=AF.Exp), [h["cexc"]], [h["eexc"]])
            b.op("act", lambda e, h=h: e.activation(out=h["eW"][:], in_=h["dW"][:], func=AF.Exp), [h["dW"]], [h["eW"]])
            b.op("pool", lambda e, h=h: e.tensor_copy(out=h["wC"][ti % 2][:], in_=h["epos"][:, 63::64]), [h["epos"]], [h["wC"][ti % 2]])
            yield
            b.op("dve", lambda e, h=h: e.tensor_tensor(out=h["QR"][:, :, 0, :], in0=v3(h["kkn"], slice(0, 128)), in1=v3(h["eexc"], slice(0, 128)), op=ALU.mult), [h["kkn"], h["eexc"]], [(h["QR"], 0)])
            b.op("pool", lambda e, h=h, rp=rp: e.tensor_tensor(out=h["QR"][:, :, 1, :], in0=v3(rp, slice(0, 128)), in1=v3(h["epos"], slice(0, 128)), op=ALU.mult), [rp, h["epos"]], [(h["QR"], 1)])
            b.op("dve", lambda e, h=h: e.tensor_tensor(out=h["akkh"][:], in0=h["akk"][:], in1=h["eneg"][:], op=ALU.mult), [h["akk"], h["eneg"]], [h["akkh"]])
            yield

    def stage2(ti):
        for hp in range(2):
            h = H[hp]
            vp = lp[4 + hp]
            for hh in range(2):
                R = slice(64 * hh, 64 * hh + 64)
                eng = "pool" if hh else "dve"
                b.op(eng, lambda e, h=h, R=R: e.tensor_copy(out=h["QRbd"][R, :, 0, R].bitcast(F32R), in_=h["QR"][R, :, 0, :]), [(h["QR"], 0)], [(h["QRbd"], hh)])
                b.op(eng, lambda e, h=h, R=R: e.tensor_copy(out=h["QRbd"][R, :, 1, R].bitcast(F32R), in_=h["QR"][R, :, 1, :]), [(h["QR"], 1)], [(h["QRbd"], hh)])
                b.op(eng, lambda e, h=h, R=R: e.tensor_copy(out=h["AKKbd"][R, :, R].bitcast(F32R), in_=v3(h["akkh"], R)), [h["akkh"]], [(h["AKKbd"], hh)])
                b.op(eng, lambda e, h=h, R=R: e.tensor_tensor(out=h["KHbd"][R, :, R].bitcast(F32R), in0=v3(h["kpp"], R), in1=v3(h["eneg"], R), op=ALU.mult), [h["kpp"], h["eneg"]], [(h["KHbd"], hh)])
                b.op(eng, lambda e, h=h, R=R: e.tensor_tensor(out=h["AKKWbd"][R, :, R], in0=v3(h["akk"], R), in1=v3(h["eW"], R), op=ALU.mult), [h["akk"], h["eW"]], [(h["AKKWbd"], hh)])
                b.op(eng, lambda e, h=h, R=R: e.tensor_tensor(out=h["KWbd"][R, :, R], in0=v3(h["kpp"], R), in1=v3(h["eW"], R), op=ALU.mult), [h["kpp"], h["eW"]], [(h["KWbd"], hh)])
                b.op(eng, lambda e, h=h, R=R, vp=vp: e.tensor_copy(out=h["Vbd"][R, :, R], in_=v3(vp, R)), [vp], [(h["Vbd"], hh)])


    NTL = ntiles or (S // TT_)
    run_rr([stage1_gen(0)])
    for ti in range(NTL):
        t0 = ti * TT_
        stage2(ti)
        bg = stage1_gen(ti + 1) if ti + 1 < NTL else None

        def prep_gen(hp, c):
            h = H[hp]
            par = c % 2
            LP, AKWT, Vst, TT = h["LP"][par], h["AKWT"][par], h["Vst"][par], h["TT"][par]
            bX, bY = psT2[hp], psN2[hp]
            MY, NN = h["MY"], h["N"]
            qr = h["QRbd"][:, c, :, :].rearrange("p a t -> p (a t)")
            b.tr(bX[:, 0:128], h["AKKWbd"][:, c, :], ident[:], [h["AKKWbd"], ident], [bX])
            b.tr(bX[:, 128:256], h["KWbd"][:, c, :], ident[:], [h["KWbd"], ident], [bX])
            b.tr(bX[:, 256:384], h["Vbd"][:, c, :], ident[:], [h["Vbd"], ident], [bX])
            mmr(b, bX[:, 384:512], h["QRbd"][:, c, 0, :], h["AKKbd"][:, c, :], True, True, [h["QRbd"], h["AKKbd"]], [bX])
            mmr(b, bY[:, 0:256], h["AKKbd"][:, c, :], qr, True, True, [h["AKKbd"], h["QRbd"]], [bY])
            mmr(b, bY[:, 256:512], h["KHbd"][:, c, :], qr, True, True, [h["KHbd"], h["QRbd"]], [bY])
            yield
            b.op("dve", lambda e: e.tensor_tensor(out=NN[0][:].bitcast(F32R), in0=bX[:, 384:512], in1=Lamask[:], op=ALU.mult), [bX, Lamask], [NN[0]])
            b.op("act", lambda e: e.activation(out=AKWT[:].bitcast(F32R), in_=bX[:, 0:256], func=AF.Copy), [bX], [AKWT])
            b.op("act", lambda e: e.activation(out=Vst[0:64, :].bitcast(F32R), in_=bX[0:64, 256:320], func=AF.Copy), [bX], [(Vst, 0)])
            b.op("act", lambda e: e.activation(out=Vst[64:128, :].bitcast(F32R), in_=bX[64:128, 320:384], func=AF.Copy), [bX], [(Vst, 1)])
            b.op("dve", lambda e: e.tensor_tensor(out=LP[:].bitcast(F32R), in0=bY[:], in1=LPmask[:], op=ALU.mult), [bY, LPmask], [LP])
            b.op("pool", lambda e: e.tensor_tensor(out=MY[1][:, 128:256].bitcast(F32R), in0=LP[:, 0:128], in1=ident[:], op=ALU.add), [LP, ident], [(MY[1], 1)])
            yield
            mmr(b, bX[:, 0:128], NN[0][:], LP[:, 0:128], True, True, [NN[0], LP], [bX])
            mmr(b, bX[:, 256:384], LP[:, 0:128], NN[0][:], True, True, [LP, NN[0]], [bX])
            yield
            b.op("act", lambda e: e.activation(out=MY[1][:, 0:128].bitcast(F32R), in_=bX[:, 0:128], func=AF.Copy), [bX], [(MY[1], 0)])
            b.op("dve", lambda e: e.tensor_copy(out=NN[1][:].bitcast(F32R), in_=bX[:, 256:384]), [bX], [NN[1]])
            yield
            for k in range(1, 6):
                bank = bY if k % 2 else bX
                i, o = k % 2, (k + 1) % 2
                if k < 5:
                    mmr(b, bank[:, 0:256], NN[i][:], MY[i][:, 0:256], True, True, [NN[i], MY[i]], [bank])
                    mmr(b, bank[:, 256:384], MY[i][:, 0:128], NN[i][:], True, True, [MY[i], NN[i]], [bank])
                    yield
                    b.op("dve", lambda e, bank=bank, i=i, o=o: e.tensor_tensor(out=MY[o][:, 128:256].bitcast(F32R), in0=bank[:, 128:256], in1=MY[i][:, 128:256], op=ALU.add),
                         [bank, (MY[i], 1)], [(MY[o], 1)])
                    b.op("act", lambda e, bank=bank, o=o: e.activation(out=MY[o][:, 0:128].bitcast(F32R), in_=bank[:, 0:128], func=AF.Copy), [bank], [(MY[o], 0)])
                    b.op("act", lambda e, bank=bank, o=o: e.activation(out=NN[o][:].bitcast(F32R), in_=bank[:, 256:384], func=AF.Copy), [bank], [NN[o]])
                    yield
                else:
                    mmr(b, bank[:, 0:128], NN[i][:], MY[i][:, 128:256], True, True, [NN[i], MY[i]], [bank])
                    yield
                    b.op("dve", lambda e, bank=bank, i=i: e.tensor_tensor(out=TT[:].bitcast(F32R), in0=bank[:, 0:128], in1=MY[i][:, 128:256], op=ALU.add),
                         [bank, (MY[i], 1)], [TT])
                    yield

        def seq_gen(hp, c):
            h = H[hp]
            par = c % 2
            LP, AKWT, Vst, TT = h["LP"][par], h["AKWT"][par], h["Vst"][par], h["TT"][par]
            Sst, Xst, Ust = h["Sst"], h["Xst"], h["Ust"]
            so = 0
            psS = psS2[hp]
            sb_ = lambda k: psS
            X_, U_, Y_, S_ = (psS[:, so + 64 * k:so + 64 * k + 64] for k in range(4))
            mmr(b, X_, LP[:, 256:384], Vst[:], True, False, [LP, Vst], [sb_(0)])
            mmr(b, X_, h["QRbd"][:, c, 0, :], Sst[:], False, True, [h["QRbd"], Sst], [sb_(0)])
            yield
            b.op("dve", lambda e: e.tensor_copy(out=Xst[:].bitcast(F32R), in_=X_), [sb_(0)], [Xst])
            yield
            mmr(b, U_, TT[:], Xst[:], True, True, [TT, Xst], [sb_(1)])
            yield
            b.op("dve", lambda e: e.tensor_scalar(out=Ust[:].bitcast(F32R), in0=U_, scalar1=-1.0, scalar2=None, op0=ALU.mult), [sb_(1)], [Ust])
            yield
            mmr(b, Y_, h["QRbd"][:, c, 1, :], Sst[:], True, False, [h["QRbd"], Sst], [sb_(2)])
            mmr(b, Y_, LP[:, 128:256], Ust[:], False, False, [LP, Ust], [sb_(2)])
            mmr(b, Y_, LP[:, 384:512], Vst[:], False, True, [LP, Vst], [sb_(2)])
            mmr(b, S_, AKWT[:, 128:256], Vst[:], True, False, [AKWT, Vst], [sb_(3)])
            mmr(b, S_, AKWT[:, 0:128], Ust[:], False, True, [AKWT, Ust], [sb_(3)])
            yield
            wCt = h["wC"][ti % 2]
            b.op("dve", lambda e: e.scalar_tensor_tensor(out=Sst[:].bitcast(F32R), in0=Sst[:], scalar=wCt[:, c:c + 1], in1=S_, op0=ALU.mult, op1=ALU.add),
                 [Sst, wCt, sb_(3)], [Sst])
            b.op("act", lambda e: e.activation(out=h["Ybd"][0:64, c, 0:64], in_=psS[0:64, so + 128:so + 192], func=AF.Copy), [sb_(2)], [(h["Ybd"], c)])
            b.op("act", lambda e: e.activation(out=h["Ybd"][64:128, c, 64:128], in_=psS[64:128, so + 128:so + 192], func=AF.Copy), [sb_(2)], [(h["Ybd"], c)])
            yield

        import os as _os
        STOP = int(_os.environ.get("RW_STOP", "9"))
        if STOP <= 1:
            continue
        for c in range(NCH + 1):
            gens = []
            if c < NCH:
                import itertools as _it
                PST = int(_os.environ.get("P_STOP", "999"))
                g0 = _it.islice(prep_gen(0, c), PST)
                g1 = _it.islice(prep_gen(1, c), PST)
                next(g0, None)
                gens += [g0, g1]
            if c >= 1 and STOP >= 3:
                s0 = seq_gen(0, c - 1)
                s1 = seq_gen(1, c - 1)
                next(s0, None)
                gens += [s0, s1]
            run_rr(gens, bg)
        if bg is not None:
            for _ in bg:
                pass
        if STOP <= 3:
            continue

        for hp in range(2):
            h = H[hp]
            vv = lambda i, hp=hp: vec[:, hp, i:i + 1]
            py = wide()
            for c in range(NCH):
                b.mm(py[:, c * 64:(c + 1) * 64], h["Ybd"][:, c, :], sel2[:], True, True, [(h["Ybd"], c), sel2], [py])
            b.op("act", lambda e, py=py, h=h: e.activation(out=h["YT"][:], in_=py[:], func=AF.Copy), [py], [h["YT"]])
            if ti == 0:
                dump(f"YT{hp}", h["YT"], h["YT"][:], [128, TT_])
                dump(f"Sst{hp}", h["Sst"], h["Sst"][:], [128, 64])
                dump(f"TT{hp}", h["TT"][1], h["TT"][1][:], [128, 128])
                dump(f"LP{hp}", h["LP"][1], h["LP"][1][:], [128, 512])
            pm = wide()
            b.mm(pm[:], bones64[:], h["YT"][:], True, True, [bones64, h["YT"]], [pm])
            b.op("dve", lambda e, pm=pm, h=h: e.tensor_tensor(out=h["yc"][:], in0=h["YT"][:], in1=pm[:], op=ALU.subtract), [h["YT"], pm], [h["yc"]])
            b.op("act", lambda e, h=h: e.activation(out=h["sq"][:], in_=h["yc"][:], func=AF.Square), [h["yc"]], [h["sq"]])
            pv = wide()
            b.mm(pv[:], bones64[:], h["sq"][:], True, True, [bones64, h["sq"]], [pv])
            b.op("act", lambda e, pv=pv, h=h: e.activation(out=h["rs"][:], in_=pv[:], func=AF.Sqrt, bias=64e-5), [pv], [h["rs"]])
            b.op("dve", lambda e, h=h: e.reciprocal(out=h["rs"][:], in_=h["rs"][:]), [h["rs"]], [h["rs"]])
            b.op("dve", lambda e, h=h: e.tensor_tensor(out=h["yc"][:], in0=h["yc"][:], in1=h["rs"][:], op=ALU.mult), [h["yc"], h["rs"]], [h["yc"]])
            b.op("dve", lambda e, h=h, vv=vv: e.tensor_scalar(out=h["yc"][:], in0=h["yc"][:], scalar1=vv(V_LNW), scalar2=vv(V_LNB), op0=ALU.mult, op1=ALU.add), [h["yc"], vec], [h["yc"]])
            bon, gg = h["bonus"][ti % 2], h["g"][ti % 2]
            b.op("dve", lambda e, h=h, bon=bon: e.tensor_tensor(out=h["yc"][:], in0=h["yc"][:], in1=bon[:], op=ALU.add), [h["yc"], bon], [h["yc"]])
            b.op("dve", lambda e, h=h, gg=gg: e.tensor_tensor(out=h["ob"][:], in0=h["yc"][:], in1=gg[:], op=ALU.mult), [h["yc"], gg], [h["ob"]])
            b.dma(yT[2 + hp, :, t0:t0 + TT_], h["ob"][:], [h["ob"]], [(yT, (2 + hp, ti))])
    b.end()


NBIG = 30000.0


def nsa_phase(b, pT, yT, d, Cn, ngroups=None, debug=False):
    b.begin()
    DBGN = []

    def dump(name, buf, ap, shape, dt=F32):
        if not debug:
            return
        dd = b.dram("dbg_" + name, list(shape), dt, "ExternalOutput")
        b.dma(dd[:], ap, [buf], [dd])

    def ld(nm, src, shp):
        t = b.sb(nm, shp, F32)
        b.dma(t[:], src[:], [src], [t])
        return t

    def ldc(nm, src_ap, srcbuf, shp):
        t = b.sb(nm, shp, BF16)
        b.dma(t[:], src_ap, [srcbuf], [t], q="pool", max_dma_last_dim=4096)
        return t
    ident = ld("ident", Cn["ident"], [128, 128])
    identb = ldc("identb", Cn["ident"][:], Cn["ident"], [128, 128])
    bones = ld("bones", Cn["bones"], [128, 128])
    eall = b.sb("eall", [128, S], BF16)
    b.op("pool", lambda e: e.memset(eall[64:128, :], 0.0), [], [eall])
    b.dma(eall[0:64, :], Cn["n_eall"][:], [Cn["n_eall"]], [eall], q="pool", max_dma_last_dim=4096)
    cmsel = b.sb("cmsel", [128, 4, 512], BF16)
    for r in range(4):
        b.dma(cmsel[:, r, :], Cn["n_cmsel"][r], [Cn["n_cmsel"]], [cmsel], q="pool", max_dma_last_dim=4096)
    cmwin = b.sb("cmwin", [128, 8, 512], BF16)
    for r in range(8):
        b.dma(cmwin[:, r, :], Cn["n_cmwin"][r], [Cn["n_cmwin"]], [cmwin], q="pool", max_dma_last_dim=4096)
    negc = b.sb("negc", [128, 5, 512], BF16)
    for r in range(5):
        b.dma(negc[:, r, :], Cn["n_negc"][r], [Cn["n_negc"]], [negc], q="pool", max_dma_last_dim=4096)
    qnw = ld("qnw", d["ns_qnw"], [128, 1])
    knw = ld("knw", d["ns_knw"], [128, 3])
    w2 = ld("w2", d["ns_w2"], [128, 2, 64])
    pos2 = ld("pos2", d["ns_pos2"], [128, 32, 2])
    QA = [b.sb(f"QA{h}", [128, S], BF16) for h in range(4)]
    KS = b.sb("KS", [128, S], BF16)
    KW = b.sb("KW", [128, S], BF16)
    KC = b.sb("KC", [128, 256], BF16)
    NEGM = b.sb("NEGM", [128, S], BF16)
    b.op("pool", lambda e: e.memset(NEGM[64:128, :], 0.0), [], [NEGM])
    Vs = b.sb("Vs", [128, 32, 65], BF16)
    Vw = b.sb("Vw", [128, 32, 65], BF16)
    VC = b.sb("VC", [128, 2, 129], BF16)
    Gtm = b.sb("Gtm", [128, 32, 32], F32)
    rawA = b.sb("rawA", [128, S], F32)
    rawB = b.sb("rawB", [128, S], F32)
    tF = [b.sb(f"tF{i}", [128, 512], F32) for i in range(3)]
    psc = [b.ps("psc", [128, 512]) for _ in range(4)]
    pacc = [b.ps("pacc", [128, 512]) for _ in range(2)]
    pcA = pacc[0]
    pcB = pacc[1]
    pm = [b.ps("pm", [128, 512]) for _ in range(2)]
    mcnt = [0]

    def misc():
        mcnt[0] += 1
        return pm[mcnt[0] % 2]

    rT = [b.sb(f"rT{i}", [64, 512], F32) for i in range(6)]
    rcnt = [0]

    def rms_rows(raw, out, wcol, NC=S, width=512):
        for c0 in range(0, NC, width):
            wd = min(width, NC - c0)
            cs = slice(c0, c0 + wd)
            rcnt[0] += 1
            t0, t1 = rT[(rcnt[0] % 3) * 2], rT[(rcnt[0] % 3) * 2 + 1]
            b.op("act", lambda e, cs=cs, wd=wd, t0=t0: e.activation(out=t0[0:64, 0:wd], in_=raw[0:64, cs], func=AF.Square), [raw], [t0])
            P = misc()
            b.mm(P[0:64, 0:wd], bones[0:64, 0:64], t0[0:64, 0:wd], True, True, [bones, t0], [P])
            b.op("act", lambda e, P=P, wd=wd, t1=t1: e.activation(out=t1[0:64, 0:wd], in_=P[0:64, 0:wd], func=AF.Sqrt, scale=1.0 / 64, bias=EPS), [P], [t1])
            b.op("dve", lambda e, wd=wd, t1=t1: e.reciprocal(out=t1[0:64, 0:wd], in_=t1[0:64, 0:wd]), [t1], [t1])
            b.op("dve", lambda e, cs=cs, wd=wd, t1=t1: e.scalar_tensor_tensor(out=out[0:64, cs], in0=raw[0:64, cs], scalar=wcol, in1=t1[0:64, 0:wd],
                                                                    op0=ALU.mult, op1=ALU.mult), [raw, t1, qnw, knw], [(out, c0)])

    for h in range(4):
        b.dma(rawA[:], pT[BLK_Q[h]], [pT], [rawA])
        rms_rows(rawA, QA[h], qnw[0:64, 0:1])
        b.dma(QA[h][64:68, :], Cn["n_qaug"][h], [Cn["n_qaug"]], [QA[h]], q="pool", max_dma_last_dim=4096)
    for (blk, Kt, Vt, wi) in ((BLK_KVS, KS, Vs, 1), (BLK_KVW, KW, Vw, 2)):
        b.dma(rawA[:], pT[blk], [pT], [rawA])
        rms_rows(rawA, Kt, knw[0:64, wi:wi + 1])
        b.dma(Kt[64:68, :], Cn["n_kaug"][:], [Cn["n_kaug"]], [Kt], q="pool", max_dma_last_dim=4096)
        b.op("pool", lambda e, Vt=Vt: e.memset(Vt[:, :, 64:65], 1.0), [], [Vt])
        for g4 in range(8):
            P = misc()
            for i in range(4):
                kt = g4 * 4 + i
                b.tr(P[:, i * 128:(i + 1) * 128], rawA[:, kt * 128:(kt + 1) * 128], ident[:], [rawA, ident], [P])
            b.op("dve", lambda e, P=P, Vt=Vt, g4=g4: e.tensor_copy(out=Vt[:, g4 * 4:g4 * 4 + 4, 0:64],
                                                                  in_=P[:].rearrange("p (i c) -> p i c", c=128)[:, :, 64:128]), [P], [Vt])
    b.dma(rawA[:], pT[BLK_GT], [pT], [rawA])
    b.op("act", lambda e: e.activation(out=rawA[0:32, :], in_=rawA[0:32, :], func=AF.Sigmoid), [rawA], [rawA])
    for g16 in range(2):
        P = misc()
        for i in range(16):
            kt = g16 * 16 + i
            b.tr(P[:, i * 32:(i + 1) * 32], rawA[0:32, kt * 128:(kt + 1) * 128], ident[0:32, 0:32], [rawA, ident], [P])
        b.op("dve", lambda e, P=P, g16=g16: e.tensor_copy(out=Gtm[:, g16 * 16:(g16 + 1) * 16, :], in_=P[:].rearrange("p (i c) -> p i c", c=32)), [P], [Gtm])
    b.dma(rawA[:], pT[BLK_KVC], [pT], [rawA])
    b.dma(rawB[:], d["ns_w1"][:].rearrange("p l m -> p (l m)"), [d["ns_w1"]], [rawB])
    b.op("pool", lambda e: e.memset(KC[:], 0.0), [], [KC])
    b.op("pool", lambda e: e.memset(VC[:], 0.0), [], [VC])
    bias_sb = b.sb("bias_sb", [128, 4], F32)
    for half in range(2):
        R = slice(64 * half, 64 * half + 64)
        Pb = misc()
        for l in range(32):
            b.mm(Pb[:, 0:2], rawB[R, l * 128:(l + 1) * 128], pos2[R, l, :], l == 0, l == 31, [rawB, pos2], [Pb])
        b.op("dve", lambda e, Pb=Pb, half=half: e.tensor_copy(out=bias_sb[:, 2 * half:2 * half + 2], in_=Pb[:, 0:2]), [Pb], [bias_sb])
    gl = [b.sb(f"gl{i}", [128, 256], F32) for i in range(2)]
    for half in range(2):
        R = slice(64 * half, 64 * half + 64)
        Ph = misc()
        for l in range(32):
            b.mm(Ph[:, 0:255], rawB[R, l * 128:(l + 1) * 128], rawA[R, l:l + 4065:16], l == 0, l == 31, [rawB, rawA], [Ph])
        hb, h2 = tF[0], tF[1]
        b.op("act", lambda e, Ph=Ph, half=half: e.activation(out=hb[:, 0:255], in_=Ph[:, 0:255], func=AF.Identity, bias=bias_sb[:, 2 * half:2 * half + 1]), [Ph, bias_sb], [hb])
        b.op("act", lambda e: e.activation(out=h2[:, 0:255], in_=hb[:, 0:255], func=AF.Square), [hb], [h2])
        b.op("dve", lambda e: e.tensor_scalar(out=h2[:, 0:255], in0=h2[:, 0:255], scalar1=0.044715, scalar2=1.0, op0=ALU.mult, op1=ALU.add), [h2], [h2])
        b.op("dve", lambda e: e.tensor_tensor(out=h2[:, 0:255], in0=h2[:, 0:255], in1=hb[:, 0:255], op=ALU.mult), [h2, hb], [h2])
        b.op("act", lambda e: e.activation(out=h2[:, 0:255], in_=h2[:, 0:255], func=AF.Sigmoid, scale=1.5957691216057308), [h2], [h2])
        b.op("dve", lambda e, half=half: e.tensor_tensor(out=gl[half][:, 0:255], in0=h2[:, 0:255], in1=hb[:, 0:255], op=ALU.mult), [h2, hb], [gl[half]])
    Pk = misc()
    b.mm(Pk[0:64, 0:255], w2[:, 0, :], gl[0][:, 0:255], True, True, [w2, gl[0]], [Pk])
    kc_sb = b.sb("kc_sb", [128, 256], F32)
    b.op("dve", lambda e: e.tensor_copy(out=kc_sb[0:64, 0:255], in_=Pk[0:64, 0:255]), [Pk], [kc_sb])
    rms_rows(kc_sb, KC, knw[0:64, 0:1], NC=255, width=255)
    b.dma(KC[64:68, :], Cn["n_kcaug"][:], [Cn["n_kcaug"]], [KC], q="pool")
    for nt in range(2):
        ncols = 128 if nt == 0 else 127
        Pv = misc()
        b.mm(Pv[0:ncols, 0:64], gl[1][:, nt * 128:nt * 128 + ncols], w2[:, 1, :], True, True, [gl[1], w2], [Pv])
        b.op("dve", lambda e, Pv=Pv, nt=nt, ncols=ncols: e.tensor_copy(out=VC[0:ncols, nt, 0:64], in_=Pv[0:ncols, 0:64]), [Pv], [VC])
        b.dma(VC[:, nt, 64:129], Cn["n_ovl1"][:, nt, :], [Cn["n_ovl1"]], [VC], q="pool")
    if debug:
        dump("gl0", gl[0], gl[0][:, 0:255], [128, 255])
        dump("gl1", gl[1], gl[1][:, 0:255], [128, 255])
        dump("kc_sb", kc_sb, kc_sb[0:64, 0:255], [64, 255])
        dump("bias_sb", bias_sb, bias_sb[:], [128, 4])
        for h in range(4):
            dump(f"QA{h}", QA[h], QA[h][0:68, :], [68, S], BF16)
        dump("KS", KS, KS[0:68, :], [68, S], BF16)
        dump("KC", KC, KC[0:68, :], [68, 256], BF16)
        dump("VC", VC, VC[:], [128, 2, 129], BF16)
        dump("Vs", Vs, Vs[:], [128, 32, 65], BF16)
        dump("Gtm", Gtm, Gtm[:], [128, 32, 32])

    PC = [[b.sb(f"PC{h}_{nt}", [128, 512], BF16) for nt in range(2)] for h in range(4)]
    PT = [b.sb(f"PT{i}", [128, 512], BF16) for i in range(5)]
    OUT = b.sb("OUT", [128, 4, 256], F32)
    okt = b.sb("okt", [128, 4, 64], F32)
    addt = b.sb("addt", [128, 4, 64], F32)
    imp = b.sb("imp", [128, 64], F32)
    imp2 = b.sb("imp2", [128, 64], F32)
    imp3 = b.sb("imp3", [128, 64], F32)
    m8a = b.sb("m8a", [128, 8], F32)
    m8b = b.sb("m8b", [128, 8], F32)
    msk = b.sb("msk", [128, 4, 64], F32)
    l4 = b.sb("l4", [128, 4], F32)
    rg4 = b.sb("rg4", [128, 4], F32)
    rl4 = b.sb("rl4", [128, 4], F32)
    YC = [b.sb(f"YC{i}", [128, 512], BF16) for i in range(2)]
    scnt = [0]
    pcnt = [0]
    acnt = [0]

    for qg in range(ngroups or (S // 512)):
        qs = slice(qg * 512, (qg + 1) * 512)
        b.dma(okt[:], Cn["n_ok"][:, 4 * qg:4 * qg + 4, :], [Cn["n_ok"]], [okt])
        b.dma(addt[:], Cn["n_addc"][:, 4 * qg:4 * qg + 4, :], [Cn["n_addc"]], [addt])
        nts = [0] if qg < 4 else [0, 1]
        for h in range(4):
            for nt in nts:
                g = qg if nt == 0 else qg - 4
                masked = (g <= 4)
                sc = psc[scnt[0] % 4]
                scnt[0] += 1
                b.mm(sc[:], KC[0:68, nt * 128:(nt + 1) * 128], QA[h][0:68, qs], True, not masked, [KC, QA[h]], [sc])
                if masked:
                    b.mm(sc[:], identb[:], negc[:, g, :], False, True, [identb, negc], [sc])
                b.op("act", lambda e, sc=sc, h=h, nt=nt: e.activation(out=PC[h][nt][:], in_=sc[:], func=AF.Exp, scale=0.125), [sc], [PC[h][nt]])
        for s in range(4):
            tile = 4 * qg + s
            ss = slice(s * 128, (s + 1) * 128)
            for h in range(4):
                bank = pcA if h < 2 else pcB
                reg = slice((h % 2) * 129, (h % 2) * 129 + 129)
                for i, nt in enumerate(nts):
                    b.mm(bank[:, reg], PC[h][nt][:, ss], VC[:, nt, :], i == 0, i == len(nts) - 1, [PC[h][nt], VC], [bank])
            for bi, bank in enumerate((pcA, pcB)):
                b.op("dve", lambda e, bank=bank, bi=bi: e.tensor_scalar(out=l4[:, 2 * bi:2 * bi + 2],
                                                                       in0=bank[:, 0:258].rearrange("p (h c) -> p h c", c=129)[:, :, 64],
                                                                       scalar1=1e-30, scalar2=None, op0=ALU.max), [bank], [(l4, bi)])
            b.op("dve", lambda e: e.reciprocal(out=rl4[:], in_=l4[:]), [l4], [rl4])
            b.op("dve", lambda e, tile=tile: e.tensor_tensor(out=rg4[:], in0=rl4[:], in1=Gtm[:, tile, 0:12:3], op=ALU.mult), [rl4, Gtm], [rg4])
            for h in range(4):
                bank = pcA if h < 2 else pcB
                o0 = (h % 2) * 129
                b.op("dve", lambda e, bank=bank, o0=o0, h=h, s=s: e.tensor_scalar(out=OUT[:, s, h * 64:(h + 1) * 64], in0=bank[:, o0:o0 + 64],
                                                                             scalar1=rg4[:, h:h + 1], scalar2=None, op0=ALU.mult), [bank, rg4], [(OUT, s)])
                if h == 0:
                    b.op("dve", lambda e, bank=bank, o0=o0: e.tensor_scalar(out=imp[:], in0=bank[:, o0 + 65:o0 + 129], scalar1=rl4[:, 0:1], scalar2=None, op0=ALU.mult),
                         [bank, rl4], [imp])
                else:
                    b.op("dve", lambda e, bank=bank, o0=o0, h=h: e.scalar_tensor_tensor(out=imp[:], in0=bank[:, o0 + 65:o0 + 129], scalar=rl4[:, h:h + 1], in1=imp[:],
                                                                                   op0=ALU.mult, op1=ALU.add), [bank, rl4, imp], [imp])
            b.op("dve", lambda e, s=s: e.tensor_tensor(out=imp2[:], in0=imp[:], in1=okt[:, s, :], op=ALU.mult), [imp, okt], [imp2])
            b.op("dve", lambda e, s=s: e.tensor_tensor(out=imp2[:], in0=imp2[:], in1=addt[:, s, :], op=ALU.add), [imp2, addt], [imp2])
            b.op("dve", lambda e: e.max(out=m8a[:], in_=imp2[:]), [imp2], [m8a])
            b.op("dve", lambda e: e.match_replace(out=imp3[:], in_to_replace=m8a[:], in_values=imp2[:], imm_value=-1e30), [m8a, imp2], [imp3])
            b.op("dve", lambda e: e.max(out=m8b[:], in_=imp3[:]), [imp3], [m8b])
            b.op("dve", lambda e, s=s: e.tensor_scalar(out=msk[:, s, :], in0=imp2[:], scalar1=m8b[:, 7:8], scalar2=None, op0=ALU.is_ge), [imp2, m8b], [(msk, s)])
        if debug and qg == 0:
            dump("NEGM0", NEGM, NEGM[0:64, 0:512], [64, 512], BF16)
            dump("msk", msk, msk[:], [128, 4, 64])
            dump("OUTcmp", OUT, OUT[:], [128, 4, 256])
        for br, (Kt, Vt) in ((2, (KW, Vw)), (1, (KS, Vs))):
            if br == 1:
                Pm = misc()
                for s in range(4):
                    b.tr(Pm[0:64, s * 128:(s + 1) * 128], msk[:, s, :], ident[:], [(msk, s), ident], [Pm])
                b.op("dve", lambda e, Pm=Pm, qs=qs: e.tensor_scalar(out=NEGM[0:64, qs], in0=Pm[0:64, :], scalar1=-1.0, scalar2=NBIG, op0=ALU.add, op1=ALU.mult), [Pm], [(NEGM, qg)])
            for h in range(4):
                acc = pacc[acnt[0] % 2]
                acnt[0] += 1
                if br == 1:
                    kts = list(range(0, 4 * qg + 4))
                else:
                    kts = list(range(max(0, 4 * qg - 4), 4 * qg + 4))
                pv = []
                for kt in kts:
                    for s in range(4):
                        u = 4 * qg + s - kt
                        if u < 0 or (br == 2 and u > 4):
                            continue
                        pv.append((kt, s))
                npv = len(pv)
                pvset = set(pv)
                ipv = [0]

                def emit_sc(kt):
                    sc = psc[scnt[0] % 4]
                    scnt[0] += 1
                    ks = slice(kt * 128, (kt + 1) * 128)
                    b.mm(sc[:], Kt[0:68, ks], QA[h][0:68, qs], True, False, [Kt, QA[h]], [sc])
                    if br == 1:
                        diag = kt >= 4 * qg
                        b.mm(sc[:], eall[:, ks], NEGM[:, qs], False, not diag, [eall, (NEGM, qg)], [sc])
                        if diag:
                            b.mm(sc[:], identb[:], cmsel[:, kt - 4 * qg, :], False, True, [identb, cmsel], [sc])
                    else:
                        b.mm(sc[:], identb[:], cmwin[:, kt - 4 * qg + 4, :], False, True, [identb, cmwin], [sc])
                    P_ = PT[pcnt[0] % 5]
                    pcnt[0] += 1
                    b.op("act", lambda e, sc=sc, P_=P_: e.activation(out=P_[:], in_=sc[:], func=AF.Exp, scale=0.125), [sc], [P_])
                    return P_

                def emit_pv(kt, P_):
                    for s in range(4):
                        if (kt, s) not in pvset:
                            continue
                        b.mm(acc[:, s * 65:(s + 1) * 65], P_[:, s * 128:(s + 1) * 128], Vt[:, kt, :], ipv[0] == 0, ipv[0] == npv - 1, [P_, Vt], [acc])
                        ipv[0] += 1
                LA = 3
                pend = []
                nk = len(kts)
                for i in range(min(LA, nk)):
                    pend.append(emit_sc(kts[i]))
                for i in range(nk):
                    if i + LA < nk:
                        pend.append(emit_sc(kts[i + LA]))
                    emit_pv(kts[i], pend.pop(0))
                if debug and qg == (ngroups or 8) - 1:
                    b.op("dve", lambda e, acc=acc: e.tensor_copy(out=tF[2][:, 0:260], in_=acc[:, 0:260]), [acc], [tF[2]])
                    dump(f"acc_br{br}_h{h}", tF[2], tF[2][:, 0:260], [128, 260])
                accv = acc[:, 0:260].rearrange("p (s c) -> p s c", c=65)
                b.op("dve", lambda e, accv=accv: e.reciprocal(out=rl4[:], in_=accv[:, :, 64]), [acc], [rl4])
                b.op("dve", lambda e, h=h, br=br, qg=qg: e.tensor_tensor(out=rg4[:], in0=rl4[:], in1=Gtm[:, 4 * qg:4 * qg + 4, h * 3 + br], op=ALU.mult), [rl4, Gtm], [rg4])
                for s in range(4):
                    b.op("dve", lambda e, acc=acc, s=s, h=h: e.scalar_tensor_tensor(out=OUT[:, s, h * 64:(h + 1) * 64], in0=acc[:, s * 65:s * 65 + 64], scalar=rg4[:, s:s + 1],
                                                                              in1=OUT[:, s, h * 64:(h + 1) * 64], op0=ALU.mult, op1=ALU.add), [acc, rg4, (OUT, s)], [(OUT, s)])
        if debug and qg == 0:
            dump("OUTfin", OUT, OUT[:], [128, 4, 256])
        for hp in range(2):
            Py = misc()
            for s in range(4):
                b.tr(Py[:, s * 128:(s + 1) * 128], OUT[:, s, hp * 128:(hp + 1) * 128], ident[:], [(OUT, s), ident], [Py])
            b.op("dve" if hp else "act", (lambda e, Py=Py, hp=hp: e.tensor_copy(out=YC[hp][:], in_=Py[:])) if hp else
                 (lambda e, Py=Py, hp=hp: e.activation(out=YC[hp][:], in_=Py[:], func=AF.Copy)), [Py], [YC[hp]])
            b.dma(yT[4 + hp, :, qs], YC[hp][:], [YC[hp]], [(yT, (4 + hp, qg))])
            if debug and qg == 0:
                dump(f"YC{hp}", YC[hp], YC[hp][:], [128, 512], BF16)
    b.end()


import numpy as np

S = 4096


def lay_in(w):
    F = w.shape[1]
    return np.ascontiguousarray(w.reshape(8, 128, F // 128, 128).transpose(2, 1, 0, 3))


def lay_dn(w):
    return np.ascontiguousarray(w.reshape(22, 128, 8, 128).transpose(2, 1, 0, 3))


def lay_vec(v):
    return np.ascontiguousarray(v.reshape(-1, 128).T)


def win_padded(w_in):
    out = np.zeros((1024, 24 * 128), np.float32)
    def put(blk, off, c0, n):
        out[:, blk * 128 + off: blk * 128 + off + n] = w_in[:, c0:c0 + n]
    put(0, 0, 0, 128); put(1, 0, 128, 128)
    B = 256
    put(2, 0, B, 128); put(3, 0, B + 128, 128)
    put(4, 0, B + 256, 128); put(5, 0, B + 384, 128)
    put(6, 0, B + 512, 128); put(7, 0, B + 640, 128)
    put(8, 0, B + 768, 64)
    put(8, 64, B + 768 + 64 + 32, 64)
    put(9, 0, B + 768 + 64, 32)
    C = 256 + 928
    for hh in range(4):
        put(10 + hh, 0, C + 64 * hh, 64)
    put(14, 0, C + 256, 128)
    put(15, 0, C + 384, 128)
    put(16, 0, C + 512, 128)
    put(17, 0, C + 640, 12)
    D = 256 + 928 + 652
    for i in range(6):
        put(18 + i, 0, D + 128 * i, 128)
    return out


def prep_layer(inp, l):
    d = {}
    for nm, key in (("f1", "ffn1"), ("f2", "ffn2")):
        d[f"{nm}_g"] = lay_vec(inp[f"{key}_norm"][l])
        d[f"{nm}_wg"] = lay_in(inp[f"{key}_w_gate"][l])
        d[f"{nm}_wu"] = lay_in(inp[f"{key}_w_up"][l])
        d[f"{nm}_wd"] = lay_dn(inp[f"{key}_w_down"][l])
    d["mix_g"] = lay_vec(inp["mix_norm"][l])
    d["win"] = lay_in(win_padded(inp["w_in"][l]))
    d["wout"] = np.ascontiguousarray(inp["w_out"][l].reshape(8, 128, 8, 128).transpose(2, 1, 0, 3))
    d["cw"] = np.ascontiguousarray(inp["conv_w"][l].reshape(3, 2, 128).transpose(2, 1, 0))
    pw = inp["pool_w"][l]
    pwbd = np.zeros((2, 128, 128), np.float32)
    for j in range(2):
        for gg in range(2):
            pwbd[j, gg * 64:(gg + 1) * 64, gg * 64:(gg + 1) * 64] = pw[2 * j + gg]
    d["pw"] = pwbd
    d["psc"] = lay_vec(inp["pool_scale"][l])
    prep_rwkv(inp, l, d)
    prep_nsa(inp, l, d)
    return d


def consts():
    c = {}
    wins = np.array([2, 4, 8, 16], np.float32)
    w_p = np.zeros((128, 2), np.float32)
    for j in range(2):
        w_p[0:64, j] = wins[2 * j]
        w_p[64:128, j] = wins[2 * j + 1]
    c["pinv"] = (1.0 / w_p).astype(np.float32)
    t = np.arange(16, dtype=np.float32)
    c["pcorr"] = (1.0 / np.minimum(t[None, None, :] + 1, w_p[:, :, None])).astype(np.float32)
    consts_rwkv(c)
    consts_nsa(c)
    return c


def prep_rwkv(inp, l, d):
    mu = inp["rwkv_mu"][l]
    mu8 = np.zeros((128, 8), np.float32)
    for i in range(6):
        mu8[:, i] = mu[i * 128:(i + 1) * 128]
    mu8[0:64, 6] = mu[768:832]
    mu8[64:128, 6] = mu[864:928]
    mu8[0:32, 7] = mu[832:864]
    d["rw_mu8"] = mu8
    vec = np.zeros((128, 2, 7), np.float32)
    names = ["rwkv_w0", "rwkv_a0", "rwkv_k_k", "rwkv_k_a", "rwkv_r_k", "rwkv_ln_w", "rwkv_ln_b"]
    for i, nm in enumerate(names):
        v = inp[nm][l].reshape(256)
        vec[:, 0, i] = v[0:128]
        vec[:, 1, i] = v[128:256]
    d["rw_vec"] = vec
    wup = np.zeros((128, 2, 128), np.float32)
    gup = np.zeros((128, 2, 128), np.float32)
    aup = np.zeros((128, 2, 128), np.float32)
    for hp in range(2):
        wup[0:64, hp, :] = inp["rwkv_w_up"][l][:, hp * 128:(hp + 1) * 128]
        gup[64:128, hp, :] = inp["rwkv_g_up"][l][:, hp * 128:(hp + 1) * 128]
        aup[0:32, hp, :] = inp["rwkv_a_up"][l][:, hp * 128:(hp + 1) * 128]
    d["rw_wup"], d["rw_gup"], d["rw_aup"] = wup, gup, aup


def consts_rwkv(c):
    bones = np.zeros((128, 128), np.float32)
    bones[0:64, 0:64] = 1
    bones[64:128, 64:128] = 1
    c["bones"] = bones
    c["bones64"] = bones / 64.0
    c["ident"] = np.eye(128, dtype=np.float32)
    c["sel2"] = np.concatenate([np.eye(64, dtype=np.float32)] * 2, axis=0)
    rm = np.ones((128, 512), np.float32)
    rm[:, 0::64] = 0
    c["rmask"] = rm
    j = np.arange(64)[:, None]
    t = np.arange(64)[None, :]
    strict = (j < t).astype(np.float32)
    incl = (j <= t).astype(np.float32)
    def bd(m):
        o = np.zeros((128, 128), np.float32)
        o[0:64, 0:64] = m
        o[64:128, 64:128] = m
        return o
    c["LPmask"] = np.concatenate([-bd(strict), bd(incl), bd(strict), bd(incl)], axis=1)
    c["Lamask"] = -bd((j > t).astype(np.float32))


def prep_nsa(inp, l, d):
    qn = np.zeros((128, 1), np.float32)
    qn[0:64, 0] = inp["nsa_q_norm"][l]
    d["ns_qnw"] = qn
    kn = np.zeros((128, 3), np.float32)
    kn[0:64, :] = inp["nsa_k_norm"][l].T
    d["ns_knw"] = kn
    w1 = np.zeros((128, 32, 128), np.float32)
    w1[0:64] = inp["nsa_cmp_k_w1"][l].reshape(32, 64, 128).transpose(1, 0, 2)
    w1[64:128] = inp["nsa_cmp_v_w1"][l].reshape(32, 64, 128).transpose(1, 0, 2)
    d["ns_w1"] = w1
    pos = inp["nsa_cmp_pos"][l]
    p2 = np.zeros((128, 32, 2), np.float32)
    p2[0:64, :, 0] = pos.T
    p2[0:64, :, 1] = pos.T
    p2[64:128] = p2[0:64]
    d["ns_pos2"] = p2
    w2 = np.zeros((128, 2, 64), np.float32)
    w2[:, 0, :] = inp["nsa_cmp_k_w2"][l]
    w2[:, 1, :] = inp["nsa_cmp_v_w2"][l]
    d["ns_w2"] = w2


def consts_nsa(c):
    BIG = 30000.0
    key = np.arange(S)
    c["n_eall"] = (key[None, :] // 64 == np.arange(64)[:, None]).astype(np.float32)
    pk = np.arange(128)[:, None]
    tq = np.arange(512)[None, :]
    c["n_cmsel"] = np.stack([np.where(tq - pk - 128 * r >= 0, 0.0, -BIG) for r in range(4)]).astype(np.float32)
    cw = []
    for r in range(8):
        dd = tq - pk - (r - 4) * 128
        cw.append(np.where((dd >= 0) & (dd < 512), 0.0, -BIG))
    c["n_cmwin"] = np.stack(cw).astype(np.float32)
    c["n_negc"] = np.stack([np.where(tq >= 16 * pk + 31 - 512 * g, 0.0, -BIG) for g in range(5)]).astype(np.float32)
    t = np.arange(S)
    cur = t // 64
    j = np.arange(64)
    ok = (j[None, :] <= cur[:, None]).astype(np.float32)
    forced = ((j[None, :] == 0) | (j[None, :] == cur[:, None]) | (j[None, :] == cur[:, None] - 1)).astype(np.float32)
    addc = forced * 1e4 * ok + (ok - 1.0)
    c["n_ok"] = np.ascontiguousarray(ok.reshape(32, 128, 64).transpose(1, 0, 2))
    c["n_addc"] = np.ascontiguousarray(addc.reshape(32, 128, 64).transpose(1, 0, 2)).astype(np.float32)
    slopes = 2.0 ** (-8.0 * np.arange(1, 5) / 4)
    a_t, b_t = t // 64, t % 64
    qaug = np.zeros((4, 4, S), np.float32)
    for h in range(4):
        s8 = 8.0 * slopes[h]
        qaug[h, 0] = -s8 * 64 * a_t
        qaug[h, 1] = -s8 * b_t
        qaug[h, 2] = s8
        qaug[h, 3] = s8
    c["n_qaug"] = qaug
    kaug = np.zeros((4, S), np.float32)
    kaug[0] = 1
    kaug[1] = 1
    kaug[2] = 64 * a_t
    kaug[3] = b_t
    c["n_kaug"] = kaug
    n = np.arange(256)
    be = 16 * n + 31
    kc = np.zeros((4, 256), np.float32)
    kc[0] = 1
    kc[1] = 1
    kc[2] = 64 * (be // 64)
    kc[3] = be % 64
    c["n_kcaug"] = kc
    ovl = ((16 * n[:, None] <= 64 * j[None, :] + 63) & (16 * n[:, None] + 31 >= 64 * j[None, :])).astype(np.float32)
    o1 = np.concatenate([np.ones((256, 1), np.float32), ovl], axis=1)
    o1[255] = 0
    c["n_ovl1"] = np.ascontiguousarray(o1.reshape(2, 128, 65).transpose(1, 0, 2))


HAVE_RWKV = True
HAVE_NSA = True
_CACHE = {}


def build_program(layer_shapes, const_shapes):
    b = Builder()
    xin = b.dram("xin", [8, 128, S], F32, "ExternalInput")
    xout = b.dram("xout", [8, 128, S], F32, "ExternalOutput")
    xT = b.dram("xT", [8, 128, S], F32)
    pT = b.dram("pT", [NBLK, 128, S], F32)
    yT = b.dram("yT", [8, 128, S], BF16)
    D = []
    for l in range(2):
        D.append({k: b.dram(f"L{l}_{k}", list(shp), F32, "ExternalInput") for k, shp in layer_shapes.items()})
    Cn = {k: b.dram(f"C_{k}", list(shp), F32, "ExternalInput") for k, shp in const_shapes.items()}
    for l in range(2):
        d = D[l]
        ffn_phase(b, xT, d["f1_g"], d["f1_wg"], d["f1_wu"], d["f1_wd"], xsrc=(xin if l == 0 else None))
        inproj_phase(b, xT, d["mix_g"], d["win"], pT)
        conv_pool_phase(b, pT, yT, d["cw"], d["pw"], d["psc"], Cn["pinv"], Cn["pcorr"])
        if HAVE_RWKV:
            rwkv_phase(b, pT, yT, d, Cn)
        if HAVE_NSA:
            nsa_phase(b, pT, yT, d, Cn)
        outproj_phase(b, xT, yT, d["wout"])
        ffn_phase(b, xT, d["f2_g"], d["f2_wg"], d["f2_wu"], d["f2_wd"], xdst=(xout if l == 1 else None))
    return b


def kernel(**inputs):
    inp = {k: np.asarray(v) for k, v in inputs.items()}
    layers = [prep_layer(inp, l) for l in range(2)]
    cn = consts()
    b = build_program({k: v.shape for k, v in layers[0].items()}, {k: v.shape for k, v in cn.items()})
    shared = {}
    for l in range(2):
        for k, v in layers[l].items():
            shared[f"L{l}_{k}"] = v
    for k, v in cn.items():
        shared[f"C_{k}"] = v
    x = inp["x"]
    in_maps = []
    for c in range(8):
        m = dict(shared)
        m["xin"] = np.ascontiguousarray(x[c].T.reshape(8, 128, S))
        in_maps.append(m)
    res = run_bass_kernel_spmd(b.nc, in_maps, core_ids=list(range(8)))
    out = np.stack([res.results[c]["xout"].reshape(1024, S).T for c in range(8)], axis=0)
    return np.ascontiguousarray(out.astype(np.float32))
```

```python
import numpy as np
from contextlib import ExitStack
import concourse.bass as bass
import concourse.mybir as mybir
from concourse.bass_utils import run_bass_kernel_spmd

F32 = mybir.dt.float32
BF16 = mybir.dt.bfloat16
AF = mybir.ActivationFunctionType
ALU = mybir.AluOpType
AX = mybir.AxisListType


class Buf:
    def __init__(self, name, t, space):
        self.name = name
        self.t = t
        self.space = space

    def __getitem__(self, idx):
        return self.t[idx]


def _norm(lst):
    out = []
    for x in lst:
        if isinstance(x, tuple):
            out.append(x)
        else:
            out.append((x, None))
    return out


def _conf(a, b):
    return a is None or b is None or a == b


class Builder:
    ENG = ("pe", "act", "dve", "pool", "sp")
    NDMA = 12

    def __init__(self):
        self.nc = bass.Bass("TRN2", target_bir_lowering=False)
        self.gstack = ExitStack()
        self.sems = []
        self.ekey = {}
        for e in ("pe", "act", "dve", "pool"):
            self.ekey[e] = self._newsem("c_" + e)
        self.dkeys = {q: [self._newsem(f"d_{q}{i}") for i in range(self.NDMA)] for q in ("sp", "pool", "act")}
        self.drr = {"sp": 0, "pool": 0, "act": 0}
        self.semval = [0] * len(self.sems)
        self.seen = {e: {} for e in self.ENG}
        self.recs = {}
        self.ops = {e: [] for e in self.ENG}
        self.pstack = None
        self.nphase = 0
        self.uid = 0

    def _newsem(self, name):
        s = self.gstack.enter_context(self.nc.semaphore(name))
        self.sems.append(s)
        return len(self.sems) - 1

    def dram(self, name, shape, dtype, kind="Internal"):
        t = self.nc.dram_tensor(name, list(shape), dtype, kind=kind)
        return Buf(name, t.ap(), "dram")

    def begin(self):
        assert self.pstack is None
        self.pstack = ExitStack()
        self.nphase += 1

    def sb(self, name, shape, dtype):
        self.uid += 1
        nm = f"{name}_{self.uid}"
        t = self.pstack.enter_context(self.nc.sbuf_tensor(nm, list(shape), dtype))
        return Buf(nm, t, "sbuf")

    def ps(self, name, shape, dtype=F32):
        self.uid += 1
        nm = f"{name}_{self.uid}"
        t = self.pstack.enter_context(self.nc.psum_tensor(nm, list(shape), dtype))
        return Buf(nm, t, "psum")

    def op(self, eng, fn, reads=(), writes=(), dma=False, acc=False):
        reads = _norm(reads)
        writes = _norm(writes)
        if not acc:
            for (bb, tt) in reads:
                if bb.space == "psum" and (bb, tt) not in writes:
                    writes = writes + [(bb, tt)]
        waits = {}

        def need(k, v):
            if v > waits.get(k, 0):
                waits[k] = v

        for (b, t) in _norm(reads):
            for (tag, isw, k), v in self.recs.get(b.name, {}).items():
                if isw and _conf(tag, t):
                    need(k, v)
        for (b, t) in _norm(writes):
            for (tag, isw, k), v in self.recs.get(b.name, {}).items():
                if _conf(tag, t):
                    if acc and isw and k == self.ekey["pe"]:
                        continue
                    need(k, v)
        if dma:
            q = eng
            kd = self.dkeys[q][self.drr[q] % self.NDMA]
            self.drr[q] += 1
            need(kd, self.semval[kd])
            self.semval[kd] += 16
            tok = (kd, self.semval[kd])
            inc = 16
        else:
            kd = self.ekey[eng]
            self.semval[kd] += 1
            tok = (kd, self.semval[kd])
            inc = 1
        wl = []
        for k, v in waits.items():
            if v > 0 and self.seen[eng].get(k, 0) < v:
                self.seen[eng][k] = v
                wl.append((k, v))
        self.ops[eng].append((wl, fn, tok[0], inc))
        for (b, t) in _norm(reads):
            r = self.recs.setdefault(b.name, {})
            key = (t, False, tok[0])
            r[key] = max(r.get(key, 0), tok[1])
        for (b, t) in _norm(writes):
            r = self.recs.setdefault(b.name, {})
            for key in list(r.keys()):
                if t is None or key[0] == t:
                    del r[key]
            r[(t, True, tok[0])] = tok[1]
        return tok

    def end(self):
        wl = []
        for q in self.dkeys:
            for kd in self.dkeys[q]:
                v = self.semval[kd]
                if v > 0 and self.seen["sp"].get(kd, 0) < v:
                    self.seen["sp"][kd] = v
                    wl.append((kd, v))
        self.ops["sp"].append((wl, None, None, 0))
        nc = self.nc
        ops = self.ops
        sems = self.sems

        def replay(lst, e):
            for wl_, fn, k, inc in lst:
                for kk, v in wl_:
                    e.wait_ge(sems[kk], v)
                if fn is not None:
                    ins = fn(e)
                    ins.then_inc(sems[k], inc)

        with nc.Block() as block:
            @block.tensor
            def _(e):
                replay(ops["pe"], e)

            @block.scalar
            def _(e):
                replay(ops["act"], e)

            @block.vector
            def _(e):
                replay(ops["dve"], e)

            @block.gpsimd
            def _(e):
                replay(ops["pool"], e)

            @block.sync
            def _(e):
                replay(ops["sp"], e)
        self.ops = {e: [] for e in self.ENG}
        self.pstack.close()
        self.pstack = None
        for name in list(self.recs.keys()):
            pass

    def dma(self, out, in_, reads, writes, q="sp", **kw):
        return self.op(q, lambda e: e.dma_start(out=out, in_=in_, **kw), reads, writes, dma=True)

    def mm(self, out, lhsT, rhs, start, stop, reads, writes):
        return self.op("pe", lambda e: e.matmul(out, lhsT, rhs, start=start, stop=stop), reads, writes, acc=True)

    def tr(self, out, in_, ident, reads, writes):
        return self.op("pe", lambda e: e.transpose(out, in_, ident), reads, writes, acc=True)

    def actf(self, out, in_, func, reads, writes, bias=None, scale=1.0, eng="act"):
        kw = {}
        if bias is not None:
            kw["bias"] = bias
        return self.op(eng, lambda e: e.activation(out=out, in_=in_, func=func, scale=scale, **kw), reads, writes)


S = 4096
T = 1024
NT = S // T
NH = T // 512
EPS = 1e-6
import os as _os1
_NOW = bool(_os1.environ.get('NO_WDMA'))


def load_norm_gen(b, xT, ti, x, g, ones, hT, sq, rstd, pst):
    tsl = slice(ti * T, (ti + 1) * T)
    b.dma(x[:], xT[:, :, tsl].rearrange("k p t -> p k t"), [(xT, ti)], [x])
    yield
    for k in range(8):
        s = sq[k % 2]
        b.op("act", lambda e, s=s, k=k: e.activation(out=s[:], in_=x[:, k, :], func=AF.Square), [x], [s])
        for h in range(NH):
            b.mm(pst[h][:], ones[:], s[:, h * 512:(h + 1) * 512], k == 0, k == 7, [ones, s], [pst[h]])
        yield
    for h in range(NH):
        sl = slice(h * 512, (h + 1) * 512)
        b.op("act", lambda e, h=h, sl=sl: e.activation(out=rstd[:, sl], in_=pst[h][:], func=AF.Sqrt, scale=1.0 / 1024, bias=EPS),
             [pst[h]], [(rstd, h)])
        b.op("dve", lambda e, sl=sl: e.reciprocal(out=rstd[:, sl], in_=rstd[:, sl]), [(rstd, h)], [(rstd, h)])
    yield
    for k in range(8):
        b.op("dve", lambda e, k=k: e.scalar_tensor_tensor(out=hT[:, k, :], in0=x[:, k, :], scalar=g[:, k:k + 1], in1=rstd[:],
                                                         op0=ALU.mult, op1=ALU.mult), [x, g, rstd], [(hT, k)])
        yield


def gstep(gen):
    if gen is None:
        return None
    try:
        next(gen)
        return gen
    except StopIteration:
        return None


def gdrain(gen):
    while gen is not None:
        gen = gstep(gen)


def ffn_phase(b, xT, gD, WgD, WuD, WdD, xsrc=None, xdst=None):
    xsrc = xsrc or xT
    xdst = xdst or xT
    b.begin()
    ones = b.sb("ones", [128, 128], BF16)
    b.op("pool", lambda e: e.memset(ones[:], 1.0), [], [ones])
    epsb = None
    g = b.sb("g", [128, 8], F32)
    b.dma(g[:], gD[:], [gD], [g])
    xs = [b.sb("x", [128, 8, T], F32) for _ in range(2)]
    hTs = [b.sb("hT", [128, 8, T], BF16) for _ in range(2)]
    hid = b.sb("hid", [128, 22, T], BF16)
    sq = [b.sb("sq", [128, T], BF16) for _ in range(2)]
    rstds = [b.sb("rstd", [128, T], F32) for _ in range(2)]
    wg = [b.sb("wg", [128, 8, 128], BF16) for _ in range(2)]
    wu = [b.sb("wu", [128, 8, 128], BF16) for _ in range(2)]
    wd = [b.sb("wd", [128, 22, 128], BF16) for _ in range(2)]
    sg = [b.sb("sg", [128, 512], F32) for _ in range(2)]
    pg = [b.ps("pg", [128, 512]) for _ in range(2)]
    pu = [b.ps("pu", [128, 512]) for _ in range(2)]
    po = [b.ps("po", [128, 512]) for _ in range(2)]
    pst = [b.ps("pst", [128, 512]) for _ in range(2)]
    cnt = 0
    gdrain(load_norm_gen(b, xsrc, 0, xs[0], g, ones, hTs[0], sq, rstds[0], pst))
    for ti in range(NT):
        x = xs[ti % 2]
        hT = hTs[ti % 2]
        tsl = slice(ti * T, (ti + 1) * T)
        nxt = None
        for f in range(22):
            if f == 2 and ti + 1 < NT:
                nxt = load_norm_gen(b, xsrc, ti + 1, xs[(ti + 1) % 2], g, ones, hTs[(ti + 1) % 2], sq, rstds[(ti + 1) % 2], pst)
            nxt = gstep(nxt)
            a, u = wg[f % 2], wu[f % 2]
            if not (_NOW and (ti > 0 or f > 1)):
                b.dma(a[:], WgD[f], [WgD], [a], q="pool", max_dma_last_dim=4096)
                b.dma(u[:], WuD[f], [WuD], [u], q="pool", max_dma_last_dim=4096)
            for h in range(NH):
                sl = slice(h * 512, (h + 1) * 512)
                G, U, SG = pg[cnt % 2], pu[cnt % 2], sg[cnt % 2]
                cnt += 1
                for k in range(8):
                    b.mm(G[:], a[:, k, :], hT[:, k, sl], k == 0, k == 7, [a, (hT, k)], [G])
                for k in range(8):
                    b.mm(U[:], u[:, k, :], hT[:, k, sl], k == 0, k == 7, [u, (hT, k)], [U])
                b.op("act", lambda e, G=G, SG=SG: e.activation(out=SG[:], in_=G[:], func=AF.Silu), [G], [SG])
                b.op("dve", lambda e, U=U, SG=SG, f=f, sl=sl: e.tensor_tensor(out=hid[:, f, sl], in0=U[:], in1=SG[:], op=ALU.mult),
                     [U, SG], [(hid, (f, h))])
        for d in range(8):
            w = wd[d % 2]
            if not (_NOW and (ti > 0 or d > 1)):
                b.dma(w[:], WdD[d], [WdD], [w], q="pool", max_dma_last_dim=4096)
            for h in range(NH):
                sl = slice(h * 512, (h + 1) * 512)
                O = po[cnt % 2]
                cnt += 1
                for f in range(22):
                    b.mm(O[:], w[:, f, :], hid[:, f, sl], f == 0, f == 21, [w, (hid, (f, h))], [O])
                b.op("dve", lambda e, O=O, d=d, sl=sl, x=x: e.scalar_tensor_tensor(out=x[:, d, sl], in0=O[:], scalar=0.5, in1=x[:, d, sl],
                                                                               op0=ALU.mult, op1=ALU.add), [O, x], [x])
        gdrain(nxt)
        b.dma(xdst[:, :, tsl].rearrange("k p t -> p k t"), x[:], [x], [(xdst, ti)])
    b.end()


BLK_A = (0, 1)
BLK_R, BLK_K, BLK_V, BLK_WG, BLK_AD = (2, 3), (4, 5), (6, 7), 8, 9
BLK_Q, BLK_KVC, BLK_KVS, BLK_KVW, BLK_GT = (10, 11, 12, 13), 14, 15, 16, 17
BLK_U, BLK_B, BLK_C = (18, 19), (20, 21), (22, 23)
NBLK = 24


def inproj_phase(b, xT, gD, WinD, pT):
    b.begin()
    ones = b.sb("ones", [128, 128], BF16)
    b.op("pool", lambda e: e.memset(ones[:], 1.0), [], [ones])
    g = b.sb("g", [128, 8], F32)
    b.dma(g[:], gD[:], [gD], [g])
    xs = [b.sb("x", [128, 8, T], F32) for _ in range(2)]
    hTs = [b.sb("hT", [128, 8, T], BF16) for _ in range(2)]
    sq = [b.sb("sq", [128, T], BF16) for _ in range(2)]
    rstds = [b.sb("rstd", [128, T], F32) for _ in range(2)]
    ws = [b.sb("w", [128, 8, 128], BF16) for _ in range(2)]
    os_ = [b.sb("o", [128, T], F32) for _ in range(2)]
    pp = [b.ps("pp", [128, 512]) for _ in range(4)]
    pst = [b.ps("pst", [128, 512]) for _ in range(2)]
    cnt = 0
    gdrain(load_norm_gen(b, xT, 0, xs[0], g, ones, hTs[0], sq, rstds[0], pst))
    for ti in range(NT):
        x = xs[ti % 2]
        hT = hTs[ti % 2]
        tsl = slice(ti * T, (ti + 1) * T)
        nxt = None
        for blk in range(NBLK):
            if blk == 2 and ti + 1 < NT:
                nxt = load_norm_gen(b, xT, ti + 1, xs[(ti + 1) % 2], g, ones, hTs[(ti + 1) % 2], sq, rstds[(ti + 1) % 2], pst)
            nxt = gstep(nxt)
            w = ws[blk % 2]
            o = os_[blk % 2]
            b.dma(w[:], WinD[blk], [WinD], [w], q="pool", max_dma_last_dim=4096)
            for h in range(NH):
                sl = slice(h * 512, (h + 1) * 512)
                P = pp[cnt % 4]
                cnt += 1
                for k in range(8):
                    b.mm(P[:], w[:, k, :], hT[:, k, sl], k == 0, k == 7, [w, (hT, k)], [P])
                if h % 2 == 0:
                    b.op("act", lambda e, P=P, o=o, sl=sl: e.activation(out=o[:, sl], in_=P[:], func=AF.Copy), [P], [(o, h)])
                else:
                    b.op("dve", lambda e, P=P, o=o, sl=sl: e.tensor_copy(out=o[:, sl], in_=P[:]), [P], [(o, h)])
            b.dma(pT[blk, :, tsl], o[:], [o], [(pT, (blk, ti))])
        gdrain(nxt)
    b.end()


def outproj_phase(b, xT, yT, WoutD):
    b.begin()
    xs = [b.sb("x", [128, 8, T], F32) for _ in range(2)]
    ys = [b.sb("y", [128, 8, T], BF16) for _ in range(2)]
    ws = [b.sb("w", [128, 8, 128], BF16) for _ in range(2)]
    pp = [b.ps("pp", [128, 512]) for _ in range(4)]
    cnt = 0
    def loads(ti):
        tsl_ = slice(ti * T, (ti + 1) * T)
        b.dma(xs[ti % 2][:], xT[:, :, tsl_].rearrange("k p t -> p k t"), [(xT, ti)], [xs[ti % 2]])
        b.dma(ys[ti % 2][:], yT[:, :, tsl_].rearrange("k p t -> p k t"), [yT], [ys[ti % 2]])
    loads(0)
    for ti in range(NT):
        x = xs[ti % 2]
        y = ys[ti % 2]
        tsl = slice(ti * T, (ti + 1) * T)
        if ti + 1 < NT:
            loads(ti + 1)
        for d in range(8):
            w = ws[d % 2]
            b.dma(w[:], WoutD[d], [WoutD], [w], q="pool", max_dma_last_dim=4096)
            for h in range(NH):
                sl = slice(h * 512, (h + 1) * 512)
                P = pp[cnt % 4]
                cnt += 1
                for c in range(8):
                    b.mm(P[:], w[:, c, :], y[:, c, sl], c == 0, c == 7, [w, y], [P])
                b.op("dve", lambda e, P=P, x=x, d=d, sl=sl: e.tensor_tensor(out=x[:, d, sl], in0=P[:], in1=x[:, d, sl], op=ALU.add), [P, x], [x])
        b.dma(xT[:, :, tsl].rearrange("k p t -> p k t"), x[:], [x], [(xT, ti)])
    b.end()


def conv_pool_phase(b, pT, yT, cwD, pwD, pscD, pinvD, pcorrD):
    b.begin()
    cw = b.sb("cw", [128, 2, 3], F32)
    b.dma(cw[:], cwD[:], [cwD], [cw])
    psc = b.sb("psc", [128, 2], F32)
    b.dma(psc[:], pscD[:], [pscD], [psc])
    pinv = b.sb("pinv", [128, 2], F32)
    b.dma(pinv[:], pinvD[:], [pinvD], [pinv])
    pcorr = b.sb("pcorr", [128, 2, 16], F32)
    b.dma(pcorr[:], pcorrD[:], [pcorrD], [pcorr])
    pw = b.sb("pw", [128, 2, 128], BF16)
    for j in range(2):
        b.dma(pw[:, j, :], pwD[j], [pwD], [pw], q="pool")
    u = b.sb("u", [128, 16 + S], F32)
    bb = b.sb("bb", [128, S], F32)
    cc = b.sb("cc", [128, S], F32)
    z = b.sb("z", [128, 16 + S], F32)
    acc = b.sb("acc", [128, 16 + S], F32)
    yb = b.sb("yb", [128, S], BF16)
    pp = [b.ps("pp", [128, 512]) for _ in range(2)]
    for j in range(2):
        b.dma(u[:, 16:], pT[BLK_U[j]], [pT], [u])
        b.dma(bb[:], pT[BLK_B[j]], [pT], [bb])
        b.dma(cc[:], pT[BLK_C[j]], [pT], [cc])
        b.op("pool", lambda e: e.memset(z[:, 0:16], 0.0), [], [z])
        b.op("dve", lambda e: e.tensor_tensor(out=z[:, 16:], in0=cc[:], in1=u[:, 16:], op=ALU.mult), [cc, u], [z])
        b.op("dve", lambda e, j=j: e.tensor_scalar(out=acc[:, 16:], in0=z[:, 16:], scalar1=cw[:, j, 2:3], scalar2=None, op0=ALU.mult), [z, cw], [acc])
        b.op("dve", lambda e, j=j: e.scalar_tensor_tensor(out=acc[:, 16:], in0=z[:, 15:15 + S], scalar=cw[:, j, 1:2], in1=acc[:, 16:],
                                                         op0=ALU.mult, op1=ALU.add), [z, cw, acc], [acc])
        b.op("dve", lambda e, j=j: e.scalar_tensor_tensor(out=acc[:, 16:], in0=z[:, 14:14 + S], scalar=cw[:, j, 0:1], in1=acc[:, 16:],
                                                         op0=ALU.mult, op1=ALU.add), [z, cw, acc], [acc])
        b.op("dve", lambda e: e.tensor_tensor(out=yb[:], in0=bb[:], in1=acc[:, 16:], op=ALU.mult), [bb, acc], [yb])
        b.dma(yT[6 + j], yb[:], [yb], [(yT, 6 + j)])
    for j in range(2):
        b.op("pool", lambda e: e.memset(u[:, 0:16], 0.0), [], [u])
        b.op("pool", lambda e: e.memset(z[:, 0:16], 0.0), [], [z])
        b.op("pool", lambda e: e.memset(acc[:, 0:16], 0.0), [], [acc])
        b.dma(u[:, 16:], pT[BLK_A[j]], [pT], [u])
        R = slice(16, 16 + S)

        def sh(k):
            return slice(16 - k, 16 - k + S)
        if j == 0:
            b.op("dve", lambda e: e.tensor_tensor(out=z[:, R], in0=u[:, R], in1=u[:, sh(1)], op=ALU.add), [u], [z])
            b.op("dve", lambda e: e.tensor_copy(out=acc[0:64, R], in_=z[0:64, R]), [z], [acc])
            b.op("dve", lambda e: e.tensor_tensor(out=acc[64:128, R], in0=z[64:128, R], in1=z[64:128, sh(2)], op=ALU.add), [z], [acc])
        else:
            b.op("dve", lambda e: e.tensor_tensor(out=z[:, R], in0=u[:, R], in1=u[:, sh(1)], op=ALU.add), [u], [z])
            b.op("dve", lambda e: e.tensor_tensor(out=acc[:, R], in0=z[:, R], in1=z[:, sh(2)], op=ALU.add), [z], [acc])
            b.op("dve", lambda e: e.tensor_tensor(out=z[:, R], in0=acc[:, R], in1=acc[:, sh(4)], op=ALU.add), [acc], [z])
            b.op("dve", lambda e: e.tensor_copy(out=acc[0:64, R], in_=z[0:64, R]), [z], [acc])
            b.op("dve", lambda e: e.tensor_tensor(out=acc[64:128, R], in0=z[64:128, R], in1=z[64:128, sh(8)], op=ALU.add), [z], [acc])
        b.op("dve", lambda e, j=j: e.tensor_scalar(out=cc[:], in0=acc[:, R], scalar1=pinv[:, j:j + 1], scalar2=None, op0=ALU.mult), [acc, pinv], [cc])
        b.op("dve", lambda e, j=j: e.tensor_tensor(out=cc[:, 0:16], in0=acc[:, 16:32], in1=pcorr[:, j, :], op=ALU.mult), [acc, pcorr, cc], [cc])
        b.op("dve", lambda e: e.tensor_tensor(out=yb[:], in0=cc[:], in1=u[:, R], op=ALU.subtract), [cc, u], [yb])
        for h in range(S // 512):
            sl = slice(h * 512, (h + 1) * 512)
            P = pp[h % 2]
            b.mm(P[:], pw[:, j, :], yb[:, sl], True, True, [pw, yb], [P])
            b.op("act", lambda e, P=P, sl=sl, j=j: e.activation(out=z[:, sl], in_=P[:], func=AF.Copy, scale=psc[:, j:j + 1]), [P, psc], [(z, h)])
        b.op("dve", lambda e: e.tensor_copy(out=yb[:], in_=z[:, 0:S]), [z], [yb])
        b.dma(yT[j], yb[:], [yb], [(yT, j)])
    b.end()


TT_ = 512
NCH = 8


F32R = mybir.dt.float32r


def mmr(b, out, lhsT, rhs, start, stop, reads, writes):
    return b.op("pe", lambda e: e.matmul(out, lhsT.bitcast(F32R), rhs.bitcast(F32R), start=start, stop=stop), reads, writes, acc=True)


def run_rr(gens, bg=None):
    gens = list(gens)
    while gens:
        for g in list(gens):
            try:
                next(g)
            except StopIteration:
                gens.remove(g)
        if bg is not None:
            try:
                next(bg)
            except StopIteration:
                pass


DBG = []


def rwkv_phase(b, pT, yT, d, Cn, ntiles=None, debug=False):
    b.begin()

    def dump(name, buf, ap, shape):
        if not debug:
            return
        dd = b.dram("dbg_" + name, list(shape), F32, "ExternalOutput")
        DBG.append("dbg_" + name)
        b.dma(dd[:], ap, [buf], [dd])
    sbF = lambda nm, shp=(128, TT_): b.sb(nm, list(shp), F32)
    def ld(nm, src, shp):
        t = b.sb(nm, shp, F32)
        b.dma(t[:], src[:], [src], [t])
        return t
    bones = ld("bones", Cn["bones"], [128, 128])
    bones64 = ld("bones64", Cn["bones64"], [128, 128])
    ident = ld("ident", Cn["ident"], [128, 128])
    sel2 = ld("sel2", Cn["sel2"], [128, 64])
    rmask = ld("rmask", Cn["rmask"], [128, TT_])
    LPmask = ld("LPmask", Cn["LPmask"], [128, 512])
    Lamask = ld("Lamask", Cn["Lamask"], [128, 128])
    mu8 = ld("mu8", d["rw_mu8"], [128, 8])
    vec = ld("vec", d["rw_vec"], [128, 2, 7])
    WUP = ld("WUP", d["rw_wup"], [128, 2, 128])
    GUP = ld("GUP", d["rw_gup"], [128, 2, 128])
    AUP = ld("AUP", d["rw_aup"], [128, 2, 128])
    V_W0, V_A0, V_KK, V_KA, V_RK, V_LNW, V_LNB = range(7)

    pb2 = [b.sb(f"pb{i}", [128, 1 + TT_], F32) for i in range(2)]
    pb = [pb2[i % 2] for i in range(8)]
    dtmp = sbF("dtmp")
    lp = [sbF(f"lp{i}") for i in range(8)]
    twg = sbF("twg")
    psw = [b.ps("psw", [128, 512]) for _ in range(1)]
    psT2 = [b.ps("psT", [128, 512]) for _ in range(2)]
    psN2 = [b.ps("psN", [128, 512]) for _ in range(2)]
    psS2 = [b.ps("psS", [128, 512]) for _ in range(2)]
    wcnt = [0]

    def wide():
        wcnt[0] += 1
        return psw[0]

    H = {}
    SH = {nm: sbF("sh_" + nm) for nm in ("lw", "cum", "cexc", "dW", "epos", "eexc", "t1", "t2", "kk", "YT", "yc", "sq", "rs")}
    for hp in range(2):
        h = {}
        for nm in ("a", "kkn", "kpp", "akk", "akkh", "eneg", "eW"):
            h[nm] = sbF(nm + str(hp))
        for nm in ("g", "bonus"):
            h[nm] = [sbF(nm + str(hp) + "_" + str(i)) for i in range(2)]
        for nm in ("lw", "cum", "cexc", "dW", "epos", "eexc", "t1", "t2", "kk", "YT", "yc", "sq", "rs"):
            h[nm] = SH[nm]
        h["wC"] = [b.sb(f"wC{hp}_{i}", [128, NCH], F32) for i in range(2)]
        h["QR"] = b.sb(f"QR{hp}", [128, NCH, 2, 64], F32)
        for nm in ("AKKbd", "KHbd", "AKKWbd", "KWbd", "Vbd", "Ybd"):
            h[nm] = b.sb(nm + str(hp), [128, NCH, 128], F32)
            b.op("pool", lambda e, t=h[nm]: e.memset(t[:], 0.0), [], [h[nm]])
        h["QRbd"] = b.sb(f"QRbd{hp}", [128, NCH, 2, 128], F32)
        b.op("pool", lambda e, t=h["QRbd"]: e.memset(t[:], 0.0), [], [h["QRbd"]])
        h["Sst"] = b.sb(f"Sst{hp}", [128, 64], F32)
        b.op("pool", lambda e, t=h["Sst"]: e.memset(t[:], 0.0), [], [h["Sst"]])
        h["LP"] = [b.sb(f"LP{hp}_{i}", [128, 512], F32) for i in range(2)]
        h["AKWT"] = [b.sb(f"AKWT{hp}_{i}", [128, 256], F32) for i in range(2)]
        h["Vst"] = [b.sb(f"Vst{hp}_{i}", [128, 64], F32) for i in range(2)]
        h["TT"] = [b.sb(f"TT{hp}_{i}", [128, 128], F32) for i in range(2)]
        h["MY"] = [b.sb(f"MY{hp}_{i}", [128, 256], F32) for i in range(2)]
        h["N"] = [b.sb(f"N{hp}_{i}", [128, 128], F32) for i in range(2)]
        h["Xst"] = b.sb(f"Xst{hp}", [128, 64], F32)
        h["Ust"] = b.sb(f"Ust{hp}", [128, 64], F32)
        h["ob"] = b.sb(f"ob{hp}", [128, TT_], BF16)
        H[hp] = h

    def v3(t, R):
        return t[R, :].rearrange("p (c t) -> p c t", t=64)

    def stage1_gen(ti):
        t0 = ti * TT_
        for i in range(8):
            blk = 2 + i
            if ti == 0:
                b.op("pool", lambda e, i=i: e.memset(pb[i][:, 0:1], 0.0), [], [pb[i]])
                b.dma(pb[i][:, 1:], pT[blk, :, 0:TT_], [pT], [pb[i]])
            else:
                b.dma(pb[i][:], pT[blk, :, t0 - 1:t0 + TT_], [pT], [pb[i]])
            eng = "dve" if i % 2 == 0 else "pool"
            b.op("dve", lambda e, i=i: e.tensor_tensor(out=dtmp[:], in0=pb[i][:, 0:TT_], in1=pb[i][:, 1:], op=ALU.subtract), [pb[i]], [dtmp])
            b.op("dve", lambda e, i=i: e.scalar_tensor_tensor(out=lp[i][:], in0=dtmp[:], scalar=mu8[:, i:i + 1], in1=pb[i][:, 1:],
                                                             op0=ALU.mult, op1=ALU.add), [dtmp, mu8, pb[i]], [lp[i]])
            yield
        b.op("act", lambda e: e.activation(out=twg[0:64, :], in_=lp[6][0:64, :], func=AF.Tanh), [lp[6]], [(twg, 0)])
        b.op("act", lambda e: e.activation(out=twg[64:128, :], in_=lp[6][64:128, :], func=AF.Sigmoid), [lp[6]], [(twg, 1)])
        for hp in range(2):
            h = H[hp]
            rp, kp, vp = lp[hp], lp[2 + hp], lp[4 + hp]
            vv = lambda i, hp=hp: vec[:, hp, i:i + 1]
            pw_ = wide()
            b.mm(pw_[:], WUP[:, hp, :], twg[:], True, True, [WUP, twg], [pw_])
            b.op("act", lambda e, pw_=pw_, h=h, vv=vv: e.activation(out=h["t1"][:], in_=pw_[:], func=AF.Sigmoid, bias=vv(V_W0)), [pw_, vec], [h["t1"]])
            b.op("dve", lambda e, h=h: e.tensor_scalar(out=h["lw"][:], in0=h["t1"][:], scalar1=-0.6065306597126334, scalar2=None, op0=ALU.mult), [h["t1"]], [h["lw"]])
            pa_ = wide()
            b.mm(pa_[:], AUP[:, hp, :], lp[7][:], True, True, [AUP, lp[7]], [pa_])
            b.op("act", lambda e, pa_=pa_, h=h, vv=vv: e.activation(out=h["a"][:], in_=pa_[:], func=AF.Sigmoid, bias=vv(V_A0)), [pa_, vec], [h["a"]])
            pg_ = wide()
            b.mm(pg_[:], GUP[:, hp, :], twg[:], True, True, [GUP, twg], [pg_])
            b.op("act", lambda e, pg_=pg_, h=h: e.activation(out=h["g"][ti % 2][:], in_=pg_[:], func=AF.Copy), [pg_], [h["g"][ti % 2]])
            yield
            b.op("dve", lambda e, h=h, kp=kp, vv=vv: e.tensor_scalar(out=h["kk"][:], in0=kp[:], scalar1=vv(V_KK), scalar2=None, op0=ALU.mult), [kp, vec], [h["kk"]])
            b.op("act", lambda e, h=h: e.activation(out=h["t1"][:], in_=h["kk"][:], func=AF.Square), [h["kk"]], [h["t1"]])
            pn_ = wide()
            b.mm(pn_[:], bones[:], h["t1"][:], True, True, [bones, h["t1"]], [pn_])
            b.op("dve", lambda e, pn_=pn_, h=h: e.tensor_scalar(out=h["t2"][:], in0=pn_[:], scalar1=1e-24, scalar2=None, op0=ALU.max), [pn_], [h["t2"]])
            b.op("act", lambda e, h=h: e.activation(out=h["t2"][:], in_=h["t2"][:], func=AF.Sqrt), [h["t2"]], [h["t2"]])
            b.op("dve", lambda e, h=h: e.reciprocal(out=h["t2"][:], in_=h["t2"][:]), [h["t2"]], [h["t2"]])
            b.op("dve", lambda e, h=h: e.tensor_tensor(out=h["kkn"][:], in0=h["kk"][:], in1=h["t2"][:], op=ALU.mult), [h["kk"], h["t2"]], [h["kkn"]])
            yield
            b.op("dve", lambda e, h=h, vv=vv: e.tensor_scalar(out=h["t1"][:], in0=h["a"][:], scalar1=-1.0, scalar2=vv(V_KA), op0=ALU.add, op1=ALU.mult), [h["a"], vec], [h["t1"]])
            b.op("dve", lambda e, h=h, kp=kp: e.scalar_tensor_tensor(out=h["kpp"][:], in0=h["t1"][:], scalar=1.0, in1=kp[:], op0=ALU.add, op1=ALU.mult), [h["t1"], kp], [h["kpp"]])
            b.op("dve", lambda e, h=h: e.tensor_tensor(out=h["akk"][:], in0=h["kkn"][:], in1=h["a"][:], op=ALU.mult), [h["kkn"], h["a"]], [h["akk"]])
            yield
            b.op("dve", lambda e, h=h, rp=rp, vv=vv: e.scalar_tensor_tensor(out=h["t1"][:], in0=rp[:], scalar=vv(V_RK), in1=h["kpp"][:], op0=ALU.mult, op1=ALU.mult), [rp, vec, h["kpp"]], [h["t1"]])
            pb_ = wide()
            b.mm(pb_[:], bones[:], h["t1"][:], True, True, [bones, h["t1"]], [pb_])
            b.op("dve", lambda e, pb_=pb_, h=h, vp=vp: e.tensor_tensor(out=h["bonus"][ti % 2][:], in0=pb_[:], in1=vp[:], op=ALU.mult), [pb_, vp], [h["bonus"][ti % 2]])
            yield
            b.op("dve", lambda e, h=h: e.tensor_tensor_scan(out=h["cum"][:], data0=rmask[:], data1=h["lw"][:], initial=0.0, op0=ALU.mult, op1=ALU.add), [rmask, h["lw"]], [h["cum"]])
            b.op("dve", lambda e, h=h: e.tensor_tensor(out=h["cexc"][:], in0=h["cum"][:], in1=h["lw"][:], op=ALU.subtract), [h["cum"], h["lw"]], [h["cexc"]])
            for c in range(NCH):
                cs = slice(c * 64, (c + 1) * 64)
                b.op("pool" if c % 2 else "dve", lambda e, h=h, cs=cs, c=c: e.tensor_scalar(out=h["dW"][:, cs], in0=h["cum"][:, cs], scalar1=-1.0,
                                                                 scalar2=h["cum"][:, c * 64 + 63:c * 64 + 64], op0=ALU.mult, op1=ALU.add), [h["cum"]], [(h["dW"], c)])
            yield
            b.op("act", lambda e, h=h: e.activation(out=h["epos"][:], in_=h["cum"][:], func=AF.Exp), [h["cum"]], [h["epos"]])
            b.op("act", lambda e, h=h: e.activation(out=h["eneg"][:], in_=h["cum"][:], func=AF.Exp, scale=-1.0), [h["cum"]], [h["eneg"]])
            b.op("act", lambda e, h=h: e.activation(out=h["eexc"][:], in_=h["cexc"][:], func=AF.Exp), [h["cexc"]], [h["eexc"]])
            b.op("act", lambda e, h=h: e.activation(out=h["eW"][:], in_=h["dW"][:], func=AF.Exp), [h["dW"]], [h["eW"]])
            b.op("pool", lambda e, h=h: e.tensor_copy(out=h["wC"][ti % 2][:], in_=h["epos"][:, 63::64]), [h["epos"]], [h["wC"][ti % 2]])
            yield
            b.op("dve", lambda e, h=h: e.tensor_tensor(out=h["QR"][:, :, 0, :], in0=v3(h["kkn"], slice(0, 128)), in1=v3(h["eexc"], slice(0, 128)), op=ALU.mult), [h["kkn"], h["eexc"]], [(h["QR"], 0)])
            b.op("pool", lambda e, h=h, rp=rp: e.tensor_tensor(out=h["QR"][:, :, 1, :], in0=v3(rp, slice(0, 128)), in1=v3(h["epos"], slice(0, 128)), op=ALU.mult), [rp, h["epos"]], [(h["QR"], 1)])
            b.op("dve", lambda e, h=h: e.tensor_tensor(out=h["akkh"][:], in0=h["akk"][:], in1=h["eneg"][:], op=ALU.mult), [h["akk"], h["eneg"]], [h["akkh"]])
            yield

    def stage2(ti):
        for hp in range(2):
            h = H[hp]
            vp = lp[4 + hp]
            for hh in range(2):
                R = slice(64 * hh, 64 * hh + 64)
                eng = "pool" if hh else "dve"
                b.op(eng, lambda e, h=h, R=R: e.tensor_copy(out=h["QRbd"][R, :, 0, R].bitcast(F32R), in_=h["QR"][R, :, 0, :]), [(h["QR"], 0)], [(h["QRbd"], hh)])
                b.op(eng, lambda e, h=h, R=R: e.tensor_copy(out=h["QRbd"][R, :, 1, R].bitcast(F32R), in_=h["QR"][R, :, 1, :]), [(h["QR"], 1)], [(h["QRbd"], hh)])
                b.op(eng, lambda e, h=h, R=R: e.tensor_copy(out=h["AKKbd"][R, :, R].bitcast(F32R), in_=v3(h["akkh"], R)), [h["akkh"]], [(h["AKKbd"], hh)])
                b.op(eng, lambda e, h=h, R=R: e.tensor_tensor(out=h["KHbd"][R, :, R].bitcast(F32R), in0=v3(h["kpp"], R), in1=v3(h["eneg"], R), op=ALU.mult), [h["kpp"], h["eneg"]], [(h["KHbd"], hh)])
                b.op(eng, lambda e, h=h, R=R: e.tensor_tensor(out=h["AKKWbd"][R, :, R], in0=v3(h["akk"], R), in1=v3(h["eW"], R), op=ALU.mult), [h["akk"], h["eW"]], [(h["AKKWbd"], hh)])
                b.op(eng, lambda e, h=h, R=R: e.tensor_tensor(out=h["KWbd"][R, :, R], in0=v3(h["kpp"], R), in1=v3(h["eW"], R), op=ALU.mult), [h["kpp"], h["eW"]], [(h["KWbd"], hh)])
                b.op(eng, lambda e, h=h, R=R, vp=vp: e.tensor_copy(out=h["Vbd"][R, :, R], in_=v3(vp, R)), [vp], [(h["Vbd"], hh)])


    NTL = ntiles or (S // TT_)
    run_rr([stage1_gen(0)])
    for ti in range(NTL):
        t0 = ti * TT_
        stage2(ti)
        bg = stage1_gen(ti + 1) if ti + 1 < NTL else None

        def prep_gen(hp, c):
            h = H[hp]
            par = c % 2
            LP, AKWT, Vst, TT = h["LP"][par], h["AKWT"][par], h["Vst"][par], h["TT"][par]
            bX, bY = psT2[hp], psN2[hp]
            MY, NN = h["MY"], h["N"]
            qr = h["QRbd"][:, c, :, :].rearrange("p a t -> p (a t)")
            b.tr(bX[:, 0:128], h["AKKWbd"][:, c, :], ident[:], [h["AKKWbd"], ident], [bX])
            b.tr(bX[:, 128:256], h["KWbd"][:, c, :], ident[:], [h["KWbd"], ident], [bX])
            b.tr(bX[:, 256:384], h["Vbd"][:, c, :], ident[:], [h["Vbd"], ident], [bX])
            mmr(b, bX[:, 384:512], h["QRbd"][:, c, 0, :], h["AKKbd"][:, c, :], True, True, [h["QRbd"], h["AKKbd"]], [bX])
            mmr(b, bY[:, 0:256], h["AKKbd"][:, c, :], qr, True, True, [h["AKKbd"], h["QRbd"]], [bY])
            mmr(b, bY[:, 256:512], h["KHbd"][:, c, :], qr, True, True, [h["KHbd"], h["QRbd"]], [bY])
            yield
            b.op("dve", lambda e: e.tensor_tensor(out=NN[0][:].bitcast(F32R), in0=bX[:, 384:512], in1=Lamask[:], op=ALU.mult), [bX, Lamask], [NN[0]])
            b.op("act", lambda e: e.activation(out=AKWT[:].bitcast(F32R), in_=bX[:, 0:256], func=AF.Copy), [bX], [AKWT])
            b.op("act", lambda e: e.activation(out=Vst[0:64, :].bitcast(F32R), in_=bX[0:64, 256:320], func=AF.Copy), [bX], [(Vst, 0)])
            b.op("act", lambda e: e.activation(out=Vst[64:128, :].bitcast(F32R), in_=bX[64:128, 320:384], func=AF.Copy), [bX], [(Vst, 1)])
            b.op("dve", lambda e: e.tensor_tensor(out=LP[:].bitcast(F32R), in0=bY[:], in1=LPmask[:], op=ALU.mult), [bY, LPmask], [LP])
            b.op("pool", lambda e: e.tensor_tensor(out=MY[1][:, 128:256].bitcast(F32R), in0=LP[:, 0:128], in1=ident[:], op=ALU.add), [LP, ident], [(MY[1], 1)])
            yield
            mmr(b, bX[:, 0:128], NN[0][:], LP[:, 0:128], True, True, [NN[0], LP], [bX])
            mmr(b, bX[:, 256:384], LP[:, 0:128], NN[0][:], True, True, [LP, NN[0]], [bX])
            yield
            b.op("act", lambda e: e.activation(out=MY[1][:, 0:128].bitcast(F32R), in_=bX[:, 0:128], func=AF.Copy), [bX], [(MY[1], 0)])
            b.op("dve", lambda e: e.tensor_copy(out=NN[1][:].bitcast(F32R), in_=bX[:, 256:384]), [bX], [NN[1]])
            yield
            for k in range(1, 6):
                bank = bY if k % 2 else bX
                i, o = k % 2, (k + 1) % 2
                if k < 5:
                    mmr(b, bank[:, 0:256], NN[i][:], MY[i][:, 0:256], True, True, [NN[i], MY[i]], [bank])
                    mmr(b, bank[:, 256:384], MY[i][:, 0:128], NN[i][:], True, True, [MY[i], NN[i]], [bank])
                    yield
                    b.op("dve", lambda e, bank=bank, i=i, o=o: e.tensor_tensor(out=MY[o][:, 128:256].bitcast(F32R), in0=bank[:, 128:256], in1=MY[i][:, 128:256], op=ALU.add),
                         [bank, (MY[i], 1)], [(MY[o], 1)])
                    b.op("act", lambda e, bank=bank, o=o: e.activation(out=MY[o][:, 0:128].bitcast(F32R), in_=bank[:, 0:128], func=AF.Copy), [bank], [(MY[o], 0)])
                    b.op("act", lambda e, bank=bank, o=o: e.activation(out=NN[o][:].bitcast(F32R), in_=bank[:, 256:384], func=AF.Copy), [bank], [NN[o]])
                    yield
                else:
                    mmr(b, bank[:, 0:128], NN[i][:], MY[i][:, 128:256], True, True, [NN[i], MY[i]], [bank])
                    yield
                    b.op("dve", lambda e, bank=bank, i=i: e.tensor_tensor(out=TT[:].bitcast(F32R), in0=bank[:, 0:128], in1=MY[i][:, 128:256], op=ALU.add),
                         [bank, (MY[i], 1)], [TT])
                    yield

        def seq_gen(hp, c):
            h = H[hp]
            par = c % 2
            LP, AKWT, Vst, TT = h["LP"][par], h["AKWT"][par], h["Vst"][par], h["TT"][par]
            Sst, Xst, Ust = h["Sst"], h["Xst"], h["Ust"]
            so = 0
            psS = psS2[hp]
            sb_ = lambda k: psS
            X_, U_, Y_, S_ = (psS[:, so + 64 * k:so + 64 * k + 64] for k in range(4))
            mmr(b, X_, LP[:, 256:384], Vst[:], True, False, [LP, Vst], [sb_(0)])
            mmr(b, X_, h["QRbd"][:, c, 0, :], Sst[:], False, True, [h["QRbd"], Sst], [sb_(0)])
            yield
            b.op("dve", lambda e: e.tensor_copy(out=Xst[:].bitcast(F32R), in_=X_), [sb_(0)], [Xst])
            yield
            mmr(b, U_, TT[:], Xst[:], True, True, [TT, Xst], [sb_(1)])
            yield
            b.op("dve", lambda e: e.tensor_scalar(out=Ust[:].bitcast(F32R), in0=U_, scalar1=-1.0, scalar2=None, op0=ALU.mult), [sb_(1)], [Ust])
            yield
            mmr(b, Y_, h["QRbd"][:, c, 1, :], Sst[:], True, False, [h["QRbd"], Sst], [sb_(2)])
            mmr(b, Y_, LP[:, 128:256], Ust[:], False, False, [LP, Ust], [sb_(2)])
            mmr(b, Y_, LP[:, 384:512], Vst[:], False, True, [LP, Vst], [sb_(2)])
            mmr(b, S_, AKWT[:, 128:256], Vst[:], True, False, [AKWT, Vst], [sb_(3)])
            mmr(b, S_, AKWT[:, 0:128], Ust[:], False, True, [AKWT, Ust], [sb_(3)])
            yield
            wCt = h["wC"][ti % 2]
            b.op("dve", lambda e: e.scalar_tensor_tensor(out=Sst[:].bitcast(F32R), in0=Sst[:], scalar=wCt[:, c:c + 1], in1=S_, op0=ALU.mult, op1=ALU.add),
                 [Sst, wCt, sb_(3)], [Sst])
            b.op("act", lambda e: e.activation(out=h["Ybd"][0:64, c, 0:64], in_=psS[0:64, so + 128:so + 192], func=AF.Copy), [sb_(2)], [(h["Ybd"], c)])
            b.op("act", lambda e: e.activation(out=h["Ybd"][64:128, c, 64:128], in_=psS[64:128, so + 128:so + 192], func=AF.Copy), [sb_(2)], [(h["Ybd"], c)])
            yield

        import os as _os
        STOP = int(_os.environ.get("RW_STOP", "9"))
        if STOP <= 1:
            continue
        for c in range(NCH + 1):
            gens = []
            if c < NCH:
                import itertools as _it
                PST = int(_os.environ.get("P_STOP", "999"))
                g0 = _it.islice(prep_gen(0, c), PST)
                g1 = _it.islice(prep_gen(1, c), PST)
                next(g0, None)
                gens += [g0, g1]
            if c >= 1 and STOP >= 3:
                s0 = seq_gen(0, c - 1)
                s1 = seq_gen(1, c - 1)
                next(s0, None)
                gens += [s0, s1]
            run_rr(gens, bg)
        if bg is not None:
            for _ in bg:
                pass
        if STOP <= 3:
            continue

        for hp in range(2):
            h = H[hp]
            vv = lambda i, hp=hp: vec[:, hp, i:i + 1]
            py = wide()
            for c in range(NCH):
                b.mm(py[:, c * 64:(c + 1) * 64], h["Ybd"][:, c, :], sel2[:], True, True, [(h["Ybd"], c), sel2], [py])
            b.op("act", lambda e, py=py, h=h: e.activation(out=h["YT"][:], in_=py[:], func=AF.Copy), [py], [h["YT"]])
            if ti == 0:
                dump(f"YT{hp}", h["YT"], h["YT"][:], [128, TT_])
                dump(f"Sst{hp}", h["Sst"], h["Sst"][:], [128, 64])
                dump(f"TT{hp}", h["TT"][1], h["TT"][1][:], [128, 128])
                dump(f"LP{hp}", h["LP"][1], h["LP"][1][:], [128, 512])
            pm = wide()
            b.mm(pm[:], bones64[:], h["YT"][:], True, True, [bones64, h["YT"]], [pm])
            b.op("dve", lambda e, pm=pm, h=h: e.tensor_tensor(out=h["yc"][:], in0=h["YT"][:], in1=pm[:], op=ALU.subtract), [h["YT"], pm], [h["yc"]])
            b.op("act", lambda e, h=h: e.activation(out=h["sq"][:], in_=h["yc"][:], func=AF.Square), [h["yc"]], [h["sq"]])
            pv = wide()
            b.mm(pv[:], bones64[:], h["sq"][:], True, True, [bones64, h["sq"]], [pv])
            b.op("act", lambda e, pv=pv, h=h: e.activation(out=h["rs"][:], in_=pv[:], func=AF.Sqrt, bias=64e-5), [pv], [h["rs"]])
            b.op("dve", lambda e, h=h: e.reciprocal(out=h["rs"][:], in_=h["rs"][:]), [h["rs"]], [h["rs"]])
            b.op("dve", lambda e, h=h: e.tensor_tensor(out=h["yc"][:], in0=h["yc"][:], in1=h["rs"][:], op=ALU.mult), [h["yc"], h["rs"]], [h["yc"]])
            b.op("dve", lambda e, h=h, vv=vv: e.tensor_scalar(out=h["yc"][:], in0=h["yc"][:], scalar1=vv(V_LNW), scalar2=vv(V_LNB), op0=ALU.mult, op1=ALU.add), [h["yc"], vec], [h["yc"]])
            bon, gg = h["bonus"][ti % 2], h["g"][ti % 2]
            b.op("dve", lambda e, h=h, bon=bon: e.tensor_tensor(out=h["yc"][:], in0=h["yc"][:], in1=bon[:], op=ALU.add), [h["yc"], bon], [h["yc"]])
            b.op("dve", lambda e, h=h, gg=gg: e.tensor_tensor(out=h["ob"][:], in0=h["yc"][:], in1=gg[:], op=ALU.mult), [h["yc"], gg], [h["ob"]])
            b.dma(yT[2 + hp, :, t0:t0 + TT_], h["ob"][:], [h["ob"]], [(yT, (2 + hp, ti))])
    b.end()


NBIG = 30000.0


def nsa_phase(b, pT, yT, d, Cn, ngroups=None, debug=False):
    b.begin()
    DBGN = []

    def dump(name, buf, ap, shape, dt=F32):
        if not debug:
            return
        dd = b.dram("dbg_" + name, list(shape), dt, "ExternalOutput")
        b.dma(dd[:], ap, [buf], [dd])

    def ld(nm, src, shp):
        t = b.sb(nm, shp, F32)
        b.dma(t[:], src[:], [src], [t])
        return t

    def ldc(nm, src_ap, srcbuf, shp):
        t = b.sb(nm, shp, BF16)
        b.dma(t[:], src_ap, [srcbuf], [t], q="pool", max_dma_last_dim=4096)
        return t
    ident = ld("ident", Cn["ident"], [128, 128])
    identb = ldc("identb", Cn["ident"][:], Cn["ident"], [128, 128])
    bones = ld("bones", Cn["bones"], [128, 128])
    eall = b.sb("eall", [128, S], BF16)
    b.op("pool", lambda e: e.memset(eall[64:128, :], 0.0), [], [eall])
    b.dma(eall[0:64, :], Cn["n_eall"][:], [Cn["n_eall"]], [eall], q="pool", max_dma_last_dim=4096)
    cmsel = b.sb("cmsel", [128, 4, 512], BF16)
    for r in range(4):
        b.dma(cmsel[:, r, :], Cn["n_cmsel"][r], [Cn["n_cmsel"]], [cmsel], q="pool", max_dma_last_dim=4096)
    cmwin = b.sb("cmwin", [128, 8, 512], BF16)
    for r in range(8):
        b.dma(cmwin[:, r, :], Cn["n_cmwin"][r], [Cn["n_cmwin"]], [cmwin], q="pool", max_dma_last_dim=4096)
    negc = b.sb("negc", [128, 5, 512], BF16)
    for r in range(5):
        b.dma(negc[:, r, :], Cn["n_negc"][r], [Cn["n_negc"]], [negc], q="pool", max_dma_last_dim=4096)
    qnw = ld("qnw", d["ns_qnw"], [128, 1])
    knw = ld("knw", d["ns_knw"], [128, 3])
    w2 = ld("w2", d["ns_w2"], [128, 2, 64])
    pos2 = ld("pos2", d["ns_pos2"], [128, 32, 2])
    QA = [b.sb(f"QA{h}", [128, S], BF16) for h in range(4)]
    KS = b.sb("KS", [128, S], BF16)
    KW = b.sb("KW", [128, S], BF16)
    KC = b.sb("KC", [128, 256], BF16)
    NEGM = b.sb("NEGM", [128, S], BF16)
    b.op("pool", lambda e: e.memset(NEGM[64:128, :], 0.0), [], [NEGM])
    Vs = b.sb("Vs", [128, 32, 65], BF16)
    Vw = b.sb("Vw", [128, 32, 65], BF16)
    VC = b.sb("VC", [128, 2, 129], BF16)
    Gtm = b.sb("Gtm", [128, 32, 32], F32)
    rawA = b.sb("rawA", [128, S], F32)
    rawB = b.sb("rawB", [128, S], F32)
    tF = [b.sb(f"tF{i}", [128, 512], F32) for i in range(3)]
    psc = [b.ps("psc", [128, 512]) for _ in range(4)]
    pacc = [b.ps("pacc", [128, 512]) for _ in range(2)]
    pcA = pacc[0]
    pcB = pacc[1]
    pm = [b.ps("pm", [128, 512]) for _ in range(2)]
    mcnt = [0]

    def misc():
        mcnt[0] += 1
        return pm[mcnt[0] % 2]

    rT = [b.sb(f"rT{i}", [64, 512], F32) for i in range(6)]
    rcnt = [0]

    def rms_rows(raw, out, wcol, NC=S, width=512):
        for c0 in range(0, NC, width):
            wd = min(width, NC - c0)
            cs = slice(c0, c0 + wd)
            rcnt[0] += 1
            t0, t1 = rT[(rcnt[0] % 3) * 2], rT[(rcnt[0] % 3) * 2 + 1]
            b.op("act", lambda e, cs=cs, wd=wd, t0=t0: e.activation(out=t0[0:64, 0:wd], in_=raw[0:64, cs], func=AF.Square), [raw], [t0])
            P = misc()
            b.mm(P[0:64, 0:wd], bones[0:64, 0:64], t0[0:64, 0:wd], True, True, [bones, t0], [P])
            b.op("act", lambda e, P=P, wd=wd, t1=t1: e.activation(out=t1[0:64, 0:wd], in_=P[0:64, 0:wd], func=AF.Sqrt, scale=1.0 / 64, bias=EPS), [P], [t1])
            b.op("dve", lambda e, wd=wd, t1=t1: e.reciprocal(out=t1[0:64, 0:wd], in_=t1[0:64, 0:wd]), [t1], [t1])
            b.op("dve", lambda e, cs=cs, wd=wd, t1=t1: e.scalar_tensor_tensor(out=out[0:64, cs], in0=raw[0:64, cs], scalar=wcol, in1=t1[0:64, 0:wd],
                                                                    op0=ALU.mult, op1=ALU.mult), [raw, t1, qnw, knw], [(out, c0)])

    for h in range(4):
        b.dma(rawA[:], pT[BLK_Q[h]], [pT], [rawA])
        rms_rows(rawA, QA[h], qnw[0:64, 0:1])
        b.dma(QA[h][64:68, :], Cn["n_qaug"][h], [Cn["n_qaug"]], [QA[h]], q="pool", max_dma_last_dim=4096)
    for (blk, Kt, Vt, wi) in ((BLK_KVS, KS, Vs, 1), (BLK_KVW, KW, Vw, 2)):
        b.dma(rawA[:], pT[blk], [pT], [rawA])
        rms_rows(rawA, Kt, knw[0:64, wi:wi + 1])
        b.dma(Kt[64:68, :], Cn["n_kaug"][:], [Cn["n_kaug"]], [Kt], q="pool", max_dma_last_dim=4096)
        b.op("pool", lambda e, Vt=Vt: e.memset(Vt[:, :, 64:65], 1.0), [], [Vt])
        for g4 in range(8):
            P = misc()
            for i in range(4):
                kt = g4 * 4 + i
                b.tr(P[:, i * 128:(i + 1) * 128], rawA[:, kt * 128:(kt + 1) * 128], ident[:], [rawA, ident], [P])
            b.op("dve", lambda e, P=P, Vt=Vt, g4=g4: e.tensor_copy(out=Vt[:, g4 * 4:g4 * 4 + 4, 0:64],
                                                                  in_=P[:].rearrange("p (i c) -> p i c", c=128)[:, :, 64:128]), [P], [Vt])
    b.dma(rawA[:], pT[BLK_GT], [pT], [rawA])
    b.op("act", lambda e: e.activation(out=rawA[0:32, :], in_=rawA[0:32, :], func=AF.Sigmoid), [rawA], [rawA])
    for g16 in range(2):
        P = misc()
        for i in range(16):
            kt = g16 * 16 + i
            b.tr(P[:, i * 32:(i + 1) * 32], rawA[0:32, kt * 128:(kt + 1) * 128], ident[0:32, 0:32], [rawA, ident], [P])
        b.op("dve", lambda e, P=P, g16=g16: e.tensor_copy(out=Gtm[:, g16 * 16:(g16 + 1) * 16, :], in_=P[:].rearrange("p (i c) -> p i c", c=32)), [P], [Gtm])
    b.dma(rawA[:], pT[BLK_KVC], [pT], [rawA])
    b.dma(rawB[:], d["ns_w1"][:].rearrange("p l m -> p (l m)"), [d["ns_w1"]], [rawB])
    b.op("pool", lambda e: e.memset(KC[:], 0.0), [], [KC])
    b.op("pool", lambda e: e.memset(VC[:], 0.0), [], [VC])
    bias_sb = b.sb("bias_sb", [128, 4], F32)
    for half in range(2):
        R = slice(64 * half, 64 * half + 64)
        Pb = misc()
        for l in range(32):
            b.mm(Pb[:, 0:2], rawB[R, l * 128:(l + 1) * 128], pos2[R, l, :], l == 0, l == 31, [rawB, pos2], [Pb])
        b.op("dve", lambda e, Pb=Pb, half=half: e.tensor_copy(out=bias_sb[:, 2 * half:2 * half + 2], in_=Pb[:, 0:2]), [Pb], [bias_sb])
    gl = [b.sb(f"gl{i}", [128, 256], F32) for i in range(2)]
    for half in range(2):
        R = slice(64 * half, 64 * half + 64)
        Ph = misc()
        for l in range(32):
            b.mm(Ph[:, 0:255], rawB[R, l * 128:(l + 1) * 128], rawA[R, l:l + 4065:16], l == 0, l == 31, [rawB, rawA], [Ph])
        hb, h2 = tF[0], tF[1]
        b.op("act", lambda e, Ph=Ph, half=half: e.activation(out=hb[:, 0:255], in_=Ph[:, 0:255], func=AF.Identity, bias=bias_sb[:, 2 * half:2 * half + 1]), [Ph, bias_sb], [hb])
        b.op("act", lambda e: e.activation(out=h2[:, 0:255], in_=hb[:, 0:255], func=AF.Square), [hb], [h2])
        b.op("dve", lambda e: e.tensor_scalar(out=h2[:, 0:255], in0=h2[:, 0:255], scalar1=0.044715, scalar2=1.0, op0=ALU.mult, op1=ALU.add), [h2], [h2])
        b.op("dve", lambda e: e.tensor_tensor(out=h2[:, 0:255], in0=h2[:, 0:255], in1=hb[:, 0:255], op=ALU.mult), [h2, hb], [h2])
        b.op("act", lambda e: e.activation(out=h2[:, 0:255], in_=h2[:, 0:255], func=AF.Sigmoid, scale=1.5957691216057308), [h2], [h2])
        b.op("dve", lambda e, half=half: e.tensor_tensor(out=gl[half][:, 0:255], in0=h2[:, 0:255], in1=hb[:, 0:255], op=ALU.mult), [h2, hb], [gl[half]])
    Pk = misc()
    b.mm(Pk[0:64, 0:255], w2[:, 0, :], gl[0][:, 0:255], True, True, [w2, gl[0]], [Pk])
    kc_sb = b.sb("kc_sb", [128, 256], F32)
    b.op("dve", lambda e: e.tensor_copy(out=kc_sb[0:64, 0:255], in_=Pk[0:64, 0:255]), [Pk], [kc_sb])
    rms_rows(kc_sb, KC, knw[0:64, 0:1], NC=255, width=255)
    b.dma(KC[64:68, :], Cn["n_kcaug"][:], [Cn["n_kcaug"]], [KC], q="pool")
    for nt in range(2):
        ncols = 128 if nt == 0 else 127
        Pv = misc()
        b.mm(Pv[0:ncols, 0:64], gl[1][:, nt * 128:nt * 128 + ncols], w2[:, 1, :], True, True, [gl[1], w2], [Pv])
        b.op("dve", lambda e, Pv=Pv, nt=nt, ncols=ncols: e.tensor_copy(out=VC[0:ncols, nt, 0:64], in_=Pv[0:ncols, 0:64]), [Pv], [VC])
        b.dma(VC[:, nt, 64:129], Cn["n_ovl1"][:, nt, :], [Cn["n_ovl1"]], [VC], q="pool")
    if debug:
        dump("gl0", gl[0], gl[0][:, 0:255], [128, 255])
        dump("gl1", gl[1], gl[1][:, 0:255], [128, 255])
        dump("kc_sb", kc_sb, kc_sb[0:64, 0:255], [64, 255])
        dump("bias_sb", bias_sb, bias_sb[:], [128, 4])
        for h in range(4):
            dump(f"QA{h}", QA[h], QA[h][0:68, :], [68, S], BF16)
        dump("KS", KS, KS[0:68, :], [68, S], BF16)
        dump("KC", KC, KC[0:68, :], [68, 256], BF16)
        dump("VC", VC, VC[:], [128, 2, 129], BF16)
        dump("Vs", Vs, Vs[:], [128, 32, 65], BF16)
        dump("Gtm", Gtm, Gtm[:], [128, 32, 32])

    PC = [[b.sb(f"PC{h}_{nt}", [128, 512], BF16) for nt in range(2)] for h in range(4)]
    PT = [b.sb(f"PT{i}", [128, 512], BF16) for i in range(5)]
    OUT = b.sb("OUT", [128, 4, 256], F32)
    okt = b.sb("okt", [128, 4, 64], F32)
    addt = b.sb("addt", [128, 4, 64], F32)
    imp = b.sb("imp", [128, 64], F32)
    imp2 = b.sb("imp2", [128, 64], F32)
    imp3 = b.sb("imp3", [128, 64], F32)
    m8a = b.sb("m8a", [128, 8], F32)
    m8b = b.sb("m8b", [128, 8], F32)
    msk = b.sb("msk", [128, 4, 64], F32)
    l4 = b.sb("l4", [128, 4], F32)
    rg4 = b.sb("rg4", [128, 4], F32)
    rl4 = b.sb("rl4", [128, 4], F32)
    YC = [b.sb(f"YC{i}", [128, 512], BF16) for i in range(2)]
    scnt = [0]
    pcnt = [0]
    acnt = [0]

    for qg in range(ngroups or (S // 512)):
        qs = slice(qg * 512, (qg + 1) * 512)
        b.dma(okt[:], Cn["n_ok"][:, 4 * qg:4 * qg + 4, :], [Cn["n_ok"]], [okt])
        b.dma(addt[:], Cn["n_addc"][:, 4 * qg:4 * qg + 4, :], [Cn["n_addc"]], [addt])
        nts = [0] if qg < 4 else [0, 1]
        for h in range(4):
            for nt in nts:
                g = qg if nt == 0 else qg - 4
                masked = (g <= 4)
                sc = psc[scnt[0] % 4]
                scnt[0] += 1
                b.mm(sc[:], KC[0:68, nt * 128:(nt + 1) * 128], QA[h][0:68, qs], True, not masked, [KC, QA[h]], [sc])
                if masked:
                    b.mm(sc[:], identb[:], negc[:, g, :], False, True, [identb, negc], [sc])
                b.op("act", lambda e, sc=sc, h=h, nt=nt: e.activation(out=PC[h][nt][:], in_=sc[:], func=AF.Exp, scale=0.125), [sc], [PC[h][nt]])
        for s in range(4):
            tile = 4 * qg + s
            ss = slice(s * 128, (s + 1) * 128)
            for h in range(4):
                bank = pcA if h < 2 else pcB
                reg = slice((h % 2) * 129, (h % 2) * 129 + 129)
                for i, nt in enumerate(nts):
                    b.mm(bank[:, reg], PC[h][nt][:, ss], VC[:, nt, :], i == 0, i == len(nts) - 1, [PC[h][nt], VC], [bank])
            for bi, bank in enumerate((pcA, pcB)):
                b.op("dve", lambda e, bank=bank, bi=bi: e.tensor_scalar(out=l4[:, 2 * bi:2 * bi + 2],
                                                                       in0=bank[:, 0:258].rearrange("p (h c) -> p h c", c=129)[:, :, 64],
                                                                       scalar1=1e-30, scalar2=None, op0=ALU.max), [bank], [(l4, bi)])
            b.op("dve", lambda e: e.reciprocal(out=rl4[:], in_=l4[:]), [l4], [rl4])
            b.op("dve", lambda e, tile=tile: e.tensor_tensor(out=rg4[:], in0=rl4[:], in1=Gtm[:, tile, 0:12:3], op=ALU.mult), [rl4, Gtm], [rg4])
            for h in range(4):
                bank = pcA if h < 2 else pcB
                o0 = (h % 2) * 129
                b.op("dve", lambda e, bank=bank, o0=o0, h=h, s=s: e.tensor_scalar(out=OUT[:, s, h * 64:(h + 1) * 64], in0=bank[:, o0:o0 + 64],
                                                                             scalar1=rg4[:, h:h + 1], scalar2=None, op0=ALU.mult), [bank, rg4], [(OUT, s)])
                if h == 0:
                    b.op("dve", lambda e, bank=bank, o0=o0: e.tensor_scalar(out=imp[:], in0=bank[:, o0 + 65:o0 + 129], scalar1=rl4[:, 0:1], scalar2=None, op0=ALU.mult),
                         [bank, rl4], [imp])
                else:
                    b.op("dve", lambda e, bank=bank, o0=o0, h=h: e.scalar_tensor_tensor(out=imp[:], in0=bank[:, o0 + 65:o0 + 129], scalar=rl4[:, h:h + 1], in1=imp[:],
                                                                                   op0=ALU.mult, op1=ALU.add), [bank, rl4, imp], [imp])
            b.op("dve", lambda e, s=s: e.tensor_tensor(out=imp2[:], in0=imp[:], in1=okt[:, s, :], op=ALU.mult), [imp, okt], [imp2])
            b.op("dve", lambda e, s=s: e.tensor_tensor(out=imp2[:], in0=imp2[:], in1=addt[:, s, :], op=ALU.add), [imp2, addt], [imp2])
            b.op("dve", lambda e: e.max(out=m8a[:], in_=imp2[:]), [imp2], [m8a])
            b.op("dve", lambda e: e.match_replace(out=imp3[:], in_to_replace=m8a[:], in_values=imp2[:], imm_value=-1e30), [m8a, imp2], [imp3])
            b.op("dve", lambda e: e.max(out=m8b[:], in_=imp3[:]), [imp3], [m8b])
            b.op("dve", lambda e, s=s: e.tensor_scalar(out=msk[:, s, :], in0=imp2[:], scalar1=m8b[:, 7:8], scalar2=None, op0=ALU.is_ge), [imp2, m8b], [(msk, s)])
        if debug and qg == 0:
            dump("NEGM0", NEGM, NEGM[0:64, 0:512], [64, 512], BF16)
            dump("msk", msk, msk[:], [128, 4, 64])
            dump("OUTcmp", OUT, OUT[:], [128, 4, 256])
        for br, (Kt, Vt) in ((2, (KW, Vw)), (1, (KS, Vs))):
            if br == 1:
                Pm = misc()
                for s in range(4):
                    b.tr(Pm[0:64, s * 128:(s + 1) * 128], msk[:, s, :], ident[:], [(msk, s), ident], [Pm])
                b.op("dve", lambda e, Pm=Pm, qs=qs: e.tensor_scalar(out=NEGM[0:64, qs], in0=Pm[0:64, :], scalar1=-1.0, scalar2=NBIG, op0=ALU.add, op1=ALU.mult), [Pm], [(NEGM, qg)])
            for h in range(4):
                acc = pacc[acnt[0] % 2]
                acnt[0] += 1
                if br == 1:
                    kts = list(range(0, 4 * qg + 4))
                else:
                    kts = list(range(max(0, 4 * qg - 4), 4 * qg + 4))
                pv = []
                for kt in kts:
                    for s in range(4):
                        u = 4 * qg + s - kt
                        if u < 0 or (br == 2 and u > 4):
                            continue
                        pv.append((kt, s))
                npv = len(pv)
                pvset = set(pv)
                ipv = [0]

                def emit_sc(kt):
                    sc = psc[scnt[0] % 4]
                    scnt[0] += 1
                    ks = slice(kt * 128, (kt + 1) * 128)
                    vs = [s_ for s_ in range(4) if (kt, s_) in pvset]
                    c0, c1 = vs[0] * 128, (vs[-1] + 1) * 128
                    qcs = slice(qg * 512 + c0, qg * 512 + c1)
                    b.mm(sc[:, c0:c1], Kt[0:68, ks], QA[h][0:68, qcs], True, False, [Kt, QA[h]], [sc])
                    if br == 1:
                        diag = kt >= 4 * qg
                        b.mm(sc[:, c0:c1], eall[:, ks], NEGM[:, qcs], False, not diag, [eall, (NEGM, qg)], [sc])
                        if diag:
                            b.mm(sc[:, c0:c1], identb[:], cmsel[:, kt - 4 * qg, c0:c1], False, True, [identb, cmsel], [sc])
                    else:
                        b.mm(sc[:, c0:c1], identb[:], cmwin[:, kt - 4 * qg + 4, c0:c1], False, True, [identb, cmwin], [sc])
                    P_ = PT[pcnt[0] % 5]
                    pcnt[0] += 1
                    b.op("act", lambda e, sc=sc, P_=P_, c0=c0, c1=c1: e.activation(out=P_[:, c0:c1], in_=sc[:, c0:c1], func=AF.Exp, scale=0.125), [sc], [P_])
                    return P_

                def emit_pv(kt, P_):
                    for s in range(4):
                        if (kt, s) not in pvset:
                            continue
                        b.mm(acc[:, s * 65:(s + 1) * 65], P_[:, s * 128:(s + 1) * 128], Vt[:, kt, :], ipv[0] == 0, ipv[0] == npv - 1, [P_, Vt], [acc])
                        ipv[0] += 1
                LA = 3
                pend = []
                nk = len(kts)
                for i in range(min(LA, nk)):
                    pend.append(emit_sc(kts[i]))
                for i in range(nk):
                    if i + LA < nk:
                        pend.append(emit_sc(kts[i + LA]))
                    emit_pv(kts[i], pend.pop(0))
                if debug and qg == (ngroups or 8) - 1:
                    b.op("dve", lambda e, acc=acc: e.tensor_copy(out=tF[2][:, 0:260], in_=acc[:, 0:260]), [acc], [tF[2]])
                    dump(f"acc_br{br}_h{h}", tF[2], tF[2][:, 0:260], [128, 260])
                accv = acc[:, 0:260].rearrange("p (s c) -> p s c", c=65)
                b.op("dve", lambda e, accv=accv: e.reciprocal(out=rl4[:], in_=accv[:, :, 64]), [acc], [rl4])
                b.op("dve", lambda e, h=h, br=br, qg=qg: e.tensor_tensor(out=rg4[:], in0=rl4[:], in1=Gtm[:, 4 * qg:4 * qg + 4, h * 3 + br], op=ALU.mult), [rl4, Gtm], [rg4])
                for s in range(4):
                    b.op("dve", lambda e, acc=acc, s=s, h=h: e.scalar_tensor_tensor(out=OUT[:, s, h * 64:(h + 1) * 64], in0=acc[:, s * 65:s * 65 + 64], scalar=rg4[:, s:s + 1],
                                                                              in1=OUT[:, s, h * 64:(h + 1) * 64], op0=ALU.mult, op1=ALU.add), [acc, rg4, (OUT, s)], [(OUT, s)])
        if debug and qg == 0:
            dump("OUTfin", OUT, OUT[:], [128, 4, 256])
        for hp in range(2):
            Py = misc()
            for s in range(4):
                b.tr(Py[:, s * 128:(s + 1) * 128], OUT[:, s, hp * 128:(hp + 1) * 128], ident[:], [(OUT, s), ident], [Py])
            b.op("dve" if hp else "act", (lambda e, Py=Py, hp=hp: e.tensor_copy(out=YC[hp][:], in_=Py[:])) if hp else
                 (lambda e, Py=Py, hp=hp: e.activation(out=YC[hp][:], in_=Py[:], func=AF.Copy)), [Py], [YC[hp]])
            b.dma(yT[4 + hp, :, qs], YC[hp][:], [YC[hp]], [(yT, (4 + hp, qg))])
            if debug and qg == 0:
                dump(f"YC{hp}", YC[hp], YC[hp][:], [128, 512], BF16)
    b.end()


import numpy as np

S = 4096


def lay_in(w):
    F = w.shape[1]
    return np.ascontiguousarray(w.reshape(8, 128, F // 128, 128).transpose(2, 1, 0, 3))


def lay_dn(w):
    return np.ascontiguousarray(w.reshape(22, 128, 8, 128).transpose(2, 1, 0, 3))


def lay_vec(v):
    return np.ascontiguousarray(v.reshape(-1, 128).T)


def win_padded(w_in):
    out = np.zeros((1024, 24 * 128), np.float32)
    def put(blk, off, c0, n):
        out[:, blk * 128 + off: blk * 128 + off + n] = w_in[:, c0:c0 + n]
    put(0, 0, 0, 128); put(1, 0, 128, 128)
    B = 256
    put(2, 0, B, 128); put(3, 0, B + 128, 128)
    put(4, 0, B + 256, 128); put(5, 0, B + 384, 128)
    put(6, 0, B + 512, 128); put(7, 0, B + 640, 128)
    put(8, 0, B + 768, 64)
    put(8, 64, B + 768 + 64 + 32, 64)
    put(9, 0, B + 768 + 64, 32)
    C = 256 + 928
    for hh in range(4):
        put(10 + hh, 0, C + 64 * hh, 64)
    put(14, 0, C + 256, 128)
    put(15, 0, C + 384, 128)
    put(16, 0, C + 512, 128)
    put(17, 0, C + 640, 12)
    D = 256 + 928 + 652
    for i in range(6):
        put(18 + i, 0, D + 128 * i, 128)
    return out


def prep_layer(inp, l):
    d = {}
    for nm, key in (("f1", "ffn1"), ("f2", "ffn2")):
        d[f"{nm}_g"] = lay_vec(inp[f"{key}_norm"][l])
        d[f"{nm}_wg"] = lay_in(inp[f"{key}_w_gate"][l])
        d[f"{nm}_wu"] = lay_in(inp[f"{key}_w_up"][l])
        d[f"{nm}_wd"] = lay_dn(inp[f"{key}_w_down"][l])
    d["mix_g"] = lay_vec(inp["mix_norm"][l])
    d["win"] = lay_in(win_padded(inp["w_in"][l]))
    d["wout"] = np.ascontiguousarray(inp["w_out"][l].reshape(8, 128, 8, 128).transpose(2, 1, 0, 3))
    d["cw"] = np.ascontiguousarray(inp["conv_w"][l].reshape(3, 2, 128).transpose(2, 1, 0))
    pw = inp["pool_w"][l]
    pwbd = np.zeros((2, 128, 128), np.float32)
    for j in range(2):
        for gg in range(2):
            pwbd[j, gg * 64:(gg + 1) * 64, gg * 64:(gg + 1) * 64] = pw[2 * j + gg]
    d["pw"] = pwbd
    d["psc"] = lay_vec(inp["pool_scale"][l])
    prep_rwkv(inp, l, d)
    prep_nsa(inp, l, d)
    return d


def consts():
    c = {}
    wins = np.array([2, 4, 8, 16], np.float32)
    w_p = np.zeros((128, 2), np.float32)
    for j in range(2):
        w_p[0:64, j] = wins[2 * j]
        w_p[64:128, j] = wins[2 * j + 1]
    c["pinv"] = (1.0 / w_p).astype(np.float32)
    t = np.arange(16, dtype=np.float32)
    c["pcorr"] = (1.0 / np.minimum(t[None, None, :] + 1, w_p[:, :, None])).astype(np.float32)
    consts_rwkv(c)
    consts_nsa(c)
    return c


def prep_rwkv(inp, l, d):
    mu = inp["rwkv_mu"][l]
    mu8 = np.zeros((128, 8), np.float32)
    for i in range(6):
        mu8[:, i] = mu[i * 128:(i + 1) * 128]
    mu8[0:64, 6] = mu[768:832]
    mu8[64:128, 6] = mu[864:928]
    mu8[0:32, 7] = mu[832:864]
    d["rw_mu8"] = mu8
    vec = np.zeros((128, 2, 7), np.float32)
    names = ["rwkv_w0", "rwkv_a0", "rwkv_k_k", "rwkv_k_a", "rwkv_r_k", "rwkv_ln_w", "rwkv_ln_b"]
    for i, nm in enumerate(names):
        v = inp[nm][l].reshape(256)
        vec[:, 0, i] = v[0:128]
        vec[:, 1, i] = v[128:256]
    d["rw_vec"] = vec
    wup = np.zeros((128, 2, 128), np.float32)
    gup = np.zeros((128, 2, 128), np.float32)
    aup = np.zeros((128, 2, 128), np.float32)
    for hp in range(2):
        wup[0:64, hp, :] = inp["rwkv_w_up"][l][:, hp * 128:(hp + 1) * 128]
        gup[64:128, hp, :] = inp["rwkv_g_up"][l][:, hp * 128:(hp + 1) * 128]
        aup[0:32, hp, :] = inp["rwkv_a_up"][l][:, hp * 128:(hp + 1) * 128]
    d["rw_wup"], d["rw_gup"], d["rw_aup"] = wup, gup, aup


def consts_rwkv(c):
    bones = np.zeros((128, 128), np.float32)
    bones[0:64, 0:64] = 1
    bones[64:128, 64:128] = 1
    c["bones"] = bones
    c["bones64"] = bones / 64.0
    c["ident"] = np.eye(128, dtype=np.float32)
    c["sel2"] = np.concatenate([np.eye(64, dtype=np.float32)] * 2, axis=0)
    rm = np.ones((128, 512), np.float32)
    rm[:, 0::64] = 0
    c["rmask"] = rm
    j = np.arange(64)[:, None]
    t = np.arange(64)[None, :]
    strict = (j < t).astype(np.float32)
    incl = (j <= t).astype(np.float32)
    def bd(m):
        o = np.zeros((128, 128), np.float32)
        o[0:64, 0:64] = m
        o[64:128, 64:128] = m
        return o
    c["LPmask"] = np.concatenate([-bd(strict), bd(incl), bd(strict), bd(incl)], axis=1)
    c["Lamask"] = -bd((j > t).astype(np.float32))


def prep_nsa(inp, l, d):
    qn = np.zeros((128, 1), np.float32)
    qn[0:64, 0] = inp["nsa_q_norm"][l]
    d["ns_qnw"] = qn
    kn = np.zeros((128, 3), np.float32)
    kn[0:64, :] = inp["nsa_k_norm"][l].T
    d["ns_knw"] = kn
    w1 = np.zeros((128, 32, 128), np.float32)
    w1[0:64] = inp["nsa_cmp_k_w1"][l].reshape(32, 64, 128).transpose(1, 0, 2)
    w1[64:128] = inp["nsa_cmp_v_w1"][l].reshape(32, 64, 128).transpose(1, 0, 2)
    d["ns_w1"] = w1
    pos = inp["nsa_cmp_pos"][l]
    p2 = np.zeros((128, 32, 2), np.float32)
    p2[0:64, :, 0] = pos.T
    p2[0:64, :, 1] = pos.T
    p2[64:128] = p2[0:64]
    d["ns_pos2"] = p2
    w2 = np.zeros((128, 2, 64), np.float32)
    w2[:, 0, :] = inp["nsa_cmp_k_w2"][l]
    w2[:, 1, :] = inp["nsa_cmp_v_w2"][l]
    d["ns_w2"] = w2


def consts_nsa(c):
    BIG = 30000.0
    key = np.arange(S)
    c["n_eall"] = (key[None, :] // 64 == np.arange(64)[:, None]).astype(np.float32)
    pk = np.arange(128)[:, None]
    tq = np.arange(512)[None, :]
    c["n_cmsel"] = np.stack([np.where(tq - pk - 128 * r >= 0, 0.0, -BIG) for r in range(4)]).astype(np.float32)
    cw = []
    for r in range(8):
        dd = tq - pk - (r - 4) * 128
        cw.append(np.where((dd >= 0) & (dd < 512), 0.0, -BIG))
    c["n_cmwin"] = np.stack(cw).astype(np.float32)
    c["n_negc"] = np.stack([np.where(tq >= 16 * pk + 31 - 512 * g, 0.0, -BIG) for g in range(5)]).astype(np.float32)
    t = np.arange(S)
    cur = t // 64
    j = np.arange(64)
    ok = (j[None, :] <= cur[:, None]).astype(np.float32)
    forced = ((j[None, :] == 0) | (j[None, :] == cur[:, None]) | (j[None, :] == cur[:, None] - 1)).astype(np.float32)
    addc = forced * 1e4 * ok + (ok - 1.0)
    c["n_ok"] = np.ascontiguousarray(ok.reshape(32, 128, 64).transpose(1, 0, 2))
    c["n_addc"] = np.ascontiguousarray(addc.reshape(32, 128, 64).transpose(1, 0, 2)).astype(np.float32)
    slopes = 2.0 ** (-8.0 * np.arange(1, 5) / 4)
    a_t, b_t = t // 64, t % 64
    qaug = np.zeros((4, 4, S), np.float32)
    for h in range(4):
        s8 = 8.0 * slopes[h]
        qaug[h, 0] = -s8 * 64 * a_t
        qaug[h, 1] = -s8 * b_t
        qaug[h, 2] = s8
        qaug[h, 3] = s8
    c["n_qaug"] = qaug
    kaug = np.zeros((4, S), np.float32)
    kaug[0] = 1
    kaug[1] = 1
    kaug[2] = 64 * a_t
    kaug[3] = b_t
    c["n_kaug"] = kaug
    n = np.arange(256)
    be = 16 * n + 31
    kc = np.zeros((4, 256), np.float32)
    kc[0] = 1
    kc[1] = 1
    kc[2] = 64 * (be // 64)
    kc[3] = be % 64
    c["n_kcaug"] = kc
    ovl = ((16 * n[:, None] <= 64 * j[None, :] + 63) & (16 * n[:, None] + 31 >= 64 * j[None, :])).astype(np.float32)
    o1 = np.concatenate([np.ones((256, 1), np.float32), ovl], axis=1)
    o1[255] = 0
    c["n_ovl1"] = np.ascontiguousarray(o1.reshape(2, 128, 65).transpose(1, 0, 2))


HAVE_RWKV = True
HAVE_NSA = True
_CACHE = {}


def build_program(layer_shapes, const_shapes):
    b = Builder()
    xin = b.dram("xin", [8, 128, S], F32, "ExternalInput")
    xout = b.dram("xout", [8, 128, S], F32, "ExternalOutput")
    xT = b.dram("xT", [8, 128, S], F32)
    pT = b.dram("pT", [NBLK, 128, S], F32)
    yT = b.dram("yT", [8, 128, S], BF16)
    D = []
    for l in range(2):
        D.append({k: b.dram(f"L{l}_{k}", list(shp), F32, "ExternalInput") for k, shp in layer_shapes.items()})
    Cn = {k: b.dram(f"C_{k}", list(shp), F32, "ExternalInput") for k, shp in const_shapes.items()}
    for l in range(2):
        d = D[l]
        ffn_phase(b, xT, d["f1_g"], d["f1_wg"], d["f1_wu"], d["f1_wd"], xsrc=(xin if l == 0 else None))
        inproj_phase(b, xT, d["mix_g"], d["win"], pT)
        conv_pool_phase(b, pT, yT, d["cw"], d["pw"], d["psc"], Cn["pinv"], Cn["pcorr"])
        if HAVE_RWKV:
            rwkv_phase(b, pT, yT, d, Cn)
        if HAVE_NSA:
            nsa_phase(b, pT, yT, d, Cn)
        outproj_phase(b, xT, yT, d["wout"])
        ffn_phase(b, xT, d["f2_g"], d["f2_wg"], d["f2_wu"], d["f2_wd"], xdst=(xout if l == 1 else None))
    return b


def kernel(**inputs):
    inp = {k: np.asarray(v) for k, v in inputs.items()}
    layers = [prep_layer(inp, l) for l in range(2)]
    cn = consts()
    b = build_program({k: v.shape for k, v in layers[0].items()}, {k: v.shape for k, v in cn.items()})
    shared = {}
    for l in range(2):
        for k, v in layers[l].items():
            shared[f"L{l}_{k}"] = v
    for k, v in cn.items():
        shared[f"C_{k}"] = v
    x = inp["x"]
    in_maps = []
    for c in range(8):
        m = dict(shared)
        m["xin"] = np.ascontiguousarray(x[c].T.reshape(8, 128, S))
        in_maps.append(m)
    res = run_bass_kernel_spmd(b.nc, in_maps, core_ids=list(range(8)))
    out = np.stack([res.results[c]["xout"].reshape(1024, S).T for c in range(8)], axis=0)
    return np.ascontiguousarray(out.astype(np.float32))
```
